# Optimizing a Trainium2 kernel written in Bass

```python
import math
import jax, jax.numpy as jnp
from jax import lax
import numpy as np

D_MODEL = 1024
BATCH = 8
SEQ = 4096
DEPTH = 4

GRID_W = 64
CTX_LEN = 256
N_EVEN = (DEPTH + 1) // 2
N_ODD = DEPTH // 2
S5_WIDTH = D_MODEL // 2
S5_GROUP = 16
S5_GROUPS = S5_WIDTH // S5_GROUP
S5_STATE = 64
NA_WIDTH = D_MODEL - S5_WIDTH
NA_HEAD_DIM = 64
NA_HEADS = NA_WIDTH // NA_HEAD_DIM
NA_WIN_R = 8
NA_WIN_C = 16
IN_WIDTH = S5_WIDTH + 3 * NA_WIDTH
MIX_WIDTH = S5_WIDTH + NA_WIDTH
D_FF = 4 * D_MODEL
N_MOD = 6
EPS = 1e-6
NEG_INF = -1e30

kernel_name = 'hybrid_s5_natten_fnet_dit_trunk'


def rms_norm(x, g):
    xf = x.astype(jnp.float32)
    y = xf * lax.rsqrt(jnp.mean(jnp.square(xf), axis=-1, keepdims=True) + EPS)
    return (y * g.astype(jnp.float32)).astype(x.dtype)


def modulate(x, shift, scale):
    return x * (1.0 + scale[:, None, :]) + shift[:, None, :]


def s5_discretise(lam_re, lam_im, log_dt, b_re, b_im):
    f32 = jnp.float32
    lam_re = jnp.minimum(lam_re.astype(f32), -1e-4)
    lam_im = lam_im.astype(f32)
    dt = jnp.exp(log_dt.astype(f32))[:, None]
    mag = jnp.exp(lam_re * dt)
    a_re = mag * jnp.cos(lam_im * dt)
    a_im = mag * jnp.sin(lam_im * dt)
    den = lam_re * lam_re + lam_im * lam_im
    num_re = a_re - 1.0
    f_re = (num_re * lam_re + a_im * lam_im) / den
    f_im = (a_im * lam_re - num_re * lam_im) / den
    b_re = b_re.astype(f32)
    b_im = b_im.astype(f32)
    bb_re = f_re[..., None] * b_re - f_im[..., None] * b_im
    bb_im = f_re[..., None] * b_im + f_im[..., None] * b_re
    return a_re, a_im, bb_re, bb_im


def _ssm_combine(left, right):
    a1r, a1i, b1r, b1i = left
    a2r, a2i, b2r, b2i = right
    ar = a2r * a1r - a2i * a1i
    ai = a2r * a1i + a2i * a1r
    br = a2r * b1r - a2i * b1i + b2r
    bi = a2r * b1i + a2i * b1r + b2i
    return ar, ai, br, bi


def s5_scan(u_tm, disc, s0):
    a_re, a_im, bb_re, bb_im = disc
    L = u_tm.shape[0]
    bu_re = jnp.einsum('lbgh,gph->lbgp', u_tm, bb_re)
    bu_im = jnp.einsum('lbgh,gph->lbgp', u_tm, bb_im)
    shape = (L, 1) + a_re.shape
    ar = jnp.broadcast_to(a_re, shape)
    ai = jnp.broadcast_to(a_im, shape)
    cum_re, cum_im, s_re, s_im = lax.associative_scan(_ssm_combine, (ar, ai, bu_re, bu_im), axis=0)
    if s0 is not None:
        s0_re, s0_im = s0
        s_re = s_re + cum_re * s0_re - cum_im * s0_im
        s_im = s_im + cum_re * s0_im + cum_im * s0_re
    return s_re, s_im


def s5_readout(s_re, s_im, c_re, c_im):
    return (jnp.einsum('lbgp,ghp->lbgh', s_re, c_re.astype(jnp.float32))
            - jnp.einsum('lbgp,ghp->lbgh', s_im, c_im.astype(jnp.float32)))


def s5_glu(y, w_glu):
    g = jax.nn.gelu(y)
    return g * jax.nn.sigmoid(g @ w_glu.astype(jnp.float32))


def s5_mixer(u_lat, u_ctx, lam_re, lam_im, log_dt, b_re, b_im, c_re, c_im, d_skip, w_glu, ctx_out):
    B, L, W = u_lat.shape
    Lc = u_ctx.shape[1]
    f32 = jnp.float32
    ul_f = u_lat.astype(f32)
    uc_f = u_ctx.astype(f32)
    ul = jnp.transpose(ul_f.reshape(B, L, S5_GROUPS, S5_GROUP), (1, 0, 2, 3))
    uc = jnp.transpose(uc_f.reshape(B, Lc, S5_GROUPS, S5_GROUP), (1, 0, 2, 3))
    d = d_skip.astype(f32)
    y_lat = d * ul_f
    y_ctx = d * uc_f if ctx_out else None
    for direction in range(2):
        rev = direction == 1
        disc = s5_discretise(lam_re[direction], lam_im[direction], log_dt[direction],
                             b_re[direction], b_im[direction])
        ucd = uc[::-1] if rev else uc
        uld = ul[::-1] if rev else ul
        sc_re, sc_im = s5_scan(ucd, disc, None)
        sl_re, sl_im = s5_scan(uld, disc, (sc_re[-1], sc_im[-1]))
        yl = s5_readout(sl_re, sl_im, c_re[direction], c_im[direction])
        yl = yl[::-1] if rev else yl
        y_lat = y_lat + jnp.transpose(yl, (1, 0, 2, 3)).reshape(B, L, W)
        if ctx_out:
            yc = s5_readout(sc_re, sc_im, c_re[direction], c_im[direction])
            yc = yc[::-1] if rev else yc
            y_ctx = y_ctx + jnp.transpose(yc, (1, 0, 2, 3)).reshape(B, Lc, W)
    out_lat = s5_glu(y_lat, w_glu).astype(u_lat.dtype)
    out_ctx = s5_glu(y_ctx, w_glu).astype(u_ctx.dtype) if ctx_out else None
    return out_lat, out_ctx


def na_mixer(q_l, k_l, v_l, q_c, k_c, v_c, rpb, ctx_out):
    f32 = jnp.float32
    B, L, NH, DH = q_l.shape
    rows = L // GRID_W
    win_r = min(NA_WIN_R, rows)
    win_c = NA_WIN_C
    scale = DH ** -0.5
    q_rows = jnp.arange(rows)
    r0 = jnp.clip(q_rows - win_r // 2, 0, rows - win_r)
    key_rows = r0[:, None] + jnp.arange(win_r)[None, :]
    cols = jnp.arange(GRID_W)
    c0 = jnp.clip(cols - win_c // 2, 0, GRID_W - win_c)
    col_ok = (cols[None, :] >= c0[:, None]) & (cols[None, :] < c0[:, None] + win_c)
    mask = jnp.broadcast_to(col_ok[:, None, :], (GRID_W, win_r, GRID_W)).reshape(GRID_W, win_r * GRID_W)
    dr = key_rows - q_rows[:, None] + (NA_WIN_R - 1)
    dc = jnp.clip(cols[None, :] - cols[:, None], -(win_c - 1), win_c - 1) + (win_c - 1)
    bias = rpb.astype(f32)[:, dr[:, None, :, None], dc[None, :, None, :]]
    bias = bias.reshape(NH, rows, GRID_W, win_r * GRID_W)
    qg = q_l.reshape(B, rows, GRID_W, NH, DH)
    k_blk = k_l.reshape(B, rows, GRID_W, NH, DH)[:, key_rows].reshape(B, rows, win_r * GRID_W, NH, DH)
    v_blk = v_l.reshape(B, rows, GRID_W, NH, DH)[:, key_rows].reshape(B, rows, win_r * GRID_W, NH, DH)
    s_lat = jnp.einsum('brqhd,brkhd->bhrqk', qg, k_blk).astype(f32) * scale + bias[None]
    s_lat = jnp.where(mask, s_lat, NEG_INF)
    s_ctx = jnp.einsum('brqhd,bkhd->bhrqk', qg, k_c).astype(f32) * scale
    p = jax.nn.softmax(jnp.concatenate([s_lat, s_ctx], axis=-1), axis=-1).astype(v_l.dtype)
    n_lat = win_r * GRID_W
    o = (jnp.einsum('bhrqk,brkhd->brqhd', p[..., :n_lat], v_blk)
         + jnp.einsum('bhrqk,bkhd->brqhd', p[..., n_lat:], v_c))
    out_lat = o.reshape(B, L, NH * DH)
    out_ctx = None
    if ctx_out:
        Lc = q_c.shape[1]
        s_cc = jnp.einsum('bqhd,bkhd->bhqk', q_c, k_c).astype(f32) * scale
        p_cc = jax.nn.softmax(s_cc, axis=-1).astype(v_c.dtype)
        out_ctx = jnp.einsum('bhqk,bkhd->bqhd', p_cc, v_c).reshape(B, Lc, NH * DH)
    return out_lat, out_ctx


def split_heads(z):
    B, L, _ = z.shape
    u = z[..., :S5_WIDTH]
    q, k, v = jnp.split(z[..., S5_WIDTH:], 3, axis=-1)
    shp = (B, L, NA_HEADS, NA_HEAD_DIM)
    return u, q.reshape(shp), k.reshape(shp), v.reshape(shp)


def even_mixer(hl, hc, w_in, w_out, lam_re, lam_im, log_dt, b_re, b_im, c_re, c_im, d_skip, w_glu, rpb, ctx_out):
    u_l, q_l, k_l, v_l = split_heads(hl @ w_in)
    u_c, q_c, k_c, v_c = split_heads(hc @ w_in)
    s5_l, s5_c = s5_mixer(u_l, u_c, lam_re, lam_im, log_dt, b_re, b_im, c_re, c_im, d_skip, w_glu, ctx_out)
    na_l, na_c = na_mixer(q_l, k_l, v_l, q_c, k_c, v_c, rpb, ctx_out)
    out_l = jnp.concatenate([s5_l, na_l], axis=-1) @ w_out
    out_c = jnp.concatenate([s5_c, na_c], axis=-1) @ w_out if ctx_out else None
    return out_l, out_c


def fourier_mix(h, w_f):
    hf = jnp.fft.fft2(h.astype(jnp.float32), axes=(1, 2), norm='ortho').real
    return hf.astype(h.dtype) @ w_f


def sqrelu_mlp(x, w1, w2):
    return jnp.square(jax.nn.relu(x @ w1)) @ w2


def setup_inputs(seed: int = 0) -> dict:
    key = jax.random.key(seed)
    ks = jax.random.split(key, 24)
    f32 = jnp.float32

    def nrm(k, shape, std):
        return jax.random.normal(k, shape, f32) * std

    G, P, H = S5_GROUPS, S5_STATE, S5_GROUP
    x = nrm(ks[0], (BATCH, SEQ, D_MODEL), 1.0)
    c = nrm(ks[1], (BATCH, D_MODEL), 1.0)
    ctx = nrm(ks[2], (BATCH, CTX_LEN, D_MODEL), 1.0)
    c_ctx = nrm(ks[3], (D_MODEL,), 1.0)
    w_mod = nrm(ks[4], (DEPTH, D_MODEL, N_MOD * D_MODEL), 0.5 * D_MODEL ** -0.5)
    b_mod = nrm(ks[5], (DEPTH, N_MOD * D_MODEL), 0.02)
    norm_g = 1.0 + nrm(ks[6], (DEPTH, 4, D_MODEL), 0.05)
    w_in = nrm(ks[7], (N_EVEN, D_MODEL, IN_WIDTH), D_MODEL ** -0.5)
    w_out_even = nrm(ks[8], (N_EVEN, MIX_WIDTH, D_MODEL), MIX_WIDTH ** -0.5)
    s5_lam_re = -0.5 + nrm(ks[9], (N_EVEN, 2, G, P), 0.01)
    s5_lam_im = math.pi * jnp.arange(P, dtype=f32) + nrm(ks[10], (N_EVEN, 2, G, P), 0.01)
    s5_log_dt = jax.random.uniform(ks[11], (N_EVEN, 2, G), f32, minval=math.log(1e-3), maxval=math.log(1e-1))
    s5_b_re = nrm(ks[12], (N_EVEN, 2, G, P, H), (2.0 * H) ** -0.5)
    s5_b_im = nrm(ks[13], (N_EVEN, 2, G, P, H), (2.0 * H) ** -0.5)
    s5_c_re = nrm(ks[14], (N_EVEN, 2, G, H, P), (2.0 * P) ** -0.5)
    s5_c_im = nrm(ks[15], (N_EVEN, 2, G, H, P), (2.0 * P) ** -0.5)
    s5_d = nrm(ks[16], (N_EVEN, S5_WIDTH), 1.0)
    s5_w_glu = nrm(ks[17], (N_EVEN, S5_WIDTH, S5_WIDTH), S5_WIDTH ** -0.5)
    na_rpb = nrm(ks[18], (N_EVEN, NA_HEADS, 2 * NA_WIN_R - 1, 2 * NA_WIN_C - 1), 0.1)
    w_fourier = nrm(ks[19], (N_ODD, D_MODEL, D_MODEL), D_MODEL ** -0.5)
    w_ff1 = nrm(ks[20], (DEPTH, D_MODEL, D_FF), D_MODEL ** -0.5)
    w_ff2 = nrm(ks[21], (DEPTH, D_FF, D_MODEL), D_FF ** -0.5)
    return {'x': x, 'c': c, 'ctx': ctx, 'c_ctx': c_ctx, 'w_mod': w_mod, 'b_mod': b_mod, 'norm_g': norm_g,
            'w_in': w_in, 'w_out_even': w_out_even, 's5_lam_re': s5_lam_re, 's5_lam_im': s5_lam_im,
            's5_log_dt': s5_log_dt, 's5_b_re': s5_b_re, 's5_b_im': s5_b_im, 's5_c_re': s5_c_re,
            's5_c_im': s5_c_im, 's5_d': s5_d, 's5_w_glu': s5_w_glu, 'na_rpb': na_rpb,
            'w_fourier': w_fourier, 'w_ff1': w_ff1, 'w_ff2': w_ff2}


def reference(x, c, ctx, c_ctx, w_mod, b_mod, norm_g, w_in, w_out_even, s5_lam_re, s5_lam_im, s5_log_dt,
              s5_b_re, s5_b_im, s5_c_re, s5_c_im, s5_d, s5_w_glu, na_rpb, w_fourier, w_ff1, w_ff2):
    last_ctx_layer = 2 * ((DEPTH - 1) // 2)
    h = x
    s = ctx
    c_act = jax.nn.silu(c)
    cc_act = jax.nn.silu(c_ctx)[None, :]
    for layer in range(DEPTH):
        need_ctx = layer <= last_ctx_layer
        upd_ctx = layer < last_ctx_layer
        mod_l = c_act @ w_mod[layer] + b_mod[layer]
        sh1, sc1, g1, sh2, sc2, g2 = jnp.split(mod_l, N_MOD, axis=-1)
        hl = modulate(rms_norm(h, norm_g[layer, 0]), sh1, sc1)
        hc = None
        if need_ctx:
            mod_c = cc_act @ w_mod[layer] + b_mod[layer]
            csh1, csc1, cg1, csh2, csc2, cg2 = jnp.split(mod_c, N_MOD, axis=-1)
            hc = modulate(rms_norm(s, norm_g[layer, 0]), csh1, csc1)
        if layer % 2 == 0:
            e = layer // 2
            out_l, out_c = even_mixer(hl, hc, w_in[e], w_out_even[e], s5_lam_re[e], s5_lam_im[e], s5_log_dt[e],
                                      s5_b_re[e], s5_b_im[e], s5_c_re[e], s5_c_im[e], s5_d[e], s5_w_glu[e],
                                      na_rpb[e], upd_ctx)
        else:
            o = layer // 2
            out_l = fourier_mix(hl, w_fourier[o])
            out_c = fourier_mix(hc, w_fourier[o]) if upd_ctx else None
        h = h + g1[:, None, :] * rms_norm(out_l, norm_g[layer, 1])
        hf = modulate(rms_norm(h, norm_g[layer, 2]), sh2, sc2)
        h = h + g2[:, None, :] * rms_norm(sqrelu_mlp(hf, w_ff1[layer], w_ff2[layer]), norm_g[layer, 3])
        if upd_ctx:
            s = s + cg1[:, None, :] * rms_norm(out_c, norm_g[layer, 1])
            sf = modulate(rms_norm(s, norm_g[layer, 2]), csh2, csc2)
            s = s + cg2[:, None, :] * rms_norm(sqrelu_mlp(sf, w_ff1[layer], w_ff2[layer]), norm_g[layer, 3])
    return h
```

```python
import contextlib
import math
import os

import numpy as np
import concourse.bass as bass
import concourse.mybir as mybir
from concourse.bass_utils import run_bass_kernel_spmd

F32 = mybir.dt.float32
BF16 = mybir.dt.bfloat16
I32 = mybir.dt.int32
AF = mybir.ActivationFunctionType
ALU = mybir.AluOpType

D = 1024
L = 4096
LC = 256
DFF = 4096
DEPTH = 4
EPS = 1e-6
ENGS = ["pe", "act", "dve", "pool", "sp"]


class Res:
    __slots__ = ("name", "lastw", "readers")

    def __init__(self, name="r"):
        self.name = name
        self.lastw = None
        self.readers = []


class Op:
    __slots__ = ("eng", "fn", "deps", "is_dma", "signal", "sigval", "dslot", "dval")


class Prog:
    NDSLOT = 8

    def __init__(self, nc):
        self.nc = nc
        self.ops = []
        self.per = {e: [] for e in ENGS}
        self.ndma = {e: 0 for e in ENGS}
        self.pending_barrier = {e: None for e in ENGS}
        self.dmas_since_barrier = []

    def barrier(self):
        deps = []
        for e in ENGS:
            for op in reversed(self.per[e]):
                if not op.is_dma:
                    deps.append(op)
                    break
        deps.extend(self.dmas_since_barrier)
        self.dmas_since_barrier = []
        for e in ENGS:
            old = self.pending_barrier[e]
            self.pending_barrier[e] = (old or []) + deps

    def _add(self, eng, fn, reads, writes, is_dma):
        op = Op()
        op.eng = eng
        op.fn = fn
        op.is_dma = is_dma
        op.signal = False
        deps = set()
        for r in reads:
            if r.lastw is not None:
                deps.add(r.lastw)
        for w in writes:
            if w.lastw is not None:
                deps.add(w.lastw)
            for rd in w.readers:
                deps.add(rd)
        if self.pending_barrier[eng] is not None:
            deps.update(self.pending_barrier[eng])
            self.pending_barrier[eng] = None
        deps.discard(op)
        op.deps = [d for d in deps if not (eng == "pe" and d.eng == "pe" and not d.is_dma and not is_dma)]
        for r in reads:
            r.readers.append(op)
        for w in writes:
            w.lastw = op
            w.readers = []
        self.per[eng].append(op)
        self.ops.append(op)
        if is_dma:
            k = self.ndma[eng]
            self.ndma[eng] += 1
            op.dslot = k % self.NDSLOT
            op.dval = 16 * (k // self.NDSLOT + 1)
            self.dmas_since_barrier.append(op)
        return op

    def op(self, eng, fn, reads=(), writes=()):
        return self._add(eng, fn, list(reads), list(writes), False)

    def dma(self, eng, out, in_, reads=(), writes=()):
        return self._add(eng, lambda e: e.dma_start(out=out, in_=in_), list(reads), list(writes), True)

    def emit(self):
        nc = self.nc
        for op in self.ops:
            for d in op.deps:
                if not d.is_dma:
                    d.signal = True
        for e in ENGS:
            cnt = 0
            for op in self.per[e]:
                if not op.is_dma and op.signal:
                    cnt += 1
                    op.sigval = cnt
        with contextlib.ExitStack() as st:
            sems = {e: st.enter_context(nc.semaphore("s_" + e)) for e in ENGS}
            dsems = {e: [st.enter_context(nc.semaphore("d_%s%d" % (e, i))) for i in range(self.NDSLOT)]
                     for e in ENGS if self.ndma[e] > 0}
            block = st.enter_context(nc.Block())

            def run_engine(ename, eng):
                seen = {}
                dseen = {}
                for op in self.per[ename]:
                    for d in op.deps:
                        if d.is_dma:
                            key = (d.eng, d.dslot)
                            if dseen.get(key, 0) < d.dval:
                                eng.wait_ge(dsems[d.eng][d.dslot], d.dval)
                                dseen[key] = d.dval
                        else:
                            if seen.get(d.eng, 0) < d.sigval:
                                eng.wait_ge(sems[d.eng], d.sigval)
                                seen[d.eng] = d.sigval
                    if op.is_dma:
                        key = (ename, op.dslot)
                        if op.dval > 16 and dseen.get(key, 0) < op.dval - 16:
                            eng.wait_ge(dsems[ename][op.dslot], op.dval - 16)
                            dseen[key] = op.dval - 16
                        ins = op.fn(eng)
                        ins.then_inc(dsems[ename][op.dslot], 16)
                    else:
                        ins = op.fn(eng)
                        if op.signal:
                            ins.then_inc(sems[ename], 1)
                if ename in dsems:
                    k = self.ndma[ename]
                    for s in range(self.NDSLOT):
                        n = (k - s + self.NDSLOT - 1) // self.NDSLOT
                        if n > 0 and dseen.get((ename, s), 0) < 16 * n:
                            eng.wait_ge(dsems[ename][s], 16 * n)

            block.tensor(lambda e: run_engine("pe", e))
            block.scalar(lambda e: run_engine("act", e))
            block.vector(lambda e: run_engine("dve", e))
            block.gpsimd(lambda e: run_engine("pool", e))
            block.sync(lambda e: run_engine("sp", e))


class Tile:
    __slots__ = ("ap", "res")

    def __init__(self, ap, res=None):
        self.ap = ap
        self.res = res or Res()

    def __getitem__(self, k):
        return self.ap[k]


class Arena:
    def __init__(self, tensor, ncols):
        self.t = tensor
        self.ncols = ncols
        self.top = 0
        self.base = 0

    def alloc(self, dtype, shape):
        n = 1
        for s in shape:
            n *= s
        units = n * (2 if dtype in (F32, I32) else 1)
        units = (units + 31) // 32 * 32
        assert self.top + units <= self.ncols, ("arena overflow", self.top, units, self.ncols)
        ap = self.t[:, self.top:self.top + n * (2 if dtype in (F32, I32) else 1)]
        self.top += units
        if dtype != BF16:
            ap = ap.bitcast(dtype)
        if len(shape) == 2:
            ap = ap.rearrange("p (a b) -> p a b", a=shape[0])
        elif len(shape) == 3:
            ap = ap.rearrange("p (a b c) -> p a b c", a=shape[0], b=shape[1])
        return Tile(ap)

    def mark_persistent(self):
        self.base = self.top

    def reset(self):
        self.top = self.base


class Builder:
    def __init__(self, step, dbg=""):
        self.dbg = dbg
        self.step = step
        kind, l = step
        nc = bass.Bass("TRN2", target_bir_lowering=False)
        self.nc = nc
        self.P = Prog(nc)

        def din(name, shape, dt=F32):
            return nc.dram_tensor(name, list(shape), dt, kind="ExternalInput").ap()

        self.x = din("x", [L, D])
        self.c = din("c", [D])
        self.ctx = din("ctx", [LC, D])
        self.c_ctx = din("c_ctx", [D])
        self.w_mod = din("w_mod", [D, 6 * D])
        self.b_mod = din("b_mod", [6 * D])
        self.norm_g = din("norm_g", [4, D])
        if kind == "mlp":
            self.w_ff1 = din("w_ff1", [D, DFF])
            self.w_ff2 = din("w_ff2", [DFF, D])
        if kind == "odd":
            self.w_fourier = din("w_fourier", [D, D])
        if kind in ("evenA", "evenB"):
            self.w_in = din("w_in", [D, 2048])
        if kind == "evenA":
            self.s5_lam_re = din("s5_lam_re", [2, 32, 64])
            self.s5_lam_im = din("s5_lam_im", [2, 32, 64])
            self.s5_log_dt = din("s5_log_dt", [2, 32])
            self.s5_b_re = din("s5_b_re", [2, 32, 64, 16])
            self.s5_b_im = din("s5_b_im", [2, 32, 64, 16])
            self.s5_c_re = din("s5_c_re", [2, 32, 16, 64])
            self.s5_c_im = din("s5_c_im", [2, 32, 16, 64])
            self.s5_d = din("s5_d", [512])
            self.s5_w_glu = din("s5_w_glu", [512, 512])
            self.s5T_out = nc.dram_tensor("s5T", [512, L + LC], BF16, kind="ExternalOutput").ap()
        if kind == "evenB":
            self.w_out_even = din("w_out_even", [D, D])
            self.na_rpb = din("na_rpb", [8, 15, 31])
            self.s5T_in = din("s5T", [512, L + LC], BF16)
        if kind != "evenA":
            self.out = nc.dram_tensor("out", [L, D], F32, kind="ExternalOutput").ap()
            self.sout = nc.dram_tensor("sout", [LC, D], F32, kind="ExternalOutput").ap()
        self.out_res = [Res("out%d" % i) for i in range(L // 128)]
        if kind == "odd":
            self.Ctab = nc.dram_tensor("Ctab", [L, L], BF16).ap()
            self.Stab = nc.dram_tensor("Stab", [L, L], BF16).ap()
            self.tab_res = Res("tab")
            self.Yc = nc.dram_tensor("Yc", [D, L], BF16).ap()
            self.Ys = nc.dram_tensor("Ys", [D, L], BF16).ap()
            self.Y_res = [Res("Y%d" % i) for i in range(16)]

    def mm(self, out, lhsT, rhs, start, stop, reads, writes):
        self.P.op("pe", lambda e: e.matmul(out, lhsT=lhsT, rhs=rhs, start=start, stop=stop), reads, writes)

    def tr(self, out, in_, ident, reads, writes):
        self.P.op("pe", lambda e: e.transpose(out, in_, ident), reads, writes)

    def act(self, out, in_, func, reads, writes, bias=None, scale=None, accum_out=None, eng="act"):
        kw = {}
        if bias is not None:
            kw["bias"] = bias
        if scale is not None:
            kw["scale"] = scale
        if accum_out is not None:
            kw["accum_out"] = accum_out
        self.P.op(eng, lambda e: e.activation(out=out, in_=in_, func=func, **kw), reads, writes)

    def tt(self, eng, out, in0, in1, op, reads, writes):
        self.P.op(eng, lambda e: e.tensor_tensor(out=out, in0=in0, in1=in1, op=op), reads, writes)

    def ts(self, eng, out, in0, s1, s2, op0, op1, reads, writes):
        if op1 is None:
            self.P.op(eng, lambda e: e.tensor_single_scalar(out=out, in_=in0, scalar=s1, op=op0), reads, writes)
        else:
            self.P.op(eng, lambda e: e.tensor_scalar(out=out, in0=in0, scalar1=s1, scalar2=s2, op0=op0, op1=op1), reads, writes)

    def stt(self, eng, out, in0, scalar, in1, op0, op1, reads, writes):
        self.P.op(eng, lambda e: e.scalar_tensor_tensor(out=out, in0=in0, scalar=scalar, in1=in1, op0=op0, op1=op1), reads, writes)

    def cp(self, eng, out, in_, reads, writes):
        if eng == "act":
            self.P.op(eng, lambda e: e.copy(out=out, in_=in_), reads, writes)
        else:
            self.P.op(eng, lambda e: e.tensor_copy(out=out, in_=in_), reads, writes)

    def memset(self, eng, ap, val, writes):
        self.P.op(eng, lambda e: e.memset(ap, val), [], writes)

    def build(self):
        nc = self.nc
        P = self.P
        with contextlib.ExitStack() as st:
            NCOLS = 106000
            arena_t = st.enter_context(nc.sbuf_tensor("arena", [128, NCOLS], BF16))
            self.A = Arena(arena_t, NCOLS)
            psall = st.enter_context(nc.psum_tensor("psall", [128, 4096], F32))
            self.psall = psall
            self.bank = [Tile(psall[:, 512 * i:512 * (i + 1)], Res("bank%d" % i)) for i in range(8)]
            self.setup_consts()
            kind, l = self.step
            self.mod_phase(l)
            if kind == "mlp":
                self.mlp_phase(l)
            elif kind == "odd":
                self.gen_tables()
                self.odd_mixer(l)
            elif kind == "evenA":
                self.even_a(l)
            elif kind == "evenB":
                self.even_b(l)
            if kind != "evenA":
                P.barrier()
                for i in range(2):
                    P.dma("sp", self.sout[128 * i:128 * (i + 1), :], self.sctx.ap[:, i, :], [self.sctx_res[i]], [Res()])
            P.emit()
        return nc

    def setup_consts(self):
        A = self.A
        P = self.P
        it = A.alloc(I32, [128])
        self.ident_f = A.alloc(F32, [128])
        self.ident_b = A.alloc(BF16, [128])
        P.op("pool", lambda e: e.iota(it.ap, [[1, 128]], base=0, channel_multiplier=-1), [], [it.res])
        self.cp("dve", self.ident_f.ap, it.ap, [it.res], [self.ident_f.res])
        self.ts("dve", self.ident_f.ap, self.ident_f.ap, 0.0, None, ALU.is_equal, None, [self.ident_f.res], [self.ident_f.res])
        self.cp("dve", self.ident_b.ap, self.ident_f.ap, [self.ident_f.res], [self.ident_b.res])
        self.cact2 = A.alloc(F32, [8, 2])
        craw = A.alloc(F32, [2, 128])
        P.dma("sp", craw.ap[0:8, 0, :], self.c.rearrange("(k p) -> k p", p=128), [], [craw.res])
        P.dma("sp", craw.ap[0:8, 1, :], self.c_ctx.rearrange("(k p) -> k p", p=128), [], [craw.res])
        for v in range(2):
            pv = Tile(self.bank[7].ap[:, 8 * v:8 * v + 8], self.bank[7].res)
            self.tr(pv.ap, craw.ap[0:8, v, :], self.ident_f.ap[0:8, 0:8], [craw.res, self.ident_f.res], [pv.res])
            self.act(self.cact2.ap[:, :, v], pv.ap, AF.Silu, [pv.res], [self.cact2.res])
        self.crep = [A.alloc(F32, [8, 128]) for _ in range(2)]
        ones = A.alloc(F32, [128])
        self.memset("pool", ones.ap, 1.0, [ones.res])
        for v in range(2):
            for k in range(8):
                self.ts("dve", self.crep[v].ap[:, k, :], ones.ap, self.cact2.ap[:, k, v:v + 1], None, ALU.mult, None,
                        [ones.res, self.cact2.res], [self.crep[v].res])
        self.modpp = A.alloc(F32, [48, 2])
        self.gpp = A.alloc(F32, [4, 8])
        self.A1 = A.alloc(F32, [8, 2])
        self.A2 = A.alloc(F32, [8, 2])
        self.G = [[A.alloc(F32, [1024]) for v in range(2)] for i in range(2)]
        self.sctx = A.alloc(F32, [2, 1024])
        self.sctx_res = [Res("sctx0"), Res("sctx1")]
        for i in range(2):
            P.dma("sp", self.sctx.ap[:, i, :], self.ctx[128 * i:128 * (i + 1), :], [], [self.sctx_res[i]])
        self.negpi = A.alloc(F32, [1])
        self.memset("pool", self.negpi.ap, -math.pi, [self.negpi.res])
        self.small = A.alloc(F32, [64])
        self.small_tiles = [Tile(self.small.ap[:, i:i + 1]) for i in range(64)]
        self.small_n = 0
        A.mark_persistent()

    def scalar_slot(self):
        i = self.small_n % 64
        self.small_n += 1
        return self.small_tiles[i]

    def bcast_tile(self, l, ntile, v, wm, dst_ap, dst_res, gain_idx, tmpb, tmpg, psb):
        P = self.P
        for k in range(8):
            self.mm(psb.ap, self.crep[v].ap[:, k, :], wm.ap[:, k, :], k == 0, k == 7, [self.crep[v].res, wm.res], [psb.res])
        if gain_idx is None:
            self.tt("dve", dst_ap, psb.ap, tmpb.ap, ALU.add, [psb.res, tmpb.res], [dst_res])
        else:
            self.tt("dve", dst_ap, psb.ap, tmpb.ap, ALU.add, [psb.res, tmpb.res], [dst_res])
            self.tt("dve", dst_ap, dst_ap, tmpg.ap, ALU.mult, [dst_res, tmpg.res], [dst_res])

    def load_bcast_row(self, eng, tile, src_row):
        self.P.dma(eng, tile.ap, src_row.partition_broadcast(128), [], [tile.res])

    def mod_phase(self, l):
        A = self.A
        P = self.P
        P.barrier()
        A.reset()
        wms = [A.alloc(F32, [8, 512]) for _ in range(2)]
        bpp = A.alloc(F32, [48])
        tmpb = [A.alloc(F32, [512]) for _ in range(2)]
        tmpg = [A.alloc(F32, [512]) for _ in range(2)]
        braw = A.alloc(F32, [128])
        graw = A.alloc(F32, [128])
        P.dma("sp", braw.ap[0:48, :], self.b_mod.rearrange("(j p) -> j p", p=128), [], [braw.res])
        P.dma("sp", graw.ap[0:32, :], self.norm_g.rearrange("g (k p) -> (g k) p", p=128), [], [graw.res])
        pb_ = Tile(self.bank[7].ap[:, 64:112], self.bank[7].res)
        self.tr(pb_.ap, braw.ap[0:48, :], self.ident_f.ap[0:48, 0:48], [braw.res, self.ident_f.res], [pb_.res])
        self.cp("dve", bpp.ap, pb_.ap, [pb_.res], [bpp.res])
        pg_ = Tile(self.bank[7].ap[:, 128:160], self.bank[7].res)
        self.tr(pg_.ap, graw.ap[0:32, :], self.ident_f.ap[0:32, 0:32], [graw.res, self.ident_f.res], [pg_.res])
        self.cp("dve", self.gpp.ap, pg_.ap.rearrange("p (g k) -> p g k", g=4), [pg_.res], [self.gpp.res])
        psA = Tile(self.bank[7].ap[:, 0:8], self.bank[7].res)
        for nt in range(12):
            wm = wms[nt % 2]
            P.dma("sp", wm.ap, self.w_mod[:, nt * 512:(nt + 1) * 512].rearrange("(k p) n -> p k n", p=128), [], [wm.res])
            for jj in range(4):
                for k in range(8):
                    self.mm(psA.ap[:, 2 * jj:2 * jj + 2], wm.ap[:, k, jj * 128:(jj + 1) * 128], self.cact2.ap[:, k, :],
                            k == 0, k == 7, [wm.res, self.cact2.res], [psA.res])
            self.tt("dve", self.modpp.ap[:, nt * 4:(nt + 1) * 4, :], psA.ap.rearrange("p (a b) -> p a b", b=2),
                    bpp.ap[:, nt * 4:(nt + 1) * 4].unsqueeze(2).broadcast_to([128, 4, 2]), ALU.add,
                    [psA.res, bpp.res], [self.modpp.res])
            gi = {4: (0, 0), 5: (0, 1), 10: (1, 0), 11: (1, 1)}.get(nt)
            if gi is not None:
                i, half = gi
                tb = tmpb[half]
                tg = tmpg[half]
                self.load_bcast_row("sp", tb, self.b_mod[nt * 512:(nt + 1) * 512])
                self.load_bcast_row("sp", tg, self.norm_g[1 + 2 * i, half * 512:(half + 1) * 512])
                for v in range(2):
                    psb = self.bank[5 + v]
                    self.bcast_tile(l, nt, v, wm, self.G[i][v].ap[:, half * 512:(half + 1) * 512], self.G[i][v].res, 1, tb, tg, psb)
        for (Ax, sc_off, gidx) in ((self.A1, 8, 0), (self.A2, 32, 2)):
            self.stt("dve", Ax.ap, self.modpp.ap[:, sc_off:sc_off + 8, :], 1.0,
                     self.gpp.ap[:, gidx, :].unsqueeze(2).broadcast_to([128, 8, 2]), ALU.add, ALU.mult,
                     [self.modpp.res, self.gpp.res], [Ax.res])

    def rstd_of(self, src_ap, src_reads, junk, n=1024):
        ss = self.scalar_slot()
        self.memset("pool", ss.ap, 0.0, [ss.res])
        self.act(junk.ap, src_ap, AF.Square, src_reads + [ss.res], [junk.res, ss.res], accum_out=ss.ap)
        self.act(ss.ap, ss.ap, AF.Sqrt, [ss.res], [ss.res], bias=EPS, scale=1.0 / n)
        self.P.op("dve", lambda e: e.reciprocal(out=ss.ap, in_=ss.ap), [ss.res], [ss.res])
        return ss

    def prenorm_T(self, h_ap, h_res, Ax, Bx_ap, Bx_res, v, hs, dstT_ap, dstT_res, psT):
        rstd = self.rstd_of(h_ap, [h_res], hs)
        self.act(hs.ap, h_ap, AF.Copy, [h_res, rstd.res], [hs.res], scale=rstd.ap)
        pst = psT.ap.bitcast(BF16)
        for k in range(8):
            self.tr(pst[:, k * 128:(k + 1) * 128], hs.ap[:, k * 128:(k + 1) * 128], self.ident_b.ap,
                    [hs.res, self.ident_b.res], [psT.res])
        p3 = pst.rearrange("p (a b) -> p a b", a=8)
        self.tt("dve", dstT_ap, p3, Ax.ap[:, :, v].unsqueeze(2).broadcast_to([128, 8, 128]), ALU.mult,
                [psT.res, Ax.res], [dstT_res])
        self.tt("pool", dstT_ap, dstT_ap, Bx_ap[:, :, v].unsqueeze(2).broadcast_to([128, 8, 128]), ALU.add,
                [dstT_res, Bx_res], [dstT_res])

    def postnorm_residual(self, po_ap, po_res, Gt, h_ap, h_res, junk, tmp):
        rstd = self.rstd_of(po_ap, [po_res], junk)
        self.stt("dve", tmp.ap, po_ap, rstd.ap, Gt.ap, ALU.mult, ALU.mult, [po_res, rstd.res, Gt.res], [tmp.res])
        self.tt("pool", h_ap, tmp.ap, h_ap, ALU.add, [tmp.res, h_res], [h_res])

    def mlp_phase(self, l):
        A = self.A
        P = self.P
        P.barrier()
        A.reset()
        w1 = A.alloc(BF16, [8, 4096])
        w2 = A.alloc(BF16, [32, 1024])
        w1r = [Res() for _ in range(8)]
        w2r = [Res() for _ in range(8)]
        for k in range(8):
            P.dma("pool", w1.ap[:, k, :], self.w_ff1[128 * k:128 * (k + 1), :], [], [w1r[k]])
        for q in range(8):
            P.dma("pool", w2.ap[:, 4 * q:4 * q + 4, :],
                  self.w_ff2[512 * q:512 * (q + 1), :].rearrange("(j p) n -> p j n", p=128), [], [w2r[q]])
        hid = A.alloc(BF16, [32, 256])
        hidr = [Res() for _ in range(32)]
        hnT = A.alloc(BF16, [8, 256])
        ht = [A.alloc(F32, [1024]) for _ in range(2)]
        hs = A.alloc(BF16, [1024])
        tmp = A.alloc(F32, [1024])
        rl = [A.alloc(F32, [256]) for _ in range(2)]
        B2_ap = self.modpp.ap[:, 24:32, :]
        upd_ctx = l < 2
        tiles = [("lat", i) for i in range(16)] + ([("ctx", 0)] if upd_ctx else [])
        if self.dbg.startswith("mlp1"):
            tiles = tiles[:1]
        for (kind, ti) in tiles:
            v = 0 if kind == "lat" else 1
            hview = []
            for sub in range(2):
                if kind == "lat":
                    t128 = ti * 2 + sub
                    P.dma("sp", ht[sub].ap, self.x[128 * t128:128 * (t128 + 1), :], [], [ht[sub].res])
                    hview.append((ht[sub].ap, ht[sub].res))
                else:
                    hview.append((self.sctx.ap[:, sub, :], self.sctx_res[sub]))
                self.prenorm_T(hview[sub][0], hview[sub][1], self.A2, B2_ap, self.modpp.res, v, hs,
                               hnT.ap[:, :, sub * 128:(sub + 1) * 128], hnT.res, self.bank[0])
            for j in range(32):
                pb = self.bank[1 + (j % 2)]
                pj = Tile(pb.ap[:, 0:256], pb.res)
                for k in range(8):
                    self.mm(pj.ap, w1.ap[:, k, 128 * j:128 * (j + 1)], hnT.ap[:, k, :], k == 0, k == 7,
                            [w1r[k], hnT.res], [pj.res])
                r = rl[j % 2]
                self.act(r.ap, pj.ap, AF.Relu, [pj.res], [r.res])
                self.tt("pool" if j % 2 else "dve", hid.ap[:, j, :], r.ap, r.ap, ALU.mult, [r.res], [hidr[j]])
            for sub in range(2):
                for half in range(2):
                    pb = self.bank[3 + 2 * sub + half]
                    for j in range(32):
                        self.mm(pb.ap, hid.ap[:, j, sub * 128:(sub + 1) * 128], w2.ap[:, j, half * 512:(half + 1) * 512],
                                j == 0, j == 31, [hidr[j], w2r[j // 4]], [pb.res])
                po_ap = self.psall[:, (3 + 2 * sub) * 512:(5 + 2 * sub) * 512]
                pres = [self.bank[3 + 2 * sub].res, self.bank[4 + 2 * sub].res]
                rstd = self.rstd_of(po_ap, pres, hs)
                self.stt("dve", tmp.ap, po_ap, rstd.ap, self.G[1][v].ap, ALU.mult, ALU.mult,
                         pres + [rstd.res, self.G[1][v].res], [tmp.res])
                h_ap, h_res = hview[sub]
                self.tt("pool", h_ap, tmp.ap, h_ap, ALU.add, [tmp.res, h_res], [h_res])
                if kind == "lat":
                    t128 = ti * 2 + sub
                    P.dma("sp", self.out[128 * t128:128 * (t128 + 1), :], h_ap, [h_res], [self.out_res[t128]])

    def cmul(self, eng, out_re, out_im, a_re, a_im, b_re, b_im, t1, t2, reads, wres, neg_im=False):
        rs_ = reads
        self.tt(eng, t1.ap, a_re, b_re, ALU.mult, rs_, [t1.res])
        self.tt(eng, t2.ap, a_im, b_im, ALU.mult, rs_, [t2.res])
        self.tt(eng, out_re, t1.ap, t2.ap, ALU.subtract, [t1.res, t2.res], wres)
        self.tt(eng, t1.ap, a_re, b_im, ALU.mult, rs_, [t1.res])
        self.tt(eng, t2.ap, a_im, b_re, ALU.mult, rs_, [t2.res])
        if neg_im:
            self.stt(eng, out_im, t1.ap, -1.0, t2.ap, ALU.mult, ALU.subtract, [t1.res, t2.res], wres)
        else:
            self.tt(eng, out_im, t1.ap, t2.ap, ALU.add, [t1.res, t2.res], wres)

    def even_a(self, l):
        A = self.A
        P = self.P
        nc = self.nc
        AX = mybir.AxisListType.X
        P.barrier()
        A.reset()
        NB = 544
        RTm = [A.alloc(BF16, [32, 2, 128]) for _ in range(2)]
        for r_ in range(2):
            self.memset("pool", RTm[r_].ap, 0.0, [RTm[r_].res])
        Ob = [[A.alloc(BF16, [16, 128]) for _ in range(2)] for _ in range(2)]
        Tm = A.alloc(BF16, [32, 128])
        ASr = A.alloc(F32, [10, 32])
        ASi = A.alloc(F32, [10, 32])
        ASn = A.alloc(F32, [10, 32])
        keep_top = A.top
        T32 = A.alloc(F32, [32, 128])
        nat = A.alloc(F32, [4, 128])
        lre = A.alloc(F32, [32]); lim = A.alloc(F32, [32]); ldt = A.alloc(F32, [32])
        for (src, dst, bnk) in ((self.s5_lam_re, lre, 0), (self.s5_lam_im, lim, 1)):
            P.dma("sp", nat.ap[0:32, bnk, :], src.rearrange("d (q r) p -> (d q) (r p)", r=2), [], [nat.res])
            pt = Tile(self.bank[7].ap[:, 32 * bnk:32 * bnk + 32], self.bank[7].res)
            self.tr(pt.ap, nat.ap[0:32, bnk, :], self.ident_f.ap[0:32, 0:32], [nat.res, self.ident_f.res], [pt.res])
            self.cp("dve", dst.ap, pt.ap, [pt.res], [dst.res])
        P.dma("sp", nat.ap[0:32, 2, 0:2], self.s5_log_dt.rearrange("d (q r) -> (d q) r", r=2), [], [nat.res])
        dtT = A.alloc(F32, [32])
        pt = Tile(self.bank[7].ap[0:2, 64:96], self.bank[7].res)
        self.tr(pt.ap, nat.ap[0:32, 2, 0:2], self.ident_f.ap[0:32, 0:32], [nat.res, self.ident_f.res], [pt.res])
        self.cp("dve", dtT.ap[0:2, :], pt.ap, [pt.res], [dtT.res])
        sel_i = A.alloc(I32, [128]); sel = A.alloc(F32, [128]); sel2 = A.alloc(F32, [128])
        P.op("pool", lambda e: e.iota(sel_i.ap[0:2, :], [[1, 128]], base=0, channel_multiplier=-64), [], [sel_i.res])
        self.cp("dve", sel.ap[0:2, :], sel_i.ap[0:2, :], [sel_i.res], [sel.res])
        self.ts("dve", sel2.ap[0:2, :], sel.ap[0:2, :], 0.0, None, ALU.is_ge, None, [sel.res], [sel2.res])
        self.ts("dve", sel.ap[0:2, :], sel.ap[0:2, :], 64.0, None, ALU.is_lt, None, [sel.res], [sel.res])
        self.tt("dve", sel.ap[0:2, :], sel.ap[0:2, :], sel2.ap[0:2, :], ALU.mult, [sel.res, sel2.res], [sel.res])
        pt = Tile(self.bank[7].ap[:, 96:128], self.bank[7].res)
        self.mm(pt.ap, sel.ap[0:2, :], dtT.ap[0:2, :], True, True, [sel.res, dtT.res], [pt.res])
        self.cp("dve", ldt.ap, pt.ap, [pt.res], [ldt.res])
        def v32():
            return A.alloc(F32, [32])
        dt = v32(); xm = v32(); mag = v32(); imag = v32(); th = v32(); fr = v32(); frc = v32(); w1 = v32(); w2 = v32()
        sn = v32(); cs = v32(); are = v32(); aim = v32(); ire = v32(); iim = v32(); den = v32(); fre = v32(); fim = v32(); nre = v32()
        self.act(dt.ap, ldt.ap, AF.Exp, [ldt.res], [dt.res])
        self.ts("dve", lre.ap, lre.ap, -1e-4, None, ALU.min, None, [lre.res], [lre.res])
        self.tt("dve", xm.ap, lre.ap, dt.ap, ALU.mult, [lre.res, dt.res], [xm.res])
        self.act(mag.ap, xm.ap, AF.Exp, [xm.res], [mag.res])
        self.act(imag.ap, xm.ap, AF.Exp, [xm.res], [imag.res], scale=-1.0)
        self.tt("dve", th.ap, lim.ap, dt.ap, ALU.mult, [lim.res, dt.res], [th.res])
        ki = A.alloc(I32, [32]); kf = v32()
        self.ts("dve", fr.ap, th.ap, 1.0 / (2.0 * math.pi), None, ALU.mult, None, [th.res], [fr.res])
        self.cp("dve", ki.ap, fr.ap, [fr.res], [ki.res])
        self.cp("dve", kf.ap, ki.ap, [ki.res], [kf.res])
        self.tt("dve", fr.ap, fr.ap, kf.ap, ALU.subtract, [fr.res, kf.res], [fr.res])

        def wrap(x):
            self.ts("dve", w1.ap, x.ap, 0.5, None, ALU.is_gt, None, [x.res], [w1.res])
            self.ts("dve", w2.ap, x.ap, -0.5, None, ALU.is_lt, None, [x.res], [w2.res])
            self.tt("dve", x.ap, x.ap, w1.ap, ALU.subtract, [x.res, w1.res], [x.res])
            self.tt("dve", x.ap, x.ap, w2.ap, ALU.add, [x.res, w2.res], [x.res])
        wrap(fr)
        self.ts("dve", frc.ap, fr.ap, 0.25, None, ALU.add, None, [fr.res], [frc.res])
        wrap(frc)
        self.act(sn.ap, fr.ap, AF.Sin, [fr.res], [sn.res], scale=2.0 * math.pi)
        self.act(cs.ap, frc.ap, AF.Sin, [frc.res], [cs.res], scale=2.0 * math.pi)
        self.tt("dve", are.ap, mag.ap, cs.ap, ALU.mult, [mag.res, cs.res], [are.res])
        self.tt("dve", aim.ap, mag.ap, sn.ap, ALU.mult, [mag.res, sn.res], [aim.res])
        self.tt("dve", ire.ap, imag.ap, cs.ap, ALU.mult, [imag.res, cs.res], [ire.res])
        self.stt("dve", iim.ap, imag.ap, -1.0, sn.ap, ALU.mult, ALU.mult, [imag.res, sn.res], [iim.res])
        self.tt("dve", den.ap, lre.ap, lre.ap, ALU.mult, [lre.res], [den.res])
        self.tt("dve", w1.ap, lim.ap, lim.ap, ALU.mult, [lim.res], [w1.res])
        self.tt("dve", den.ap, den.ap, w1.ap, ALU.add, [den.res, w1.res], [den.res])
        self.P.op("dve", lambda e: e.reciprocal(out=den.ap, in_=den.ap), [den.res], [den.res])
        self.ts("dve", nre.ap, are.ap, -1.0, None, ALU.add, None, [are.res], [nre.res])
        self.tt("dve", w1.ap, nre.ap, lre.ap, ALU.mult, [nre.res, lre.res], [w1.res])
        self.tt("dve", w2.ap, aim.ap, lim.ap, ALU.mult, [aim.res, lim.res], [w2.res])
        self.tt("dve", fre.ap, w1.ap, w2.ap, ALU.add, [w1.res, w2.res], [fre.res])
        self.tt("dve", fre.ap, fre.ap, den.ap, ALU.mult, [fre.res, den.res], [fre.res])
        self.tt("dve", w1.ap, aim.ap, lre.ap, ALU.mult, [aim.res, lre.res], [w1.res])
        self.tt("dve", w2.ap, nre.ap, lim.ap, ALU.mult, [nre.res, lim.res], [w2.res])
        self.tt("dve", fim.ap, w1.ap, w2.ap, ALU.subtract, [w1.res, w2.res], [fim.res])
        self.tt("dve", fim.ap, fim.ap, den.ap, ALU.mult, [fim.res, den.res], [fim.res])
        Epr = A.alloc(F32, [9, 32]); Epi = A.alloc(F32, [9, 32]); Enr = A.alloc(F32, [8, 32]); Eni = A.alloc(F32, [8, 32])
        s1 = v32(); s2 = v32()
        self.memset("pool", Epr.ap[:, 0, :], 1.0, [Epr.res]); self.memset("pool", Epi.ap[:, 0, :], 0.0, [Epi.res])
        self.memset("pool", Enr.ap[:, 0, :], 1.0, [Enr.res]); self.memset("pool", Eni.ap[:, 0, :], 0.0, [Eni.res])
        for j in range(1, 9):
            self.cmul("dve", Epr.ap[:, j, :], Epi.ap[:, j, :], Epr.ap[:, j - 1, :], Epi.ap[:, j - 1, :], are.ap, aim.ap, s1, s2,
                      [Epr.res, Epi.res, are.res, aim.res], [Epr.res, Epi.res])
        for j in range(1, 8):
            self.cmul("dve", Enr.ap[:, j, :], Eni.ap[:, j, :], Enr.ap[:, j - 1, :], Eni.ap[:, j - 1, :], ire.ap, iim.ap, s1, s2,
                      [Enr.res, Eni.res, ire.res, iim.res], [Enr.res, Eni.res])
        self.cp("dve", ASr.ap[:, 0, :], Epr.ap[:, 8, :], [Epr.res], [ASr.res])
        self.cp("dve", ASi.ap[:, 0, :], Epi.ap[:, 8, :], [Epi.res], [ASi.res])
        for k in range(1, 10):
            self.cmul("dve", ASr.ap[:, k, :], ASi.ap[:, k, :], ASr.ap[:, k - 1, :], ASi.ap[:, k - 1, :],
                      ASr.ap[:, k - 1, :], ASi.ap[:, k - 1, :], s1, s2, [ASr.res, ASi.res], [ASr.res, ASi.res])
        self.ts("dve", ASn.ap, ASi.ap, -1.0, None, ALU.mult, None, [ASi.res], [ASn.res])
        if self.dbg.startswith("s5preA"):
            return
        Br = A.alloc(F32, [2, 16, 16]); Bi = A.alloc(F32, [2, 16, 16]); Bbr = A.alloc(F32, [2, 16, 16]); Bbi = A.alloc(F32, [2, 16, 16])
        P.dma("sp", Br.ap, self.s5_b_re.rearrange("d (q r) p h -> (r p) d q h", r=2), [], [Br.res])
        P.dma("sp", Bi.ap, self.s5_b_im.rearrange("d (q r) p h -> (r p) d q h", r=2), [], [Bi.res])
        b1 = A.alloc(F32, [2, 16, 16]); b2 = A.alloc(F32, [2, 16, 16])
        fre3 = fre.ap.rearrange("p (d q) -> p d q", d=2).unsqueeze(3).broadcast_to([128, 2, 16, 16])
        fim3 = fim.ap.rearrange("p (d q) -> p d q", d=2).unsqueeze(3).broadcast_to([128, 2, 16, 16])
        self.cmul("dve", Bbr.ap, Bbi.ap, fre3, fim3, Br.ap, Bi.ap, b1, b2, [fre.res, fim.res, Br.res, Bi.res], [Bbr.res, Bbi.res])
        Cr = A.alloc(F32, [2, 16, 16]); Ci = A.alloc(F32, [2, 16, 16])
        cnat = [A.alloc(F32, [128]) for _ in range(2)]
        ci_ = 0
        for (src, dstC) in ((self.s5_c_re, Cr), (self.s5_c_im, Ci)):
            for d in range(2):
                for ch in range(2):
                    cn = cnat[ci_ % 2]
                    for ql in range(8):
                        q = ch * 8 + ql
                        P.dma("sp", cn.ap[16 * ql:16 * ql + 16, :].rearrange("h (r p) -> h r p", r=2),
                              src[d, 2 * q:2 * q + 2].rearrange("r h p -> h r p"), [], [cn.res])
                    pt = Tile(self.bank[6].ap[:, 128 * (ci_ % 4):128 * (ci_ % 4) + 128], self.bank[6].res)
                    self.tr(pt.ap, cn.ap, self.ident_f.ap, [cn.res, self.ident_f.res], [pt.res])
                    self.cp("dve", dstC.ap[:, d, ch * 8:ch * 8 + 8, :], pt.ap.rearrange("p (q h) -> p q h", q=8), [pt.res], [dstC.res])
                    ci_ += 1
        dnat = A.alloc(F32, [16]); dT = A.alloc(F32, [32]); rep_i = A.alloc(I32, [128]); rep = A.alloc(F32, [128]); dcol = A.alloc(F32, [32])
        P.dma("sp", dnat.ap[0:32, :], self.s5_d.rearrange("(g h) -> g h", h=16), [], [dnat.res])
        pt = Tile(self.bank[7].ap[0:16, 128:160], self.bank[7].res)
        self.tr(pt.ap, dnat.ap[0:32, :], self.ident_f.ap[0:32, 0:32], [dnat.res, self.ident_f.res], [pt.res])
        self.cp("dve", dT.ap[0:16, :], pt.ap, [pt.res], [dT.res])
        P.op("pool", lambda e: e.iota(rep_i.ap[0:16, :], [[1, 128]], base=16, channel_multiplier=-1), [], [rep_i.res])
        self.ts("dve", rep_i.ap[0:16, :], rep_i.ap[0:16, :], 15, None, ALU.bitwise_and, None, [rep_i.res], [rep_i.res])
        self.cp("dve", rep.ap[0:16, :], rep_i.ap[0:16, :], [rep_i.res], [rep.res])
        self.ts("dve", rep.ap[0:16, :], rep.ap[0:16, :], 0.0, None, ALU.is_equal, None, [rep.res], [rep.res])
        pt = Tile(self.bank[7].ap[:, 160:192], self.bank[7].res)
        self.mm(pt.ap, rep.ap[0:16, :], dT.ap[0:16, :], True, True, [rep.res, dT.res], [pt.res])
        self.cp("dve", dcol.ap, pt.ap, [pt.res], [dcol.res])
        cbi = A.alloc(I32, [8, 16]); cbf = A.alloc(F32, [8, 16]); rbi = A.alloc(I32, [1]); rbf = A.alloc(F32, [1])
        mkf = A.alloc(F32, [128]); mkb = A.alloc(F32, [128])
        P.op("pool", lambda e: e.iota(cbi.ap, [[1, 8], [0, 16]], base=0, channel_multiplier=0), [], [cbi.res])
        self.cp("dve", cbf.ap, cbi.ap, [cbi.res], [cbf.res])
        P.op("pool", lambda e: e.iota(rbi.ap, [[1, 1]], base=0, channel_multiplier=1), [], [rbi.res])
        self.ts("dve", rbi.ap, rbi.ap, 4, None, ALU.arith_shift_right, None, [rbi.res], [rbi.res])
        self.cp("dve", rbf.ap, rbi.ap, [rbi.res], [rbf.res])
        cbf2 = cbf.ap.rearrange("p a b -> p (a b)")
        self.ts("dve", mkf.ap, cbf2, rbf.ap, None, ALU.is_ge, None, [cbf.res, rbf.res], [mkf.res])
        self.ts("dve", mkb.ap, cbf2, rbf.ap, None, ALU.is_le, None, [cbf.res, rbf.res], [mkb.res])
        if self.dbg.startswith("s5preB"):
            return
        hmi = A.alloc(I32, [2]); hm = A.alloc(F32, [2]); tmpT = A.alloc(F32, [128])
        P.op("pool", lambda e: e.iota(hmi.ap, [[0, 2]], base=0, channel_multiplier=1), [], [hmi.res])
        self.ts("dve", hmi.ap, hmi.ap, 6, None, ALU.arith_shift_right, None, [hmi.res], [hmi.res])
        self.cp("dve", hm.ap, hmi.ap, [hmi.res], [hm.res])
        self.ts("dve", hm.ap[:, 0:1], hm.ap[:, 0:1], -1.0, -1.0, ALU.add, ALU.mult, [hm.res], [hm.res])
        big = [A.alloc(F32, [16, 8, 16]) for _ in range(6)]
        Pr, Pi, Qr, Qi, g1, g2 = big

        def esel(E, d, j0=0, n=8):
            return E.ap[:, j0:j0 + n, 16 * d:16 * d + 16].rearrange("p j q -> p q j").unsqueeze(3).broadcast_to([128, 16, n, 16])

        def ebc(E, d, j):
            return E.ap[:, j, 16 * d:16 * d + 16].unsqueeze(2).unsqueeze(3).broadcast_to([128, 16, 8, 16])

        def bcj(X, d):
            return X.ap[:, d, :, :].unsqueeze(2).broadcast_to([128, 16, 8, 16])
        for d in range(2):
            EP_r, EP_i = (Enr, Eni) if d == 0 else (Epr, Epi)
            EQ_r, EQ_i = (Epr, Epi) if d == 0 else (Enr, Eni)
            rr = [Epr.res, Epi.res, Enr.res, Eni.res, Bbr.res, Bbi.res, Cr.res, Ci.res]
            if self.dbg.startswith("s5preE"):
                continue
            self.cmul("dve", Pr.ap, Pi.ap, esel(EP_r, d), esel(EP_i, d), bcj(Bbr, d), bcj(Bbi, d), g1, g2, rr, [Pr.res, Pi.res])
            self.cmul("dve", Qr.ap, Qi.ap, esel(EQ_r, d), esel(EQ_i, d), bcj(Cr, d), bcj(Ci, d), g1, g2, rr, [Qr.res, Qi.res], neg_im=True)
            for r in range(2):
                self.ts("dve", g1.ap, Pr.ap, hm.ap[:, r:r + 1], None, ALU.mult, None, [Pr.res, hm.res], [g1.res])
                self.ts("dve", g2.ap, Pi.ap, hm.ap[:, r:r + 1], None, ALU.mult, None, [Pi.res, hm.res], [g2.res])
                for q in range(16):
                    g = 2 * q + r
                    pt = Tile(self.bank[1 + (q % 2)].ap[:, 0:128], self.bank[1 + (q % 2)].res)
                    self.mm(pt.ap, g1.ap[:, q].rearrange("p a b -> p (a b)"), Qr.ap[:, q].rearrange("p a b -> p (a b)"),
                            True, False, [g1.res, Qr.res], [pt.res])
                    self.mm(pt.ap, g2.ap[:, q].rearrange("p a b -> p (a b)"), Qi.ap[:, q].rearrange("p a b -> p (a b)"),
                            False, True, [g2.res, Qi.res], [pt.res])
                    if d == 0:
                        self.tt("dve", T32.ap[:, g, :], pt.ap, mkf.ap, ALU.mult, [pt.res, mkf.res], [T32.res])
                    else:
                        self.tt("dve", tmpT.ap, pt.ap, mkb.ap, ALU.mult, [pt.res, mkb.res], [tmpT.res])
                        self.tt("dve", T32.ap[:, g, :], T32.ap[:, g, :], tmpT.ap, ALU.add, [T32.res, tmpT.res], [T32.res])
            jO = 1 if d == 0 else 8
            self.tt("dve", g1.ap, ebc(Epr, d, jO), Qr.ap, ALU.mult, rr + [Qr.res], [g1.res])
            self.tt("dve", g2.ap, ebc(Epi, d, jO), Qi.ap, ALU.mult, rr + [Qi.res], [g2.res])
            self.tt("dve", Ob[d][0].ap.rearrange("p q (a b) -> p q a b", a=8), g1.ap, g2.ap, ALU.add, [g1.res, g2.res], [Ob[d][0].res])
            self.tt("dve", g1.ap, ebc(Epr, d, jO), Qi.ap, ALU.mult, rr + [Qi.res], [g1.res])
            self.tt("dve", g2.ap, ebc(Epi, d, jO), Qr.ap, ALU.mult, rr + [Qr.res], [g2.res])
            self.tt("dve", Ob[d][1].ap.rearrange("p q (a b) -> p q a b", a=8), g1.ap, g2.ap, ALU.subtract, [g1.res, g2.res], [Ob[d][1].res])
            if d == 0:
                self.cmul("dve", Qr.ap, Qi.ap, ebc(Epr, 0, 7), ebc(Epi, 0, 7), Pr.ap, Pi.ap, g1, g2, rr + [Pr.res, Pi.res], [Qr.res, Qi.res])
                Rr_, Ri_ = Qr, Qi
            else:
                Rr_, Ri_ = Pr, Pi
            for q in range(16 if not self.dbg.startswith("s5preD") else 0):
                for (ri, Rx) in ((0, Rr_), (1, Ri_)):
                    pt = Tile(self.bank[3 + (q % 2)].ap[:, 128 * ri:128 * ri + 128], self.bank[3 + (q % 2)].res)
                    self.tr(pt.ap, Rx.ap[:, q].rearrange("p a b -> p (a b)"), self.ident_f.ap, [Rx.res, self.ident_f.res], [pt.res])
                pb2 = self.bank[3 + (q % 2)]
                for r_ in range(2 if not self.dbg.startswith("s5preCF") else 0):
                    self.cp("dve", RTm[r_].ap[:, 2 * q + d, :, 64 * r_:64 * r_ + 64],
                            pb2.ap[:, 0:256].rearrange("p (a b) -> p a b", a=2)[:, :, 64 * r_:64 * r_ + 64], [pb2.res], [RTm[r_].res])
        for g in range(32):
            self.stt("dve", Tm.ap[:, g, :], self.ident_f.ap, dcol.ap[:, g:g + 1], T32.ap[:, g, :], ALU.mult, ALU.add,
                     [self.ident_f.res, dcol.res, T32.res], [Tm.res])
        self.Ob = Ob
        if self.dbg.startswith("s5pre"):
            return
        P.barrier()
        A.top = keep_top
        U = A.alloc(BF16, [32, NB])
        Ur = [[Res() for _ in range(5)] for _ in range(32)]
        u_top = A.top
        wu = A.alloc(BF16, [8, 512])
        P.dma("pool", wu.ap, self.w_in[:, 0:512].rearrange("(k p) n -> p k n", p=128), [], [wu.res])
        hTg = A.alloc(BF16, [8, 1024])
        ub2 = A.alloc(BF16, [32, 8, 16])
        hTs = A.alloc(BF16, [8, 8, 128])
        self.memset("pool", ub2.ap, 0.0, [ub2.res])
        hs = A.alloc(BF16, [1024])
        ht = [A.alloc(F32, [1024]) for _ in range(2)]
        B1_ap = self.modpp.ap[:, 0:8, :]
        groups = [(0, 32)] + [(32 + 128 * i, 128) for i in range(4)]
        for gi, (n0, nb) in enumerate(groups):
            ntile = nb // 16
            for i in range(ntile):
                if gi == 0:
                    h_ap, h_res, v = self.sctx.ap[:, i, :], self.sctx_res[i], 1
                else:
                    h = ht[i % 2]
                    t128 = (gi - 1) * 8 + i
                    P.dma("sp", h.ap, self.x[128 * t128:128 * (t128 + 1), :], [], [h.res])
                    h_ap, h_res, v = h.ap, h.res, 0
                self.prenorm_T(h_ap, h_res, self.A1, B1_ap, self.modpp.res, v, hs, hTg.ap[:, :, 128 * i:128 * (i + 1)], hTg.res, self.bank[0])
            for k in range(8):
                self.cp("dve" if k % 2 else "pool", hTs.ap[:, k, :, 0:nb], hTg.ap[:, k, 0:8 * nb].rearrange("p (b t) -> p t b", t=8),
                        [hTg.res], [hTs.res])
            for tau in range(8):
                pb = self.bank[1 + (tau % 2)]
                for k in range(8):
                    self.mm(pb.ap[0:nb, :], hTs.ap[:, k, tau, 0:nb], wu.ap[:, k, :], k == 0, k == 7, [hTs.res, wu.res], [pb.res])
                self.cp("dve", ub2.ap[0:nb, :, tau, :], pb.ap[0:nb, :].rearrange("p (g h) -> p g h", h=16),
                        [pb.res], [ub2.res])
            for g8 in range(4 if not self.dbg.startswith("s5a1x") else 0):
                pb = self.bank[3 + (g8 % 2)]
                pbt = pb.ap.bitcast(BF16)
                for gl in range(8):
                    g = g8 * 8 + gl
                    self.tr(pbt[:, 128 * gl:128 * gl + 128], ub2.ap[:, g].rearrange("p a b -> p (a b)"), self.ident_b.ap,
                            [ub2.res, self.ident_b.res], [pb.res])
                for gl in range(8 if not self.dbg.startswith("s5a1y") else 0):
                    g = g8 * 8 + gl
                    self.cp("dve", U.ap[:, g, n0:n0 + nb], pbt[:, 128 * gl:128 * gl + nb], [pb.res], [Ur[g][gi]])
        if self.dbg.startswith("s5a1"):
            return
        P.barrier()
        A.top = u_top
        gbm = A.alloc(BF16, [5, 8, 512])
        gbr = [Res() for _ in range(5)]
        gbm_end = A.top
        Sx = [[A.alloc(F32, [NB]) for _ in range(2)] for _ in range(2)]
        SE = [[[A.alloc(BF16, [NB + 1]) for _ in range(2)] for _ in range(2)] for _ in range(2)]
        for par in range(2):
            for d in range(2):
                for ri in range(2):
                    self.memset("pool", SE[par][d][ri].ap, 0.0, [SE[par][d][ri].res])
        gtmp = [A.alloc(BF16, [128]) for _ in range(2)]
        nq = 16 if not self.dbg.startswith("s5q1") else 1
        for q in range(nq):
            par = q % 2
            for d in range(2):
                qd = 2 * q + d
                psV = [Tile(self.psall[:, 512:1056], Res()), Tile(self.psall[:, 1536:2080], Res())]
                vres = [[self.bank[1].res, self.bank[2].res], [self.bank[3].res, self.bank[4].res]]
                if d == 0:
                    splits = [(0, 0, 512), (512, 512, 32)]
                else:
                    splits = [(0, 32, 512), (512, 0, 32)]
                for ri in range(2):
                    for (oc, uc, n) in splits:
                        for r in range(2):
                            g = 2 * q + r
                            ur = Ur[g]
                            self.mm(psV[ri].ap[:, oc:oc + n], RTm[r].ap[:, qd, ri, :], U.ap[:, g, uc:uc + n],
                                    r == 0, r == 1, [RTm[r].res] + ur, vres[ri])
                cur = 0
                self.cp("dve", Sx[0][0].ap, psV[0].ap, vres[0], [Sx[0][0].res])
                self.cp("dve", Sx[0][1].ap, psV[1].ap, vres[1], [Sx[0][1].res])
                col = 16 * d + q
                for k in range(10):
                    dl = 1 << k
                    a_r = ASr.ap[:, k, col:col + 1]
                    a_i = ASi.ap[:, k, col:col + 1]
                    a_n = ASn.ap[:, k, col:col + 1]
                    o_re, o_im = Sx[cur][0], Sx[cur][1]
                    n_re, n_im = Sx[1 - cur][0], Sx[1 - cur][1]
                    if d == 0:
                        dst_s, src_s, same_s = slice(dl, NB), slice(0, NB - dl), slice(0, dl)
                    else:
                        dst_s, src_s, same_s = slice(0, NB - dl), slice(dl, NB), slice(NB - dl, NB)
                    rd = [o_re.res, o_im.res, ASr.res, ASi.res, ASn.res]
                    self.cp("act", n_re.ap[:, same_s], o_re.ap[:, same_s], [o_re.res], [n_re.res])
                    self.cp("act", n_im.ap[:, same_s], o_im.ap[:, same_s], [o_im.res], [n_im.res])
                    self.stt("dve", n_re.ap[:, dst_s], o_re.ap[:, src_s], a_r, o_re.ap[:, dst_s], ALU.mult, ALU.add, rd, [n_re.res])
                    self.stt("dve", n_re.ap[:, dst_s], o_im.ap[:, src_s], a_n, n_re.ap[:, dst_s], ALU.mult, ALU.add, rd + [n_re.res], [n_re.res])
                    self.stt("dve", n_im.ap[:, dst_s], o_im.ap[:, src_s], a_r, o_im.ap[:, dst_s], ALU.mult, ALU.add, rd, [n_im.res])
                    self.stt("dve", n_im.ap[:, dst_s], o_re.ap[:, src_s], a_i, n_im.ap[:, dst_s], ALU.mult, ALU.add, rd + [n_im.res], [n_im.res])
                    cur = 1 - cur
                off = 1 if d == 0 else 0
                for ri in range(2):
                    self.cp("act", SE[par][d][ri].ap[:, off:off + NB], Sx[cur][ri].ap, [Sx[cur][ri].res], [SE[par][d][ri].res])
            for r in range(2):
                g = 2 * q + r
                for ci, (n0, nb) in enumerate(groups):
                    pb = self.bank[5 + ((2 * q + r + ci) % 2)]
                    py = Tile(pb.ap[0:nb, 0:128], pb.res)
                    self.mm(py.ap, U.ap[:, g, n0:n0 + nb], Tm.ap[:, g, :], True, False, Ur[g] + [Tm.res], [py.res])
                    bidx = (n0 - 32) if n0 >= 32 else 512 + n0
                    for d, c0_ in ((0, n0), (1, bidx + 1)):
                        for ri in range(2):
                            last = (d == 1 and ri == 1)
                            self.mm(py.ap, SE[par][d][ri].ap[64 * r:64 * r + 64, c0_:c0_ + nb], self.Ob[d][ri].ap[64 * r:64 * r + 64, q, :],
                                    False, last, [SE[par][d][ri].res, self.Ob[d][ri].res], [py.res])
                    gt_ = gtmp[(2 * q + r + ci) % 2]
                    self.act(gt_.ap[0:nb, :], py.ap, AF.Gelu, [py.res], [gt_.res])
                    self.cp("dve", gbm.ap[0:nb, ci, :, 16 * g:16 * g + 16], gt_.ap[0:nb, :].rearrange("p (t h) -> p t h", t=8), [gt_.res], [gbr[ci]])
        if self.dbg.startswith("s5a3"):
            return
        P.barrier()
        A.top = keep_top
        gT = A.alloc(BF16, [4, L + LC])
        A.top = gbm_end
        wg = A.alloc(BF16, [4, 512])
        P.dma("pool", wg.ap, self.s5_w_glu.rearrange("(k p) n -> p k n", p=128), [], [wg.res])
        for ci, (n0, nb) in enumerate(groups):
            for tp in range(8):
                pb = self.bank[1 + (tp % 2)]
                pbt = pb.ap.bitcast(BF16)
                for kk in range(4):
                    self.tr(pbt[:, 128 * kk:128 * kk + nb], gbm.ap[0:nb, ci, tp, 128 * kk:128 * kk + 128], self.ident_b.ap[0:nb, 0:nb],
                            [gbr[ci], self.ident_b.res], [pb.res])
                dst = gT.ap[:, :, 8 * n0:8 * (n0 + nb)].rearrange("p k (b t) -> p k b t", t=8)[:, :, :, tp]
                src = pbt[:, 0:512].rearrange("p (k b) -> p k b", k=4)[:, :, 0:nb]
                self.cp("dve", dst, src, [pb.res], [gT.res])
        sg = [A.alloc(F32, [512]) for _ in range(2)]
        so = [A.alloc(BF16, [4, 512]) for _ in range(2)]
        NTOK = L + LC
        ti = 0
        for t0 in range(0, NTOK, 512):
            n = min(512, NTOK - t0)
            sot = so[ti % 2]
            for mcol in range(4):
                pb = self.bank[3 + (mcol % 2)]
                for kk in range(4):
                    self.mm(pb.ap[:, 0:n], wg.ap[:, kk, 128 * mcol:128 * (mcol + 1)], gT.ap[:, kk, t0:t0 + n], kk == 0, kk == 3,
                            [wg.res, gT.res], [pb.res])
                sgt = sg[mcol % 2]
                self.act(sgt.ap[:, 0:n], pb.ap[:, 0:n], AF.Sigmoid, [pb.res], [sgt.res])
                self.tt("dve", sot.ap[:, mcol, 0:n], sgt.ap[:, 0:n], gT.ap[:, mcol, t0:t0 + n], ALU.mult, [sgt.res, gT.res], [sot.res])
            P.dma("sp", self.s5T_out[:, t0:t0 + n].rearrange("(k p) t -> p k t", p=128), sot.ap[:, :, 0:n], [sot.res], [Res()])
            ti += 1

    def even_b(self, l):
        A = self.A
        P = self.P
        upd_ctx = l < 2
        P.barrier()
        A.reset()
        NT = L + LC
        NTILE = NT // 128
        AX = mybir.AxisListType.X
        win = A.alloc(BF16, [8, 1536])
        winr = [Res() for _ in range(8)]
        for k in range(8):
            P.dma("pool", win.ap[:, k, :], self.w_in[128 * k:128 * (k + 1), 512:2048], [], [winr[k]])
        wout = A.alloc(BF16, [8, 1024])
        P.dma("pool", wout.ap, self.w_out_even.rearrange("(k p) n -> p k n", p=128), [], [wout.res])
        kT = A.alloc(BF16, [4, NT])
        kTr = [Res() for _ in range(NTILE)]
        vt = A.alloc(BF16, [NTILE, 512])
        vtr = [Res() for _ in range(NTILE)]
        hT = A.alloc(BF16, [8, 128])
        hs = A.alloc(BF16, [1024])
        ht = [A.alloc(F32, [1024]) for _ in range(2)]
        B1_ap = self.modpp.ap[:, 0:8, :]

        def load_norm(t, i):
            if t < 2:
                h_ap, h_res, v = self.sctx.ap[:, t, :], self.sctx_res[t], 1
            else:
                h = ht[i % 2]
                P.dma("sp", h.ap, self.x[128 * (t - 2):128 * (t - 1), :], [], [h.res])
                h_ap, h_res, v = h.ap, h.res, 0
            self.prenorm_T(h_ap, h_res, self.A1, B1_ap, self.modpp.res, v, hs, hT.ap, hT.res, self.bank[0])
            return h_ap, h_res

        for t in range(NTILE):
            load_norm(t, t)
            pk = self.bank[1]
            for mc in range(4):
                for k in range(8):
                    self.mm(pk.ap[:, 128 * mc:128 * (mc + 1)], win.ap[:, k, 512 + 128 * mc:512 + 128 * (mc + 1)], hT.ap[:, k, :],
                            k == 0, k == 7, [winr[k], hT.res], [pk.res])
            self.cp("act", kT.ap[:, :, 128 * t:128 * (t + 1)], pk.ap.rearrange("p (a b) -> p a b", a=4), [pk.res], [kTr[t]])
            pv = self.bank[7]
            for k in range(8):
                self.mm(pv.ap, hT.ap[:, k, :], win.ap[:, k, 1024:1536], k == 0, k == 7, [winr[k], hT.res], [pv.res])
            self.cp("dve", vt.ap[:, t, :], pv.ap, [pv.res], [vtr[t]])
        Bd = self.nc.dram_tensor("Bd", [8, 15, 64, 94], F32).ap()
        Bd_res = Res()
        Bc = A.alloc(F32, [8, 15, 64])
        top_b2 = A.top
        fill = A.alloc(F32, [15 * 94])
        self.memset("pool", fill.ap, -30000.0, [fill.res])
        for hd in range(8):
            P.dma("sp", Bd[hd].rearrange("a q k -> q a k"), fill.ap[0:64, :].rearrange("p (a k) -> p a k", a=15), [fill.res], [Bd_res])
        bt = Bd.tensor
        dst = bass.AP(bt, 0, [[15 * 64 * 94, 8], [64 * 94, 15], [95, 64], [1, 31]])
        rt = self.na_rpb.tensor
        srcp = bass.AP(rt, 0, [[465, 8], [31, 15], [0, 64], [1, 31]])
        P.dma("sp", dst, srcp, [Bd_res], [Bd_res])
        for half in range(2):
            P.dma("sp", Bc.ap[64 * half:64 * half + 64], Bd[:, :, :, 15:79].rearrange("h a q k -> q h a k"), [Bd_res], [Bc.res])
        ii = A.alloc(I32, [64])
        kcf = A.alloc(F32, [64])
        qi = A.alloc(I32, [1])
        qf = A.alloc(F32, [1])
        c0 = A.alloc(F32, [1])
        m1 = A.alloc(F32, [64])
        m2 = A.alloc(F32, [64])
        P.op("pool", lambda e: e.iota(ii.ap, [[1, 64]], base=0, channel_multiplier=0), [], [ii.res])
        self.cp("dve", kcf.ap, ii.ap, [ii.res], [kcf.res])
        P.op("pool", lambda e: e.iota(qi.ap, [[1, 1]], base=0, channel_multiplier=1), [], [qi.res])
        self.ts("dve", qi.ap, qi.ap, 63, None, ALU.bitwise_and, None, [qi.res], [qi.res])
        self.cp("dve", qf.ap, qi.ap, [qi.res], [qf.res])
        self.ts("dve", c0.ap, qf.ap, -8.0, 0.0, ALU.add, ALU.max, [qf.res], [c0.res])
        self.ts("dve", c0.ap, c0.ap, 48.0, None, ALU.min, None, [c0.res], [c0.res])
        self.ts("dve", m1.ap, kcf.ap, c0.ap, None, ALU.is_ge, None, [kcf.res, c0.res], [m1.res])
        self.ts("dve", m2.ap, kcf.ap, -16.0, c0.ap, ALU.add, ALU.is_lt, [kcf.res, c0.res], [m2.res])
        self.tt("dve", m1.ap, m1.ap, m2.ap, ALU.mult, [m1.res, m2.res], [m1.res])
        self.ts("dve", m1.ap, m1.ap, -1.0, 30000.0, ALU.add, ALU.mult, [m1.res], [m1.res])
        Bc2 = Bc.ap.rearrange("p h a k -> p (h a) k")
        self.tt("dve", Bc2, Bc2, m1.ap.unsqueeze(1).broadcast_to([128, 120, 64]), ALU.add, [Bc.res, m1.res], [Bc.res])
        P.barrier()
        A.top = top_b2
        qT = A.alloc(BF16, [4, 128])
        tS = A.alloc(F32, [768])
        Pt = A.alloc(BF16, [896])
        PtT = A.alloc(BF16, [896])
        natok = A.alloc(BF16, [512])
        naT = A.alloc(BF16, [4, 128])
        s5t = [A.alloc(BF16, [4, 128]) for _ in range(2)]
        sm = A.alloc(F32, [8, 2])
        rinv = A.alloc(F32, [8])
        mx = A.alloc(F32, [8])
        nmx = A.alloc(F32, [8])
        tmp = A.alloc(F32, [1024])
        junk = A.alloc(BF16, [1024])
        psS = Tile(self.psall[:, 1024:2048], Res())
        psS_res = [self.bank[2].res, self.bank[3].res]
        psT = self.bank[6]
        psO = self.bank[7]
        units = [("lat", m) for m in range(32)] + ([("ctx", i) for i in range(2)] if upd_ctx else [])
        if self.dbg.startswith("na1"):
            units = units[:1] + units[5:6] + units[31:32] + units[32:]
        for ui, (kind, m) in enumerate(units):
            t = 2 + m if kind == "lat" else m
            h_ap, h_res = load_norm(t, ui)
            pq = self.bank[1]
            for mc in range(4):
                for k in range(8):
                    self.mm(pq.ap[:, 128 * mc:128 * (mc + 1)], win.ap[:, k, 128 * mc:128 * (mc + 1)], hT.ap[:, k, :],
                            k == 0, k == 7, [winr[k], hT.res], [pq.res])
            self.cp("act", qT.ap, pq.ap.rearrange("p (a b) -> p a b", a=4), [pq.res], [qT.res])
            if kind == "lat":
                rs = min(max(2 * m - 4, 0), 54)
                wt0 = 2 + rs // 2
                nwin = 640
            else:
                rs = 0
                wt0 = 0
                nwin = 0
            ncol = nwin + 256
            nchunk = ncol // 128
            for hd in range(8):
                mc, po = hd // 2, 64 * (hd % 2)
                q_l = qT.ap[po:po + 64, mc, :]
                if kind == "lat":
                    kr = [kTr[wt0 + i] for i in range(5)]
                    self.mm(psS.ap[:, 0:512], q_l, kT.ap[po:po + 64, mc, 128 * wt0:128 * wt0 + 512], True, True,
                            [qT.res] + kr, psS_res)
                    self.mm(psS.ap[:, 512:640], q_l, kT.ap[po:po + 64, mc, 128 * wt0 + 512:128 * wt0 + 640], True, True,
                            [qT.res] + kr, psS_res)
                self.mm(psS.ap[:, nwin:nwin + 256], q_l, kT.ap[po:po + 64, mc, 0:256], True, True,
                        [qT.res, kTr[0], kTr[1]], psS_res)
                wins = []
                if kind == "lat":
                    for e in range(2):
                        qr = 2 * m + e
                        r0 = min(max(qr - 4, 0), 56)
                        j0 = r0 - rs
                        a0 = r0 - qr + 7
                        wins.append((e, j0))
                        self.stt("dve", tS.ap[64 * e:64 * e + 64, 0:512].rearrange("p (a k) -> p a k", a=8),
                                 psS.ap[64 * e:64 * e + 64, 64 * j0:64 * j0 + 512].rearrange("p (a k) -> p a k", a=8), 0.125,
                                 Bc.ap[64 * e:64 * e + 64, hd, a0:a0 + 8, :], ALU.mult, ALU.add,
                                 psS_res + [Bc.res], [tS.res])
                    self.memset("pool", Pt.ap[:, 0:640], 0.0, [Pt.res])
                o0 = 512 if kind == "lat" else 0
                self.ts("dve", tS.ap[:, o0:o0 + 256], psS.ap[:, nwin:nwin + 256], 0.125, None, ALU.mult, None, psS_res, [tS.res])
                self.P.op("dve", lambda e, o_=mx.ap[:, hd:hd + 1], i_=tS.ap[:, 0:o0 + 256]: e.reduce_max(out=o_, in_=i_, axis=AX),
                          [tS.res], [mx.res])
                self.ts("dve", nmx.ap[:, hd:hd + 1], mx.ap[:, hd:hd + 1], -1.0, None, ALU.mult, None, [mx.res], [nmx.res])
                self.memset("pool", sm.ap[:, hd, :], 0.0, [sm.res])
                for (e, j0) in wins:
                    self.act(Pt.ap[64 * e:64 * e + 64, 64 * j0:64 * j0 + 512], tS.ap[64 * e:64 * e + 64, 0:512], AF.Exp,
                             [tS.res, nmx.res, sm.res], [Pt.res, sm.res], bias=nmx.ap[64 * e:64 * e + 64, hd:hd + 1], scale=1.0,
                             accum_out=sm.ap[64 * e:64 * e + 64, hd, 0:1])
                self.act(Pt.ap[:, nwin:nwin + 256], tS.ap[:, o0:o0 + 256], AF.Exp, [tS.res, nmx.res, sm.res], [Pt.res, sm.res],
                         bias=nmx.ap[:, hd:hd + 1], scale=1.0, accum_out=sm.ap[:, hd, 1:2])
                pst = psT.ap.bitcast(BF16)
                for c in range(nchunk):
                    self.tr(pst[:, 128 * c:128 * (c + 1)], Pt.ap[:, 128 * c:128 * (c + 1)], self.ident_b.ap,
                            [Pt.res, self.ident_b.res], [psT.res])
                self.cp("act" if hd % 2 else "dve", PtT.ap[:, 0:ncol], pst[:, 0:ncol], [psT.res], [PtT.res])
                for c in range(nchunk):
                    if kind == "lat" and c < 5:
                        vtile = wt0 + c
                    else:
                        vtile = c - (5 if kind == "lat" else 0)
                    self.mm(psO.ap[:, 64 * hd:64 * hd + 64], PtT.ap[:, 128 * c:128 * (c + 1)], vt.ap[:, vtile, 64 * hd:64 * hd + 64],
                            c == 0, c == nchunk - 1, [PtT.res, vtr[vtile]], [psO.res])
            self.tt("dve", rinv.ap, sm.ap[:, :, 0], sm.ap[:, :, 1], ALU.add, [sm.res], [rinv.res])
            self.P.op("dve", lambda e: e.reciprocal(out=rinv.ap, in_=rinv.ap), [rinv.res], [rinv.res])
            self.tt("dve", natok.ap.rearrange("p (h d) -> p h d", h=8), psO.ap.rearrange("p (h d) -> p h d", h=8),
                    rinv.ap.unsqueeze(2).broadcast_to([128, 8, 64]), ALU.mult, [psO.res, rinv.res], [natok.res])
            pst = psT.ap.bitcast(BF16)
            for c in range(4):
                self.tr(pst[:, 128 * c:128 * (c + 1)], natok.ap[:, 128 * c:128 * (c + 1)], self.ident_b.ap,
                        [natok.res, self.ident_b.res], [psT.res])
            self.cp("act", naT.ap, pst[:, 0:512].rearrange("p (a b) -> p a b", a=4), [psT.res], [naT.res])
            s5 = s5t[ui % 2]
            P.dma("sp", s5.ap, self.s5T_in[:, 128 * t:128 * (t + 1)].rearrange("(k p) t -> p k t", p=128), [], [s5.res])
            for half in range(2):
                pb = self.bank[4 + half]
                for k in range(8):
                    lhsT = s5.ap[:, k, :] if k < 4 else naT.ap[:, k - 4, :]
                    self.mm(pb.ap, lhsT, wout.ap[:, k, 512 * half:512 * (half + 1)], k == 0, k == 7,
                            [s5.res, naT.res, wout.res], [pb.res])
            po_ap = self.psall[:, 2048:3072]
            pres = [self.bank[4].res, self.bank[5].res]
            v = 0 if kind == "lat" else 1
            rstd = self.rstd_of(po_ap, pres, junk)
            self.stt("dve", tmp.ap, po_ap, rstd.ap, self.G[0][v].ap, ALU.mult, ALU.mult, pres + [rstd.res, self.G[0][v].res], [tmp.res])
            self.tt("pool", h_ap, tmp.ap, h_ap, ALU.add, [tmp.res, h_res], [h_res])
            if kind == "lat":
                P.dma("sp", self.out[128 * m:128 * (m + 1), :], h_ap, [h_res], [self.out_res[m]])


    def gen_tables(self):
        A = self.A
        P = self.P
        P.barrier()
        A.reset()
        W = 1024
        kf = A.alloc(F32, [4096])
        ki = A.alloc(I32, [4096])
        tcol = A.alloc(F32, [32])
        ti = A.alloc(I32, [32])
        P.op("pool", lambda e: e.iota(ki.ap, [[1, 4096]], base=0, channel_multiplier=0), [], [ki.res])
        self.cp("dve", kf.ap, ki.ap, [ki.res], [kf.res])
        P.op("pool", lambda e: e.iota(ti.ap, [[128, 32]], base=0, channel_multiplier=1), [], [ti.res])
        self.cp("dve", tcol.ap, ti.ap, [ti.res], [tcol.res])
        negpi = self.negpi
        t1 = [A.alloc(I32, [W]) for _ in range(2)]
        t2 = [A.alloc(I32, [W]) for _ in range(2)]
        t3 = [A.alloc(F32, [W]) for _ in range(2)]
        t4 = [A.alloc(F32, [W]) for _ in range(2)]
        ob = [A.alloc(BF16, [W]) for _ in range(4)]
        it = 0
        nchunks = 32 if not self.dbg.startswith("tab1") else 1
        sc = 2.0 * math.pi / 4096.0
        for c in range(nchunks):
            for kt in range(4096 // W if not (self.dbg.startswith("odd1") or self.dbg.startswith("tab1")) else 1):
                ai, ci, bf, cf = t1[it % 2], t2[it % 2], t3[it % 2], t4[it % 2]
                os_, oc = ob[(2 * it) % 4], ob[(2 * it + 1) % 4]
                self.ts("dve", ai.ap, kf.ap[:, kt * W:(kt + 1) * W], tcol.ap[:, c:c + 1], 2048.0, ALU.mult, ALU.add,
                        [kf.res, tcol.res], [ai.res])
                self.ts("dve", ci.ap, kf.ap[:, kt * W:(kt + 1) * W], tcol.ap[:, c:c + 1], 3072.0, ALU.mult, ALU.add,
                        [kf.res, tcol.res], [ci.res])
                self.ts("dve", ai.ap, ai.ap, 4095, None, ALU.bitwise_and, None, [ai.res], [ai.res])
                self.ts("dve", ci.ap, ci.ap, 4095, None, ALU.bitwise_and, None, [ci.res], [ci.res])
                self.cp("pool", bf.ap, ai.ap, [ai.res], [bf.res])
                self.cp("pool", cf.ap, ci.ap, [ci.res], [cf.res])
                self.act(os_.ap, bf.ap, AF.Sin, [bf.res, negpi.res], [os_.res], bias=negpi.ap, scale=sc)
                self.act(oc.ap, cf.ap, AF.Sin, [cf.res, negpi.res], [oc.res], bias=negpi.ap, scale=sc)
                P.dma("sp", self.Stab[128 * c:128 * (c + 1), kt * W:(kt + 1) * W], os_.ap, [os_.res], [self.tab_res])
                P.dma("sp", self.Ctab[128 * c:128 * (c + 1), kt * W:(kt + 1) * W], oc.ap, [oc.res], [self.tab_res])
                it += 1

    def gen_channel_tables(self, CD, SDn):
        A = self.A
        P = self.P
        kf = A.alloc(F32, [1024])
        ki = A.alloc(I32, [1024])
        tcol = A.alloc(F32, [8])
        ti = A.alloc(I32, [8])
        P.op("pool", lambda e: e.iota(ki.ap, [[1, 1024]], base=0, channel_multiplier=0), [], [ki.res])
        self.cp("dve", kf.ap, ki.ap, [ki.res], [kf.res])
        P.op("pool", lambda e: e.iota(ti.ap, [[128, 8]], base=0, channel_multiplier=1), [], [ti.res])
        self.cp("dve", tcol.ap, ti.ap, [ti.res], [tcol.res])
        ai = A.alloc(I32, [1024])
        ci = A.alloc(I32, [1024])
        b_ = A.alloc(F32, [1024])
        c3 = A.alloc(F32, [1024])
        negpi = self.negpi
        sc = 2.0 * math.pi / 1024.0
        for k in range(8):
            self.ts("dve", ai.ap, kf.ap, tcol.ap[:, k:k + 1], 512.0, ALU.mult, ALU.add, [kf.res, tcol.res], [ai.res])
            self.ts("dve", ci.ap, kf.ap, tcol.ap[:, k:k + 1], 768.0, ALU.mult, ALU.add, [kf.res, tcol.res], [ci.res])
            self.ts("dve", ai.ap, ai.ap, 1023, None, ALU.bitwise_and, None, [ai.res], [ai.res])
            self.ts("dve", ci.ap, ci.ap, 1023, None, ALU.bitwise_and, None, [ci.res], [ci.res])
            self.cp("pool", b_.ap, ai.ap, [ai.res], [b_.res])
            self.cp("pool", c3.ap, ci.ap, [ci.res], [c3.res])
            self.act(b_.ap, b_.ap, AF.Sin, [b_.res, negpi.res], [b_.res], bias=negpi.ap, scale=sc)
            self.act(c3.ap, c3.ap, AF.Sin, [c3.res, negpi.res], [c3.res], bias=negpi.ap, scale=sc)
            self.ts("dve", SDn.ap[:, k, :], b_.ap, -1.0 / 2048.0, None, ALU.mult, None, [b_.res], [SDn.res])
            self.ts("pool", CD.ap[:, k, :], c3.ap, 1.0 / 2048.0, None, ALU.mult, None, [c3.res], [CD.res])

    def odd_mixer(self, l):
        A = self.A
        P = self.P
        o = l // 2
        upd_ctx = l < 2
        P.barrier()
        A.reset()
        ycc = A.alloc(BF16, [8, 256])
        ysc = A.alloc(BF16, [8, 256])
        base2 = A.top
        hl = A.alloc(BF16, [32, 1024])
        hlr = [Res() for _ in range(32)]
        hc = A.alloc(BF16, [2, 1024])
        hcr = [Res() for _ in range(2)]
        nv = 2 if upd_ctx else 1
        with_tmp = A.top
        A1b = [A.alloc(F32, [1024]) for _ in range(nv)]
        B1b = [A.alloc(F32, [1024]) for _ in range(nv)]
        wms = [A.alloc(F32, [8, 512]) for _ in range(2)]
        tb = A.alloc(F32, [512])
        tg = A.alloc(F32, [512])
        ones = A.alloc(F32, [512])
        self.memset("pool", ones.ap, 1.0, [ones.res])
        for nt in range(4):
            wm = wms[nt % 2]
            half = nt % 2
            P.dma("sp", wm.ap, self.w_mod[:, nt * 512:(nt + 1) * 512].rearrange("(k p) n -> p k n", p=128), [], [wm.res])
            self.load_bcast_row("sp", tb, self.b_mod[nt * 512:(nt + 1) * 512])
            if nt >= 2:
                self.load_bcast_row("sp", tg, self.norm_g[0, half * 512:(half + 1) * 512])
            for v in range(nv):
                psb = self.bank[5 + v]
                dst = (B1b if nt < 2 else A1b)[v]
                d_ap = dst.ap[:, half * 512:(half + 1) * 512]
                for k in range(8):
                    self.mm(psb.ap, self.crep[v].ap[:, k, :], wm.ap[:, k, :], k == 0, k == 7, [self.crep[v].res, wm.res], [psb.res])
                if nt < 2:
                    self.tt("dve", d_ap, psb.ap, tb.ap, ALU.add, [psb.res, tb.res], [dst.res])
                else:
                    self.stt("dve", d_ap, psb.ap, 1.0, tb.ap, ALU.add, ALU.add, [psb.res, tb.res], [dst.res])
                    self.tt("dve", d_ap, d_ap, tg.ap, ALU.mult, [dst.res, tg.res], [dst.res])
        ht = [A.alloc(F32, [1024]) for _ in range(2)]
        junk = A.alloc(BF16, [1024])
        tmp = A.alloc(F32, [1024])
        src = self.x
        for t in range(32 + (2 if upd_ctx else 0)):
            if t < 32:
                h = ht[t % 2]
                P.dma("sp", h.ap, src[128 * t:128 * (t + 1), :], [], [h.res])
                h_ap, h_res, v, d_ap, d_res = h.ap, h.res, 0, hl.ap[:, t, :], hlr[t]
            else:
                i = t - 32
                h_ap, h_res, v, d_ap, d_res = self.sctx.ap[:, i, :], self.sctx_res[i], 1, hc.ap[:, i, :], hcr[i]
            rstd = self.rstd_of(h_ap, [h_res], junk)
            self.stt("dve", tmp.ap, h_ap, rstd.ap, A1b[v].ap, ALU.mult, ALU.mult, [h_res, rstd.res, A1b[v].res], [tmp.res])
            self.tt("pool", d_ap, tmp.ap, B1b[v].ap, ALU.add, [tmp.res, B1b[v].res], [d_res])
        P.barrier()
        A.top = with_tmp
        NT = 256
        ctile = [A.alloc(BF16, [32, NT]) for _ in range(2)]
        stile = [A.alloc(BF16, [32, NT]) for _ in range(2)]
        yev = [A.alloc(BF16, [8, NT]) for _ in range(4)]
        nkt = 4096 // NT
        if self.dbg.startswith("odd1"):
            nkt = 1
        for kt in range(nkt):
            ct, stl = ctile[kt % 2], stile[kt % 2]
            P.dma("sp", ct.ap, self.Ctab[:, kt * NT:(kt + 1) * NT].rearrange("(c p) k -> p c k", p=128), [self.tab_res], [ct.res])
            P.dma("sp", stl.ap, self.Stab[:, kt * NT:(kt + 1) * NT].rearrange("(c p) k -> p c k", p=128), [self.tab_res], [stl.res])
            yc, ys = yev[(2 * kt) % 4], yev[(2 * kt + 1) % 4]
            for dc in range(8):
                for (tab, yo, bi) in ((ct, yc, 1), (stl, ys, 2)):
                    pb = self.bank[bi + 2 * (dc % 2)]
                    pj = Tile(pb.ap[:, 0:NT], pb.res)
                    for c in range(32):
                        self.mm(pj.ap, hl.ap[:, c, 128 * dc:128 * (dc + 1)], tab.ap[:, c, :], c == 0, c == 31,
                                [hlr[c], tab.res], [pj.res])
                    if bi == 1:
                        self.cp("act", yo.ap[:, dc, :], pj.ap, [pj.res], [yo.res])
                    else:
                        self.cp("dve", yo.ap[:, dc, :], pj.ap, [pj.res], [yo.res])
            P.dma("sp", self.Yc[:, kt * NT:(kt + 1) * NT].rearrange("(c p) k -> p c k", p=128), yc.ap, [yc.res], [self.Y_res[kt]])
            P.dma("sp", self.Ys[:, kt * NT:(kt + 1) * NT].rearrange("(c p) k -> p c k", p=128), ys.ap, [ys.res], [self.Y_res[kt]])
        if upd_ctx:
            ct, stl = ctile[nkt % 2], stile[nkt % 2]
            csrc = self.Ctab.rearrange("(t s) k -> t s k", s=16)[:, 0, 0:256].rearrange("(c p) k -> p c k", p=128)
            ssrc = self.Stab.rearrange("(t s) k -> t s k", s=16)[:, 0, 0:256].rearrange("(c p) k -> p c k", p=128)
            P.dma("sp", ct.ap[:, 0:2, :], csrc, [self.tab_res], [ct.res])
            P.dma("sp", stl.ap[:, 0:2, :], ssrc, [self.tab_res], [stl.res])
            for dc in range(8):
                for (tab, yo, bi) in ((ct, ycc, 1), (stl, ysc, 2)):
                    pb = self.bank[bi + 2 * (dc % 2)]
                    pj = Tile(pb.ap[:, 0:256], pb.res)
                    for c in range(2):
                        self.mm(pj.ap, hc.ap[:, c, 128 * dc:128 * (dc + 1)], tab.ap[:, c, :], c == 0, c == 1,
                                [hcr[c], tab.res], [pj.res])
                    self.ts("dve", yo.ap[:, dc, :], pj.ap, 4.0, None, ALU.mult, None, [pj.res], [yo.res])
        P.barrier()
        A.top = base2
        CD = A.alloc(BF16, [8, 1024])
        SDn = A.alloc(BF16, [8, 1024])
        wf = A.alloc(BF16, [8, 1024])
        P.dma("pool", wf.ap, self.w_fourier.rearrange("(k p) n -> p k n", p=128), [], [wf.res])
        yct = [A.alloc(BF16, [8, 256]) for _ in range(2)]
        yst = [A.alloc(BF16, [8, 256]) for _ in range(2)]
        hfT = [A.alloc(BF16, [8, 256]) for _ in range(2)]
        ht = [A.alloc(F32, [1024]) for _ in range(2)]
        junk = A.alloc(BF16, [1024])
        tmp = A.alloc(F32, [1024])
        ycc2, ysc2 = ycc, ysc
        mark = A.top
        self.gen_channel_tables(CD, SDn)
        A.top = mark
        units = [("lat", kt) for kt in range(nkt)] + ([("ctx", 0)] if upd_ctx else [])
        for ui, (kind, kt) in enumerate(units):
            v = 0 if kind == "lat" else 1
            if kind == "lat":
                yc, ys = yct[ui % 2], yst[ui % 2]
                P.dma("sp", yc.ap, self.Yc[:, kt * 256:(kt + 1) * 256].rearrange("(c p) k -> p c k", p=128), [self.Y_res[kt]], [yc.res])
                P.dma("sp", ys.ap, self.Ys[:, kt * 256:(kt + 1) * 256].rearrange("(c p) k -> p c k", p=128), [self.Y_res[kt]], [ys.res])
            else:
                yc, ys = ycc2, ysc2
            hf = hfT[ui % 2]
            for ec in range(8):
                pb = self.bank[1 + (ec % 2)]
                pj = Tile(pb.ap[:, 0:256], pb.res)
                for dc in range(8):
                    self.mm(pj.ap, CD.ap[:, dc, 128 * ec:128 * (ec + 1)], yc.ap[:, dc, :], dc == 0, False, [CD.res, yc.res], [pj.res])
                for dc in range(8):
                    self.mm(pj.ap, SDn.ap[:, dc, 128 * ec:128 * (ec + 1)], ys.ap[:, dc, :], False, dc == 7, [SDn.res, ys.res], [pj.res])
                self.cp("act" if ec % 2 else "dve", hf.ap[:, ec, :], pj.ap, [pj.res], [hf.res])
            for sub in range(2):
                for half in range(2):
                    pb = self.bank[3 + 2 * sub + half]
                    for ec in range(8):
                        self.mm(pb.ap, hf.ap[:, ec, sub * 128:(sub + 1) * 128], wf.ap[:, ec, half * 512:(half + 1) * 512],
                                ec == 0, ec == 7, [hf.res, wf.res], [pb.res])
                po_ap = self.psall[:, (3 + 2 * sub) * 512:(5 + 2 * sub) * 512]
                pres = [self.bank[3 + 2 * sub].res, self.bank[4 + 2 * sub].res]
                if kind == "lat":
                    t128 = kt * 2 + sub
                    h = ht[sub]
                    P.dma("sp", h.ap, self.x[128 * t128:128 * (t128 + 1), :], [], [h.res])
                    h_ap, h_res = h.ap, h.res
                else:
                    h_ap, h_res = self.sctx.ap[:, sub, :], self.sctx_res[sub]
                rstd = self.rstd_of(po_ap, pres, junk)
                self.stt("dve", tmp.ap, po_ap, rstd.ap, self.G[0][v].ap, ALU.mult, ALU.mult,
                         pres + [rstd.res, self.G[0][v].res], [tmp.res])
                self.tt("pool", h_ap, tmp.ap, h_ap, ALU.add, [tmp.res, h_res], [h_res])
                if kind == "lat":
                    P.dma("sp", self.out[128 * t128:128 * (t128 + 1), :], h_ap, [h_res], [self.out_res[t128]])


_CACHE = {}


def _get_nc(step, dbg=""):
    key = (step, dbg)
    if key not in _CACHE:
        _CACHE[key] = Builder(step, dbg).build()
    return _CACHE[key]


def _launch(step, per_core, shared, dbg=""):
    nc = _get_nc(step, dbg)
    in_maps = []
    for b in range(len(per_core)):
        m = dict(shared)
        m.update(per_core[b])
        in_maps.append(m)
    res = run_bass_kernel_spmd(nc, in_maps, core_ids=list(range(len(per_core))))
    return res.results


def _f32(a):
    return np.ascontiguousarray(a, dtype=np.float32)


def run_steps(inputs, steps, ncores=8, dbg=""):
    h = [_f32(inputs["x"][b]) for b in range(ncores)]
    s = [_f32(inputs["ctx"][b]) for b in range(ncores)]
    cs = [_f32(inputs["c"][b]) for b in range(ncores)]
    s5T = None
    for step in steps:
        kind, l = step
        e = l // 2
        shared = {"c_ctx": _f32(inputs["c_ctx"]), "w_mod": _f32(inputs["w_mod"][l]), "b_mod": _f32(inputs["b_mod"][l]),
                  "norm_g": _f32(inputs["norm_g"][l])}
        if kind == "mlp":
            shared["w_ff1"] = _f32(inputs["w_ff1"][l])
            shared["w_ff2"] = _f32(inputs["w_ff2"][l])
        elif kind == "odd":
            shared["w_fourier"] = _f32(inputs["w_fourier"][e])
        elif kind == "evenA":
            shared["w_in"] = _f32(inputs["w_in"][e])
            for n in ["s5_lam_re", "s5_lam_im", "s5_log_dt", "s5_b_re", "s5_b_im", "s5_c_re", "s5_c_im", "s5_d", "s5_w_glu"]:
                shared[n] = _f32(inputs[n][e])
        elif kind == "evenB":
            shared["w_in"] = _f32(inputs["w_in"][e])
            shared["w_out_even"] = _f32(inputs["w_out_even"][e])
            shared["na_rpb"] = _f32(inputs["na_rpb"][e])
        per_core = []
        for b in range(ncores):
            m = {"x": h[b], "c": cs[b], "ctx": s[b]}
            if kind == "evenB":
                m["s5T"] = s5T[b]
            per_core.append(m)
        res = _launch(step, per_core, shared, dbg)
        if kind == "evenA":
            s5T = [np.ascontiguousarray(r["s5T"]) for r in res]
        else:
            h = [_f32(r["out"]) for r in res]
            s = [_f32(r["sout"]) for r in res]
    return h, s, s5T


def kernel(**inputs):
    steps = []
    for l in range(DEPTH):
        if l % 2 == 0:
            steps += [("evenA", l), ("evenB", l)]
        else:
            steps += [("odd", l)]
        steps += [("mlp", l)]
    h, s, _ = run_steps(inputs, steps, 8)
    return np.stack(h, 0)
```

```python
import contextlib
import math
import os

import numpy as np
import concourse.bass as bass
import concourse.mybir as mybir
from concourse.bass_utils import run_bass_kernel_spmd

F32 = mybir.dt.float32
BF16 = mybir.dt.bfloat16
I32 = mybir.dt.int32
AF = mybir.ActivationFunctionType
ALU = mybir.AluOpType

D = 1024
L = 4096
LC = 256
DFF = 4096
DEPTH = 4
EPS = 1e-6
ENGS = ["pe", "act", "dve", "pool", "sp"]


class Res:
    __slots__ = ("name", "lastw", "readers")

    def __init__(self, name="r"):
        self.name = name
        self.lastw = None
        self.readers = []


class Op:
    __slots__ = ("eng", "fn", "deps", "is_dma", "signal", "sigval", "dslot", "dval")


class Prog:
    NDSLOT = 8

    def __init__(self, nc):
        self.nc = nc
        self.ops = []
        self.per = {e: [] for e in ENGS}
        self.ndma = {e: 0 for e in ENGS}
        self.pending_barrier = {e: None for e in ENGS}
        self.dmas_since_barrier = []

    def barrier(self):
        deps = []
        for e in ENGS:
            for op in reversed(self.per[e]):
                if not op.is_dma:
                    deps.append(op)
                    break
        deps.extend(self.dmas_since_barrier)
        self.dmas_since_barrier = []
        for e in ENGS:
            old = self.pending_barrier[e]
            self.pending_barrier[e] = (old or []) + deps

    def _add(self, eng, fn, reads, writes, is_dma):
        op = Op()
        op.eng = eng
        op.fn = fn
        op.is_dma = is_dma
        op.signal = False
        deps = set()
        for r in reads:
            if r.lastw is not None:
                deps.add(r.lastw)
        for w in writes:
            if w.lastw is not None:
                deps.add(w.lastw)
            for rd in w.readers:
                deps.add(rd)
        if self.pending_barrier[eng] is not None:
            deps.update(self.pending_barrier[eng])
            self.pending_barrier[eng] = None
        deps.discard(op)
        op.deps = [d for d in deps if not (eng == "pe" and d.eng == "pe" and not d.is_dma and not is_dma)]
        for r in reads:
            r.readers.append(op)
        for w in writes:
            w.lastw = op
            w.readers = []
        self.per[eng].append(op)
        self.ops.append(op)
        if is_dma:
            k = self.ndma[eng]
            self.ndma[eng] += 1
            op.dslot = k % self.NDSLOT
            op.dval = 16 * (k // self.NDSLOT + 1)
            self.dmas_since_barrier.append(op)
        return op

    def op(self, eng, fn, reads=(), writes=()):
        return self._add(eng, fn, list(reads), list(writes), False)

    def dma(self, eng, out, in_, reads=(), writes=()):
        return self._add(eng, lambda e: e.dma_start(out=out, in_=in_), list(reads), list(writes), True)

    def emit(self):
        nc = self.nc
        for op in self.ops:
            for d in op.deps:
                if not d.is_dma:
                    d.signal = True
        for e in ENGS:
            cnt = 0
            for op in self.per[e]:
                if not op.is_dma and op.signal:
                    cnt += 1
                    op.sigval = cnt
        with contextlib.ExitStack() as st:
            sems = {e: st.enter_context(nc.semaphore("s_" + e)) for e in ENGS}
            dsems = {e: [st.enter_context(nc.semaphore("d_%s%d" % (e, i))) for i in range(self.NDSLOT)]
                     for e in ENGS if self.ndma[e] > 0}
            block = st.enter_context(nc.Block())

            def run_engine(ename, eng):
                seen = {}
                dseen = {}
                for op in self.per[ename]:
                    for d in op.deps:
                        if d.is_dma:
                            key = (d.eng, d.dslot)
                            if dseen.get(key, 0) < d.dval:
                                eng.wait_ge(dsems[d.eng][d.dslot], d.dval)
                                dseen[key] = d.dval
                        else:
                            if seen.get(d.eng, 0) < d.sigval:
                                eng.wait_ge(sems[d.eng], d.sigval)
                                seen[d.eng] = d.sigval
                    if op.is_dma:
                        key = (ename, op.dslot)
                        if op.dval > 16 and dseen.get(key, 0) < op.dval - 16:
                            eng.wait_ge(dsems[ename][op.dslot], op.dval - 16)
                            dseen[key] = op.dval - 16
                        ins = op.fn(eng)
                        ins.then_inc(dsems[ename][op.dslot], 16)
                    else:
                        ins = op.fn(eng)
                        if op.signal:
                            ins.then_inc(sems[ename], 1)
                if ename in dsems:
                    k = self.ndma[ename]
                    for s in range(self.NDSLOT):
                        n = (k - s + self.NDSLOT - 1) // self.NDSLOT
                        if n > 0 and dseen.get((ename, s), 0) < 16 * n:
                            eng.wait_ge(dsems[ename][s], 16 * n)

            block.tensor(lambda e: run_engine("pe", e))
            block.scalar(lambda e: run_engine("act", e))
            block.vector(lambda e: run_engine("dve", e))
            block.gpsimd(lambda e: run_engine("pool", e))
            block.sync(lambda e: run_engine("sp", e))


class Tile:
    __slots__ = ("ap", "res")

    def __init__(self, ap, res=None):
        self.ap = ap
        self.res = res or Res()

    def __getitem__(self, k):
        return self.ap[k]


class Arena:
    def __init__(self, tensor, ncols):
        self.t = tensor
        self.ncols = ncols
        self.top = 0
        self.base = 0

    def alloc(self, dtype, shape):
        n = 1
        for s in shape:
            n *= s
        units = n * (2 if dtype in (F32, I32) else 1)
        units = (units + 31) // 32 * 32
        assert self.top + units <= self.ncols, ("arena overflow", self.top, units, self.ncols)
        ap = self.t[:, self.top:self.top + n * (2 if dtype in (F32, I32) else 1)]
        self.top += units
        if dtype != BF16:
            ap = ap.bitcast(dtype)
        if len(shape) == 2:
            ap = ap.rearrange("p (a b) -> p a b", a=shape[0])
        elif len(shape) == 3:
            ap = ap.rearrange("p (a b c) -> p a b c", a=shape[0], b=shape[1])
        return Tile(ap)

    def mark_persistent(self):
        self.base = self.top

    def reset(self):
        self.top = self.base


class Builder:
    def __init__(self, step, dbg=""):
        self.dbg = dbg
        self.step = step
        kind, l = step
        nc = bass.Bass("TRN2", target_bir_lowering=False)
        self.nc = nc
        self.P = Prog(nc)

        def din(name, shape, dt=F32):
            return nc.dram_tensor(name, list(shape), dt, kind="ExternalInput").ap()

        if kind == "fused":
            self.init_fused(din)
            return
        self.x = din("x", [L, D])
        self.c = din("c", [D])
        self.ctx = din("ctx", [LC, D])
        self.c_ctx = din("c_ctx", [D])
        self.w_mod = din("w_mod", [D, 6 * D])
        self.b_mod = din("b_mod", [6 * D])
        self.norm_g = din("norm_g", [4, D])
        if kind == "mlp":
            self.w_ff1 = din("w_ff1", [D, DFF])
            self.w_ff2 = din("w_ff2", [DFF, D])
        if kind == "odd":
            self.w_fourier = din("w_fourier", [D, D])
        if kind in ("evenA", "evenB"):
            self.w_in = din("w_in", [D, 2048])
        if kind == "evenA":
            self.s5_lam_re = din("s5_lam_re", [2, 32, 64])
            self.s5_lam_im = din("s5_lam_im", [2, 32, 64])
            self.s5_log_dt = din("s5_log_dt", [2, 32])
            self.s5_b_re = din("s5_b_re", [2, 32, 64, 16])
            self.s5_b_im = din("s5_b_im", [2, 32, 64, 16])
            self.s5_c_re = din("s5_c_re", [2, 32, 16, 64])
            self.s5_c_im = din("s5_c_im", [2, 32, 16, 64])
            self.s5_d = din("s5_d", [512])
            self.s5_w_glu = din("s5_w_glu", [512, 512])
            self.s5T_out = nc.dram_tensor("s5T", [512, L + LC], BF16, kind="ExternalOutput").ap()
        if kind == "evenB":
            self.w_out_even = din("w_out_even", [D, D])
            self.na_rpb = din("na_rpb", [8, 15, 31])
            self.s5T_in = din("s5T", [512, L + LC], BF16)
        if kind != "evenA":
            self.out = nc.dram_tensor("out", [L, D], F32, kind="ExternalOutput").ap()
            self.sout = nc.dram_tensor("sout", [LC, D], F32, kind="ExternalOutput").ap()
        self.out_res = [Res("out%d" % i) for i in range(L // 128)]
        if kind == "odd":
            self.Ctab = nc.dram_tensor("Ctab", [L, L], BF16).ap()
            self.Stab = nc.dram_tensor("Stab", [L, L], BF16).ap()
            self.tab_res = Res("tab")
            self.Yc = nc.dram_tensor("Yc", [D, L], BF16).ap()
            self.Ys = nc.dram_tensor("Ys", [D, L], BF16).ap()
            self.Y_res = [Res("Y%d" % i) for i in range(16)]

    def init_fused(self, din):
        nc = self.nc
        self.x_in = din("x", [L, D])
        self.c = din("c", [D])
        self.ctx = din("ctx", [LC, D])
        self.c_ctx = din("c_ctx", [D])
        W = {}
        W["w_mod"] = din("w_mod", [DEPTH, D, 6 * D])
        W["b_mod"] = din("b_mod", [DEPTH, 6 * D])
        W["norm_g"] = din("norm_g", [DEPTH, 4, D])
        W["w_in"] = din("w_in", [2, D, 2048])
        W["w_out_even"] = din("w_out_even", [2, D, D])
        W["s5_lam_re"] = din("s5_lam_re", [2, 2, 32, 64])
        W["s5_lam_im"] = din("s5_lam_im", [2, 2, 32, 64])
        W["s5_log_dt"] = din("s5_log_dt", [2, 2, 32])
        W["s5_b_re"] = din("s5_b_re", [2, 2, 32, 64, 16])
        W["s5_b_im"] = din("s5_b_im", [2, 2, 32, 64, 16])
        W["s5_c_re"] = din("s5_c_re", [2, 2, 32, 16, 64])
        W["s5_c_im"] = din("s5_c_im", [2, 2, 32, 16, 64])
        W["s5_d"] = din("s5_d", [2, 512])
        W["s5_w_glu"] = din("s5_w_glu", [2, 512, 512])
        W["na_rpb"] = din("na_rpb", [2, 8, 15, 31])
        W["w_fourier"] = din("w_fourier", [2, D, D])
        W["w_ff1"] = din("w_ff1", [DEPTH, D, DFF])
        W["w_ff2"] = din("w_ff2", [DEPTH, DFF, D])
        self.W = W
        self.out_final = nc.dram_tensor("out", [L, D], F32, kind="ExternalOutput").ap()
        self.hA = nc.dram_tensor("hA", [L, D], F32).ap()
        s5 = nc.dram_tensor("s5T", [512, L + LC], BF16).ap()
        self.s5T_out = s5
        self.s5T_in = s5
        self.out_res = [Res("out%d" % i) for i in range(L // 128)]
        self.Ctab = nc.dram_tensor("Ctab", [L, L], BF16).ap()
        self.Stab = nc.dram_tensor("Stab", [L, L], BF16).ap()
        self.tab_res = Res("tab")
        self.Yc = nc.dram_tensor("Yc", [D, L], BF16).ap()
        self.Ys = nc.dram_tensor("Ys", [D, L], BF16).ap()
        self.Y_res = [Res("Y%d" % i) for i in range(16)]

    def set_layer(self, l):
        W = self.W
        e = l // 2
        self.w_mod = W["w_mod"][l]
        self.b_mod = W["b_mod"][l]
        self.norm_g = W["norm_g"][l]
        self.w_ff1 = W["w_ff1"][l]
        self.w_ff2 = W["w_ff2"][l]
        if l % 2 == 0:
            self.w_in = W["w_in"][e]
            self.w_out_even = W["w_out_even"][e]
            for n in ["s5_lam_re", "s5_lam_im", "s5_log_dt", "s5_b_re", "s5_b_im", "s5_c_re", "s5_c_im", "s5_d", "s5_w_glu", "na_rpb"]:
                setattr(self, n, W[n][e])
        else:
            self.w_fourier = W["w_fourier"][e]

    def build_fused(self):
        P = self.P
        self.setup_consts()
        bufs = [self.x_in, self.hA, self.out_final]
        step = 0
        for l in range(DEPTH):
            self.set_layer(l)
            self.mod_phase(l)
            self.x = bufs[0] if step == 0 else (self.out_final if step % 2 == 0 else self.hA)
            self.out = self.hA if step % 2 == 0 else self.out_final
            if l % 2 == 0:
                self.even_a(l)
                self.even_b(l)
            else:
                if l == 1:
                    self.gen_tables()
                self.odd_mixer(l)
            step += 1
            self.x = self.out_final if step % 2 == 0 else self.hA
            self.out = self.hA if step % 2 == 0 else self.out_final
            self.mlp_phase(l)
            step += 1

    def mm(self, out, lhsT, rhs, start, stop, reads, writes):
        self.P.op("pe", lambda e: e.matmul(out, lhsT=lhsT, rhs=rhs, start=start, stop=stop), reads, writes)

    def tr(self, out, in_, ident, reads, writes):
        self.P.op("pe", lambda e: e.transpose(out, in_, ident), reads, writes)

    def act(self, out, in_, func, reads, writes, bias=None, scale=None, accum_out=None, eng="act"):
        kw = {}
        if bias is not None:
            kw["bias"] = bias
        if scale is not None:
            kw["scale"] = scale
        if accum_out is not None:
            kw["accum_out"] = accum_out
        self.P.op(eng, lambda e: e.activation(out=out, in_=in_, func=func, **kw), reads, writes)

    def tt(self, eng, out, in0, in1, op, reads, writes):
        self.P.op(eng, lambda e: e.tensor_tensor(out=out, in0=in0, in1=in1, op=op), reads, writes)

    def ts(self, eng, out, in0, s1, s2, op0, op1, reads, writes):
        if op1 is None:
            self.P.op(eng, lambda e: e.tensor_single_scalar(out=out, in_=in0, scalar=s1, op=op0), reads, writes)
        else:
            self.P.op(eng, lambda e: e.tensor_scalar(out=out, in0=in0, scalar1=s1, scalar2=s2, op0=op0, op1=op1), reads, writes)

    def stt(self, eng, out, in0, scalar, in1, op0, op1, reads, writes):
        self.P.op(eng, lambda e: e.scalar_tensor_tensor(out=out, in0=in0, scalar=scalar, in1=in1, op0=op0, op1=op1), reads, writes)

    def cp(self, eng, out, in_, reads, writes):
        if eng == "act":
            self.P.op(eng, lambda e: e.copy(out=out, in_=in_), reads, writes)
        else:
            self.P.op(eng, lambda e: e.tensor_copy(out=out, in_=in_), reads, writes)

    def memset(self, eng, ap, val, writes):
        self.P.op(eng, lambda e: e.memset(ap, val), [], writes)

    def build(self):
        nc = self.nc
        P = self.P
        with contextlib.ExitStack() as st:
            NCOLS = 106000
            arena_t = st.enter_context(nc.sbuf_tensor("arena", [128, NCOLS], BF16))
            self.A = Arena(arena_t, NCOLS)
            psall = st.enter_context(nc.psum_tensor("psall", [128, 4096], F32))
            self.psall = psall
            self.bank = [Tile(psall[:, 512 * i:512 * (i + 1)], Res("bank%d" % i)) for i in range(8)]
            kind, l = self.step
            if kind == "fused":
                self.build_fused()
                P.emit()
                return nc
            self.setup_consts()
            self.mod_phase(l)
            if kind == "mlp":
                self.mlp_phase(l)
            elif kind == "odd":
                self.gen_tables()
                self.odd_mixer(l)
            elif kind == "evenA":
                self.even_a(l)
            elif kind == "evenB":
                self.even_b(l)
            if kind != "evenA":
                P.barrier()
                for i in range(2):
                    P.dma("sp", self.sout[128 * i:128 * (i + 1), :], self.sctx.ap[:, i, :], [self.sctx_res[i]], [Res()])
            P.emit()
        return nc

    def setup_consts(self):
        A = self.A
        P = self.P
        it = A.alloc(I32, [128])
        self.ident_f = A.alloc(F32, [128])
        self.ident_b = A.alloc(BF16, [128])
        P.op("pool", lambda e: e.iota(it.ap, [[1, 128]], base=0, channel_multiplier=-1), [], [it.res])
        self.cp("dve", self.ident_f.ap, it.ap, [it.res], [self.ident_f.res])
        self.ts("dve", self.ident_f.ap, self.ident_f.ap, 0.0, None, ALU.is_equal, None, [self.ident_f.res], [self.ident_f.res])
        self.cp("dve", self.ident_b.ap, self.ident_f.ap, [self.ident_f.res], [self.ident_b.res])
        self.cact2 = A.alloc(F32, [8, 2])
        craw = A.alloc(F32, [2, 128])
        P.dma("sp", craw.ap[0:8, 0, :], self.c.rearrange("(k p) -> k p", p=128), [], [craw.res])
        P.dma("sp", craw.ap[0:8, 1, :], self.c_ctx.rearrange("(k p) -> k p", p=128), [], [craw.res])
        for v in range(2):
            pv = Tile(self.bank[7].ap[:, 8 * v:8 * v + 8], self.bank[7].res)
            self.tr(pv.ap, craw.ap[0:8, v, :], self.ident_f.ap[0:8, 0:8], [craw.res, self.ident_f.res], [pv.res])
            self.act(self.cact2.ap[:, :, v], pv.ap, AF.Silu, [pv.res], [self.cact2.res])
        self.crep = [A.alloc(F32, [8, 128]) for _ in range(2)]
        ones = A.alloc(F32, [128])
        self.memset("pool", ones.ap, 1.0, [ones.res])
        for v in range(2):
            for k in range(8):
                self.ts("dve", self.crep[v].ap[:, k, :], ones.ap, self.cact2.ap[:, k, v:v + 1], None, ALU.mult, None,
                        [ones.res, self.cact2.res], [self.crep[v].res])
        self.modpp = A.alloc(F32, [48, 2])
        self.gpp = A.alloc(F32, [4, 8])
        self.A1 = A.alloc(F32, [8, 2])
        self.A2 = A.alloc(F32, [8, 2])
        self.G = [[A.alloc(F32, [1024]) for v in range(2)] for i in range(2)]
        self.sctx = A.alloc(F32, [2, 1024])
        self.sctx_res = [Res("sctx0"), Res("sctx1")]
        for i in range(2):
            P.dma("sp", self.sctx.ap[:, i, :], self.ctx[128 * i:128 * (i + 1), :], [], [self.sctx_res[i]])
        self.negpi = A.alloc(F32, [1])
        self.memset("pool", self.negpi.ap, -math.pi, [self.negpi.res])
        self.small = A.alloc(F32, [64])
        self.small_tiles = [Tile(self.small.ap[:, i:i + 1]) for i in range(64)]
        self.small_n = 0
        A.mark_persistent()

    def scalar_slot(self):
        i = self.small_n % 64
        self.small_n += 1
        return self.small_tiles[i]

    def bcast_tile(self, l, ntile, v, wm, dst_ap, dst_res, gain_idx, tmpb, tmpg, psb):
        P = self.P
        for k in range(8):
            self.mm(psb.ap, self.crep[v].ap[:, k, :], wm.ap[:, k, :], k == 0, k == 7, [self.crep[v].res, wm.res], [psb.res])
        if gain_idx is None:
            self.tt("dve", dst_ap, psb.ap, tmpb.ap, ALU.add, [psb.res, tmpb.res], [dst_res])
        else:
            self.tt("dve", dst_ap, psb.ap, tmpb.ap, ALU.add, [psb.res, tmpb.res], [dst_res])
            self.tt("dve", dst_ap, dst_ap, tmpg.ap, ALU.mult, [dst_res, tmpg.res], [dst_res])

    def load_bcast_row(self, eng, tile, src_row):
        self.P.dma(eng, tile.ap, src_row.partition_broadcast(128), [], [tile.res])

    def mod_phase(self, l):
        A = self.A
        P = self.P
        P.barrier()
        A.reset()
        wms = [A.alloc(F32, [8, 512]) for _ in range(2)]
        bpp = A.alloc(F32, [48])
        tmpb = [A.alloc(F32, [512]) for _ in range(2)]
        tmpg = [A.alloc(F32, [512]) for _ in range(2)]
        braw = A.alloc(F32, [128])
        graw = A.alloc(F32, [128])
        P.dma("sp", braw.ap[0:48, :], self.b_mod.rearrange("(j p) -> j p", p=128), [], [braw.res])
        P.dma("sp", graw.ap[0:32, :], self.norm_g.rearrange("g (k p) -> (g k) p", p=128), [], [graw.res])
        pb_ = Tile(self.bank[7].ap[:, 64:112], self.bank[7].res)
        self.tr(pb_.ap, braw.ap[0:48, :], self.ident_f.ap[0:48, 0:48], [braw.res, self.ident_f.res], [pb_.res])
        self.cp("dve", bpp.ap, pb_.ap, [pb_.res], [bpp.res])
        pg_ = Tile(self.bank[7].ap[:, 128:160], self.bank[7].res)
        self.tr(pg_.ap, graw.ap[0:32, :], self.ident_f.ap[0:32, 0:32], [graw.res, self.ident_f.res], [pg_.res])
        self.cp("dve", self.gpp.ap, pg_.ap.rearrange("p (g k) -> p g k", g=4), [pg_.res], [self.gpp.res])
        psA = Tile(self.bank[7].ap[:, 0:8], self.bank[7].res)
        for nt in range(12):
            wm = wms[nt % 2]
            P.dma("sp", wm.ap, self.w_mod[:, nt * 512:(nt + 1) * 512].rearrange("(k p) n -> p k n", p=128), [], [wm.res])
            for jj in range(4):
                for k in range(8):
                    self.mm(psA.ap[:, 2 * jj:2 * jj + 2], wm.ap[:, k, jj * 128:(jj + 1) * 128], self.cact2.ap[:, k, :],
                            k == 0, k == 7, [wm.res, self.cact2.res], [psA.res])
            self.tt("dve", self.modpp.ap[:, nt * 4:(nt + 1) * 4, :], psA.ap.rearrange("p (a b) -> p a b", b=2),
                    bpp.ap[:, nt * 4:(nt + 1) * 4].unsqueeze(2).broadcast_to([128, 4, 2]), ALU.add,
                    [psA.res, bpp.res], [self.modpp.res])
            gi = {4: (0, 0), 5: (0, 1), 10: (1, 0), 11: (1, 1)}.get(nt)
            if gi is not None:
                i, half = gi
                tb = tmpb[half]
                tg = tmpg[half]
                self.load_bcast_row("sp", tb, self.b_mod[nt * 512:(nt + 1) * 512])
                self.load_bcast_row("sp", tg, self.norm_g[1 + 2 * i, half * 512:(half + 1) * 512])
                for v in range(2):
                    psb = self.bank[5 + v]
                    self.bcast_tile(l, nt, v, wm, self.G[i][v].ap[:, half * 512:(half + 1) * 512], self.G[i][v].res, 1, tb, tg, psb)
        for (Ax, sc_off, gidx) in ((self.A1, 8, 0), (self.A2, 32, 2)):
            self.stt("dve", Ax.ap, self.modpp.ap[:, sc_off:sc_off + 8, :], 1.0,
                     self.gpp.ap[:, gidx, :].unsqueeze(2).broadcast_to([128, 8, 2]), ALU.add, ALU.mult,
                     [self.modpp.res, self.gpp.res], [Ax.res])

    def rstd_of(self, src_ap, src_reads, junk, n=1024):
        ss = self.scalar_slot()
        self.memset("pool", ss.ap, 0.0, [ss.res])
        self.act(junk.ap, src_ap, AF.Square, src_reads + [ss.res], [junk.res, ss.res], accum_out=ss.ap)
        self.act(ss.ap, ss.ap, AF.Sqrt, [ss.res], [ss.res], bias=EPS, scale=1.0 / n)
        self.P.op("dve", lambda e: e.reciprocal(out=ss.ap, in_=ss.ap), [ss.res], [ss.res])
        return ss

    def prenorm_T(self, h_ap, h_res, Ax, Bx_ap, Bx_res, v, hs, dstT_ap, dstT_res, psT):
        rstd = self.rstd_of(h_ap, [h_res], hs)
        self.act(hs.ap, h_ap, AF.Copy, [h_res, rstd.res], [hs.res], scale=rstd.ap)
        pst = psT.ap.bitcast(BF16)
        for k in range(8):
            self.tr(pst[:, k * 128:(k + 1) * 128], hs.ap[:, k * 128:(k + 1) * 128], self.ident_b.ap,
                    [hs.res, self.ident_b.res], [psT.res])
        p3 = pst.rearrange("p (a b) -> p a b", a=8)
        self.tt("dve", dstT_ap, p3, Ax.ap[:, :, v].unsqueeze(2).broadcast_to([128, 8, 128]), ALU.mult,
                [psT.res, Ax.res], [dstT_res])
        self.tt("pool", dstT_ap, dstT_ap, Bx_ap[:, :, v].unsqueeze(2).broadcast_to([128, 8, 128]), ALU.add,
                [dstT_res, Bx_res], [dstT_res])

    def postnorm_residual(self, po_ap, po_res, Gt, h_ap, h_res, junk, tmp):
        rstd = self.rstd_of(po_ap, [po_res], junk)
        self.stt("dve", tmp.ap, po_ap, rstd.ap, Gt.ap, ALU.mult, ALU.mult, [po_res, rstd.res, Gt.res], [tmp.res])
        self.tt("pool", h_ap, tmp.ap, h_ap, ALU.add, [tmp.res, h_res], [h_res])

    def mlp_phase(self, l):
        A = self.A
        P = self.P
        P.barrier()
        A.reset()
        w1 = A.alloc(BF16, [8, 4096])
        w2 = A.alloc(BF16, [32, 1024])
        w1r = [Res() for _ in range(8)]
        w2r = [Res() for _ in range(8)]
        for k in range(8):
            P.dma("pool", w1.ap[:, k, :], self.w_ff1[128 * k:128 * (k + 1), :], [], [w1r[k]])
        for q in range(8):
            P.dma("pool", w2.ap[:, 4 * q:4 * q + 4, :],
                  self.w_ff2[512 * q:512 * (q + 1), :].rearrange("(j p) n -> p j n", p=128), [], [w2r[q]])
        hid = A.alloc(BF16, [32, 256])
        hidr = [Res() for _ in range(32)]
        hnT = A.alloc(BF16, [8, 256])
        ht = [A.alloc(F32, [1024]) for _ in range(2)]
        hs = A.alloc(BF16, [1024])
        tmp = A.alloc(F32, [1024])
        rl = [A.alloc(F32, [256]) for _ in range(2)]
        B2_ap = self.modpp.ap[:, 24:32, :]
        upd_ctx = l < 2
        tiles = [("lat", i) for i in range(16)] + ([("ctx", 0)] if upd_ctx else [])
        if self.dbg.startswith("mlp1"):
            tiles = tiles[:1]
        for (kind, ti) in tiles:
            v = 0 if kind == "lat" else 1
            hview = []
            for sub in range(2):
                if kind == "lat":
                    t128 = ti * 2 + sub
                    P.dma("sp", ht[sub].ap, self.x[128 * t128:128 * (t128 + 1), :], [], [ht[sub].res])
                    hview.append((ht[sub].ap, ht[sub].res))
                else:
                    hview.append((self.sctx.ap[:, sub, :], self.sctx_res[sub]))
                self.prenorm_T(hview[sub][0], hview[sub][1], self.A2, B2_ap, self.modpp.res, v, hs,
                               hnT.ap[:, :, sub * 128:(sub + 1) * 128], hnT.res, self.bank[0])
            for j in range(32):
                pb = self.bank[1 + (j % 2)]
                pj = Tile(pb.ap[:, 0:256], pb.res)
                for k in range(8):
                    self.mm(pj.ap, w1.ap[:, k, 128 * j:128 * (j + 1)], hnT.ap[:, k, :], k == 0, k == 7,
                            [w1r[k], hnT.res], [pj.res])
                r = rl[j % 2]
                self.act(r.ap, pj.ap, AF.Relu, [pj.res], [r.res])
                self.tt("pool" if j % 2 else "dve", hid.ap[:, j, :], r.ap, r.ap, ALU.mult, [r.res], [hidr[j]])
            for sub in range(2):
                for half in range(2):
                    pb = self.bank[3 + 2 * sub + half]
                    for j in range(32):
                        self.mm(pb.ap, hid.ap[:, j, sub * 128:(sub + 1) * 128], w2.ap[:, j, half * 512:(half + 1) * 512],
                                j == 0, j == 31, [hidr[j], w2r[j // 4]], [pb.res])
                po_ap = self.psall[:, (3 + 2 * sub) * 512:(5 + 2 * sub) * 512]
                pres = [self.bank[3 + 2 * sub].res, self.bank[4 + 2 * sub].res]
                rstd = self.rstd_of(po_ap, pres, hs)
                self.stt("dve", tmp.ap, po_ap, rstd.ap, self.G[1][v].ap, ALU.mult, ALU.mult,
                         pres + [rstd.res, self.G[1][v].res], [tmp.res])
                h_ap, h_res = hview[sub]
                self.tt("pool", h_ap, tmp.ap, h_ap, ALU.add, [tmp.res, h_res], [h_res])
                if kind == "lat":
                    t128 = ti * 2 + sub
                    P.dma("sp", self.out[128 * t128:128 * (t128 + 1), :], h_ap, [h_res], [self.out_res[t128]])

    def cmul(self, eng, out_re, out_im, a_re, a_im, b_re, b_im, t1, t2, reads, wres, neg_im=False):
        rs_ = reads
        self.tt(eng, t1.ap, a_re, b_re, ALU.mult, rs_, [t1.res])
        self.tt(eng, t2.ap, a_im, b_im, ALU.mult, rs_, [t2.res])
        self.tt(eng, out_re, t1.ap, t2.ap, ALU.subtract, [t1.res, t2.res], wres)
        self.tt(eng, t1.ap, a_re, b_im, ALU.mult, rs_, [t1.res])
        self.tt(eng, t2.ap, a_im, b_re, ALU.mult, rs_, [t2.res])
        if neg_im:
            self.stt(eng, out_im, t1.ap, -1.0, t2.ap, ALU.mult, ALU.subtract, [t1.res, t2.res], wres)
        else:
            self.tt(eng, out_im, t1.ap, t2.ap, ALU.add, [t1.res, t2.res], wres)

    def even_a(self, l):
        A = self.A
        P = self.P
        nc = self.nc
        AX = mybir.AxisListType.X
        P.barrier()
        A.reset()
        NB = 544
        RTm = [A.alloc(BF16, [32, 2, 128]) for _ in range(2)]
        for r_ in range(2):
            self.memset("pool", RTm[r_].ap, 0.0, [RTm[r_].res])
        Ob = [[A.alloc(BF16, [16, 128]) for _ in range(2)] for _ in range(2)]
        Tm = A.alloc(BF16, [32, 128])
        ASr = A.alloc(F32, [10, 32])
        ASi = A.alloc(F32, [10, 32])
        ASn = A.alloc(F32, [10, 32])
        keep_top = A.top
        T32 = A.alloc(F32, [32, 128])
        nat = A.alloc(F32, [4, 128])
        lre = A.alloc(F32, [32]); lim = A.alloc(F32, [32]); ldt = A.alloc(F32, [32])
        for (src, dst, bnk) in ((self.s5_lam_re, lre, 0), (self.s5_lam_im, lim, 1)):
            P.dma("sp", nat.ap[0:32, bnk, :], src.rearrange("d (q r) p -> (d q) (r p)", r=2), [], [nat.res])
            pt = Tile(self.bank[7].ap[:, 32 * bnk:32 * bnk + 32], self.bank[7].res)
            self.tr(pt.ap, nat.ap[0:32, bnk, :], self.ident_f.ap[0:32, 0:32], [nat.res, self.ident_f.res], [pt.res])
            self.cp("dve", dst.ap, pt.ap, [pt.res], [dst.res])
        P.dma("sp", nat.ap[0:32, 2, 0:2], self.s5_log_dt.rearrange("d (q r) -> (d q) r", r=2), [], [nat.res])
        dtT = A.alloc(F32, [32])
        pt = Tile(self.bank[7].ap[0:2, 64:96], self.bank[7].res)
        self.tr(pt.ap, nat.ap[0:32, 2, 0:2], self.ident_f.ap[0:32, 0:32], [nat.res, self.ident_f.res], [pt.res])
        self.cp("dve", dtT.ap[0:2, :], pt.ap, [pt.res], [dtT.res])
        sel_i = A.alloc(I32, [128]); sel = A.alloc(F32, [128]); sel2 = A.alloc(F32, [128])
        P.op("pool", lambda e: e.iota(sel_i.ap[0:2, :], [[1, 128]], base=0, channel_multiplier=-64), [], [sel_i.res])
        self.cp("dve", sel.ap[0:2, :], sel_i.ap[0:2, :], [sel_i.res], [sel.res])
        self.ts("dve", sel2.ap[0:2, :], sel.ap[0:2, :], 0.0, None, ALU.is_ge, None, [sel.res], [sel2.res])
        self.ts("dve", sel.ap[0:2, :], sel.ap[0:2, :], 64.0, None, ALU.is_lt, None, [sel.res], [sel.res])
        self.tt("dve", sel.ap[0:2, :], sel.ap[0:2, :], sel2.ap[0:2, :], ALU.mult, [sel.res, sel2.res], [sel.res])
        pt = Tile(self.bank[7].ap[:, 96:128], self.bank[7].res)
        self.mm(pt.ap, sel.ap[0:2, :], dtT.ap[0:2, :], True, True, [sel.res, dtT.res], [pt.res])
        self.cp("dve", ldt.ap, pt.ap, [pt.res], [ldt.res])
        def v32():
            return A.alloc(F32, [32])
        dt = v32(); xm = v32(); mag = v32(); imag = v32(); th = v32(); fr = v32(); frc = v32(); w1 = v32(); w2 = v32()
        sn = v32(); cs = v32(); are = v32(); aim = v32(); ire = v32(); iim = v32(); den = v32(); fre = v32(); fim = v32(); nre = v32()
        self.act(dt.ap, ldt.ap, AF.Exp, [ldt.res], [dt.res])
        self.ts("dve", lre.ap, lre.ap, -1e-4, None, ALU.min, None, [lre.res], [lre.res])
        self.tt("dve", xm.ap, lre.ap, dt.ap, ALU.mult, [lre.res, dt.res], [xm.res])
        self.act(mag.ap, xm.ap, AF.Exp, [xm.res], [mag.res])
        self.act(imag.ap, xm.ap, AF.Exp, [xm.res], [imag.res], scale=-1.0)
        self.tt("dve", th.ap, lim.ap, dt.ap, ALU.mult, [lim.res, dt.res], [th.res])
        ki = A.alloc(I32, [32]); kf = v32()
        self.ts("dve", fr.ap, th.ap, 1.0 / (2.0 * math.pi), None, ALU.mult, None, [th.res], [fr.res])
        self.cp("dve", ki.ap, fr.ap, [fr.res], [ki.res])
        self.cp("dve", kf.ap, ki.ap, [ki.res], [kf.res])
        self.tt("dve", fr.ap, fr.ap, kf.ap, ALU.subtract, [fr.res, kf.res], [fr.res])

        def wrap(x):
            self.ts("dve", w1.ap, x.ap, 0.5, None, ALU.is_gt, None, [x.res], [w1.res])
            self.ts("dve", w2.ap, x.ap, -0.5, None, ALU.is_lt, None, [x.res], [w2.res])
            self.tt("dve", x.ap, x.ap, w1.ap, ALU.subtract, [x.res, w1.res], [x.res])
            self.tt("dve", x.ap, x.ap, w2.ap, ALU.add, [x.res, w2.res], [x.res])
        wrap(fr)
        self.ts("dve", frc.ap, fr.ap, 0.25, None, ALU.add, None, [fr.res], [frc.res])
        wrap(frc)
        self.act(sn.ap, fr.ap, AF.Sin, [fr.res], [sn.res], scale=2.0 * math.pi)
        self.act(cs.ap, frc.ap, AF.Sin, [frc.res], [cs.res], scale=2.0 * math.pi)
        self.tt("dve", are.ap, mag.ap, cs.ap, ALU.mult, [mag.res, cs.res], [are.res])
        self.tt("dve", aim.ap, mag.ap, sn.ap, ALU.mult, [mag.res, sn.res], [aim.res])
        self.tt("dve", ire.ap, imag.ap, cs.ap, ALU.mult, [imag.res, cs.res], [ire.res])
        self.stt("dve", iim.ap, imag.ap, -1.0, sn.ap, ALU.mult, ALU.mult, [imag.res, sn.res], [iim.res])
        self.tt("dve", den.ap, lre.ap, lre.ap, ALU.mult, [lre.res], [den.res])
        self.tt("dve", w1.ap, lim.ap, lim.ap, ALU.mult, [lim.res], [w1.res])
        self.tt("dve", den.ap, den.ap, w1.ap, ALU.add, [den.res, w1.res], [den.res])
        self.P.op("dve", lambda e: e.reciprocal(out=den.ap, in_=den.ap), [den.res], [den.res])
        self.ts("dve", nre.ap, are.ap, -1.0, None, ALU.add, None, [are.res], [nre.res])
        self.tt("dve", w1.ap, nre.ap, lre.ap, ALU.mult, [nre.res, lre.res], [w1.res])
        self.tt("dve", w2.ap, aim.ap, lim.ap, ALU.mult, [aim.res, lim.res], [w2.res])
        self.tt("dve", fre.ap, w1.ap, w2.ap, ALU.add, [w1.res, w2.res], [fre.res])
        self.tt("dve", fre.ap, fre.ap, den.ap, ALU.mult, [fre.res, den.res], [fre.res])
        self.tt("dve", w1.ap, aim.ap, lre.ap, ALU.mult, [aim.res, lre.res], [w1.res])
        self.tt("dve", w2.ap, nre.ap, lim.ap, ALU.mult, [nre.res, lim.res], [w2.res])
        self.tt("dve", fim.ap, w1.ap, w2.ap, ALU.subtract, [w1.res, w2.res], [fim.res])
        self.tt("dve", fim.ap, fim.ap, den.ap, ALU.mult, [fim.res, den.res], [fim.res])
        Epr = A.alloc(F32, [9, 32]); Epi = A.alloc(F32, [9, 32]); Enr = A.alloc(F32, [8, 32]); Eni = A.alloc(F32, [8, 32])
        s1 = v32(); s2 = v32()
        self.memset("pool", Epr.ap[:, 0, :], 1.0, [Epr.res]); self.memset("pool", Epi.ap[:, 0, :], 0.0, [Epi.res])
        self.memset("pool", Enr.ap[:, 0, :], 1.0, [Enr.res]); self.memset("pool", Eni.ap[:, 0, :], 0.0, [Eni.res])
        for j in range(1, 9):
            self.cmul("dve", Epr.ap[:, j, :], Epi.ap[:, j, :], Epr.ap[:, j - 1, :], Epi.ap[:, j - 1, :], are.ap, aim.ap, s1, s2,
                      [Epr.res, Epi.res, are.res, aim.res], [Epr.res, Epi.res])
        for j in range(1, 8):
            self.cmul("dve", Enr.ap[:, j, :], Eni.ap[:, j, :], Enr.ap[:, j - 1, :], Eni.ap[:, j - 1, :], ire.ap, iim.ap, s1, s2,
                      [Enr.res, Eni.res, ire.res, iim.res], [Enr.res, Eni.res])
        self.cp("dve", ASr.ap[:, 0, :], Epr.ap[:, 8, :], [Epr.res], [ASr.res])
        self.cp("dve", ASi.ap[:, 0, :], Epi.ap[:, 8, :], [Epi.res], [ASi.res])
        for k in range(1, 10):
            self.cmul("dve", ASr.ap[:, k, :], ASi.ap[:, k, :], ASr.ap[:, k - 1, :], ASi.ap[:, k - 1, :],
                      ASr.ap[:, k - 1, :], ASi.ap[:, k - 1, :], s1, s2, [ASr.res, ASi.res], [ASr.res, ASi.res])
        self.ts("dve", ASn.ap, ASi.ap, -1.0, None, ALU.mult, None, [ASi.res], [ASn.res])
        if self.dbg.startswith("s5preA"):
            return
        Br = A.alloc(F32, [2, 16, 16]); Bi = A.alloc(F32, [2, 16, 16]); Bbr = A.alloc(F32, [2, 16, 16]); Bbi = A.alloc(F32, [2, 16, 16])
        P.dma("sp", Br.ap, self.s5_b_re.rearrange("d (q r) p h -> (r p) d q h", r=2), [], [Br.res])
        P.dma("sp", Bi.ap, self.s5_b_im.rearrange("d (q r) p h -> (r p) d q h", r=2), [], [Bi.res])
        b1 = A.alloc(F32, [2, 16, 16]); b2 = A.alloc(F32, [2, 16, 16])
        fre3 = fre.ap.rearrange("p (d q) -> p d q", d=2).unsqueeze(3).broadcast_to([128, 2, 16, 16])
        fim3 = fim.ap.rearrange("p (d q) -> p d q", d=2).unsqueeze(3).broadcast_to([128, 2, 16, 16])
        self.cmul("dve", Bbr.ap, Bbi.ap, fre3, fim3, Br.ap, Bi.ap, b1, b2, [fre.res, fim.res, Br.res, Bi.res], [Bbr.res, Bbi.res])
        Cr = A.alloc(F32, [2, 16, 16]); Ci = A.alloc(F32, [2, 16, 16])
        cnat = [A.alloc(F32, [128]) for _ in range(2)]
        ci_ = 0
        for (src, dstC) in ((self.s5_c_re, Cr), (self.s5_c_im, Ci)):
            for d in range(2):
                for ch in range(2):
                    cn = cnat[ci_ % 2]
                    for ql in range(8):
                        q = ch * 8 + ql
                        P.dma("sp", cn.ap[16 * ql:16 * ql + 16, :].rearrange("h (r p) -> h r p", r=2),
                              src[d, 2 * q:2 * q + 2].rearrange("r h p -> h r p"), [], [cn.res])
                    pt = Tile(self.bank[6].ap[:, 128 * (ci_ % 4):128 * (ci_ % 4) + 128], self.bank[6].res)
                    self.tr(pt.ap, cn.ap, self.ident_f.ap, [cn.res, self.ident_f.res], [pt.res])
                    self.cp("dve", dstC.ap[:, d, ch * 8:ch * 8 + 8, :], pt.ap.rearrange("p (q h) -> p q h", q=8), [pt.res], [dstC.res])
                    ci_ += 1
        dnat = A.alloc(F32, [16]); dT = A.alloc(F32, [32]); rep_i = A.alloc(I32, [128]); rep = A.alloc(F32, [128]); dcol = A.alloc(F32, [32])
        P.dma("sp", dnat.ap[0:32, :], self.s5_d.rearrange("(g h) -> g h", h=16), [], [dnat.res])
        pt = Tile(self.bank[7].ap[0:16, 128:160], self.bank[7].res)
        self.tr(pt.ap, dnat.ap[0:32, :], self.ident_f.ap[0:32, 0:32], [dnat.res, self.ident_f.res], [pt.res])
        self.cp("dve", dT.ap[0:16, :], pt.ap, [pt.res], [dT.res])
        P.op("pool", lambda e: e.iota(rep_i.ap[0:16, :], [[1, 128]], base=16, channel_multiplier=-1), [], [rep_i.res])
        self.ts("dve", rep_i.ap[0:16, :], rep_i.ap[0:16, :], 15, None, ALU.bitwise_and, None, [rep_i.res], [rep_i.res])
        self.cp("dve", rep.ap[0:16, :], rep_i.ap[0:16, :], [rep_i.res], [rep.res])
        self.ts("dve", rep.ap[0:16, :], rep.ap[0:16, :], 0.0, None, ALU.is_equal, None, [rep.res], [rep.res])
        pt = Tile(self.bank[7].ap[:, 160:192], self.bank[7].res)
        self.mm(pt.ap, rep.ap[0:16, :], dT.ap[0:16, :], True, True, [rep.res, dT.res], [pt.res])
        self.cp("dve", dcol.ap, pt.ap, [pt.res], [dcol.res])
        cbi = A.alloc(I32, [8, 16]); cbf = A.alloc(F32, [8, 16]); rbi = A.alloc(I32, [1]); rbf = A.alloc(F32, [1])
        mkf = A.alloc(F32, [128]); mkb = A.alloc(F32, [128])
        P.op("pool", lambda e: e.iota(cbi.ap, [[1, 8], [0, 16]], base=0, channel_multiplier=0), [], [cbi.res])
        self.cp("dve", cbf.ap, cbi.ap, [cbi.res], [cbf.res])
        P.op("pool", lambda e: e.iota(rbi.ap, [[1, 1]], base=0, channel_multiplier=1), [], [rbi.res])
        self.ts("dve", rbi.ap, rbi.ap, 4, None, ALU.arith_shift_right, None, [rbi.res], [rbi.res])
        self.cp("dve", rbf.ap, rbi.ap, [rbi.res], [rbf.res])
        cbf2 = cbf.ap.rearrange("p a b -> p (a b)")
        self.ts("dve", mkf.ap, cbf2, rbf.ap, None, ALU.is_ge, None, [cbf.res, rbf.res], [mkf.res])
        self.ts("dve", mkb.ap, cbf2, rbf.ap, None, ALU.is_le, None, [cbf.res, rbf.res], [mkb.res])
        if self.dbg.startswith("s5preB"):
            return
        hmi = A.alloc(I32, [2]); hm = A.alloc(F32, [2]); tmpT = A.alloc(F32, [128])
        P.op("pool", lambda e: e.iota(hmi.ap, [[0, 2]], base=0, channel_multiplier=1), [], [hmi.res])
        self.ts("dve", hmi.ap, hmi.ap, 6, None, ALU.arith_shift_right, None, [hmi.res], [hmi.res])
        self.cp("dve", hm.ap, hmi.ap, [hmi.res], [hm.res])
        self.ts("dve", hm.ap[:, 0:1], hm.ap[:, 0:1], -1.0, -1.0, ALU.add, ALU.mult, [hm.res], [hm.res])
        big = [A.alloc(F32, [16, 8, 16]) for _ in range(6)]
        Pr, Pi, Qr, Qi, g1, g2 = big

        def esel(E, d, j0=0, n=8):
            return E.ap[:, j0:j0 + n, 16 * d:16 * d + 16].rearrange("p j q -> p q j").unsqueeze(3).broadcast_to([128, 16, n, 16])

        def ebc(E, d, j):
            return E.ap[:, j, 16 * d:16 * d + 16].unsqueeze(2).unsqueeze(3).broadcast_to([128, 16, 8, 16])

        def bcj(X, d):
            return X.ap[:, d, :, :].unsqueeze(2).broadcast_to([128, 16, 8, 16])
        for d in range(2):
            EP_r, EP_i = (Enr, Eni) if d == 0 else (Epr, Epi)
            EQ_r, EQ_i = (Epr, Epi) if d == 0 else (Enr, Eni)
            rr = [Epr.res, Epi.res, Enr.res, Eni.res, Bbr.res, Bbi.res, Cr.res, Ci.res]
            if self.dbg.startswith("s5preE"):
                continue
            self.cmul("dve", Pr.ap, Pi.ap, esel(EP_r, d), esel(EP_i, d), bcj(Bbr, d), bcj(Bbi, d), g1, g2, rr, [Pr.res, Pi.res])
            self.cmul("dve", Qr.ap, Qi.ap, esel(EQ_r, d), esel(EQ_i, d), bcj(Cr, d), bcj(Ci, d), g1, g2, rr, [Qr.res, Qi.res], neg_im=True)
            for r in range(2):
                self.ts("dve", g1.ap, Pr.ap, hm.ap[:, r:r + 1], None, ALU.mult, None, [Pr.res, hm.res], [g1.res])
                self.ts("dve", g2.ap, Pi.ap, hm.ap[:, r:r + 1], None, ALU.mult, None, [Pi.res, hm.res], [g2.res])
                for q in range(16):
                    g = 2 * q + r
                    pt = Tile(self.bank[1 + (q % 2)].ap[:, 0:128], self.bank[1 + (q % 2)].res)
                    self.mm(pt.ap, g1.ap[:, q].rearrange("p a b -> p (a b)"), Qr.ap[:, q].rearrange("p a b -> p (a b)"),
                            True, False, [g1.res, Qr.res], [pt.res])
                    self.mm(pt.ap, g2.ap[:, q].rearrange("p a b -> p (a b)"), Qi.ap[:, q].rearrange("p a b -> p (a b)"),
                            False, True, [g2.res, Qi.res], [pt.res])
                    if d == 0:
                        self.tt("dve", T32.ap[:, g, :], pt.ap, mkf.ap, ALU.mult, [pt.res, mkf.res], [T32.res])
                    else:
                        self.tt("dve", tmpT.ap, pt.ap, mkb.ap, ALU.mult, [pt.res, mkb.res], [tmpT.res])
                        self.tt("dve", T32.ap[:, g, :], T32.ap[:, g, :], tmpT.ap, ALU.add, [T32.res, tmpT.res], [T32.res])
            jO = 1 if d == 0 else 8
            self.tt("dve", g1.ap, ebc(Epr, d, jO), Qr.ap, ALU.mult, rr + [Qr.res], [g1.res])
            self.tt("dve", g2.ap, ebc(Epi, d, jO), Qi.ap, ALU.mult, rr + [Qi.res], [g2.res])
            self.tt("dve", Ob[d][0].ap.rearrange("p q (a b) -> p q a b", a=8), g1.ap, g2.ap, ALU.add, [g1.res, g2.res], [Ob[d][0].res])
            self.tt("dve", g1.ap, ebc(Epr, d, jO), Qi.ap, ALU.mult, rr + [Qi.res], [g1.res])
            self.tt("dve", g2.ap, ebc(Epi, d, jO), Qr.ap, ALU.mult, rr + [Qr.res], [g2.res])
            self.tt("dve", Ob[d][1].ap.rearrange("p q (a b) -> p q a b", a=8), g1.ap, g2.ap, ALU.subtract, [g1.res, g2.res], [Ob[d][1].res])
            if d == 0:
                self.cmul("dve", Qr.ap, Qi.ap, ebc(Epr, 0, 7), ebc(Epi, 0, 7), Pr.ap, Pi.ap, g1, g2, rr + [Pr.res, Pi.res], [Qr.res, Qi.res])
                Rr_, Ri_ = Qr, Qi
            else:
                Rr_, Ri_ = Pr, Pi
            for q in range(16 if not self.dbg.startswith("s5preD") else 0):
                for (ri, Rx) in ((0, Rr_), (1, Ri_)):
                    pt = Tile(self.bank[3 + (q % 2)].ap[:, 128 * ri:128 * ri + 128], self.bank[3 + (q % 2)].res)
                    self.tr(pt.ap, Rx.ap[:, q].rearrange("p a b -> p (a b)"), self.ident_f.ap, [Rx.res, self.ident_f.res], [pt.res])
                pb2 = self.bank[3 + (q % 2)]
                for r_ in range(2 if not self.dbg.startswith("s5preCF") else 0):
                    self.cp("dve", RTm[r_].ap[:, 2 * q + d, :, 64 * r_:64 * r_ + 64],
                            pb2.ap[:, 0:256].rearrange("p (a b) -> p a b", a=2)[:, :, 64 * r_:64 * r_ + 64], [pb2.res], [RTm[r_].res])
        for g in range(32):
            self.stt("dve", Tm.ap[:, g, :], self.ident_f.ap, dcol.ap[:, g:g + 1], T32.ap[:, g, :], ALU.mult, ALU.add,
                     [self.ident_f.res, dcol.res, T32.res], [Tm.res])
        self.Ob = Ob
        if self.dbg.startswith("s5pre"):
            return
        P.barrier()
        A.top = keep_top
        U = A.alloc(BF16, [32, NB])
        Ur = [[Res() for _ in range(5)] for _ in range(32)]
        u_top = A.top
        wu = A.alloc(BF16, [8, 512])
        P.dma("pool", wu.ap, self.w_in[:, 0:512].rearrange("(k p) n -> p k n", p=128), [], [wu.res])
        hTg = A.alloc(BF16, [8, 1024])
        ub2 = A.alloc(BF16, [32, 8, 16])
        hTs = A.alloc(BF16, [8, 8, 128])
        self.memset("pool", ub2.ap, 0.0, [ub2.res])
        hs = A.alloc(BF16, [1024])
        ht = [A.alloc(F32, [1024]) for _ in range(2)]
        B1_ap = self.modpp.ap[:, 0:8, :]
        groups = [(0, 32)] + [(32 + 128 * i, 128) for i in range(4)]
        for gi, (n0, nb) in enumerate(groups):
            ntile = nb // 16
            for i in range(ntile):
                if gi == 0:
                    h_ap, h_res, v = self.sctx.ap[:, i, :], self.sctx_res[i], 1
                else:
                    h = ht[i % 2]
                    t128 = (gi - 1) * 8 + i
                    P.dma("sp", h.ap, self.x[128 * t128:128 * (t128 + 1), :], [], [h.res])
                    h_ap, h_res, v = h.ap, h.res, 0
                self.prenorm_T(h_ap, h_res, self.A1, B1_ap, self.modpp.res, v, hs, hTg.ap[:, :, 128 * i:128 * (i + 1)], hTg.res, self.bank[0])
            for k in range(8):
                self.cp("dve" if k % 2 else "pool", hTs.ap[:, k, :, 0:nb], hTg.ap[:, k, 0:8 * nb].rearrange("p (b t) -> p t b", t=8),
                        [hTg.res], [hTs.res])
            for tau in range(8):
                pb = self.bank[1 + (tau % 2)]
                for k in range(8):
                    self.mm(pb.ap[0:nb, :], hTs.ap[:, k, tau, 0:nb], wu.ap[:, k, :], k == 0, k == 7, [hTs.res, wu.res], [pb.res])
                self.cp("dve", ub2.ap[0:nb, :, tau, :], pb.ap[0:nb, :].rearrange("p (g h) -> p g h", h=16),
                        [pb.res], [ub2.res])
            for g8 in range(4 if not self.dbg.startswith("s5a1x") else 0):
                pb = self.bank[3 + (g8 % 2)]
                pbt = pb.ap.bitcast(BF16)
                for gl in range(8):
                    g = g8 * 8 + gl
                    self.tr(pbt[:, 128 * gl:128 * gl + 128], ub2.ap[:, g].rearrange("p a b -> p (a b)"), self.ident_b.ap,
                            [ub2.res, self.ident_b.res], [pb.res])
                for gl in range(8 if not self.dbg.startswith("s5a1y") else 0):
                    g = g8 * 8 + gl
                    self.cp("dve", U.ap[:, g, n0:n0 + nb], pbt[:, 128 * gl:128 * gl + nb], [pb.res], [Ur[g][gi]])
        if self.dbg.startswith("s5a1"):
            return
        P.barrier()
        A.top = u_top
        gbm = A.alloc(BF16, [5, 8, 512])
        gbr = [Res() for _ in range(5)]
        gbm_end = A.top
        Sx = [[A.alloc(F32, [NB]) for _ in range(2)] for _ in range(2)]
        SE = [[[A.alloc(BF16, [NB + 1]) for _ in range(2)] for _ in range(2)] for _ in range(2)]
        for par in range(2):
            for d in range(2):
                for ri in range(2):
                    self.memset("pool", SE[par][d][ri].ap, 0.0, [SE[par][d][ri].res])
        gtmp = [A.alloc(BF16, [128]) for _ in range(2)]
        nq = 16 if not self.dbg.startswith("s5q1") else 1
        for q in range(nq):
            par = q % 2
            for d in range(2):
                qd = 2 * q + d
                psV = [Tile(self.psall[:, 512:1056], Res()), Tile(self.psall[:, 1536:2080], Res())]
                vres = [[self.bank[1].res, self.bank[2].res], [self.bank[3].res, self.bank[4].res]]
                if d == 0:
                    splits = [(0, 0, 512), (512, 512, 32)]
                else:
                    splits = [(0, 32, 512), (512, 0, 32)]
                for ri in range(2):
                    for (oc, uc, n) in splits:
                        for r in range(2):
                            g = 2 * q + r
                            ur = Ur[g]
                            self.mm(psV[ri].ap[:, oc:oc + n], RTm[r].ap[:, qd, ri, :], U.ap[:, g, uc:uc + n],
                                    r == 0, r == 1, [RTm[r].res] + ur, vres[ri])
                cur = 0
                self.cp("dve", Sx[0][0].ap, psV[0].ap, vres[0], [Sx[0][0].res])
                self.cp("dve", Sx[0][1].ap, psV[1].ap, vres[1], [Sx[0][1].res])
                col = 16 * d + q
                for k in range(10):
                    dl = 1 << k
                    a_r = ASr.ap[:, k, col:col + 1]
                    a_i = ASi.ap[:, k, col:col + 1]
                    a_n = ASn.ap[:, k, col:col + 1]
                    o_re, o_im = Sx[cur][0], Sx[cur][1]
                    n_re, n_im = Sx[1 - cur][0], Sx[1 - cur][1]
                    if d == 0:
                        dst_s, src_s, same_s = slice(dl, NB), slice(0, NB - dl), slice(0, dl)
                    else:
                        dst_s, src_s, same_s = slice(0, NB - dl), slice(dl, NB), slice(NB - dl, NB)
                    rd = [o_re.res, o_im.res, ASr.res, ASi.res, ASn.res]
                    self.cp("act", n_re.ap[:, same_s], o_re.ap[:, same_s], [o_re.res], [n_re.res])
                    self.cp("act", n_im.ap[:, same_s], o_im.ap[:, same_s], [o_im.res], [n_im.res])
                    self.stt("dve", n_re.ap[:, dst_s], o_re.ap[:, src_s], a_r, o_re.ap[:, dst_s], ALU.mult, ALU.add, rd, [n_re.res])
                    self.stt("dve", n_re.ap[:, dst_s], o_im.ap[:, src_s], a_n, n_re.ap[:, dst_s], ALU.mult, ALU.add, rd + [n_re.res], [n_re.res])
                    self.stt("dve", n_im.ap[:, dst_s], o_im.ap[:, src_s], a_r, o_im.ap[:, dst_s], ALU.mult, ALU.add, rd, [n_im.res])
                    self.stt("dve", n_im.ap[:, dst_s], o_re.ap[:, src_s], a_i, n_im.ap[:, dst_s], ALU.mult, ALU.add, rd + [n_im.res], [n_im.res])
                    cur = 1 - cur
                off = 1 if d == 0 else 0
                for ri in range(2):
                    self.cp("act", SE[par][d][ri].ap[:, off:off + NB], Sx[cur][ri].ap, [Sx[cur][ri].res], [SE[par][d][ri].res])
            for r in range(2):
                g = 2 * q + r
                for ci, (n0, nb) in enumerate(groups):
                    pb = self.bank[5 + ((2 * q + r + ci) % 2)]
                    py = Tile(pb.ap[0:nb, 0:128], pb.res)
                    self.mm(py.ap, U.ap[:, g, n0:n0 + nb], Tm.ap[:, g, :], True, False, Ur[g] + [Tm.res], [py.res])
                    bidx = (n0 - 32) if n0 >= 32 else 512 + n0
                    for d, c0_ in ((0, n0), (1, bidx + 1)):
                        for ri in range(2):
                            last = (d == 1 and ri == 1)
                            self.mm(py.ap, SE[par][d][ri].ap[64 * r:64 * r + 64, c0_:c0_ + nb], self.Ob[d][ri].ap[64 * r:64 * r + 64, q, :],
                                    False, last, [SE[par][d][ri].res, self.Ob[d][ri].res], [py.res])
                    gt_ = gtmp[(2 * q + r + ci) % 2]
                    self.act(gt_.ap[0:nb, :], py.ap, AF.Gelu, [py.res], [gt_.res])
                    self.cp("dve", gbm.ap[0:nb, ci, :, 16 * g:16 * g + 16], gt_.ap[0:nb, :].rearrange("p (t h) -> p t h", t=8), [gt_.res], [gbr[ci]])
        if self.dbg.startswith("s5a3"):
            return
        P.barrier()
        A.top = keep_top
        gT = A.alloc(BF16, [4, L + LC])
        A.top = gbm_end
        wg = A.alloc(BF16, [4, 512])
        P.dma("pool", wg.ap, self.s5_w_glu.rearrange("(k p) n -> p k n", p=128), [], [wg.res])
        for ci, (n0, nb) in enumerate(groups):
            for tp in range(8):
                pb = self.bank[1 + (tp % 2)]
                pbt = pb.ap.bitcast(BF16)
                for kk in range(4):
                    self.tr(pbt[:, 128 * kk:128 * kk + nb], gbm.ap[0:nb, ci, tp, 128 * kk:128 * kk + 128], self.ident_b.ap[0:nb, 0:nb],
                            [gbr[ci], self.ident_b.res], [pb.res])
                dst = gT.ap[:, :, 8 * n0:8 * (n0 + nb)].rearrange("p k (b t) -> p k b t", t=8)[:, :, :, tp]
                src = pbt[:, 0:512].rearrange("p (k b) -> p k b", k=4)[:, :, 0:nb]
                self.cp("dve", dst, src, [pb.res], [gT.res])
        sg = [A.alloc(F32, [512]) for _ in range(2)]
        so = [A.alloc(BF16, [4, 512]) for _ in range(2)]
        NTOK = L + LC
        ti = 0
        for t0 in range(0, NTOK, 512):
            n = min(512, NTOK - t0)
            sot = so[ti % 2]
            for mcol in range(4):
                pb = self.bank[3 + (mcol % 2)]
                for kk in range(4):
                    self.mm(pb.ap[:, 0:n], wg.ap[:, kk, 128 * mcol:128 * (mcol + 1)], gT.ap[:, kk, t0:t0 + n], kk == 0, kk == 3,
                            [wg.res, gT.res], [pb.res])
                sgt = sg[mcol % 2]
                self.act(sgt.ap[:, 0:n], pb.ap[:, 0:n], AF.Sigmoid, [pb.res], [sgt.res])
                self.tt("dve", sot.ap[:, mcol, 0:n], sgt.ap[:, 0:n], gT.ap[:, mcol, t0:t0 + n], ALU.mult, [sgt.res, gT.res], [sot.res])
            P.dma("sp", self.s5T_out[:, t0:t0 + n].rearrange("(k p) t -> p k t", p=128), sot.ap[:, :, 0:n], [sot.res], [Res()])
            ti += 1

    def even_b(self, l):
        A = self.A
        P = self.P
        upd_ctx = l < 2
        P.barrier()
        A.reset()
        NT = L + LC
        NTILE = NT // 128
        AX = mybir.AxisListType.X
        win = A.alloc(BF16, [8, 1536])
        winr = [Res() for _ in range(8)]
        for k in range(8):
            P.dma("pool", win.ap[:, k, :], self.w_in[128 * k:128 * (k + 1), 512:2048], [], [winr[k]])
        wout = A.alloc(BF16, [8, 1024])
        P.dma("pool", wout.ap, self.w_out_even.rearrange("(k p) n -> p k n", p=128), [], [wout.res])
        kT = A.alloc(BF16, [4, NT])
        kTr = [Res() for _ in range(NTILE)]
        vt = A.alloc(BF16, [NTILE, 512])
        vtr = [Res() for _ in range(NTILE)]
        hT = A.alloc(BF16, [8, 128])
        hs = A.alloc(BF16, [1024])
        ht = [A.alloc(F32, [1024]) for _ in range(2)]
        B1_ap = self.modpp.ap[:, 0:8, :]

        def load_norm(t, i):
            if t < 2:
                h_ap, h_res, v = self.sctx.ap[:, t, :], self.sctx_res[t], 1
            else:
                h = ht[i % 2]
                P.dma("sp", h.ap, self.x[128 * (t - 2):128 * (t - 1), :], [], [h.res])
                h_ap, h_res, v = h.ap, h.res, 0
            self.prenorm_T(h_ap, h_res, self.A1, B1_ap, self.modpp.res, v, hs, hT.ap, hT.res, self.bank[0])
            return h_ap, h_res

        for t in range(NTILE):
            load_norm(t, t)
            pk = self.bank[1]
            for mc in range(4):
                for k in range(8):
                    self.mm(pk.ap[:, 128 * mc:128 * (mc + 1)], win.ap[:, k, 512 + 128 * mc:512 + 128 * (mc + 1)], hT.ap[:, k, :],
                            k == 0, k == 7, [winr[k], hT.res], [pk.res])
            self.cp("act", kT.ap[:, :, 128 * t:128 * (t + 1)], pk.ap.rearrange("p (a b) -> p a b", a=4), [pk.res], [kTr[t]])
            pv = self.bank[7]
            for k in range(8):
                self.mm(pv.ap, hT.ap[:, k, :], win.ap[:, k, 1024:1536], k == 0, k == 7, [winr[k], hT.res], [pv.res])
            self.cp("dve", vt.ap[:, t, :], pv.ap, [pv.res], [vtr[t]])
        Bd = self.nc.dram_tensor("Bd%d" % l, [8, 15, 64, 94], F32).ap()
        Bd_res = Res()
        Bc = A.alloc(F32, [8, 15, 64])
        top_b2 = A.top
        fill = A.alloc(F32, [15 * 94])
        self.memset("pool", fill.ap, -30000.0, [fill.res])
        for hd in range(8):
            P.dma("sp", Bd[hd].rearrange("a q k -> q a k"), fill.ap[0:64, :].rearrange("p (a k) -> p a k", a=15), [fill.res], [Bd_res])
        bt = Bd.tensor
        dst = bass.AP(bt, 0, [[15 * 64 * 94, 8], [64 * 94, 15], [95, 64], [1, 31]])
        rt = self.na_rpb.tensor
        srcp = bass.AP(rt, self.na_rpb.offset, [[465, 8], [31, 15], [0, 64], [1, 31]])
        P.dma("sp", dst, srcp, [Bd_res], [Bd_res])
        for half in range(2):
            P.dma("sp", Bc.ap[64 * half:64 * half + 64], Bd[:, :, :, 15:79].rearrange("h a q k -> q h a k"), [Bd_res], [Bc.res])
        ii = A.alloc(I32, [64])
        kcf = A.alloc(F32, [64])
        qi = A.alloc(I32, [1])
        qf = A.alloc(F32, [1])
        c0 = A.alloc(F32, [1])
        m1 = A.alloc(F32, [64])
        m2 = A.alloc(F32, [64])
        P.op("pool", lambda e: e.iota(ii.ap, [[1, 64]], base=0, channel_multiplier=0), [], [ii.res])
        self.cp("dve", kcf.ap, ii.ap, [ii.res], [kcf.res])
        P.op("pool", lambda e: e.iota(qi.ap, [[1, 1]], base=0, channel_multiplier=1), [], [qi.res])
        self.ts("dve", qi.ap, qi.ap, 63, None, ALU.bitwise_and, None, [qi.res], [qi.res])
        self.cp("dve", qf.ap, qi.ap, [qi.res], [qf.res])
        self.ts("dve", c0.ap, qf.ap, -8.0, 0.0, ALU.add, ALU.max, [qf.res], [c0.res])
        self.ts("dve", c0.ap, c0.ap, 48.0, None, ALU.min, None, [c0.res], [c0.res])
        self.ts("dve", m1.ap, kcf.ap, c0.ap, None, ALU.is_ge, None, [kcf.res, c0.res], [m1.res])
        self.ts("dve", m2.ap, kcf.ap, -16.0, c0.ap, ALU.add, ALU.is_lt, [kcf.res, c0.res], [m2.res])
        self.tt("dve", m1.ap, m1.ap, m2.ap, ALU.mult, [m1.res, m2.res], [m1.res])
        self.ts("dve", m1.ap, m1.ap, -1.0, 30000.0, ALU.add, ALU.mult, [m1.res], [m1.res])
        Bc2 = Bc.ap.rearrange("p h a k -> p (h a) k")
        self.tt("dve", Bc2, Bc2, m1.ap.unsqueeze(1).broadcast_to([128, 120, 64]), ALU.add, [Bc.res, m1.res], [Bc.res])
        P.barrier()
        A.top = top_b2
        qT = A.alloc(BF16, [4, 128])
        tS = A.alloc(F32, [768])
        Pt = A.alloc(BF16, [896])
        PtT = A.alloc(BF16, [896])
        natok = A.alloc(BF16, [512])
        naT = A.alloc(BF16, [4, 128])
        s5t = [A.alloc(BF16, [4, 128]) for _ in range(2)]
        sm = A.alloc(F32, [8, 2])
        rinv = A.alloc(F32, [8])
        mx = A.alloc(F32, [8])
        nmx = A.alloc(F32, [8])
        tmp = A.alloc(F32, [1024])
        junk = A.alloc(BF16, [1024])
        psS = Tile(self.psall[:, 1024:2048], Res())
        psS_res = [self.bank[2].res, self.bank[3].res]
        psT = self.bank[6]
        psO = self.bank[7]
        units = [("lat", m) for m in range(32)] + ([("ctx", i) for i in range(2)] if upd_ctx else [])
        if self.dbg.startswith("na1"):
            units = units[:1] + units[5:6] + units[31:32] + units[32:]
        for ui, (kind, m) in enumerate(units):
            t = 2 + m if kind == "lat" else m
            h_ap, h_res = load_norm(t, ui)
            pq = self.bank[1]
            for mc in range(4):
                for k in range(8):
                    self.mm(pq.ap[:, 128 * mc:128 * (mc + 1)], win.ap[:, k, 128 * mc:128 * (mc + 1)], hT.ap[:, k, :],
                            k == 0, k == 7, [winr[k], hT.res], [pq.res])
            self.cp("act", qT.ap, pq.ap.rearrange("p (a b) -> p a b", a=4), [pq.res], [qT.res])
            if kind == "lat":
                rs = min(max(2 * m - 4, 0), 54)
                wt0 = 2 + rs // 2
                nwin = 640
            else:
                rs = 0
                wt0 = 0
                nwin = 0
            ncol = nwin + 256
            nchunk = ncol // 128
            for hd in range(8):
                mc, po = hd // 2, 64 * (hd % 2)
                q_l = qT.ap[po:po + 64, mc, :]
                if kind == "lat":
                    kr = [kTr[wt0 + i] for i in range(5)]
                    self.mm(psS.ap[:, 0:512], q_l, kT.ap[po:po + 64, mc, 128 * wt0:128 * wt0 + 512], True, True,
                            [qT.res] + kr, psS_res)
                    self.mm(psS.ap[:, 512:640], q_l, kT.ap[po:po + 64, mc, 128 * wt0 + 512:128 * wt0 + 640], True, True,
                            [qT.res] + kr, psS_res)
                self.mm(psS.ap[:, nwin:nwin + 256], q_l, kT.ap[po:po + 64, mc, 0:256], True, True,
                        [qT.res, kTr[0], kTr[1]], psS_res)
                wins = []
                if kind == "lat":
                    for e in range(2):
                        qr = 2 * m + e
                        r0 = min(max(qr - 4, 0), 56)
                        j0 = r0 - rs
                        a0 = r0 - qr + 7
                        wins.append((e, j0))
                        self.stt("dve", tS.ap[64 * e:64 * e + 64, 0:512].rearrange("p (a k) -> p a k", a=8),
                                 psS.ap[64 * e:64 * e + 64, 64 * j0:64 * j0 + 512].rearrange("p (a k) -> p a k", a=8), 0.125,
                                 Bc.ap[64 * e:64 * e + 64, hd, a0:a0 + 8, :], ALU.mult, ALU.add,
                                 psS_res + [Bc.res], [tS.res])
                    self.memset("pool", Pt.ap[:, 0:640], 0.0, [Pt.res])
                o0 = 512 if kind == "lat" else 0
                self.ts("dve", tS.ap[:, o0:o0 + 256], psS.ap[:, nwin:nwin + 256], 0.125, None, ALU.mult, None, psS_res, [tS.res])
                self.P.op("dve", lambda e, o_=mx.ap[:, hd:hd + 1], i_=tS.ap[:, 0:o0 + 256]: e.reduce_max(out=o_, in_=i_, axis=AX),
                          [tS.res], [mx.res])
                self.ts("dve", nmx.ap[:, hd:hd + 1], mx.ap[:, hd:hd + 1], -1.0, None, ALU.mult, None, [mx.res], [nmx.res])
                self.memset("pool", sm.ap[:, hd, :], 0.0, [sm.res])
                for (e, j0) in wins:
                    self.act(Pt.ap[64 * e:64 * e + 64, 64 * j0:64 * j0 + 512], tS.ap[64 * e:64 * e + 64, 0:512], AF.Exp,
                             [tS.res, nmx.res, sm.res], [Pt.res, sm.res], bias=nmx.ap[64 * e:64 * e + 64, hd:hd + 1], scale=1.0,
                             accum_out=sm.ap[64 * e:64 * e + 64, hd, 0:1])
                self.act(Pt.ap[:, nwin:nwin + 256], tS.ap[:, o0:o0 + 256], AF.Exp, [tS.res, nmx.res, sm.res], [Pt.res, sm.res],
                         bias=nmx.ap[:, hd:hd + 1], scale=1.0, accum_out=sm.ap[:, hd, 1:2])
                pst = psT.ap.bitcast(BF16)
                for c in range(nchunk):
                    self.tr(pst[:, 128 * c:128 * (c + 1)], Pt.ap[:, 128 * c:128 * (c + 1)], self.ident_b.ap,
                            [Pt.res, self.ident_b.res], [psT.res])
                self.cp("act" if hd % 2 else "dve", PtT.ap[:, 0:ncol], pst[:, 0:ncol], [psT.res], [PtT.res])
                for c in range(nchunk):
                    if kind == "lat" and c < 5:
                        vtile = wt0 + c
                    else:
                        vtile = c - (5 if kind == "lat" else 0)
                    self.mm(psO.ap[:, 64 * hd:64 * hd + 64], PtT.ap[:, 128 * c:128 * (c + 1)], vt.ap[:, vtile, 64 * hd:64 * hd + 64],
                            c == 0, c == nchunk - 1, [PtT.res, vtr[vtile]], [psO.res])
            self.tt("dve", rinv.ap, sm.ap[:, :, 0], sm.ap[:, :, 1], ALU.add, [sm.res], [rinv.res])
            self.P.op("dve", lambda e: e.reciprocal(out=rinv.ap, in_=rinv.ap), [rinv.res], [rinv.res])
            self.tt("dve", natok.ap.rearrange("p (h d) -> p h d", h=8), psO.ap.rearrange("p (h d) -> p h d", h=8),
                    rinv.ap.unsqueeze(2).broadcast_to([128, 8, 64]), ALU.mult, [psO.res, rinv.res], [natok.res])
            pst = psT.ap.bitcast(BF16)
            for c in range(4):
                self.tr(pst[:, 128 * c:128 * (c + 1)], natok.ap[:, 128 * c:128 * (c + 1)], self.ident_b.ap,
                        [natok.res, self.ident_b.res], [psT.res])
            self.cp("act", naT.ap, pst[:, 0:512].rearrange("p (a b) -> p a b", a=4), [psT.res], [naT.res])
            s5 = s5t[ui % 2]
            P.dma("sp", s5.ap, self.s5T_in[:, 128 * t:128 * (t + 1)].rearrange("(k p) t -> p k t", p=128), [], [s5.res])
            for half in range(2):
                pb = self.bank[4 + half]
                for k in range(8):
                    lhsT = s5.ap[:, k, :] if k < 4 else naT.ap[:, k - 4, :]
                    self.mm(pb.ap, lhsT, wout.ap[:, k, 512 * half:512 * (half + 1)], k == 0, k == 7,
                            [s5.res, naT.res, wout.res], [pb.res])
            po_ap = self.psall[:, 2048:3072]
            pres = [self.bank[4].res, self.bank[5].res]
            v = 0 if kind == "lat" else 1
            rstd = self.rstd_of(po_ap, pres, junk)
            self.stt("dve", tmp.ap, po_ap, rstd.ap, self.G[0][v].ap, ALU.mult, ALU.mult, pres + [rstd.res, self.G[0][v].res], [tmp.res])
            self.tt("pool", h_ap, tmp.ap, h_ap, ALU.add, [tmp.res, h_res], [h_res])
            if kind == "lat":
                P.dma("sp", self.out[128 * m:128 * (m + 1), :], h_ap, [h_res], [self.out_res[m]])


    def gen_tables(self):
        A = self.A
        P = self.P
        P.barrier()
        A.reset()
        W = 1024
        kf = A.alloc(F32, [4096])
        ki = A.alloc(I32, [4096])
        tcol = A.alloc(F32, [32])
        ti = A.alloc(I32, [32])
        P.op("pool", lambda e: e.iota(ki.ap, [[1, 4096]], base=0, channel_multiplier=0), [], [ki.res])
        self.cp("dve", kf.ap, ki.ap, [ki.res], [kf.res])
        P.op("pool", lambda e: e.iota(ti.ap, [[128, 32]], base=0, channel_multiplier=1), [], [ti.res])
        self.cp("dve", tcol.ap, ti.ap, [ti.res], [tcol.res])
        negpi = self.negpi
        t1 = [A.alloc(I32, [W]) for _ in range(2)]
        t2 = [A.alloc(I32, [W]) for _ in range(2)]
        t3 = [A.alloc(F32, [W]) for _ in range(2)]
        t4 = [A.alloc(F32, [W]) for _ in range(2)]
        ob = [A.alloc(BF16, [W]) for _ in range(4)]
        it = 0
        nchunks = 32 if not self.dbg.startswith("tab1") else 1
        sc = 2.0 * math.pi / 4096.0
        for c in range(nchunks):
            for kt in range(4096 // W if not (self.dbg.startswith("odd1") or self.dbg.startswith("tab1")) else 1):
                ai, ci, bf, cf = t1[it % 2], t2[it % 2], t3[it % 2], t4[it % 2]
                os_, oc = ob[(2 * it) % 4], ob[(2 * it + 1) % 4]
                self.ts("dve", ai.ap, kf.ap[:, kt * W:(kt + 1) * W], tcol.ap[:, c:c + 1], 2048.0, ALU.mult, ALU.add,
                        [kf.res, tcol.res], [ai.res])
                self.ts("dve", ci.ap, kf.ap[:, kt * W:(kt + 1) * W], tcol.ap[:, c:c + 1], 3072.0, ALU.mult, ALU.add,
                        [kf.res, tcol.res], [ci.res])
                self.ts("dve", ai.ap, ai.ap, 4095, None, ALU.bitwise_and, None, [ai.res], [ai.res])
                self.ts("dve", ci.ap, ci.ap, 4095, None, ALU.bitwise_and, None, [ci.res], [ci.res])
                self.cp("pool", bf.ap, ai.ap, [ai.res], [bf.res])
                self.cp("pool", cf.ap, ci.ap, [ci.res], [cf.res])
                self.act(os_.ap, bf.ap, AF.Sin, [bf.res, negpi.res], [os_.res], bias=negpi.ap, scale=sc)
                self.act(oc.ap, cf.ap, AF.Sin, [cf.res, negpi.res], [oc.res], bias=negpi.ap, scale=sc)
                P.dma("sp", self.Stab[128 * c:128 * (c + 1), kt * W:(kt + 1) * W], os_.ap, [os_.res], [self.tab_res])
                P.dma("sp", self.Ctab[128 * c:128 * (c + 1), kt * W:(kt + 1) * W], oc.ap, [oc.res], [self.tab_res])
                it += 1

    def gen_channel_tables(self, CD, SDn):
        A = self.A
        P = self.P
        kf = A.alloc(F32, [1024])
        ki = A.alloc(I32, [1024])
        tcol = A.alloc(F32, [8])
        ti = A.alloc(I32, [8])
        P.op("pool", lambda e: e.iota(ki.ap, [[1, 1024]], base=0, channel_multiplier=0), [], [ki.res])
        self.cp("dve", kf.ap, ki.ap, [ki.res], [kf.res])
        P.op("pool", lambda e: e.iota(ti.ap, [[128, 8]], base=0, channel_multiplier=1), [], [ti.res])
        self.cp("dve", tcol.ap, ti.ap, [ti.res], [tcol.res])
        ai = A.alloc(I32, [1024])
        ci = A.alloc(I32, [1024])
        b_ = A.alloc(F32, [1024])
        c3 = A.alloc(F32, [1024])
        negpi = self.negpi
        sc = 2.0 * math.pi / 1024.0
        for k in range(8):
            self.ts("dve", ai.ap, kf.ap, tcol.ap[:, k:k + 1], 512.0, ALU.mult, ALU.add, [kf.res, tcol.res], [ai.res])
            self.ts("dve", ci.ap, kf.ap, tcol.ap[:, k:k + 1], 768.0, ALU.mult, ALU.add, [kf.res, tcol.res], [ci.res])
            self.ts("dve", ai.ap, ai.ap, 1023, None, ALU.bitwise_and, None, [ai.res], [ai.res])
            self.ts("dve", ci.ap, ci.ap, 1023, None, ALU.bitwise_and, None, [ci.res], [ci.res])
            self.cp("pool", b_.ap, ai.ap, [ai.res], [b_.res])
            self.cp("pool", c3.ap, ci.ap, [ci.res], [c3.res])
            self.act(b_.ap, b_.ap, AF.Sin, [b_.res, negpi.res], [b_.res], bias=negpi.ap, scale=sc)
            self.act(c3.ap, c3.ap, AF.Sin, [c3.res, negpi.res], [c3.res], bias=negpi.ap, scale=sc)
            self.ts("dve", SDn.ap[:, k, :], b_.ap, -1.0 / 2048.0, None, ALU.mult, None, [b_.res], [SDn.res])
            self.ts("pool", CD.ap[:, k, :], c3.ap, 1.0 / 2048.0, None, ALU.mult, None, [c3.res], [CD.res])

    def odd_mixer(self, l):
        A = self.A
        P = self.P
        o = l // 2
        upd_ctx = l < 2
        P.barrier()
        A.reset()
        ycc = A.alloc(BF16, [8, 256])
        ysc = A.alloc(BF16, [8, 256])
        base2 = A.top
        hl = A.alloc(BF16, [32, 1024])
        hlr = [Res() for _ in range(32)]
        hc = A.alloc(BF16, [2, 1024])
        hcr = [Res() for _ in range(2)]
        nv = 2 if upd_ctx else 1
        with_tmp = A.top
        A1b = [A.alloc(F32, [1024]) for _ in range(nv)]
        B1b = [A.alloc(F32, [1024]) for _ in range(nv)]
        wms = [A.alloc(F32, [8, 512]) for _ in range(2)]
        tb = A.alloc(F32, [512])
        tg = A.alloc(F32, [512])
        ones = A.alloc(F32, [512])
        self.memset("pool", ones.ap, 1.0, [ones.res])
        for nt in range(4):
            wm = wms[nt % 2]
            half = nt % 2
            P.dma("sp", wm.ap, self.w_mod[:, nt * 512:(nt + 1) * 512].rearrange("(k p) n -> p k n", p=128), [], [wm.res])
            self.load_bcast_row("sp", tb, self.b_mod[nt * 512:(nt + 1) * 512])
            if nt >= 2:
                self.load_bcast_row("sp", tg, self.norm_g[0, half * 512:(half + 1) * 512])
            for v in range(nv):
                psb = self.bank[5 + v]
                dst = (B1b if nt < 2 else A1b)[v]
                d_ap = dst.ap[:, half * 512:(half + 1) * 512]
                for k in range(8):
                    self.mm(psb.ap, self.crep[v].ap[:, k, :], wm.ap[:, k, :], k == 0, k == 7, [self.crep[v].res, wm.res], [psb.res])
                if nt < 2:
                    self.tt("dve", d_ap, psb.ap, tb.ap, ALU.add, [psb.res, tb.res], [dst.res])
                else:
                    self.stt("dve", d_ap, psb.ap, 1.0, tb.ap, ALU.add, ALU.add, [psb.res, tb.res], [dst.res])
                    self.tt("dve", d_ap, d_ap, tg.ap, ALU.mult, [dst.res, tg.res], [dst.res])
        ht = [A.alloc(F32, [1024]) for _ in range(2)]
        junk = A.alloc(BF16, [1024])
        tmp = A.alloc(F32, [1024])
        src = self.x
        for t in range(32 + (2 if upd_ctx else 0)):
            if t < 32:
                h = ht[t % 2]
                P.dma("sp", h.ap, src[128 * t:128 * (t + 1), :], [], [h.res])
                h_ap, h_res, v, d_ap, d_res = h.ap, h.res, 0, hl.ap[:, t, :], hlr[t]
            else:
                i = t - 32
                h_ap, h_res, v, d_ap, d_res = self.sctx.ap[:, i, :], self.sctx_res[i], 1, hc.ap[:, i, :], hcr[i]
            rstd = self.rstd_of(h_ap, [h_res], junk)
            self.stt("dve", tmp.ap, h_ap, rstd.ap, A1b[v].ap, ALU.mult, ALU.mult, [h_res, rstd.res, A1b[v].res], [tmp.res])
            self.tt("pool", d_ap, tmp.ap, B1b[v].ap, ALU.add, [tmp.res, B1b[v].res], [d_res])
        P.barrier()
        A.top = with_tmp
        NT = 256
        ctile = [A.alloc(BF16, [32, NT]) for _ in range(2)]
        stile = [A.alloc(BF16, [32, NT]) for _ in range(2)]
        yev = [A.alloc(BF16, [8, NT]) for _ in range(4)]
        nkt = 4096 // NT
        if self.dbg.startswith("odd1"):
            nkt = 1
        for kt in range(nkt):
            ct, stl = ctile[kt % 2], stile[kt % 2]
            P.dma("sp", ct.ap, self.Ctab[:, kt * NT:(kt + 1) * NT].rearrange("(c p) k -> p c k", p=128), [self.tab_res], [ct.res])
            P.dma("sp", stl.ap, self.Stab[:, kt * NT:(kt + 1) * NT].rearrange("(c p) k -> p c k", p=128), [self.tab_res], [stl.res])
            yc, ys = yev[(2 * kt) % 4], yev[(2 * kt + 1) % 4]
            for dc in range(8):
                for (tab, yo, bi) in ((ct, yc, 1), (stl, ys, 2)):
                    pb = self.bank[bi + 2 * (dc % 2)]
                    pj = Tile(pb.ap[:, 0:NT], pb.res)
                    for c in range(32):
                        self.mm(pj.ap, hl.ap[:, c, 128 * dc:128 * (dc + 1)], tab.ap[:, c, :], c == 0, c == 31,
                                [hlr[c], tab.res], [pj.res])
                    if bi == 1:
                        self.cp("act", yo.ap[:, dc, :], pj.ap, [pj.res], [yo.res])
                    else:
                        self.cp("dve", yo.ap[:, dc, :], pj.ap, [pj.res], [yo.res])
            P.dma("sp", self.Yc[:, kt * NT:(kt + 1) * NT].rearrange("(c p) k -> p c k", p=128), yc.ap, [yc.res], [self.Y_res[kt]])
            P.dma("sp", self.Ys[:, kt * NT:(kt + 1) * NT].rearrange("(c p) k -> p c k", p=128), ys.ap, [ys.res], [self.Y_res[kt]])
        if upd_ctx:
            ct, stl = ctile[nkt % 2], stile[nkt % 2]
            csrc = self.Ctab.rearrange("(t s) k -> t s k", s=16)[:, 0, 0:256].rearrange("(c p) k -> p c k", p=128)
            ssrc = self.Stab.rearrange("(t s) k -> t s k", s=16)[:, 0, 0:256].rearrange("(c p) k -> p c k", p=128)
            P.dma("sp", ct.ap[:, 0:2, :], csrc, [self.tab_res], [ct.res])
            P.dma("sp", stl.ap[:, 0:2, :], ssrc, [self.tab_res], [stl.res])
            for dc in range(8):
                for (tab, yo, bi) in ((ct, ycc, 1), (stl, ysc, 2)):
                    pb = self.bank[bi + 2 * (dc % 2)]
                    pj = Tile(pb.ap[:, 0:256], pb.res)
                    for c in range(2):
                        self.mm(pj.ap, hc.ap[:, c, 128 * dc:128 * (dc + 1)], tab.ap[:, c, :], c == 0, c == 1,
                                [hcr[c], tab.res], [pj.res])
                    self.ts("dve", yo.ap[:, dc, :], pj.ap, 4.0, None, ALU.mult, None, [pj.res], [yo.res])
        P.barrier()
        A.top = base2
        CD = A.alloc(BF16, [8, 1024])
        SDn = A.alloc(BF16, [8, 1024])
        wf = A.alloc(BF16, [8, 1024])
        P.dma("pool", wf.ap, self.w_fourier.rearrange("(k p) n -> p k n", p=128), [], [wf.res])
        yct = [A.alloc(BF16, [8, 256]) for _ in range(2)]
        yst = [A.alloc(BF16, [8, 256]) for _ in range(2)]
        hfT = [A.alloc(BF16, [8, 256]) for _ in range(2)]
        ht = [A.alloc(F32, [1024]) for _ in range(2)]
        junk = A.alloc(BF16, [1024])
        tmp = A.alloc(F32, [1024])
        ycc2, ysc2 = ycc, ysc
        mark = A.top
        self.gen_channel_tables(CD, SDn)
        A.top = mark
        units = [("lat", kt) for kt in range(nkt)] + ([("ctx", 0)] if upd_ctx else [])
        for ui, (kind, kt) in enumerate(units):
            v = 0 if kind == "lat" else 1
            if kind == "lat":
                yc, ys = yct[ui % 2], yst[ui % 2]
                P.dma("sp", yc.ap, self.Yc[:, kt * 256:(kt + 1) * 256].rearrange("(c p) k -> p c k", p=128), [self.Y_res[kt]], [yc.res])
                P.dma("sp", ys.ap, self.Ys[:, kt * 256:(kt + 1) * 256].rearrange("(c p) k -> p c k", p=128), [self.Y_res[kt]], [ys.res])
            else:
                yc, ys = ycc2, ysc2
            hf = hfT[ui % 2]
            for ec in range(8):
                pb = self.bank[1 + (ec % 2)]
                pj = Tile(pb.ap[:, 0:256], pb.res)
                for dc in range(8):
                    self.mm(pj.ap, CD.ap[:, dc, 128 * ec:128 * (ec + 1)], yc.ap[:, dc, :], dc == 0, False, [CD.res, yc.res], [pj.res])
                for dc in range(8):
                    self.mm(pj.ap, SDn.ap[:, dc, 128 * ec:128 * (ec + 1)], ys.ap[:, dc, :], False, dc == 7, [SDn.res, ys.res], [pj.res])
                self.cp("act" if ec % 2 else "dve", hf.ap[:, ec, :], pj.ap, [pj.res], [hf.res])
            for sub in range(2):
                for half in range(2):
                    pb = self.bank[3 + 2 * sub + half]
                    for ec in range(8):
                        self.mm(pb.ap, hf.ap[:, ec, sub * 128:(sub + 1) * 128], wf.ap[:, ec, half * 512:(half + 1) * 512],
                                ec == 0, ec == 7, [hf.res, wf.res], [pb.res])
                po_ap = self.psall[:, (3 + 2 * sub) * 512:(5 + 2 * sub) * 512]
                pres = [self.bank[3 + 2 * sub].res, self.bank[4 + 2 * sub].res]
                if kind == "lat":
                    t128 = kt * 2 + sub
                    h = ht[sub]
                    P.dma("sp", h.ap, self.x[128 * t128:128 * (t128 + 1), :], [], [h.res])
                    h_ap, h_res = h.ap, h.res
                else:
                    h_ap, h_res = self.sctx.ap[:, sub, :], self.sctx_res[sub]
                rstd = self.rstd_of(po_ap, pres, junk)
                self.stt("dve", tmp.ap, po_ap, rstd.ap, self.G[0][v].ap, ALU.mult, ALU.mult,
                         pres + [rstd.res, self.G[0][v].res], [tmp.res])
                self.tt("pool", h_ap, tmp.ap, h_ap, ALU.add, [tmp.res, h_res], [h_res])
                if kind == "lat":
                    P.dma("sp", self.out[128 * t128:128 * (t128 + 1), :], h_ap, [h_res], [self.out_res[t128]])


_CACHE = {}


def _get_nc(step, dbg=""):
    key = (step, dbg)
    if key not in _CACHE:
        _CACHE[key] = Builder(step, dbg).build()
    return _CACHE[key]


def _launch(step, per_core, shared, dbg=""):
    nc = _get_nc(step, dbg)
    in_maps = []
    for b in range(len(per_core)):
        m = dict(shared)
        m.update(per_core[b])
        in_maps.append(m)
    res = run_bass_kernel_spmd(nc, in_maps, core_ids=list(range(len(per_core))))
    return res.results


def _f32(a):
    return np.ascontiguousarray(a, dtype=np.float32)


def run_steps(inputs, steps, ncores=8, dbg=""):
    h = [_f32(inputs["x"][b]) for b in range(ncores)]
    s = [_f32(inputs["ctx"][b]) for b in range(ncores)]
    cs = [_f32(inputs["c"][b]) for b in range(ncores)]
    s5T = None
    for step in steps:
        kind, l = step
        e = l // 2
        shared = {"c_ctx": _f32(inputs["c_ctx"]), "w_mod": _f32(inputs["w_mod"][l]), "b_mod": _f32(inputs["b_mod"][l]),
                  "norm_g": _f32(inputs["norm_g"][l])}
        if kind == "mlp":
            shared["w_ff1"] = _f32(inputs["w_ff1"][l])
            shared["w_ff2"] = _f32(inputs["w_ff2"][l])
        elif kind == "odd":
            shared["w_fourier"] = _f32(inputs["w_fourier"][e])
        elif kind == "evenA":
            shared["w_in"] = _f32(inputs["w_in"][e])
            for n in ["s5_lam_re", "s5_lam_im", "s5_log_dt", "s5_b_re", "s5_b_im", "s5_c_re", "s5_c_im", "s5_d", "s5_w_glu"]:
                shared[n] = _f32(inputs[n][e])
        elif kind == "evenB":
            shared["w_in"] = _f32(inputs["w_in"][e])
            shared["w_out_even"] = _f32(inputs["w_out_even"][e])
            shared["na_rpb"] = _f32(inputs["na_rpb"][e])
        per_core = []
        for b in range(ncores):
            m = {"x": h[b], "c": cs[b], "ctx": s[b]}
            if kind == "evenB":
                m["s5T"] = s5T[b]
            per_core.append(m)
        res = _launch(step, per_core, shared, dbg)
        if kind == "evenA":
            s5T = [np.ascontiguousarray(r["s5T"]) for r in res]
        else:
            h = [_f32(r["out"]) for r in res]
            s = [_f32(r["sout"]) for r in res]
    return h, s, s5T


def kernel_multi(**inputs):
    steps = []
    for l in range(DEPTH):
        if l % 2 == 0:
            steps += [("evenA", l), ("evenB", l)]
        else:
            steps += [("odd", l)]
        steps += [("mlp", l)]
    h, s, _ = run_steps(inputs, steps, 8)
    return np.stack(h, 0)


def kernel(**inputs):
    nc = _get_nc(("fused", None))
    names = ["c_ctx", "w_mod", "b_mod", "norm_g", "w_in", "w_out_even", "s5_lam_re", "s5_lam_im", "s5_log_dt", "s5_b_re", "s5_b_im",
             "s5_c_re", "s5_c_im", "s5_d", "s5_w_glu", "na_rpb", "w_fourier", "w_ff1", "w_ff2"]
    shared = {n: _f32(inputs[n]) for n in names}
    ncores = 8
    in_maps = []
    for b in range(ncores):
        m = dict(shared)
        m["x"] = _f32(inputs["x"][b])
        m["c"] = _f32(inputs["c"][b])
        m["ctx"] = _f32(inputs["ctx"][b])
        in_maps.append(m)
    res = run_bass_kernel_spmd(nc, in_maps, core_ids=list(range(ncores)))
    return np.stack([_f32(r["out"]) for r in res.results], 0)
```

```python
import contextlib
import math
import os

import numpy as np
import concourse.bass as bass
import concourse.mybir as mybir
from concourse.bass_utils import run_bass_kernel_spmd

F32 = mybir.dt.float32
BF16 = mybir.dt.bfloat16
I32 = mybir.dt.int32
AF = mybir.ActivationFunctionType
ALU = mybir.AluOpType

D = 1024
L = 4096
LC = 256
DFF = 4096
DEPTH = 4
EPS = 1e-6
ENGS = ["pe", "act", "dve", "pool", "sp"]


class Res:
    __slots__ = ("name", "lastw", "readers")

    def __init__(self, name="r"):
        self.name = name
        self.lastw = None
        self.readers = []


class Op:
    __slots__ = ("eng", "fn", "deps", "is_dma", "signal", "sigval", "dslot", "dval")


class Prog:
    NDSLOT = 8

    def __init__(self, nc):
        self.nc = nc
        self.ops = []
        self.per = {e: [] for e in ENGS}
        self.ndma = {e: 0 for e in ENGS}
        self.pending_barrier = {e: None for e in ENGS}
        self.dmas_since_barrier = []

    def barrier(self):
        deps = []
        for e in ENGS:
            for op in reversed(self.per[e]):
                if not op.is_dma:
                    deps.append(op)
                    break
        deps.extend(self.dmas_since_barrier)
        self.dmas_since_barrier = []
        for e in ENGS:
            old = self.pending_barrier[e]
            self.pending_barrier[e] = (old or []) + deps

    def _add(self, eng, fn, reads, writes, is_dma):
        op = Op()
        op.eng = eng
        op.fn = fn
        op.is_dma = is_dma
        op.signal = False
        deps = set()
        for r in reads:
            if r.lastw is not None:
                deps.add(r.lastw)
        for w in writes:
            if w.lastw is not None:
                deps.add(w.lastw)
            for rd in w.readers:
                deps.add(rd)
        if self.pending_barrier[eng] is not None:
            deps.update(self.pending_barrier[eng])
            self.pending_barrier[eng] = None
        deps.discard(op)
        op.deps = [d for d in deps if not (eng == "pe" and d.eng == "pe" and not d.is_dma and not is_dma)]
        for r in reads:
            r.readers.append(op)
        for w in writes:
            w.lastw = op
            w.readers = []
        self.per[eng].append(op)
        self.ops.append(op)
        if is_dma:
            k = self.ndma[eng]
            self.ndma[eng] += 1
            op.dslot = k % self.NDSLOT
            op.dval = 16 * (k // self.NDSLOT + 1)
            self.dmas_since_barrier.append(op)
        return op

    def op(self, eng, fn, reads=(), writes=()):
        return self._add(eng, fn, list(reads), list(writes), False)

    def dma(self, eng, out, in_, reads=(), writes=()):
        return self._add(eng, lambda e: e.dma_start(out=out, in_=in_), list(reads), list(writes), True)

    def emit(self):
        nc = self.nc
        for op in self.ops:
            for d in op.deps:
                if not d.is_dma:
                    d.signal = True
        for e in ENGS:
            cnt = 0
            for op in self.per[e]:
                if not op.is_dma and op.signal:
                    cnt += 1
                    op.sigval = cnt
        with contextlib.ExitStack() as st:
            sems = {e: st.enter_context(nc.semaphore("s_" + e)) for e in ENGS}
            dsems = {e: [st.enter_context(nc.semaphore("d_%s%d" % (e, i))) for i in range(self.NDSLOT)]
                     for e in ENGS if self.ndma[e] > 0}
            block = st.enter_context(nc.Block())

            def run_engine(ename, eng):
                seen = {}
                dseen = {}
                for op in self.per[ename]:
                    for d in op.deps:
                        if d.is_dma:
                            key = (d.eng, d.dslot)
                            if dseen.get(key, 0) < d.dval:
                                eng.wait_ge(dsems[d.eng][d.dslot], d.dval)
                                dseen[key] = d.dval
                        else:
                            if seen.get(d.eng, 0) < d.sigval:
                                eng.wait_ge(sems[d.eng], d.sigval)
                                seen[d.eng] = d.sigval
                    if op.is_dma:
                        key = (ename, op.dslot)
                        if op.dval > 16 and dseen.get(key, 0) < op.dval - 16:
                            eng.wait_ge(dsems[ename][op.dslot], op.dval - 16)
                            dseen[key] = op.dval - 16
                        ins = op.fn(eng)
                        ins.then_inc(dsems[ename][op.dslot], 16)
                    else:
                        ins = op.fn(eng)
                        if op.signal:
                            ins.then_inc(sems[ename], 1)
                if ename in dsems:
                    k = self.ndma[ename]
                    for s in range(self.NDSLOT):
                        n = (k - s + self.NDSLOT - 1) // self.NDSLOT
                        if n > 0 and dseen.get((ename, s), 0) < 16 * n:
                            eng.wait_ge(dsems[ename][s], 16 * n)

            block.tensor(lambda e: run_engine("pe", e))
            block.scalar(lambda e: run_engine("act", e))
            block.vector(lambda e: run_engine("dve", e))
            block.gpsimd(lambda e: run_engine("pool", e))
            block.sync(lambda e: run_engine("sp", e))


class Tile:
    __slots__ = ("ap", "res")

    def __init__(self, ap, res=None):
        self.ap = ap
        self.res = res or Res()

    def __getitem__(self, k):
        return self.ap[k]


class Arena:
    def __init__(self, tensor, ncols):
        self.t = tensor
        self.ncols = ncols
        self.top = 0
        self.base = 0

    def alloc(self, dtype, shape):
        n = 1
        for s in shape:
            n *= s
        units = n * (2 if dtype in (F32, I32) else 1)
        units = (units + 31) // 32 * 32
        assert self.top + units <= self.ncols, ("arena overflow", self.top, units, self.ncols)
        ap = self.t[:, self.top:self.top + n * (2 if dtype in (F32, I32) else 1)]
        self.top += units
        self.maxtop = max(getattr(self, "maxtop", 0), self.top)
        if dtype != BF16:
            ap = ap.bitcast(dtype)
        if len(shape) == 2:
            ap = ap.rearrange("p (a b) -> p a b", a=shape[0])
        elif len(shape) == 3:
            ap = ap.rearrange("p (a b c) -> p a b c", a=shape[0], b=shape[1])
        return Tile(ap)

    def mark_persistent(self):
        self.base = self.top

    def reset(self):
        self.top = self.base


class Builder:
    def __init__(self, step, dbg=""):
        self.dbg = dbg
        self.step = step
        kind, l = step
        nc = bass.Bass("TRN2", target_bir_lowering=False)
        self.nc = nc
        self.P = Prog(nc)

        def din(name, shape, dt=F32):
            return nc.dram_tensor(name, list(shape), dt, kind="ExternalInput").ap()

        if kind == "fused":
            self.init_fused(din)
            return
        self.x = din("x", [L, D])
        self.c = din("c", [D])
        self.ctx = din("ctx", [LC, D])
        self.c_ctx = din("c_ctx", [D])
        self.w_mod = din("w_mod", [D, 6 * D])
        self.b_mod = din("b_mod", [6 * D])
        self.norm_g = din("norm_g", [4, D])
        if kind == "mlp":
            self.w_ff1 = din("w_ff1", [D, DFF])
            self.w_ff2 = din("w_ff2", [DFF, D])
        if kind == "odd":
            self.w_fourier = din("w_fourier", [D, D])
        if kind in ("evenA", "evenB"):
            self.w_in = din("w_in", [D, 2048])
        if kind == "evenA":
            self.s5_lam_re = din("s5_lam_re", [2, 32, 64])
            self.s5_lam_im = din("s5_lam_im", [2, 32, 64])
            self.s5_log_dt = din("s5_log_dt", [2, 32])
            self.s5_b_re = din("s5_b_re", [2, 32, 64, 16])
            self.s5_b_im = din("s5_b_im", [2, 32, 64, 16])
            self.s5_c_re = din("s5_c_re", [2, 32, 16, 64])
            self.s5_c_im = din("s5_c_im", [2, 32, 16, 64])
            self.s5_d = din("s5_d", [512])
            self.s5_w_glu = din("s5_w_glu", [512, 512])
            self.s5T_out = nc.dram_tensor("s5T", [512, L + LC], BF16, kind="ExternalOutput").ap()
        if kind == "evenB":
            self.w_out_even = din("w_out_even", [D, D])
            self.na_rpb = din("na_rpb", [8, 15, 31])
            self.s5T_in = din("s5T", [512, L + LC], BF16)
        if kind != "evenA":
            self.out = nc.dram_tensor("out", [L, D], F32, kind="ExternalOutput").ap()
            self.sout = nc.dram_tensor("sout", [LC, D], F32, kind="ExternalOutput").ap()
        self.out_res = [Res("out%d" % i) for i in range(L // 128)]
        if kind == "odd":
            self.Ctab = nc.dram_tensor("Ctab", [L, L], BF16).ap()
            self.Stab = nc.dram_tensor("Stab", [L, L], BF16).ap()
            self.tab_res = Res("tab")
            self.Yc = nc.dram_tensor("Yc", [D, L], BF16).ap()
            self.Ys = nc.dram_tensor("Ys", [D, L], BF16).ap()
            self.Y_res = [Res("Y%d" % i) for i in range(16)]

    def init_fused(self, din):
        nc = self.nc
        self.x_in = din("x", [L, D])
        self.c = din("c", [D])
        self.ctx = din("ctx", [LC, D])
        self.c_ctx = din("c_ctx", [D])
        W = {}
        W["w_mod"] = din("w_mod", [DEPTH, D, 6 * D])
        W["b_mod"] = din("b_mod", [DEPTH, 6 * D])
        W["norm_g"] = din("norm_g", [DEPTH, 4, D])
        W["w_in"] = din("w_in", [2, D, 2048])
        W["w_out_even"] = din("w_out_even", [2, D, D])
        W["s5_lam_re"] = din("s5_lam_re", [2, 2, 32, 64])
        W["s5_lam_im"] = din("s5_lam_im", [2, 2, 32, 64])
        W["s5_log_dt"] = din("s5_log_dt", [2, 2, 32])
        W["s5_b_re"] = din("s5_b_re", [2, 2, 32, 64, 16])
        W["s5_b_im"] = din("s5_b_im", [2, 2, 32, 64, 16])
        W["s5_c_re"] = din("s5_c_re", [2, 2, 32, 16, 64])
        W["s5_c_im"] = din("s5_c_im", [2, 2, 32, 16, 64])
        W["s5_d"] = din("s5_d", [2, 512])
        W["s5_w_glu"] = din("s5_w_glu", [2, 512, 512])
        W["na_rpb"] = din("na_rpb", [2, 8, 15, 31])
        W["w_fourier"] = din("w_fourier", [2, D, D])
        W["w_ff1"] = din("w_ff1", [DEPTH, D, DFF])
        W["w_ff2"] = din("w_ff2", [DEPTH, DFF, D])
        self.W = W
        self.out_final = nc.dram_tensor("out", [L, D], F32, kind="ExternalOutput").ap()
        self.hA = nc.dram_tensor("hA", [L, D], F32).ap()
        s5 = nc.dram_tensor("s5T", [512, L + LC], BF16).ap()
        self.s5T_out = s5
        self.s5T_in = s5
        self.out_res = [Res("out%d" % i) for i in range(L // 128)]
        self.Ctab = nc.dram_tensor("Ctab", [L, L], BF16).ap()
        self.Stab = nc.dram_tensor("Stab", [L, L], BF16).ap()
        self.tab_res = Res("tab")
        self.Yc = nc.dram_tensor("Yc", [D, L], BF16).ap()
        self.Ys = nc.dram_tensor("Ys", [D, L], BF16).ap()
        self.Y_res = [Res("Y%d" % i) for i in range(16)]

    def set_layer(self, l):
        W = self.W
        e = l // 2
        self.w_mod = W["w_mod"][l]
        self.b_mod = W["b_mod"][l]
        self.norm_g = W["norm_g"][l]
        self.w_ff1 = W["w_ff1"][l]
        self.w_ff2 = W["w_ff2"][l]
        if l % 2 == 0:
            self.w_in = W["w_in"][e]
            self.w_out_even = W["w_out_even"][e]
            for n in ["s5_lam_re", "s5_lam_im", "s5_log_dt", "s5_b_re", "s5_b_im", "s5_c_re", "s5_c_im", "s5_d", "s5_w_glu", "na_rpb"]:
                setattr(self, n, W[n][e])
        else:
            self.w_fourier = W["w_fourier"][e]

    def build_fused(self):
        P = self.P
        self.setup_consts()
        bufs = [self.x_in, self.hA, self.out_final]
        step = 0
        for l in range(DEPTH):
            self.set_layer(l)
            self.mod_phase(l)
            self.x = bufs[0] if step == 0 else (self.out_final if step % 2 == 0 else self.hA)
            self.out = self.hA if step % 2 == 0 else self.out_final
            if l % 2 == 0:
                self.even_a(l)
                self.even_b(l)
            else:
                if l == 1:
                    self.gen_tables()
                self.odd_mixer(l)
            step += 1
            self.x = self.out_final if step % 2 == 0 else self.hA
            self.out = self.hA if step % 2 == 0 else self.out_final
            self.mlp_phase(l)
            step += 1

    def mm(self, out, lhsT, rhs, start, stop, reads, writes):
        self.P.op("pe", lambda e: e.matmul(out, lhsT=lhsT, rhs=rhs, start=start, stop=stop), reads, writes)

    def tr(self, out, in_, ident, reads, writes):
        self.P.op("pe", lambda e: e.transpose(out, in_, ident), reads, writes)

    def act(self, out, in_, func, reads, writes, bias=None, scale=None, accum_out=None, eng="act"):
        kw = {}
        if bias is not None:
            kw["bias"] = bias
        if scale is not None:
            kw["scale"] = scale
        if accum_out is not None:
            kw["accum_out"] = accum_out
        self.P.op(eng, lambda e: e.activation(out=out, in_=in_, func=func, **kw), reads, writes)

    def tt(self, eng, out, in0, in1, op, reads, writes):
        self.P.op(eng, lambda e: e.tensor_tensor(out=out, in0=in0, in1=in1, op=op), reads, writes)

    def ts(self, eng, out, in0, s1, s2, op0, op1, reads, writes):
        if op1 is None:
            self.P.op(eng, lambda e: e.tensor_single_scalar(out=out, in_=in0, scalar=s1, op=op0), reads, writes)
        else:
            self.P.op(eng, lambda e: e.tensor_scalar(out=out, in0=in0, scalar1=s1, scalar2=s2, op0=op0, op1=op1), reads, writes)

    def stt(self, eng, out, in0, scalar, in1, op0, op1, reads, writes):
        self.P.op(eng, lambda e: e.scalar_tensor_tensor(out=out, in0=in0, scalar=scalar, in1=in1, op0=op0, op1=op1), reads, writes)

    def cp(self, eng, out, in_, reads, writes):
        if eng == "act":
            self.P.op(eng, lambda e: e.copy(out=out, in_=in_), reads, writes)
        else:
            self.P.op(eng, lambda e: e.tensor_copy(out=out, in_=in_), reads, writes)

    def memset(self, eng, ap, val, writes):
        self.P.op(eng, lambda e: e.memset(ap, val), [], writes)

    def build(self):
        nc = self.nc
        P = self.P
        with contextlib.ExitStack() as st:
            NCOLS = 106400
            arena_t = st.enter_context(nc.sbuf_tensor("arena", [128, NCOLS], BF16))
            self.A = Arena(arena_t, NCOLS)
            psall = st.enter_context(nc.psum_tensor("psall", [128, 4096], F32))
            self.psall = psall
            self.bank = [Tile(psall[:, 512 * i:512 * (i + 1)], Res("bank%d" % i)) for i in range(8)]
            kind, l = self.step
            if kind == "fused":
                self.build_fused()
                P.emit()
                return nc
            self.setup_consts()
            self.mod_phase(l)
            if kind == "mlp":
                self.mlp_phase(l)
            elif kind == "odd":
                self.gen_tables()
                self.odd_mixer(l)
            elif kind == "evenA":
                self.even_a(l)
            elif kind == "evenB":
                self.even_b(l)
            if kind != "evenA":
                P.barrier()
                for i in range(2):
                    P.dma("sp", self.sout[128 * i:128 * (i + 1), :], self.sctx.ap[:, i, :], [self.sctx_res[i]], [Res()])
            P.emit()
        return nc

    def setup_consts(self):
        A = self.A
        P = self.P
        it = A.alloc(I32, [128])
        self.ident_f = A.alloc(F32, [128])
        self.ident_b = A.alloc(BF16, [128])
        P.op("pool", lambda e: e.iota(it.ap, [[1, 128]], base=0, channel_multiplier=-1), [], [it.res])
        self.cp("dve", self.ident_f.ap, it.ap, [it.res], [self.ident_f.res])
        self.ts("dve", self.ident_f.ap, self.ident_f.ap, 0.0, None, ALU.is_equal, None, [self.ident_f.res], [self.ident_f.res])
        self.cp("dve", self.ident_b.ap, self.ident_f.ap, [self.ident_f.res], [self.ident_b.res])
        self.cact2 = A.alloc(F32, [8, 2])
        craw = A.alloc(F32, [2, 128])
        P.dma("sp", craw.ap[0:8, 0, :], self.c.rearrange("(k p) -> k p", p=128), [], [craw.res])
        P.dma("sp", craw.ap[0:8, 1, :], self.c_ctx.rearrange("(k p) -> k p", p=128), [], [craw.res])
        for v in range(2):
            pv = Tile(self.bank[7].ap[:, 8 * v:8 * v + 8], self.bank[7].res)
            self.tr(pv.ap, craw.ap[0:8, v, :], self.ident_f.ap[0:8, 0:8], [craw.res, self.ident_f.res], [pv.res])
            self.act(self.cact2.ap[:, :, v], pv.ap, AF.Silu, [pv.res], [self.cact2.res])
        self.modpp = A.alloc(F32, [48, 2])
        self.gpp = A.alloc(F32, [4, 8])
        self.A1 = A.alloc(F32, [8, 2])
        self.A2 = A.alloc(F32, [8, 2])
        self.G = [[A.alloc(F32, [1024]) for v in range(2)] for i in range(2)]
        self.sctx = A.alloc(F32, [2, 1024])
        self.sctx_res = [Res("sctx0"), Res("sctx1")]
        for i in range(2):
            P.dma("sp", self.sctx.ap[:, i, :], self.ctx[128 * i:128 * (i + 1), :], [], [self.sctx_res[i]])
        self.negpi = A.alloc(F32, [1])
        self.memset("pool", self.negpi.ap, -math.pi, [self.negpi.res])
        self.small = A.alloc(F32, [64])
        self.small_tiles = [Tile(self.small.ap[:, i:i + 1]) for i in range(64)]
        self.small_n = 0
        A.mark_persistent()

    def make_crep(self):
        A = self.A
        self.crep = [A.alloc(F32, [8, 128]) for _ in range(2)]
        ones = A.alloc(F32, [128])
        self.memset("pool", ones.ap, 1.0, [ones.res])
        for v in range(2):
            for k in range(8):
                self.ts("dve", self.crep[v].ap[:, k, :], ones.ap, self.cact2.ap[:, k, v:v + 1], None, ALU.mult, None,
                        [ones.res, self.cact2.res], [self.crep[v].res])

    def scalar_slot(self):
        i = self.small_n % 64
        self.small_n += 1
        return self.small_tiles[i]

    def bcast_tile(self, l, ntile, v, wm, dst_ap, dst_res, gain_idx, tmpb, tmpg, psb):
        P = self.P
        for k in range(8):
            self.mm(psb.ap, self.crep[v].ap[:, k, :], wm.ap[:, k, :], k == 0, k == 7, [self.crep[v].res, wm.res], [psb.res])
        if gain_idx is None:
            self.tt("dve", dst_ap, psb.ap, tmpb.ap, ALU.add, [psb.res, tmpb.res], [dst_res])
        else:
            self.tt("dve", dst_ap, psb.ap, tmpb.ap, ALU.add, [psb.res, tmpb.res], [dst_res])
            self.tt("dve", dst_ap, dst_ap, tmpg.ap, ALU.mult, [dst_res, tmpg.res], [dst_res])

    def load_bcast_row(self, eng, tile, src_row):
        self.P.dma(eng, tile.ap, src_row.partition_broadcast(128), [], [tile.res])

    def mod_phase(self, l):
        A = self.A
        P = self.P
        P.barrier()
        A.reset()
        self.make_crep()
        wms = [A.alloc(F32, [8, 512]) for _ in range(2)]
        bpp = A.alloc(F32, [48])
        tmpb = [A.alloc(F32, [512]) for _ in range(2)]
        tmpg = [A.alloc(F32, [512]) for _ in range(2)]
        braw = A.alloc(F32, [128])
        graw = A.alloc(F32, [128])
        P.dma("sp", braw.ap[0:48, :], self.b_mod.rearrange("(j p) -> j p", p=128), [], [braw.res])
        P.dma("sp", graw.ap[0:32, :], self.norm_g.rearrange("g (k p) -> (g k) p", p=128), [], [graw.res])
        pb_ = Tile(self.bank[7].ap[:, 64:112], self.bank[7].res)
        self.tr(pb_.ap, braw.ap[0:48, :], self.ident_f.ap[0:48, 0:48], [braw.res, self.ident_f.res], [pb_.res])
        self.cp("dve", bpp.ap, pb_.ap, [pb_.res], [bpp.res])
        pg_ = Tile(self.bank[7].ap[:, 128:160], self.bank[7].res)
        self.tr(pg_.ap, graw.ap[0:32, :], self.ident_f.ap[0:32, 0:32], [graw.res, self.ident_f.res], [pg_.res])
        self.cp("dve", self.gpp.ap, pg_.ap.rearrange("p (g k) -> p g k", g=4), [pg_.res], [self.gpp.res])
        psA = Tile(self.bank[7].ap[:, 0:8], self.bank[7].res)
        for nt in range(12):
            wm = wms[nt % 2]
            P.dma("sp", wm.ap, self.w_mod[:, nt * 512:(nt + 1) * 512].rearrange("(k p) n -> p k n", p=128), [], [wm.res])
            for jj in range(4):
                for k in range(8):
                    self.mm(psA.ap[:, 2 * jj:2 * jj + 2], wm.ap[:, k, jj * 128:(jj + 1) * 128], self.cact2.ap[:, k, :],
                            k == 0, k == 7, [wm.res, self.cact2.res], [psA.res])
            self.tt("dve", self.modpp.ap[:, nt * 4:(nt + 1) * 4, :], psA.ap.rearrange("p (a b) -> p a b", b=2),
                    bpp.ap[:, nt * 4:(nt + 1) * 4].unsqueeze(2).broadcast_to([128, 4, 2]), ALU.add,
                    [psA.res, bpp.res], [self.modpp.res])
            gi = {4: (0, 0), 5: (0, 1), 10: (1, 0), 11: (1, 1)}.get(nt)
            if gi is not None:
                i, half = gi
                tb = tmpb[half]
                tg = tmpg[half]
                self.load_bcast_row("sp", tb, self.b_mod[nt * 512:(nt + 1) * 512])
                self.load_bcast_row("sp", tg, self.norm_g[1 + 2 * i, half * 512:(half + 1) * 512])
                for v in range(2):
                    psb = self.bank[5 + v]
                    self.bcast_tile(l, nt, v, wm, self.G[i][v].ap[:, half * 512:(half + 1) * 512], self.G[i][v].res, 1, tb, tg, psb)
        for (Ax, sc_off, gidx) in ((self.A1, 8, 0), (self.A2, 32, 2)):
            self.stt("dve", Ax.ap, self.modpp.ap[:, sc_off:sc_off + 8, :], 1.0,
                     self.gpp.ap[:, gidx, :].unsqueeze(2).broadcast_to([128, 8, 2]), ALU.add, ALU.mult,
                     [self.modpp.res, self.gpp.res], [Ax.res])

    def rstd_of(self, src_ap, src_reads, junk, n=1024):
        ss = self.scalar_slot()
        self.memset("pool", ss.ap, 0.0, [ss.res])
        self.act(junk.ap, src_ap, AF.Square, src_reads + [ss.res], [junk.res, ss.res], accum_out=ss.ap)
        self.act(ss.ap, ss.ap, AF.Sqrt, [ss.res], [ss.res], bias=EPS, scale=1.0 / n)
        self.P.op("dve", lambda e: e.reciprocal(out=ss.ap, in_=ss.ap), [ss.res], [ss.res])
        return ss

    def prenorm_a(self, h_ap, h_res, hs):
        rstd = self.rstd_of(h_ap, [h_res], hs)
        self.act(hs.ap, h_ap, AF.Copy, [h_res, rstd.res], [hs.res], scale=rstd.ap)

    def prenorm_b(self, Ax, Bx_ap, Bx_res, v, hs, dstT_ap, dstT_res, psT):
        pst = psT.ap.bitcast(BF16)
        for k in range(8):
            self.tr(pst[:, k * 128:(k + 1) * 128], hs.ap[:, k * 128:(k + 1) * 128], self.ident_b.ap,
                    [hs.res, self.ident_b.res], [psT.res])
        p3 = pst.rearrange("p (a b) -> p a b", a=8)
        self.tt("dve", dstT_ap, p3, Ax.ap[:, :, v].unsqueeze(2).broadcast_to([128, 8, 128]), ALU.mult,
                [psT.res, Ax.res], [dstT_res])
        self.tt("pool", dstT_ap, dstT_ap, Bx_ap[:, :, v].unsqueeze(2).broadcast_to([128, 8, 128]), ALU.add,
                [dstT_res, Bx_res], [dstT_res])

    def prenorm_T(self, h_ap, h_res, Ax, Bx_ap, Bx_res, v, hs, dstT_ap, dstT_res, psT):
        self.prenorm_a(h_ap, h_res, hs)
        self.prenorm_b(Ax, Bx_ap, Bx_res, v, hs, dstT_ap, dstT_res, psT)

    def postnorm_residual(self, po_ap, po_res, Gt, h_ap, h_res, junk, tmp):
        rstd = self.rstd_of(po_ap, [po_res], junk)
        self.stt("dve", tmp.ap, po_ap, rstd.ap, Gt.ap, ALU.mult, ALU.mult, [po_res, rstd.res, Gt.res], [tmp.res])
        self.tt("pool", h_ap, tmp.ap, h_ap, ALU.add, [tmp.res, h_res], [h_res])

    def mlp_phase(self, l):
        A = self.A
        P = self.P
        P.barrier()
        A.reset()
        w1 = A.alloc(BF16, [8, 4096])
        w2 = A.alloc(BF16, [32, 1024])
        w1r = [Res() for _ in range(8)]
        w2r = [Res() for _ in range(8)]
        for k in range(8):
            P.dma("pool", w1.ap[:, k, :], self.w_ff1[128 * k:128 * (k + 1), :], [], [w1r[k]])
        for q in range(8):
            P.dma("pool", w2.ap[:, 4 * q:4 * q + 4, :],
                  self.w_ff2[512 * q:512 * (q + 1), :].rearrange("(j p) n -> p j n", p=128), [], [w2r[q]])
        hid = A.alloc(BF16, [32, 256])
        hidr = [Res() for _ in range(32)]
        hnT = A.alloc(BF16, [8, 256])
        ht = [A.alloc(F32, [1024]) for _ in range(2)]
        hs = A.alloc(BF16, [1024])
        tmp = A.alloc(F32, [1024])
        rl = [A.alloc(F32, [256]) for _ in range(2)]
        B2_ap = self.modpp.ap[:, 24:32, :]
        upd_ctx = l < 2
        tiles = [("lat", i) for i in range(16)] + ([("ctx", 0)] if upd_ctx else [])
        if self.dbg.startswith("mlp1"):
            tiles = tiles[:1]
        ht4 = ht + [A.alloc(F32, [1024]) for _ in range(2)]
        hs2 = [hs, A.alloc(BF16, [1024])]
        junk2 = hs2[0]

        def pre(idx):
            kind, ti = tiles[idx]
            hview = []
            for sub in range(2):
                if kind == "lat":
                    t128 = ti * 2 + sub
                    h = ht4[2 * (idx % 2) + sub]
                    P.dma("sp", h.ap, self.x[128 * t128:128 * (t128 + 1), :], [], [h.res])
                    hview.append((h.ap, h.res))
                else:
                    hview.append((self.sctx.ap[:, sub, :], self.sctx_res[sub]))
                self.prenorm_a(hview[sub][0], hview[sub][1], hs2[sub])
            return hview

        def pre_b(idx):
            kind, ti = tiles[idx]
            v = 0 if kind == "lat" else 1
            for sub in range(2):
                self.prenorm_b(self.A2, B2_ap, self.modpp.res, v, hs2[sub], hnT.ap[:, :, sub * 128:(sub + 1) * 128], hnT.res, self.bank[0])

        hv_next = pre(0)
        pre_b(0)
        for idx, (kind, ti) in enumerate(tiles):
            v = 0 if kind == "lat" else 1
            hview = hv_next
            for j in range(32):
                pb = self.bank[1 + (j % 2)]
                pj = Tile(pb.ap[:, 0:256], pb.res)
                for k in range(8):
                    self.mm(pj.ap, w1.ap[:, k, 128 * j:128 * (j + 1)], hnT.ap[:, k, :], k == 0, k == 7,
                            [w1r[k], hnT.res], [pj.res])
                r = rl[j % 2]
                self.act(r.ap, pj.ap, AF.Relu, [pj.res], [r.res])
                self.tt("pool" if j % 2 else "dve", hid.ap[:, j, :], r.ap, r.ap, ALU.mult, [r.res], [hidr[j]])
            if idx + 1 < len(tiles):
                hv_next = pre(idx + 1)
            for sub in range(2):
                for half in range(2):
                    pb = self.bank[3 + 2 * sub + half]
                    for j in range(32):
                        self.mm(pb.ap, hid.ap[:, j, sub * 128:(sub + 1) * 128], w2.ap[:, j, half * 512:(half + 1) * 512],
                                j == 0, j == 31, [hidr[j], w2r[j // 4]], [pb.res])
            if idx + 1 < len(tiles):
                pre_b(idx + 1)
            for sub in range(2):
                po_ap = self.psall[:, (3 + 2 * sub) * 512:(5 + 2 * sub) * 512]
                pres = [self.bank[3 + 2 * sub].res, self.bank[4 + 2 * sub].res]
                rstd = self.rstd_of(po_ap, pres, junk2)
                self.stt("dve", tmp.ap, po_ap, rstd.ap, self.G[1][v].ap, ALU.mult, ALU.mult,
                         pres + [rstd.res, self.G[1][v].res], [tmp.res])
                h_ap, h_res = hview[sub]
                self.tt("pool", h_ap, tmp.ap, h_ap, ALU.add, [tmp.res, h_res], [h_res])
                if kind == "lat":
                    t128 = ti * 2 + sub
                    P.dma("sp", self.out[128 * t128:128 * (t128 + 1), :], h_ap, [h_res], [self.out_res[t128]])

    def cmul(self, eng, out_re, out_im, a_re, a_im, b_re, b_im, t1, t2, reads, wres, neg_im=False):
        rs_ = reads
        self.tt(eng, t1.ap, a_re, b_re, ALU.mult, rs_, [t1.res])
        self.tt(eng, t2.ap, a_im, b_im, ALU.mult, rs_, [t2.res])
        self.tt(eng, out_re, t1.ap, t2.ap, ALU.subtract, [t1.res, t2.res], wres)
        self.tt(eng, t1.ap, a_re, b_im, ALU.mult, rs_, [t1.res])
        self.tt(eng, t2.ap, a_im, b_re, ALU.mult, rs_, [t2.res])
        if neg_im:
            self.stt(eng, out_im, t1.ap, -1.0, t2.ap, ALU.mult, ALU.subtract, [t1.res, t2.res], wres)
        else:
            self.tt(eng, out_im, t1.ap, t2.ap, ALU.add, [t1.res, t2.res], wres)

    def even_a(self, l):
        A = self.A
        P = self.P
        nc = self.nc
        AX = mybir.AxisListType.X
        P.barrier()
        A.reset()
        NB = 544
        RTm = [A.alloc(BF16, [32, 2, 128]) for _ in range(2)]
        for r_ in range(2):
            self.memset("pool", RTm[r_].ap, 0.0, [RTm[r_].res])
        Ob = [[A.alloc(BF16, [16, 128]) for _ in range(2)] for _ in range(2)]
        Tm = A.alloc(BF16, [32, 128])
        ASr = A.alloc(F32, [10, 32])
        ASi = A.alloc(F32, [10, 32])
        ASn = A.alloc(F32, [10, 32])
        keep_top = A.top
        T32 = A.alloc(F32, [32, 128])
        nat = A.alloc(F32, [4, 128])
        lre = A.alloc(F32, [32]); lim = A.alloc(F32, [32]); ldt = A.alloc(F32, [32])
        for (src, dst, bnk) in ((self.s5_lam_re, lre, 0), (self.s5_lam_im, lim, 1)):
            P.dma("sp", nat.ap[0:32, bnk, :], src.rearrange("d (q r) p -> (d q) (r p)", r=2), [], [nat.res])
            pt = Tile(self.bank[7].ap[:, 32 * bnk:32 * bnk + 32], self.bank[7].res)
            self.tr(pt.ap, nat.ap[0:32, bnk, :], self.ident_f.ap[0:32, 0:32], [nat.res, self.ident_f.res], [pt.res])
            self.cp("dve", dst.ap, pt.ap, [pt.res], [dst.res])
        P.dma("sp", nat.ap[0:32, 2, 0:2], self.s5_log_dt.rearrange("d (q r) -> (d q) r", r=2), [], [nat.res])
        dtT = A.alloc(F32, [32])
        pt = Tile(self.bank[7].ap[0:2, 64:96], self.bank[7].res)
        self.tr(pt.ap, nat.ap[0:32, 2, 0:2], self.ident_f.ap[0:32, 0:32], [nat.res, self.ident_f.res], [pt.res])
        self.cp("dve", dtT.ap[0:2, :], pt.ap, [pt.res], [dtT.res])
        sel_i = A.alloc(I32, [128]); sel = A.alloc(F32, [128]); sel2 = A.alloc(F32, [128])
        P.op("pool", lambda e: e.iota(sel_i.ap[0:2, :], [[1, 128]], base=0, channel_multiplier=-64), [], [sel_i.res])
        self.cp("dve", sel.ap[0:2, :], sel_i.ap[0:2, :], [sel_i.res], [sel.res])
        self.ts("dve", sel2.ap[0:2, :], sel.ap[0:2, :], 0.0, None, ALU.is_ge, None, [sel.res], [sel2.res])
        self.ts("dve", sel.ap[0:2, :], sel.ap[0:2, :], 64.0, None, ALU.is_lt, None, [sel.res], [sel.res])
        self.tt("dve", sel.ap[0:2, :], sel.ap[0:2, :], sel2.ap[0:2, :], ALU.mult, [sel.res, sel2.res], [sel.res])
        pt = Tile(self.bank[7].ap[:, 96:128], self.bank[7].res)
        self.mm(pt.ap, sel.ap[0:2, :], dtT.ap[0:2, :], True, True, [sel.res, dtT.res], [pt.res])
        self.cp("dve", ldt.ap, pt.ap, [pt.res], [ldt.res])
        def v32():
            return A.alloc(F32, [32])
        dt = v32(); xm = v32(); mag = v32(); imag = v32(); th = v32(); fr = v32(); frc = v32(); w1 = v32(); w2 = v32()
        sn = v32(); cs = v32(); are = v32(); aim = v32(); ire = v32(); iim = v32(); den = v32(); fre = v32(); fim = v32(); nre = v32()
        self.act(dt.ap, ldt.ap, AF.Exp, [ldt.res], [dt.res])
        self.ts("dve", lre.ap, lre.ap, -1e-4, None, ALU.min, None, [lre.res], [lre.res])
        self.tt("dve", xm.ap, lre.ap, dt.ap, ALU.mult, [lre.res, dt.res], [xm.res])
        self.act(mag.ap, xm.ap, AF.Exp, [xm.res], [mag.res])
        self.act(imag.ap, xm.ap, AF.Exp, [xm.res], [imag.res], scale=-1.0)
        self.tt("dve", th.ap, lim.ap, dt.ap, ALU.mult, [lim.res, dt.res], [th.res])
        ki = A.alloc(I32, [32]); kf = v32()
        self.ts("dve", fr.ap, th.ap, 1.0 / (2.0 * math.pi), None, ALU.mult, None, [th.res], [fr.res])
        self.cp("dve", ki.ap, fr.ap, [fr.res], [ki.res])
        self.cp("dve", kf.ap, ki.ap, [ki.res], [kf.res])
        self.tt("dve", fr.ap, fr.ap, kf.ap, ALU.subtract, [fr.res, kf.res], [fr.res])

        def wrap(x):
            self.ts("dve", w1.ap, x.ap, 0.5, None, ALU.is_gt, None, [x.res], [w1.res])
            self.ts("dve", w2.ap, x.ap, -0.5, None, ALU.is_lt, None, [x.res], [w2.res])
            self.tt("dve", x.ap, x.ap, w1.ap, ALU.subtract, [x.res, w1.res], [x.res])
            self.tt("dve", x.ap, x.ap, w2.ap, ALU.add, [x.res, w2.res], [x.res])
        wrap(fr)
        self.ts("dve", frc.ap, fr.ap, 0.25, None, ALU.add, None, [fr.res], [frc.res])
        wrap(frc)
        self.act(sn.ap, fr.ap, AF.Sin, [fr.res], [sn.res], scale=2.0 * math.pi)
        self.act(cs.ap, frc.ap, AF.Sin, [frc.res], [cs.res], scale=2.0 * math.pi)
        self.tt("dve", are.ap, mag.ap, cs.ap, ALU.mult, [mag.res, cs.res], [are.res])
        self.tt("dve", aim.ap, mag.ap, sn.ap, ALU.mult, [mag.res, sn.res], [aim.res])
        self.tt("dve", ire.ap, imag.ap, cs.ap, ALU.mult, [imag.res, cs.res], [ire.res])
        self.stt("dve", iim.ap, imag.ap, -1.0, sn.ap, ALU.mult, ALU.mult, [imag.res, sn.res], [iim.res])
        self.tt("dve", den.ap, lre.ap, lre.ap, ALU.mult, [lre.res], [den.res])
        self.tt("dve", w1.ap, lim.ap, lim.ap, ALU.mult, [lim.res], [w1.res])
        self.tt("dve", den.ap, den.ap, w1.ap, ALU.add, [den.res, w1.res], [den.res])
        self.P.op("dve", lambda e: e.reciprocal(out=den.ap, in_=den.ap), [den.res], [den.res])
        self.ts("dve", nre.ap, are.ap, -1.0, None, ALU.add, None, [are.res], [nre.res])
        self.tt("dve", w1.ap, nre.ap, lre.ap, ALU.mult, [nre.res, lre.res], [w1.res])
        self.tt("dve", w2.ap, aim.ap, lim.ap, ALU.mult, [aim.res, lim.res], [w2.res])
        self.tt("dve", fre.ap, w1.ap, w2.ap, ALU.add, [w1.res, w2.res], [fre.res])
        self.tt("dve", fre.ap, fre.ap, den.ap, ALU.mult, [fre.res, den.res], [fre.res])
        self.tt("dve", w1.ap, aim.ap, lre.ap, ALU.mult, [aim.res, lre.res], [w1.res])
        self.tt("dve", w2.ap, nre.ap, lim.ap, ALU.mult, [nre.res, lim.res], [w2.res])
        self.tt("dve", fim.ap, w1.ap, w2.ap, ALU.subtract, [w1.res, w2.res], [fim.res])
        self.tt("dve", fim.ap, fim.ap, den.ap, ALU.mult, [fim.res, den.res], [fim.res])
        Epr = A.alloc(F32, [9, 32]); Epi = A.alloc(F32, [9, 32]); Enr = A.alloc(F32, [8, 32]); Eni = A.alloc(F32, [8, 32])
        s1 = v32(); s2 = v32()
        self.memset("pool", Epr.ap[:, 0, :], 1.0, [Epr.res]); self.memset("pool", Epi.ap[:, 0, :], 0.0, [Epi.res])
        self.memset("pool", Enr.ap[:, 0, :], 1.0, [Enr.res]); self.memset("pool", Eni.ap[:, 0, :], 0.0, [Eni.res])
        for j in range(1, 9):
            self.cmul("dve", Epr.ap[:, j, :], Epi.ap[:, j, :], Epr.ap[:, j - 1, :], Epi.ap[:, j - 1, :], are.ap, aim.ap, s1, s2,
                      [Epr.res, Epi.res, are.res, aim.res], [Epr.res, Epi.res])
        for j in range(1, 8):
            self.cmul("dve", Enr.ap[:, j, :], Eni.ap[:, j, :], Enr.ap[:, j - 1, :], Eni.ap[:, j - 1, :], ire.ap, iim.ap, s1, s2,
                      [Enr.res, Eni.res, ire.res, iim.res], [Enr.res, Eni.res])
        self.cp("dve", ASr.ap[:, 0, :], Epr.ap[:, 8, :], [Epr.res], [ASr.res])
        self.cp("dve", ASi.ap[:, 0, :], Epi.ap[:, 8, :], [Epi.res], [ASi.res])
        for k in range(1, 10):
            self.cmul("dve", ASr.ap[:, k, :], ASi.ap[:, k, :], ASr.ap[:, k - 1, :], ASi.ap[:, k - 1, :],
                      ASr.ap[:, k - 1, :], ASi.ap[:, k - 1, :], s1, s2, [ASr.res, ASi.res], [ASr.res, ASi.res])
        self.ts("dve", ASn.ap, ASi.ap, -1.0, None, ALU.mult, None, [ASi.res], [ASn.res])
        if self.dbg.startswith("s5preA"):
            return
        Br = A.alloc(F32, [2, 16, 16]); Bi = A.alloc(F32, [2, 16, 16]); Bbr = A.alloc(F32, [2, 16, 16]); Bbi = A.alloc(F32, [2, 16, 16])
        P.dma("sp", Br.ap, self.s5_b_re.rearrange("d (q r) p h -> (r p) d q h", r=2), [], [Br.res])
        P.dma("sp", Bi.ap, self.s5_b_im.rearrange("d (q r) p h -> (r p) d q h", r=2), [], [Bi.res])
        b1 = A.alloc(F32, [2, 16, 16]); b2 = A.alloc(F32, [2, 16, 16])
        fre3 = fre.ap.rearrange("p (d q) -> p d q", d=2).unsqueeze(3).broadcast_to([128, 2, 16, 16])
        fim3 = fim.ap.rearrange("p (d q) -> p d q", d=2).unsqueeze(3).broadcast_to([128, 2, 16, 16])
        self.cmul("dve", Bbr.ap, Bbi.ap, fre3, fim3, Br.ap, Bi.ap, b1, b2, [fre.res, fim.res, Br.res, Bi.res], [Bbr.res, Bbi.res])
        Cr = A.alloc(F32, [2, 16, 16]); Ci = A.alloc(F32, [2, 16, 16])
        cnat = [A.alloc(F32, [128]) for _ in range(2)]
        ci_ = 0
        for (src, dstC) in ((self.s5_c_re, Cr), (self.s5_c_im, Ci)):
            for d in range(2):
                for ch in range(2):
                    cn = cnat[ci_ % 2]
                    for ql in range(8):
                        q = ch * 8 + ql
                        P.dma("sp", cn.ap[16 * ql:16 * ql + 16, :].rearrange("h (r p) -> h r p", r=2),
                              src[d, 2 * q:2 * q + 2].rearrange("r h p -> h r p"), [], [cn.res])
                    pt = Tile(self.bank[6].ap[:, 128 * (ci_ % 4):128 * (ci_ % 4) + 128], self.bank[6].res)
                    self.tr(pt.ap, cn.ap, self.ident_f.ap, [cn.res, self.ident_f.res], [pt.res])
                    self.cp("dve", dstC.ap[:, d, ch * 8:ch * 8 + 8, :], pt.ap.rearrange("p (q h) -> p q h", q=8), [pt.res], [dstC.res])
                    ci_ += 1
        dnat = A.alloc(F32, [16]); dT = A.alloc(F32, [32]); rep_i = A.alloc(I32, [128]); rep = A.alloc(F32, [128]); dcol = A.alloc(F32, [32])
        P.dma("sp", dnat.ap[0:32, :], self.s5_d.rearrange("(g h) -> g h", h=16), [], [dnat.res])
        pt = Tile(self.bank[7].ap[0:16, 128:160], self.bank[7].res)
        self.tr(pt.ap, dnat.ap[0:32, :], self.ident_f.ap[0:32, 0:32], [dnat.res, self.ident_f.res], [pt.res])
        self.cp("dve", dT.ap[0:16, :], pt.ap, [pt.res], [dT.res])
        P.op("pool", lambda e: e.iota(rep_i.ap[0:16, :], [[1, 128]], base=16, channel_multiplier=-1), [], [rep_i.res])
        self.ts("dve", rep_i.ap[0:16, :], rep_i.ap[0:16, :], 15, None, ALU.bitwise_and, None, [rep_i.res], [rep_i.res])
        self.cp("dve", rep.ap[0:16, :], rep_i.ap[0:16, :], [rep_i.res], [rep.res])
        self.ts("dve", rep.ap[0:16, :], rep.ap[0:16, :], 0.0, None, ALU.is_equal, None, [rep.res], [rep.res])
        pt = Tile(self.bank[7].ap[:, 160:192], self.bank[7].res)
        self.mm(pt.ap, rep.ap[0:16, :], dT.ap[0:16, :], True, True, [rep.res, dT.res], [pt.res])
        self.cp("dve", dcol.ap, pt.ap, [pt.res], [dcol.res])
        cbi = A.alloc(I32, [8, 16]); cbf = A.alloc(F32, [8, 16]); rbi = A.alloc(I32, [1]); rbf = A.alloc(F32, [1])
        mkf = A.alloc(F32, [128]); mkb = A.alloc(F32, [128])
        P.op("pool", lambda e: e.iota(cbi.ap, [[1, 8], [0, 16]], base=0, channel_multiplier=0), [], [cbi.res])
        self.cp("dve", cbf.ap, cbi.ap, [cbi.res], [cbf.res])
        P.op("pool", lambda e: e.iota(rbi.ap, [[1, 1]], base=0, channel_multiplier=1), [], [rbi.res])
        self.ts("dve", rbi.ap, rbi.ap, 4, None, ALU.arith_shift_right, None, [rbi.res], [rbi.res])
        self.cp("dve", rbf.ap, rbi.ap, [rbi.res], [rbf.res])
        cbf2 = cbf.ap.rearrange("p a b -> p (a b)")
        self.ts("dve", mkf.ap, cbf2, rbf.ap, None, ALU.is_ge, None, [cbf.res, rbf.res], [mkf.res])
        self.ts("dve", mkb.ap, cbf2, rbf.ap, None, ALU.is_le, None, [cbf.res, rbf.res], [mkb.res])
        if self.dbg.startswith("s5preB"):
            return
        hmi = A.alloc(I32, [2]); hm = A.alloc(F32, [2]); tmpT = A.alloc(F32, [128])
        P.op("pool", lambda e: e.iota(hmi.ap, [[0, 2]], base=0, channel_multiplier=1), [], [hmi.res])
        self.ts("dve", hmi.ap, hmi.ap, 6, None, ALU.arith_shift_right, None, [hmi.res], [hmi.res])
        self.cp("dve", hm.ap, hmi.ap, [hmi.res], [hm.res])
        self.ts("dve", hm.ap[:, 0:1], hm.ap[:, 0:1], -1.0, -1.0, ALU.add, ALU.mult, [hm.res], [hm.res])
        big = [A.alloc(F32, [16, 8, 16]) for _ in range(6)]
        Pr, Pi, Qr, Qi, g1, g2 = big

        def esel(E, d, j0=0, n=8):
            return E.ap[:, j0:j0 + n, 16 * d:16 * d + 16].rearrange("p j q -> p q j").unsqueeze(3).broadcast_to([128, 16, n, 16])

        def ebc(E, d, j):
            return E.ap[:, j, 16 * d:16 * d + 16].unsqueeze(2).unsqueeze(3).broadcast_to([128, 16, 8, 16])

        def bcj(X, d):
            return X.ap[:, d, :, :].unsqueeze(2).broadcast_to([128, 16, 8, 16])
        for d in range(2):
            EP_r, EP_i = (Enr, Eni) if d == 0 else (Epr, Epi)
            EQ_r, EQ_i = (Epr, Epi) if d == 0 else (Enr, Eni)
            rr = [Epr.res, Epi.res, Enr.res, Eni.res, Bbr.res, Bbi.res, Cr.res, Ci.res]
            if self.dbg.startswith("s5preE"):
                continue
            self.cmul("dve", Pr.ap, Pi.ap, esel(EP_r, d), esel(EP_i, d), bcj(Bbr, d), bcj(Bbi, d), g1, g2, rr, [Pr.res, Pi.res])
            self.cmul("dve", Qr.ap, Qi.ap, esel(EQ_r, d), esel(EQ_i, d), bcj(Cr, d), bcj(Ci, d), g1, g2, rr, [Qr.res, Qi.res], neg_im=True)
            for r in range(2):
                self.ts("dve", g1.ap, Pr.ap, hm.ap[:, r:r + 1], None, ALU.mult, None, [Pr.res, hm.res], [g1.res])
                self.ts("dve", g2.ap, Pi.ap, hm.ap[:, r:r + 1], None, ALU.mult, None, [Pi.res, hm.res], [g2.res])
                for q in range(16):
                    g = 2 * q + r
                    pt = Tile(self.bank[1 + (q % 2)].ap[:, 0:128], self.bank[1 + (q % 2)].res)
                    self.mm(pt.ap, g1.ap[:, q].rearrange("p a b -> p (a b)"), Qr.ap[:, q].rearrange("p a b -> p (a b)"),
                            True, False, [g1.res, Qr.res], [pt.res])
                    self.mm(pt.ap, g2.ap[:, q].rearrange("p a b -> p (a b)"), Qi.ap[:, q].rearrange("p a b -> p (a b)"),
                            False, True, [g2.res, Qi.res], [pt.res])
                    if d == 0:
                        self.tt("dve", T32.ap[:, g, :], pt.ap, mkf.ap, ALU.mult, [pt.res, mkf.res], [T32.res])
                    else:
                        self.tt("dve", tmpT.ap, pt.ap, mkb.ap, ALU.mult, [pt.res, mkb.res], [tmpT.res])
                        self.tt("dve", T32.ap[:, g, :], T32.ap[:, g, :], tmpT.ap, ALU.add, [T32.res, tmpT.res], [T32.res])
            jO = 1 if d == 0 else 8
            self.tt("dve", g1.ap, ebc(Epr, d, jO), Qr.ap, ALU.mult, rr + [Qr.res], [g1.res])
            self.tt("dve", g2.ap, ebc(Epi, d, jO), Qi.ap, ALU.mult, rr + [Qi.res], [g2.res])
            self.tt("dve", Ob[d][0].ap.rearrange("p q (a b) -> p q a b", a=8), g1.ap, g2.ap, ALU.add, [g1.res, g2.res], [Ob[d][0].res])
            self.tt("dve", g1.ap, ebc(Epr, d, jO), Qi.ap, ALU.mult, rr + [Qi.res], [g1.res])
            self.tt("dve", g2.ap, ebc(Epi, d, jO), Qr.ap, ALU.mult, rr + [Qr.res], [g2.res])
            self.tt("dve", Ob[d][1].ap.rearrange("p q (a b) -> p q a b", a=8), g1.ap, g2.ap, ALU.subtract, [g1.res, g2.res], [Ob[d][1].res])
            if d == 0:
                self.cmul("dve", Qr.ap, Qi.ap, ebc(Epr, 0, 7), ebc(Epi, 0, 7), Pr.ap, Pi.ap, g1, g2, rr + [Pr.res, Pi.res], [Qr.res, Qi.res])
                Rr_, Ri_ = Qr, Qi
            else:
                Rr_, Ri_ = Pr, Pi
            for q in range(16 if not self.dbg.startswith("s5preD") else 0):
                for (ri, Rx) in ((0, Rr_), (1, Ri_)):
                    pt = Tile(self.bank[3 + (q % 2)].ap[:, 128 * ri:128 * ri + 128], self.bank[3 + (q % 2)].res)
                    self.tr(pt.ap, Rx.ap[:, q].rearrange("p a b -> p (a b)"), self.ident_f.ap, [Rx.res, self.ident_f.res], [pt.res])
                pb2 = self.bank[3 + (q % 2)]
                for r_ in range(2 if not self.dbg.startswith("s5preCF") else 0):
                    self.cp("dve", RTm[r_].ap[:, 2 * q + d, :, 64 * r_:64 * r_ + 64],
                            pb2.ap[:, 0:256].rearrange("p (a b) -> p a b", a=2)[:, :, 64 * r_:64 * r_ + 64], [pb2.res], [RTm[r_].res])
        for g in range(32):
            self.stt("dve", Tm.ap[:, g, :], self.ident_f.ap, dcol.ap[:, g:g + 1], T32.ap[:, g, :], ALU.mult, ALU.add,
                     [self.ident_f.res, dcol.res, T32.res], [Tm.res])
        self.Ob = Ob
        if self.dbg.startswith("s5pre"):
            return
        P.barrier()
        A.top = keep_top
        U = A.alloc(BF16, [32, NB])
        Ur = [[Res() for _ in range(5)] for _ in range(32)]
        u_top = A.top
        wu = A.alloc(BF16, [8, 512])
        P.dma("pool", wu.ap, self.w_in[:, 0:512].rearrange("(k p) n -> p k n", p=128), [], [wu.res])
        hTg = A.alloc(BF16, [8, 1024])
        ub2 = A.alloc(BF16, [32, 8, 16])
        hTs = A.alloc(BF16, [8, 8, 128])
        self.memset("pool", ub2.ap, 0.0, [ub2.res])
        hs = A.alloc(BF16, [1024])
        ht = [A.alloc(F32, [1024]) for _ in range(2)]
        B1_ap = self.modpp.ap[:, 0:8, :]
        groups = [(0, 32)] + [(32 + 128 * i, 128) for i in range(4)]
        for gi, (n0, nb) in enumerate(groups):
            ntile = nb // 16
            for i in range(ntile):
                if gi == 0:
                    h_ap, h_res, v = self.sctx.ap[:, i, :], self.sctx_res[i], 1
                else:
                    h = ht[i % 2]
                    t128 = (gi - 1) * 8 + i
                    P.dma("sp", h.ap, self.x[128 * t128:128 * (t128 + 1), :], [], [h.res])
                    h_ap, h_res, v = h.ap, h.res, 0
                self.prenorm_T(h_ap, h_res, self.A1, B1_ap, self.modpp.res, v, hs, hTg.ap[:, :, 128 * i:128 * (i + 1)], hTg.res, self.bank[0])
            for k in range(8):
                self.cp("dve" if k % 2 else "pool", hTs.ap[:, k, :, 0:nb], hTg.ap[:, k, 0:8 * nb].rearrange("p (b t) -> p t b", t=8),
                        [hTg.res], [hTs.res])
            for tau in range(8):
                pb = self.bank[1 + (tau % 2)]
                for k in range(8):
                    self.mm(pb.ap[0:nb, :], hTs.ap[:, k, tau, 0:nb], wu.ap[:, k, :], k == 0, k == 7, [hTs.res, wu.res], [pb.res])
                self.cp("dve", ub2.ap[0:nb, :, tau, :], pb.ap[0:nb, :].rearrange("p (g h) -> p g h", h=16),
                        [pb.res], [ub2.res])
            for g8 in range(4 if not self.dbg.startswith("s5a1x") else 0):
                pb = self.bank[3 + (g8 % 2)]
                pbt = pb.ap.bitcast(BF16)
                for gl in range(8):
                    g = g8 * 8 + gl
                    self.tr(pbt[:, 128 * gl:128 * gl + 128], ub2.ap[:, g].rearrange("p a b -> p (a b)"), self.ident_b.ap,
                            [ub2.res, self.ident_b.res], [pb.res])
                for gl in range(8 if not self.dbg.startswith("s5a1y") else 0):
                    g = g8 * 8 + gl
                    self.cp("dve", U.ap[:, g, n0:n0 + nb], pbt[:, 128 * gl:128 * gl + nb], [pb.res], [Ur[g][gi]])
        if self.dbg.startswith("s5a1"):
            return
        P.barrier()
        A.top = u_top
        gbm = A.alloc(BF16, [5, 8, 512])
        gbr = [Res() for _ in range(5)]
        gbm_end = A.top
        Sx = [[A.alloc(F32, [NB]) for _ in range(2)] for _ in range(2)]
        SE = [[[A.alloc(BF16, [NB + 1]) for _ in range(2)] for _ in range(2)] for _ in range(2)]
        for par in range(2):
            for d in range(2):
                for ri in range(2):
                    self.memset("pool", SE[par][d][ri].ap, 0.0, [SE[par][d][ri].res])
        gtmp = [A.alloc(BF16, [128]) for _ in range(2)]
        nq = 16 if not self.dbg.startswith("s5q1") else 1
        for q in range(nq):
            par = q % 2
            for d in range(2):
                qd = 2 * q + d
                psV = [Tile(self.psall[:, 512:1056], Res()), Tile(self.psall[:, 1536:2080], Res())]
                vres = [[self.bank[1].res, self.bank[2].res], [self.bank[3].res, self.bank[4].res]]
                if d == 0:
                    splits = [(0, 0, 512), (512, 512, 32)]
                else:
                    splits = [(0, 32, 512), (512, 0, 32)]
                for ri in range(2):
                    for (oc, uc, n) in splits:
                        for r in range(2):
                            g = 2 * q + r
                            ur = Ur[g]
                            self.mm(psV[ri].ap[:, oc:oc + n], RTm[r].ap[:, qd, ri, :], U.ap[:, g, uc:uc + n],
                                    r == 0, r == 1, [RTm[r].res] + ur, vres[ri])
                cur = 0
                self.cp("dve", Sx[0][0].ap, psV[0].ap, vres[0], [Sx[0][0].res])
                self.cp("dve", Sx[0][1].ap, psV[1].ap, vres[1], [Sx[0][1].res])
                col = 16 * d + q
                for k in range(10):
                    dl = 1 << k
                    a_r = ASr.ap[:, k, col:col + 1]
                    a_i = ASi.ap[:, k, col:col + 1]
                    a_n = ASn.ap[:, k, col:col + 1]
                    o_re, o_im = Sx[cur][0], Sx[cur][1]
                    n_re, n_im = Sx[1 - cur][0], Sx[1 - cur][1]
                    if d == 0:
                        dst_s, src_s, same_s = slice(dl, NB), slice(0, NB - dl), slice(0, dl)
                    else:
                        dst_s, src_s, same_s = slice(0, NB - dl), slice(dl, NB), slice(NB - dl, NB)
                    rd = [o_re.res, o_im.res, ASr.res, ASi.res, ASn.res]
                    self.cp("act", n_re.ap[:, same_s], o_re.ap[:, same_s], [o_re.res], [n_re.res])
                    self.cp("act", n_im.ap[:, same_s], o_im.ap[:, same_s], [o_im.res], [n_im.res])
                    self.stt("dve", n_re.ap[:, dst_s], o_re.ap[:, src_s], a_r, o_re.ap[:, dst_s], ALU.mult, ALU.add, rd, [n_re.res])
                    self.stt("dve", n_re.ap[:, dst_s], o_im.ap[:, src_s], a_n, n_re.ap[:, dst_s], ALU.mult, ALU.add, rd + [n_re.res], [n_re.res])
                    self.stt("dve", n_im.ap[:, dst_s], o_im.ap[:, src_s], a_r, o_im.ap[:, dst_s], ALU.mult, ALU.add, rd, [n_im.res])
                    self.stt("dve", n_im.ap[:, dst_s], o_re.ap[:, src_s], a_i, n_im.ap[:, dst_s], ALU.mult, ALU.add, rd + [n_im.res], [n_im.res])
                    cur = 1 - cur
                off = 1 if d == 0 else 0
                for ri in range(2):
                    self.cp("act", SE[par][d][ri].ap[:, off:off + NB], Sx[cur][ri].ap, [Sx[cur][ri].res], [SE[par][d][ri].res])
            for r in range(2):
                g = 2 * q + r
                for ci, (n0, nb) in enumerate(groups):
                    pb = self.bank[5 + ((2 * q + r + ci) % 2)]
                    py = Tile(pb.ap[0:nb, 0:128], pb.res)
                    self.mm(py.ap, U.ap[:, g, n0:n0 + nb], Tm.ap[:, g, :], True, False, Ur[g] + [Tm.res], [py.res])
                    bidx = (n0 - 32) if n0 >= 32 else 512 + n0
                    for d, c0_ in ((0, n0), (1, bidx + 1)):
                        for ri in range(2):
                            last = (d == 1 and ri == 1)
                            self.mm(py.ap, SE[par][d][ri].ap[64 * r:64 * r + 64, c0_:c0_ + nb], self.Ob[d][ri].ap[64 * r:64 * r + 64, q, :],
                                    False, last, [SE[par][d][ri].res, self.Ob[d][ri].res], [py.res])
                    gt_ = gtmp[(2 * q + r + ci) % 2]
                    self.act(gt_.ap[0:nb, :], py.ap, AF.Gelu, [py.res], [gt_.res])
                    self.cp("dve", gbm.ap[0:nb, ci, :, 16 * g:16 * g + 16], gt_.ap[0:nb, :].rearrange("p (t h) -> p t h", t=8), [gt_.res], [gbr[ci]])
        if self.dbg.startswith("s5a3"):
            return
        P.barrier()
        A.top = keep_top
        gT = A.alloc(BF16, [4, L + LC])
        A.top = gbm_end
        wg = A.alloc(BF16, [4, 512])
        P.dma("pool", wg.ap, self.s5_w_glu.rearrange("(k p) n -> p k n", p=128), [], [wg.res])
        for ci, (n0, nb) in enumerate(groups):
            for tp in range(8):
                pb = self.bank[1 + (tp % 2)]
                pbt = pb.ap.bitcast(BF16)
                for kk in range(4):
                    self.tr(pbt[:, 128 * kk:128 * kk + nb], gbm.ap[0:nb, ci, tp, 128 * kk:128 * kk + 128], self.ident_b.ap[0:nb, 0:nb],
                            [gbr[ci], self.ident_b.res], [pb.res])
                dst = gT.ap[:, :, 8 * n0:8 * (n0 + nb)].rearrange("p k (b t) -> p k b t", t=8)[:, :, :, tp]
                src = pbt[:, 0:512].rearrange("p (k b) -> p k b", k=4)[:, :, 0:nb]
                self.cp("dve", dst, src, [pb.res], [gT.res])
        sg = [A.alloc(F32, [512]) for _ in range(2)]
        so = [A.alloc(BF16, [4, 512]) for _ in range(2)]
        NTOK = L + LC
        ti = 0
        for t0 in range(0, NTOK, 512):
            n = min(512, NTOK - t0)
            sot = so[ti % 2]
            for mcol in range(4):
                pb = self.bank[3 + (mcol % 2)]
                for kk in range(4):
                    self.mm(pb.ap[:, 0:n], wg.ap[:, kk, 128 * mcol:128 * (mcol + 1)], gT.ap[:, kk, t0:t0 + n], kk == 0, kk == 3,
                            [wg.res, gT.res], [pb.res])
                sgt = sg[mcol % 2]
                self.act(sgt.ap[:, 0:n], pb.ap[:, 0:n], AF.Sigmoid, [pb.res], [sgt.res])
                self.tt("dve", sot.ap[:, mcol, 0:n], sgt.ap[:, 0:n], gT.ap[:, mcol, t0:t0 + n], ALU.mult, [sgt.res, gT.res], [sot.res])
            P.dma("sp", self.s5T_out[:, t0:t0 + n].rearrange("(k p) t -> p k t", p=128), sot.ap[:, :, 0:n], [sot.res], [Res()])
            ti += 1

    def even_b(self, l):
        A = self.A
        P = self.P
        upd_ctx = l < 2
        P.barrier()
        A.reset()
        NT = L + LC
        NTILE = NT // 128
        AX = mybir.AxisListType.X
        win = A.alloc(BF16, [8, 1536])
        winr = [Res() for _ in range(8)]
        for k in range(8):
            P.dma("pool", win.ap[:, k, :], self.w_in[128 * k:128 * (k + 1), 512:2048], [], [winr[k]])
        wout = A.alloc(BF16, [8, 1024])
        P.dma("pool", wout.ap, self.w_out_even.rearrange("(k p) n -> p k n", p=128), [], [wout.res])
        kT = A.alloc(BF16, [4, NT])
        kTr = [Res() for _ in range(NTILE)]
        vt = A.alloc(BF16, [NTILE, 512])
        vtr = [Res() for _ in range(NTILE)]
        hT = A.alloc(BF16, [8, 128])
        hs = A.alloc(BF16, [1024])
        ht = [A.alloc(F32, [1024]) for _ in range(2)]
        B1_ap = self.modpp.ap[:, 0:8, :]

        def load_norm(t, i):
            if t < 2:
                h_ap, h_res, v = self.sctx.ap[:, t, :], self.sctx_res[t], 1
            else:
                h = ht[i % 2]
                P.dma("sp", h.ap, self.x[128 * (t - 2):128 * (t - 1), :], [], [h.res])
                h_ap, h_res, v = h.ap, h.res, 0
            self.prenorm_T(h_ap, h_res, self.A1, B1_ap, self.modpp.res, v, hs, hT.ap, hT.res, self.bank[0])
            return h_ap, h_res

        for t in range(NTILE):
            load_norm(t, t)
            pk = self.bank[1]
            for mc in range(4):
                for k in range(8):
                    self.mm(pk.ap[:, 128 * mc:128 * (mc + 1)], win.ap[:, k, 512 + 128 * mc:512 + 128 * (mc + 1)], hT.ap[:, k, :],
                            k == 0, k == 7, [winr[k], hT.res], [pk.res])
            self.cp("act", kT.ap[:, :, 128 * t:128 * (t + 1)], pk.ap.rearrange("p (a b) -> p a b", a=4), [pk.res], [kTr[t]])
            pv = self.bank[7]
            for k in range(8):
                self.mm(pv.ap, hT.ap[:, k, :], win.ap[:, k, 1024:1536], k == 0, k == 7, [winr[k], hT.res], [pv.res])
            self.cp("dve", vt.ap[:, t, :], pv.ap, [pv.res], [vtr[t]])
        Bd = self.nc.dram_tensor("Bd%d" % l, [8, 15, 64, 94], F32).ap()
        Bd_res = Res()
        Bc = A.alloc(F32, [8, 15, 64])
        top_b2 = A.top
        fill = A.alloc(F32, [15 * 94])
        self.memset("pool", fill.ap, -30000.0, [fill.res])
        for hd in range(8):
            P.dma("sp", Bd[hd].rearrange("a q k -> q a k"), fill.ap[0:64, :].rearrange("p (a k) -> p a k", a=15), [fill.res], [Bd_res])
        bt = Bd.tensor
        dst = bass.AP(bt, 0, [[15 * 64 * 94, 8], [64 * 94, 15], [95, 64], [1, 31]])
        rt = self.na_rpb.tensor
        srcp = bass.AP(rt, self.na_rpb.offset, [[465, 8], [31, 15], [0, 64], [1, 31]])
        P.dma("sp", dst, srcp, [Bd_res], [Bd_res])
        for half in range(2):
            P.dma("sp", Bc.ap[64 * half:64 * half + 64], Bd[:, :, :, 15:79].rearrange("h a q k -> q h a k"), [Bd_res], [Bc.res])
        ii = A.alloc(I32, [64])
        kcf = A.alloc(F32, [64])
        qi = A.alloc(I32, [1])
        qf = A.alloc(F32, [1])
        c0 = A.alloc(F32, [1])
        m1 = A.alloc(F32, [64])
        m2 = A.alloc(F32, [64])
        P.op("pool", lambda e: e.iota(ii.ap, [[1, 64]], base=0, channel_multiplier=0), [], [ii.res])
        self.cp("dve", kcf.ap, ii.ap, [ii.res], [kcf.res])
        P.op("pool", lambda e: e.iota(qi.ap, [[1, 1]], base=0, channel_multiplier=1), [], [qi.res])
        self.ts("dve", qi.ap, qi.ap, 63, None, ALU.bitwise_and, None, [qi.res], [qi.res])
        self.cp("dve", qf.ap, qi.ap, [qi.res], [qf.res])
        self.ts("dve", c0.ap, qf.ap, -8.0, 0.0, ALU.add, ALU.max, [qf.res], [c0.res])
        self.ts("dve", c0.ap, c0.ap, 48.0, None, ALU.min, None, [c0.res], [c0.res])
        self.ts("dve", m1.ap, kcf.ap, c0.ap, None, ALU.is_ge, None, [kcf.res, c0.res], [m1.res])
        self.ts("dve", m2.ap, kcf.ap, -16.0, c0.ap, ALU.add, ALU.is_lt, [kcf.res, c0.res], [m2.res])
        self.tt("dve", m1.ap, m1.ap, m2.ap, ALU.mult, [m1.res, m2.res], [m1.res])
        self.ts("dve", m1.ap, m1.ap, -1.0, 30000.0, ALU.add, ALU.mult, [m1.res], [m1.res])
        Bc2 = Bc.ap.rearrange("p h a k -> p (h a) k")
        self.tt("dve", Bc2, Bc2, m1.ap.unsqueeze(1).broadcast_to([128, 120, 64]), ALU.add, [Bc.res, m1.res], [Bc.res])
        P.barrier()
        A.top = top_b2
        qT = A.alloc(BF16, [4, 128])
        tSs = [A.alloc(F32, [768]) for _ in range(2)]
        Pts = [A.alloc(BF16, [896]) for _ in range(2)]
        PtTs = [A.alloc(BF16, [896])] * 2
        natok = A.alloc(BF16, [512])
        naT = A.alloc(BF16, [4, 128])
        s5t = [A.alloc(BF16, [4, 128]) for _ in range(2)]
        sm = A.alloc(F32, [8, 2])
        rinv = A.alloc(F32, [8])
        mx = A.alloc(F32, [8])
        nmx = A.alloc(F32, [8])
        tmp = A.alloc(F32, [1024])
        junk = hs
        mx_r = [Res() for _ in range(8)]
        nmx_r = [Res() for _ in range(8)]
        sm_r3 = [[Res() for _ in range(3)] for _ in range(8)]
        sm_r = [r_ for l3 in sm_r3 for r_ in l3]
        tS_r = [[Res() for _ in range(3)] for _ in range(2)]
        Pt_r = [[Res() for _ in range(3)] for _ in range(2)]
        psS = Tile(self.psall[:, 1024:2048], Res())
        psS_res = [self.bank[2].res, self.bank[3].res]
        psT = self.bank[6]
        psO = self.bank[7]
        units = [("lat", m) for m in range(32)] + ([("ctx", i) for i in range(2)] if upd_ctx else [])
        if self.dbg.startswith("na1"):
            units = units[:1] + units[5:6] + units[31:32] + units[32:]
        for ui, (kind, m) in enumerate(units):
            t = 2 + m if kind == "lat" else m
            h_ap, h_res = load_norm(t, ui)
            pq = self.bank[1]
            for mc in range(4):
                for k in range(8):
                    self.mm(pq.ap[:, 128 * mc:128 * (mc + 1)], win.ap[:, k, 128 * mc:128 * (mc + 1)], hT.ap[:, k, :],
                            k == 0, k == 7, [winr[k], hT.res], [pq.res])
            self.cp("act", qT.ap, pq.ap.rearrange("p (a b) -> p a b", a=4), [pq.res], [qT.res])
            if kind == "lat":
                rs = min(max(2 * m - 4, 0), 54)
                wt0 = 2 + rs // 2
                nwin = 640
            else:
                rs = 0
                wt0 = 0
                nwin = 0
            ncol = nwin + 256
            nchunk = ncol // 128
            wins_of = {}
            def stage_a1(hd):
                tS, Pt, PtT = tSs[hd % 2], Pts[hd % 2], PtTs[hd % 2]
                tSr, Ptr = tS_r[hd % 2], Pt_r[hd % 2]
                mxr, nmxr, smr = mx_r[hd], nmx_r[hd], sm_r3[hd]
                mc, po = hd // 2, 64 * (hd % 2)
                q_l = qT.ap[po:po + 64, mc, :]
                if kind == "lat":
                    kr = [kTr[wt0 + i] for i in range(5)]
                    self.mm(psS.ap[:, 0:512], q_l, kT.ap[po:po + 64, mc, 128 * wt0:128 * wt0 + 512], True, True,
                            [qT.res] + kr, psS_res)
                    self.mm(psS.ap[:, 512:640], q_l, kT.ap[po:po + 64, mc, 128 * wt0 + 512:128 * wt0 + 640], True, True,
                            [qT.res] + kr, psS_res)
                self.mm(psS.ap[:, nwin:nwin + 256], q_l, kT.ap[po:po + 64, mc, 0:256], True, True,
                        [qT.res, kTr[0], kTr[1]], psS_res)
                wins = []
                if kind == "lat":
                    for e in range(2):
                        qr = 2 * m + e
                        r0 = min(max(qr - 4, 0), 56)
                        j0 = r0 - rs
                        a0 = r0 - qr + 7
                        wins.append((e, j0))
                        self.stt("dve", tS.ap[64 * e:64 * e + 64, 0:512].rearrange("p (a k) -> p a k", a=8),
                                 psS.ap[64 * e:64 * e + 64, 64 * j0:64 * j0 + 512].rearrange("p (a k) -> p a k", a=8), 0.125,
                                 Bc.ap[64 * e:64 * e + 64, hd, a0:a0 + 8, :], ALU.mult, ALU.add,
                                 psS_res + [Bc.res], [tSr[e]])
                o0 = 512 if kind == "lat" else 0
                self.ts("dve", tS.ap[:, o0:o0 + 256], psS.ap[:, nwin:nwin + 256], 0.125, None, ALU.mult, None, psS_res, [tSr[2]])
                self.P.op("dve", lambda e, o_=mx.ap[:, hd:hd + 1], i_=tS.ap[:, 0:o0 + 256]: e.reduce_max(out=o_, in_=i_, axis=AX),
                          tSr, [mxr])
                self.ts("dve", nmx.ap[:, hd:hd + 1], mx.ap[:, hd:hd + 1], -1.0, None, ALU.mult, None, [mxr], [nmxr])
                self.memset("pool", sm.ap[:, hd, :], 0.0, smr)
                wins_of[hd] = (wins, o0)
            def stage_a2(hd):
                tS, Pt = tSs[hd % 2], Pts[hd % 2]
                tSr, Ptr = tS_r[hd % 2], Pt_r[hd % 2]
                nmxr, smr = nmx_r[hd], sm_r3[hd]
                wins, o0 = wins_of[hd]
                if kind == "lat":
                    self.memset("pool", Pt.ap[:, 0:640], 0.0, Ptr)
                for (e, j0) in wins:
                    self.act(Pt.ap[64 * e:64 * e + 64, 64 * j0:64 * j0 + 512], tS.ap[64 * e:64 * e + 64, 0:512], AF.Exp,
                             [tSr[e], nmxr, smr[e]], [Ptr[e], smr[e]], bias=nmx.ap[64 * e:64 * e + 64, hd:hd + 1], scale=1.0,
                             accum_out=sm.ap[64 * e:64 * e + 64, hd, 0:1])
                self.act(Pt.ap[:, nwin:nwin + 256], tS.ap[:, o0:o0 + 256], AF.Exp, [tSr[2], nmxr, smr[2]], [Ptr[2], smr[2]],
                         bias=nmx.ap[:, hd:hd + 1], scale=1.0, accum_out=sm.ap[:, hd, 1:2])
            def stage_b(hd):
                Pt, PtT = Pts[hd % 2], PtTs[hd % 2]
                Ptr = Pt_r[hd % 2]
                pst = psT.ap.bitcast(BF16)
                for c in range(nchunk):
                    self.tr(pst[:, 128 * c:128 * (c + 1)], Pt.ap[:, 128 * c:128 * (c + 1)], self.ident_b.ap,
                            Ptr + [self.ident_b.res], [psT.res])
                self.cp("act" if hd % 2 else "dve", PtT.ap[:, 0:ncol], pst[:, 0:ncol], [psT.res], [PtT.res])
                for c in range(nchunk):
                    if kind == "lat" and c < 5:
                        vtile = wt0 + c
                    else:
                        vtile = c - (5 if kind == "lat" else 0)
                    self.mm(psO.ap[:, 64 * hd:64 * hd + 64], PtT.ap[:, 128 * c:128 * (c + 1)], vt.ap[:, vtile, 64 * hd:64 * hd + 64],
                            c == 0, c == nchunk - 1, [PtT.res, vtr[vtile]], [psO.res])
            stage_a1(0)
            stage_a1(1)
            stage_a2(0)
            for hd in range(8):
                if hd + 2 < 8:
                    stage_a1(hd + 2)
                if hd + 1 < 8:
                    stage_a2(hd + 1)
                stage_b(hd)
            self.tt("dve", rinv.ap, sm.ap[:, :, 0], sm.ap[:, :, 1], ALU.add, sm_r, [rinv.res])
            self.P.op("dve", lambda e: e.reciprocal(out=rinv.ap, in_=rinv.ap), [rinv.res], [rinv.res])
            self.tt("dve", natok.ap.rearrange("p (h d) -> p h d", h=8), psO.ap.rearrange("p (h d) -> p h d", h=8),
                    rinv.ap.unsqueeze(2).broadcast_to([128, 8, 64]), ALU.mult, [psO.res, rinv.res], [natok.res])
            pst = psT.ap.bitcast(BF16)
            for c in range(4):
                self.tr(pst[:, 128 * c:128 * (c + 1)], natok.ap[:, 128 * c:128 * (c + 1)], self.ident_b.ap,
                        [natok.res, self.ident_b.res], [psT.res])
            self.cp("act", naT.ap, pst[:, 0:512].rearrange("p (a b) -> p a b", a=4), [psT.res], [naT.res])
            s5 = s5t[ui % 2]
            P.dma("sp", s5.ap, self.s5T_in[:, 128 * t:128 * (t + 1)].rearrange("(k p) t -> p k t", p=128), [], [s5.res])
            for half in range(2):
                pb = self.bank[4 + half]
                for k in range(8):
                    lhsT = s5.ap[:, k, :] if k < 4 else naT.ap[:, k - 4, :]
                    self.mm(pb.ap, lhsT, wout.ap[:, k, 512 * half:512 * (half + 1)], k == 0, k == 7,
                            [s5.res, naT.res, wout.res], [pb.res])
            po_ap = self.psall[:, 2048:3072]
            pres = [self.bank[4].res, self.bank[5].res]
            v = 0 if kind == "lat" else 1
            rstd = self.rstd_of(po_ap, pres, junk)
            self.stt("dve", tmp.ap, po_ap, rstd.ap, self.G[0][v].ap, ALU.mult, ALU.mult, pres + [rstd.res, self.G[0][v].res], [tmp.res])
            self.tt("pool", h_ap, tmp.ap, h_ap, ALU.add, [tmp.res, h_res], [h_res])
            if kind == "lat":
                P.dma("sp", self.out[128 * m:128 * (m + 1), :], h_ap, [h_res], [self.out_res[m]])


    def gen_tables(self):
        A = self.A
        P = self.P
        P.barrier()
        A.reset()
        W = 1024
        kf = A.alloc(F32, [4096])
        ki = A.alloc(I32, [4096])
        tcol = A.alloc(F32, [32])
        ti = A.alloc(I32, [32])
        P.op("pool", lambda e: e.iota(ki.ap, [[1, 4096]], base=0, channel_multiplier=0), [], [ki.res])
        self.cp("dve", kf.ap, ki.ap, [ki.res], [kf.res])
        P.op("pool", lambda e: e.iota(ti.ap, [[128, 32]], base=0, channel_multiplier=1), [], [ti.res])
        self.cp("dve", tcol.ap, ti.ap, [ti.res], [tcol.res])
        negpi = self.negpi
        t1 = [A.alloc(I32, [W]) for _ in range(2)]
        t2 = [A.alloc(I32, [W]) for _ in range(2)]
        t3 = [A.alloc(F32, [W]) for _ in range(2)]
        t4 = [A.alloc(F32, [W]) for _ in range(2)]
        ob = [A.alloc(BF16, [W]) for _ in range(4)]
        it = 0
        nchunks = 32 if not self.dbg.startswith("tab1") else 1
        sc = 2.0 * math.pi / 4096.0
        for c in range(nchunks):
            for kt in range(4096 // W if not (self.dbg.startswith("odd1") or self.dbg.startswith("tab1")) else 1):
                ai, ci, bf, cf = t1[it % 2], t2[it % 2], t3[it % 2], t4[it % 2]
                os_, oc = ob[(2 * it) % 4], ob[(2 * it + 1) % 4]
                self.ts("dve", ai.ap, kf.ap[:, kt * W:(kt + 1) * W], tcol.ap[:, c:c + 1], 2048.0, ALU.mult, ALU.add,
                        [kf.res, tcol.res], [ai.res])
                self.ts("dve", ci.ap, kf.ap[:, kt * W:(kt + 1) * W], tcol.ap[:, c:c + 1], 3072.0, ALU.mult, ALU.add,
                        [kf.res, tcol.res], [ci.res])
                self.ts("dve", ai.ap, ai.ap, 4095, None, ALU.bitwise_and, None, [ai.res], [ai.res])
                self.ts("dve", ci.ap, ci.ap, 4095, None, ALU.bitwise_and, None, [ci.res], [ci.res])
                self.cp("pool", bf.ap, ai.ap, [ai.res], [bf.res])
                self.cp("pool", cf.ap, ci.ap, [ci.res], [cf.res])
                self.act(os_.ap, bf.ap, AF.Sin, [bf.res, negpi.res], [os_.res], bias=negpi.ap, scale=sc)
                self.act(oc.ap, cf.ap, AF.Sin, [cf.res, negpi.res], [oc.res], bias=negpi.ap, scale=sc)
                P.dma("sp", self.Stab[128 * c:128 * (c + 1), kt * W:(kt + 1) * W], os_.ap, [os_.res], [self.tab_res])
                P.dma("sp", self.Ctab[128 * c:128 * (c + 1), kt * W:(kt + 1) * W], oc.ap, [oc.res], [self.tab_res])
                it += 1

    def gen_channel_tables(self, CD, SDn):
        A = self.A
        P = self.P
        kf = A.alloc(F32, [1024])
        ki = A.alloc(I32, [1024])
        tcol = A.alloc(F32, [8])
        ti = A.alloc(I32, [8])
        P.op("pool", lambda e: e.iota(ki.ap, [[1, 1024]], base=0, channel_multiplier=0), [], [ki.res])
        self.cp("dve", kf.ap, ki.ap, [ki.res], [kf.res])
        P.op("pool", lambda e: e.iota(ti.ap, [[128, 8]], base=0, channel_multiplier=1), [], [ti.res])
        self.cp("dve", tcol.ap, ti.ap, [ti.res], [tcol.res])
        ai = A.alloc(I32, [1024])
        ci = A.alloc(I32, [1024])
        b_ = A.alloc(F32, [1024])
        c3 = A.alloc(F32, [1024])
        negpi = self.negpi
        sc = 2.0 * math.pi / 1024.0
        for k in range(8):
            self.ts("dve", ai.ap, kf.ap, tcol.ap[:, k:k + 1], 512.0, ALU.mult, ALU.add, [kf.res, tcol.res], [ai.res])
            self.ts("dve", ci.ap, kf.ap, tcol.ap[:, k:k + 1], 768.0, ALU.mult, ALU.add, [kf.res, tcol.res], [ci.res])
            self.ts("dve", ai.ap, ai.ap, 1023, None, ALU.bitwise_and, None, [ai.res], [ai.res])
            self.ts("dve", ci.ap, ci.ap, 1023, None, ALU.bitwise_and, None, [ci.res], [ci.res])
            self.cp("pool", b_.ap, ai.ap, [ai.res], [b_.res])
            self.cp("pool", c3.ap, ci.ap, [ci.res], [c3.res])
            self.act(b_.ap, b_.ap, AF.Sin, [b_.res, negpi.res], [b_.res], bias=negpi.ap, scale=sc)
            self.act(c3.ap, c3.ap, AF.Sin, [c3.res, negpi.res], [c3.res], bias=negpi.ap, scale=sc)
            self.ts("dve", SDn.ap[:, k, :], b_.ap, -1.0 / 2048.0, None, ALU.mult, None, [b_.res], [SDn.res])
            self.ts("pool", CD.ap[:, k, :], c3.ap, 1.0 / 2048.0, None, ALU.mult, None, [c3.res], [CD.res])

    def odd_mixer(self, l):
        A = self.A
        P = self.P
        o = l // 2
        upd_ctx = l < 2
        P.barrier()
        A.reset()
        ycc = A.alloc(BF16, [8, 256])
        ysc = A.alloc(BF16, [8, 256])
        base2 = A.top
        hl = A.alloc(BF16, [32, 1024])
        hlr = [Res() for _ in range(32)]
        hc = A.alloc(BF16, [2, 1024])
        hcr = [Res() for _ in range(2)]
        nv = 2 if upd_ctx else 1
        with_tmp = A.top
        A1b = [A.alloc(F32, [1024]) for _ in range(nv)]
        B1b = [A.alloc(F32, [1024]) for _ in range(nv)]
        self.make_crep()
        wms = [A.alloc(F32, [8, 512]) for _ in range(2)]
        tb = A.alloc(F32, [512])
        tg = A.alloc(F32, [512])
        ones = A.alloc(F32, [512])
        self.memset("pool", ones.ap, 1.0, [ones.res])
        for nt in range(4):
            wm = wms[nt % 2]
            half = nt % 2
            P.dma("sp", wm.ap, self.w_mod[:, nt * 512:(nt + 1) * 512].rearrange("(k p) n -> p k n", p=128), [], [wm.res])
            self.load_bcast_row("sp", tb, self.b_mod[nt * 512:(nt + 1) * 512])
            if nt >= 2:
                self.load_bcast_row("sp", tg, self.norm_g[0, half * 512:(half + 1) * 512])
            for v in range(nv):
                psb = self.bank[5 + v]
                dst = (B1b if nt < 2 else A1b)[v]
                d_ap = dst.ap[:, half * 512:(half + 1) * 512]
                for k in range(8):
                    self.mm(psb.ap, self.crep[v].ap[:, k, :], wm.ap[:, k, :], k == 0, k == 7, [self.crep[v].res, wm.res], [psb.res])
                if nt < 2:
                    self.tt("dve", d_ap, psb.ap, tb.ap, ALU.add, [psb.res, tb.res], [dst.res])
                else:
                    self.stt("dve", d_ap, psb.ap, 1.0, tb.ap, ALU.add, ALU.add, [psb.res, tb.res], [dst.res])
                    self.tt("dve", d_ap, d_ap, tg.ap, ALU.mult, [dst.res, tg.res], [dst.res])
        ht = [A.alloc(F32, [1024]) for _ in range(2)]
        junk = A.alloc(BF16, [1024])
        tmp = A.alloc(F32, [1024])
        src = self.x
        for t in range(32 + (2 if upd_ctx else 0)):
            if t < 32:
                h = ht[t % 2]
                P.dma("sp", h.ap, src[128 * t:128 * (t + 1), :], [], [h.res])
                h_ap, h_res, v, d_ap, d_res = h.ap, h.res, 0, hl.ap[:, t, :], hlr[t]
            else:
                i = t - 32
                h_ap, h_res, v, d_ap, d_res = self.sctx.ap[:, i, :], self.sctx_res[i], 1, hc.ap[:, i, :], hcr[i]
            rstd = self.rstd_of(h_ap, [h_res], junk)
            self.stt("dve", tmp.ap, h_ap, rstd.ap, A1b[v].ap, ALU.mult, ALU.mult, [h_res, rstd.res, A1b[v].res], [tmp.res])
            self.tt("pool", d_ap, tmp.ap, B1b[v].ap, ALU.add, [tmp.res, B1b[v].res], [d_res])
        P.barrier()
        A.top = with_tmp
        NT = 256
        ctile = [A.alloc(BF16, [32, NT]) for _ in range(2)]
        stile = [A.alloc(BF16, [32, NT]) for _ in range(2)]
        yev = [A.alloc(BF16, [8, NT]) for _ in range(4)]
        nkt = 4096 // NT
        if self.dbg.startswith("odd1"):
            nkt = 1
        for kt in range(nkt):
            ct, stl = ctile[kt % 2], stile[kt % 2]
            P.dma("sp", ct.ap, self.Ctab[:, kt * NT:(kt + 1) * NT].rearrange("(c p) k -> p c k", p=128), [self.tab_res], [ct.res])
            P.dma("sp", stl.ap, self.Stab[:, kt * NT:(kt + 1) * NT].rearrange("(c p) k -> p c k", p=128), [self.tab_res], [stl.res])
            yc, ys = yev[(2 * kt) % 4], yev[(2 * kt + 1) % 4]
            for dc in range(8):
                for (tab, yo, bi) in ((ct, yc, 1), (stl, ys, 2)):
                    pb = self.bank[bi + 2 * (dc % 2)]
                    pj = Tile(pb.ap[:, 0:NT], pb.res)
                    for c in range(32):
                        self.mm(pj.ap, hl.ap[:, c, 128 * dc:128 * (dc + 1)], tab.ap[:, c, :], c == 0, c == 31,
                                [hlr[c], tab.res], [pj.res])
                    if bi == 1:
                        self.cp("act", yo.ap[:, dc, :], pj.ap, [pj.res], [yo.res])
                    else:
                        self.cp("dve", yo.ap[:, dc, :], pj.ap, [pj.res], [yo.res])
            P.dma("sp", self.Yc[:, kt * NT:(kt + 1) * NT].rearrange("(c p) k -> p c k", p=128), yc.ap, [yc.res], [self.Y_res[kt]])
            P.dma("sp", self.Ys[:, kt * NT:(kt + 1) * NT].rearrange("(c p) k -> p c k", p=128), ys.ap, [ys.res], [self.Y_res[kt]])
        if upd_ctx:
            ct, stl = ctile[nkt % 2], stile[nkt % 2]
            csrc = self.Ctab.rearrange("(t s) k -> t s k", s=16)[:, 0, 0:256].rearrange("(c p) k -> p c k", p=128)
            ssrc = self.Stab.rearrange("(t s) k -> t s k", s=16)[:, 0, 0:256].rearrange("(c p) k -> p c k", p=128)
            P.dma("sp", ct.ap[:, 0:2, :], csrc, [self.tab_res], [ct.res])
            P.dma("sp", stl.ap[:, 0:2, :], ssrc, [self.tab_res], [stl.res])
            for dc in range(8):
                for (tab, yo, bi) in ((ct, ycc, 1), (stl, ysc, 2)):
                    pb = self.bank[bi + 2 * (dc % 2)]
                    pj = Tile(pb.ap[:, 0:256], pb.res)
                    for c in range(2):
                        self.mm(pj.ap, hc.ap[:, c, 128 * dc:128 * (dc + 1)], tab.ap[:, c, :], c == 0, c == 1,
                                [hcr[c], tab.res], [pj.res])
                    self.ts("dve", yo.ap[:, dc, :], pj.ap, 4.0, None, ALU.mult, None, [pj.res], [yo.res])
        P.barrier()
        A.top = base2
        CD = A.alloc(BF16, [8, 1024])
        SDn = A.alloc(BF16, [8, 1024])
        wf = A.alloc(BF16, [8, 1024])
        P.dma("pool", wf.ap, self.w_fourier.rearrange("(k p) n -> p k n", p=128), [], [wf.res])
        yct = [A.alloc(BF16, [8, 256]) for _ in range(2)]
        yst = [A.alloc(BF16, [8, 256]) for _ in range(2)]
        hfT = [A.alloc(BF16, [8, 256]) for _ in range(2)]
        ht = [A.alloc(F32, [1024]) for _ in range(2)]
        junk = A.alloc(BF16, [1024])
        tmp = A.alloc(F32, [1024])
        ycc2, ysc2 = ycc, ysc
        mark = A.top
        self.gen_channel_tables(CD, SDn)
        A.top = mark
        units = [("lat", kt) for kt in range(nkt)] + ([("ctx", 0)] if upd_ctx else [])
        for ui, (kind, kt) in enumerate(units):
            v = 0 if kind == "lat" else 1
            if kind == "lat":
                yc, ys = yct[ui % 2], yst[ui % 2]
                P.dma("sp", yc.ap, self.Yc[:, kt * 256:(kt + 1) * 256].rearrange("(c p) k -> p c k", p=128), [self.Y_res[kt]], [yc.res])
                P.dma("sp", ys.ap, self.Ys[:, kt * 256:(kt + 1) * 256].rearrange("(c p) k -> p c k", p=128), [self.Y_res[kt]], [ys.res])
            else:
                yc, ys = ycc2, ysc2
            hf = hfT[ui % 2]
            for ec in range(8):
                pb = self.bank[1 + (ec % 2)]
                pj = Tile(pb.ap[:, 0:256], pb.res)
                for dc in range(8):
                    self.mm(pj.ap, CD.ap[:, dc, 128 * ec:128 * (ec + 1)], yc.ap[:, dc, :], dc == 0, False, [CD.res, yc.res], [pj.res])
                for dc in range(8):
                    self.mm(pj.ap, SDn.ap[:, dc, 128 * ec:128 * (ec + 1)], ys.ap[:, dc, :], False, dc == 7, [SDn.res, ys.res], [pj.res])
                self.cp("act" if ec % 2 else "dve", hf.ap[:, ec, :], pj.ap, [pj.res], [hf.res])
            for sub in range(2):
                for half in range(2):
                    pb = self.bank[3 + 2 * sub + half]
                    for ec in range(8):
                        self.mm(pb.ap, hf.ap[:, ec, sub * 128:(sub + 1) * 128], wf.ap[:, ec, half * 512:(half + 1) * 512],
                                ec == 0, ec == 7, [hf.res, wf.res], [pb.res])
                po_ap = self.psall[:, (3 + 2 * sub) * 512:(5 + 2 * sub) * 512]
                pres = [self.bank[3 + 2 * sub].res, self.bank[4 + 2 * sub].res]
                if kind == "lat":
                    t128 = kt * 2 + sub
                    h = ht[sub]
                    P.dma("sp", h.ap, self.x[128 * t128:128 * (t128 + 1), :], [], [h.res])
                    h_ap, h_res = h.ap, h.res
                else:
                    h_ap, h_res = self.sctx.ap[:, sub, :], self.sctx_res[sub]
                rstd = self.rstd_of(po_ap, pres, junk)
                self.stt("dve", tmp.ap, po_ap, rstd.ap, self.G[0][v].ap, ALU.mult, ALU.mult,
                         pres + [rstd.res, self.G[0][v].res], [tmp.res])
                self.tt("pool", h_ap, tmp.ap, h_ap, ALU.add, [tmp.res, h_res], [h_res])
                if kind == "lat":
                    P.dma("sp", self.out[128 * t128:128 * (t128 + 1), :], h_ap, [h_res], [self.out_res[t128]])


_CACHE = {}


def _get_nc(step, dbg=""):
    key = (step, dbg)
    if key not in _CACHE:
        _CACHE[key] = Builder(step, dbg).build()
    return _CACHE[key]


def _launch(step, per_core, shared, dbg=""):
    nc = _get_nc(step, dbg)
    in_maps = []
    for b in range(len(per_core)):
        m = dict(shared)
        m.update(per_core[b])
        in_maps.append(m)
    res = run_bass_kernel_spmd(nc, in_maps, core_ids=list(range(len(per_core))))
    return res.results


def _f32(a):
    return np.ascontiguousarray(a, dtype=np.float32)


def run_steps(inputs, steps, ncores=8, dbg=""):
    h = [_f32(inputs["x"][b]) for b in range(ncores)]
    s = [_f32(inputs["ctx"][b]) for b in range(ncores)]
    cs = [_f32(inputs["c"][b]) for b in range(ncores)]
    s5T = None
    for step in steps:
        kind, l = step
        e = l // 2
        shared = {"c_ctx": _f32(inputs["c_ctx"]), "w_mod": _f32(inputs["w_mod"][l]), "b_mod": _f32(inputs["b_mod"][l]),
                  "norm_g": _f32(inputs["norm_g"][l])}
        if kind == "mlp":
            shared["w_ff1"] = _f32(inputs["w_ff1"][l])
            shared["w_ff2"] = _f32(inputs["w_ff2"][l])
        elif kind == "odd":
            shared["w_fourier"] = _f32(inputs["w_fourier"][e])
        elif kind == "evenA":
            shared["w_in"] = _f32(inputs["w_in"][e])
            for n in ["s5_lam_re", "s5_lam_im", "s5_log_dt", "s5_b_re", "s5_b_im", "s5_c_re", "s5_c_im", "s5_d", "s5_w_glu"]:
                shared[n] = _f32(inputs[n][e])
        elif kind == "evenB":
            shared["w_in"] = _f32(inputs["w_in"][e])
            shared["w_out_even"] = _f32(inputs["w_out_even"][e])
            shared["na_rpb"] = _f32(inputs["na_rpb"][e])
        per_core = []
        for b in range(ncores):
            m = {"x": h[b], "c": cs[b], "ctx": s[b]}
            if kind == "evenB":
                m["s5T"] = s5T[b]
            per_core.append(m)
        res = _launch(step, per_core, shared, dbg)
        if kind == "evenA":
            s5T = [np.ascontiguousarray(r["s5T"]) for r in res]
        else:
            h = [_f32(r["out"]) for r in res]
            s = [_f32(r["sout"]) for r in res]
    return h, s, s5T


def kernel_multi(**inputs):
    steps = []
    for l in range(DEPTH):
        if l % 2 == 0:
            steps += [("evenA", l), ("evenB", l)]
        else:
            steps += [("odd", l)]
        steps += [("mlp", l)]
    h, s, _ = run_steps(inputs, steps, 8)
    return np.stack(h, 0)


def kernel(**inputs):
    nc = _get_nc(("fused", None))
    names = ["c_ctx", "w_mod", "b_mod", "norm_g", "w_in", "w_out_even", "s5_lam_re", "s5_lam_im", "s5_log_dt", "s5_b_re", "s5_b_im",
             "s5_c_re", "s5_c_im", "s5_d", "s5_w_glu", "na_rpb", "w_fourier", "w_ff1", "w_ff2"]
    shared = {n: _f32(inputs[n]) for n in names}
    ncores = 8
    in_maps = []
    for b in range(ncores):
        m = dict(shared)
        m["x"] = _f32(inputs["x"][b])
        m["c"] = _f32(inputs["c"][b])
        m["ctx"] = _f32(inputs["ctx"][b])
        in_maps.append(m)
    res = run_bass_kernel_spmd(nc, in_maps, core_ids=list(range(ncores)))
    return np.stack([_f32(r["out"]) for r in res.results], 0)
```

```python
import contextlib
import math
import os

import numpy as np
import concourse.bass as bass
import concourse.mybir as mybir
from concourse.bass_utils import run_bass_kernel_spmd

F32 = mybir.dt.float32
BF16 = mybir.dt.bfloat16
I32 = mybir.dt.int32
AF = mybir.ActivationFunctionType
ALU = mybir.AluOpType

D = 1024
L = 4096
LC = 256
DFF = 4096
DEPTH = 4
EPS = 1e-6
ENGS = ["pe", "act", "dve", "pool", "sp"]


class Res:
    __slots__ = ("name", "lastw", "readers")

    def __init__(self, name="r"):
        self.name = name
        self.lastw = None
        self.readers = []


class Op:
    __slots__ = ("eng", "fn", "deps", "is_dma", "signal", "sigval", "dslot", "dval")


class Prog:
    NDSLOT = 8

    def __init__(self, nc):
        self.nc = nc
        self.ops = []
        self.per = {e: [] for e in ENGS}
        self.ndma = {e: 0 for e in ENGS}
        self.pending_barrier = {e: None for e in ENGS}
        self.dmas_since_barrier = []

    def barrier(self):
        deps = []
        for e in ENGS:
            for op in reversed(self.per[e]):
                if not op.is_dma:
                    deps.append(op)
                    break
        deps.extend(self.dmas_since_barrier)
        self.dmas_since_barrier = []
        for e in ENGS:
            old = self.pending_barrier[e]
            self.pending_barrier[e] = (old or []) + deps

    def _add(self, eng, fn, reads, writes, is_dma):
        op = Op()
        op.eng = eng
        op.fn = fn
        op.is_dma = is_dma
        op.signal = False
        deps = set()
        for r in reads:
            if r.lastw is not None:
                deps.add(r.lastw)
        for w in writes:
            if w.lastw is not None:
                deps.add(w.lastw)
            for rd in w.readers:
                deps.add(rd)
        if self.pending_barrier[eng] is not None:
            deps.update(self.pending_barrier[eng])
            self.pending_barrier[eng] = None
        deps.discard(op)
        op.deps = [d for d in deps if not (eng == "pe" and d.eng == "pe" and not d.is_dma and not is_dma)]
        for r in reads:
            r.readers.append(op)
        for w in writes:
            w.lastw = op
            w.readers = []
        self.per[eng].append(op)
        self.ops.append(op)
        if is_dma:
            k = self.ndma[eng]
            self.ndma[eng] += 1
            op.dslot = k % self.NDSLOT
            op.dval = 16 * (k // self.NDSLOT + 1)
            self.dmas_since_barrier.append(op)
        return op

    def op(self, eng, fn, reads=(), writes=()):
        return self._add(eng, fn, list(reads), list(writes), False)

    def dma(self, eng, out, in_, reads=(), writes=()):
        return self._add(eng, lambda e: e.dma_start(out=out, in_=in_), list(reads), list(writes), True)

    def emit(self):
        nc = self.nc
        for op in self.ops:
            for d in op.deps:
                if not d.is_dma:
                    d.signal = True
        for e in ENGS:
            cnt = 0
            for op in self.per[e]:
                if not op.is_dma and op.signal:
                    cnt += 1
                    op.sigval = cnt
        with contextlib.ExitStack() as st:
            sems = {e: st.enter_context(nc.semaphore("s_" + e)) for e in ENGS}
            dsems = {e: [st.enter_context(nc.semaphore("d_%s%d" % (e, i))) for i in range(self.NDSLOT)]
                     for e in ENGS if self.ndma[e] > 0}
            block = st.enter_context(nc.Block())

            def run_engine(ename, eng):
                seen = {}
                dseen = {}
                for op in self.per[ename]:
                    for d in op.deps:
                        if d.is_dma:
                            key = (d.eng, d.dslot)
                            if dseen.get(key, 0) < d.dval:
                                eng.wait_ge(dsems[d.eng][d.dslot], d.dval)
                                dseen[key] = d.dval
                        else:
                            if seen.get(d.eng, 0) < d.sigval:
                                eng.wait_ge(sems[d.eng], d.sigval)
                                seen[d.eng] = d.sigval
                    if op.is_dma:
                        key = (ename, op.dslot)
                        if op.dval > 16 and dseen.get(key, 0) < op.dval - 16:
                            eng.wait_ge(dsems[ename][op.dslot], op.dval - 16)
                            dseen[key] = op.dval - 16
                        ins = op.fn(eng)
                        ins.then_inc(dsems[ename][op.dslot], 16)
                    else:
                        ins = op.fn(eng)
                        if op.signal:
                            ins.then_inc(sems[ename], 1)
                if ename in dsems:
                    k = self.ndma[ename]
                    for s in range(self.NDSLOT):
                        n = (k - s + self.NDSLOT - 1) // self.NDSLOT
                        if n > 0 and dseen.get((ename, s), 0) < 16 * n:
                            eng.wait_ge(dsems[ename][s], 16 * n)

            block.tensor(lambda e: run_engine("pe", e))
            block.scalar(lambda e: run_engine("act", e))
            block.vector(lambda e: run_engine("dve", e))
            block.gpsimd(lambda e: run_engine("pool", e))
            block.sync(lambda e: run_engine("sp", e))


class Tile:
    __slots__ = ("ap", "res")

    def __init__(self, ap, res=None):
        self.ap = ap
        self.res = res or Res()

    def __getitem__(self, k):
        return self.ap[k]


class Arena:
    def __init__(self, tensor, ncols):
        self.t = tensor
        self.ncols = ncols
        self.top = 0
        self.base = 0

    def alloc(self, dtype, shape):
        n = 1
        for s in shape:
            n *= s
        units = n * (2 if dtype in (F32, I32) else 1)
        units = (units + 31) // 32 * 32
        assert self.top + units <= self.ncols, ("arena overflow", self.top, units, self.ncols)
        ap = self.t[:, self.top:self.top + n * (2 if dtype in (F32, I32) else 1)]
        self.top += units
        self.maxtop = max(getattr(self, "maxtop", 0), self.top)
        if dtype != BF16:
            ap = ap.bitcast(dtype)
        if len(shape) == 2:
            ap = ap.rearrange("p (a b) -> p a b", a=shape[0])
        elif len(shape) == 3:
            ap = ap.rearrange("p (a b c) -> p a b c", a=shape[0], b=shape[1])
        return Tile(ap)

    def mark_persistent(self):
        self.base = self.top

    def reset(self):
        self.top = self.base


class Builder:
    def __init__(self, step, dbg=""):
        self.dbg = dbg
        self.step = step
        kind, l = step
        nc = bass.Bass("TRN2", target_bir_lowering=False)
        self.nc = nc
        self.P = Prog(nc)

        def din(name, shape, dt=F32):
            return nc.dram_tensor(name, list(shape), dt, kind="ExternalInput").ap()

        if kind == "fused":
            self.init_fused(din)
            return
        self.x = din("x", [L, D])
        self.c = din("c", [D])
        self.ctx = din("ctx", [LC, D])
        self.c_ctx = din("c_ctx", [D])
        self.w_mod = din("w_mod", [D, 6 * D])
        self.b_mod = din("b_mod", [6 * D])
        self.norm_g = din("norm_g", [4, D])
        if kind == "mlp":
            self.w_ff1 = din("w_ff1", [D, DFF])
            self.w_ff2 = din("w_ff2", [DFF, D])
        if kind == "odd":
            self.w_fourier = din("w_fourier", [D, D])
        if kind in ("evenA", "evenB"):
            self.w_in = din("w_in", [D, 2048])
        if kind == "evenA":
            self.s5_lam_re = din("s5_lam_re", [2, 32, 64])
            self.s5_lam_im = din("s5_lam_im", [2, 32, 64])
            self.s5_log_dt = din("s5_log_dt", [2, 32])
            self.s5_b_re = din("s5_b_re", [2, 32, 64, 16])
            self.s5_b_im = din("s5_b_im", [2, 32, 64, 16])
            self.s5_c_re = din("s5_c_re", [2, 32, 16, 64])
            self.s5_c_im = din("s5_c_im", [2, 32, 16, 64])
            self.s5_d = din("s5_d", [512])
            self.s5_w_glu = din("s5_w_glu", [512, 512])
            self.s5T_out = nc.dram_tensor("s5T", [512, L + LC], BF16, kind="ExternalOutput").ap()
        if kind == "evenB":
            self.w_out_even = din("w_out_even", [D, D])
            self.na_rpb = din("na_rpb", [8, 15, 31])
            self.s5T_in = din("s5T", [512, L + LC], BF16)
        if kind != "evenA":
            self.out = nc.dram_tensor("out", [L, D], F32, kind="ExternalOutput").ap()
            self.sout = nc.dram_tensor("sout", [LC, D], F32, kind="ExternalOutput").ap()
        self.out_res = [Res("out%d" % i) for i in range(L // 128)]
        if kind == "odd":
            self.Ctab = nc.dram_tensor("Ctab", [L, L], BF16).ap()
            self.Stab = nc.dram_tensor("Stab", [L, L], BF16).ap()
            self.tab_res = Res("tab")
            self.Yc = nc.dram_tensor("Yc", [D, L], BF16).ap()
            self.Ys = nc.dram_tensor("Ys", [D, L], BF16).ap()
            self.Y_res = [Res("Y%d" % i) for i in range(16)]

    def init_fused(self, din):
        nc = self.nc
        self.x_in = din("x", [L, D])
        self.c = din("c", [D])
        self.ctx = din("ctx", [LC, D])
        self.c_ctx = din("c_ctx", [D])
        W = {}
        W["w_mod"] = din("w_mod", [DEPTH, D, 6 * D])
        W["b_mod"] = din("b_mod", [DEPTH, 6 * D])
        W["norm_g"] = din("norm_g", [DEPTH, 4, D])
        W["w_in"] = din("w_in", [2, D, 2048])
        W["w_out_even"] = din("w_out_even", [2, D, D])
        W["s5_lam_re"] = din("s5_lam_re", [2, 2, 32, 64])
        W["s5_lam_im"] = din("s5_lam_im", [2, 2, 32, 64])
        W["s5_log_dt"] = din("s5_log_dt", [2, 2, 32])
        W["s5_b_re"] = din("s5_b_re", [2, 2, 32, 64, 16])
        W["s5_b_im"] = din("s5_b_im", [2, 2, 32, 64, 16])
        W["s5_c_re"] = din("s5_c_re", [2, 2, 32, 16, 64])
        W["s5_c_im"] = din("s5_c_im", [2, 2, 32, 16, 64])
        W["s5_d"] = din("s5_d", [2, 512])
        W["s5_w_glu"] = din("s5_w_glu", [2, 512, 512])
        W["na_rpb"] = din("na_rpb", [2, 8, 15, 31])
        W["w_fourier"] = din("w_fourier", [2, D, D])
        W["w_ff1"] = din("w_ff1", [DEPTH, D, DFF])
        W["w_ff2"] = din("w_ff2", [DEPTH, DFF, D])
        self.W = W
        self.out_final = nc.dram_tensor("out", [L, D], F32, kind="ExternalOutput").ap()
        self.hA = nc.dram_tensor("hA", [L, D], F32).ap()
        s5 = nc.dram_tensor("s5T", [512, L + LC], BF16).ap()
        self.s5T_out = s5
        self.s5T_in = s5
        self.out_res = [Res("out%d" % i) for i in range(L // 128)]
        self.Ctab = nc.dram_tensor("Ctab", [L, L], BF16).ap()
        self.Stab = nc.dram_tensor("Stab", [L, L], BF16).ap()
        self.tab_res = Res("tab")
        self.Yc = nc.dram_tensor("Yc", [D, L], BF16).ap()
        self.Ys = nc.dram_tensor("Ys", [D, L], BF16).ap()
        self.Y_res = [Res("Y%d" % i) for i in range(16)]

    def set_layer(self, l):
        W = self.W
        e = l // 2
        self.w_mod = W["w_mod"][l]
        self.b_mod = W["b_mod"][l]
        self.norm_g = W["norm_g"][l]
        self.w_ff1 = W["w_ff1"][l]
        self.w_ff2 = W["w_ff2"][l]
        if l % 2 == 0:
            self.w_in = W["w_in"][e]
            self.w_out_even = W["w_out_even"][e]
            for n in ["s5_lam_re", "s5_lam_im", "s5_log_dt", "s5_b_re", "s5_b_im", "s5_c_re", "s5_c_im", "s5_d", "s5_w_glu", "na_rpb"]:
                setattr(self, n, W[n][e])
        else:
            self.w_fourier = W["w_fourier"][e]

    def build_fused(self):
        P = self.P
        self.setup_consts()
        bufs = [self.x_in, self.hA, self.out_final]
        step = 0
        for l in range(DEPTH):
            self.set_layer(l)
            self.mod_phase(l)
            self.x = bufs[0] if step == 0 else (self.out_final if step % 2 == 0 else self.hA)
            self.out = self.hA if step % 2 == 0 else self.out_final
            if l % 2 == 0:
                self.even_a(l)
                self.even_b(l)
            else:
                if l == 1:
                    self.gen_tables()
                self.odd_mixer(l)
            step += 1
            self.x = self.out_final if step % 2 == 0 else self.hA
            self.out = self.hA if step % 2 == 0 else self.out_final
            self.mlp_phase(l)
            step += 1

    def mm(self, out, lhsT, rhs, start, stop, reads, writes):
        self.P.op("pe", lambda e: e.matmul(out, lhsT=lhsT, rhs=rhs, start=start, stop=stop), reads, writes)

    def tr(self, out, in_, ident, reads, writes):
        self.P.op("pe", lambda e: e.transpose(out, in_, ident), reads, writes)

    def act(self, out, in_, func, reads, writes, bias=None, scale=None, accum_out=None, eng="act"):
        kw = {}
        if bias is not None:
            kw["bias"] = bias
        if scale is not None:
            kw["scale"] = scale
        if accum_out is not None:
            kw["accum_out"] = accum_out
        self.P.op(eng, lambda e: e.activation(out=out, in_=in_, func=func, **kw), reads, writes)

    def tt(self, eng, out, in0, in1, op, reads, writes):
        self.P.op(eng, lambda e: e.tensor_tensor(out=out, in0=in0, in1=in1, op=op), reads, writes)

    def ts(self, eng, out, in0, s1, s2, op0, op1, reads, writes):
        if op1 is None:
            self.P.op(eng, lambda e: e.tensor_single_scalar(out=out, in_=in0, scalar=s1, op=op0), reads, writes)
        else:
            self.P.op(eng, lambda e: e.tensor_scalar(out=out, in0=in0, scalar1=s1, scalar2=s2, op0=op0, op1=op1), reads, writes)

    def stt(self, eng, out, in0, scalar, in1, op0, op1, reads, writes):
        self.P.op(eng, lambda e: e.scalar_tensor_tensor(out=out, in0=in0, scalar=scalar, in1=in1, op0=op0, op1=op1), reads, writes)

    def cp(self, eng, out, in_, reads, writes):
        if eng == "act":
            self.P.op(eng, lambda e: e.copy(out=out, in_=in_), reads, writes)
        else:
            self.P.op(eng, lambda e: e.tensor_copy(out=out, in_=in_), reads, writes)

    def memset(self, eng, ap, val, writes):
        self.P.op(eng, lambda e: e.memset(ap, val), [], writes)

    def build(self):
        nc = self.nc
        P = self.P
        with contextlib.ExitStack() as st:
            NCOLS = 106400
            arena_t = st.enter_context(nc.sbuf_tensor("arena", [128, NCOLS], BF16))
            self.A = Arena(arena_t, NCOLS)
            psall = st.enter_context(nc.psum_tensor("psall", [128, 4096], F32))
            self.psall = psall
            self.bank = [Tile(psall[:, 512 * i:512 * (i + 1)], Res("bank%d" % i)) for i in range(8)]
            kind, l = self.step
            if kind == "fused":
                self.build_fused()
                P.emit()
                return nc
            self.setup_consts()
            self.mod_phase(l)
            if kind == "mlp":
                self.mlp_phase(l)
            elif kind == "odd":
                self.gen_tables()
                self.odd_mixer(l)
            elif kind == "evenA":
                self.even_a(l)
            elif kind == "evenB":
                self.even_b(l)
            if kind != "evenA":
                P.barrier()
                for i in range(2):
                    P.dma("sp", self.sout[128 * i:128 * (i + 1), :], self.sctx.ap[:, i, :], [self.sctx_res[i]], [Res()])
            P.emit()
        return nc

    def setup_consts(self):
        A = self.A
        P = self.P
        it = A.alloc(I32, [128])
        self.ident_f = A.alloc(F32, [128])
        self.ident_b = A.alloc(BF16, [128])
        P.op("pool", lambda e: e.iota(it.ap, [[1, 128]], base=0, channel_multiplier=-1), [], [it.res])
        self.cp("dve", self.ident_f.ap, it.ap, [it.res], [self.ident_f.res])
        self.ts("dve", self.ident_f.ap, self.ident_f.ap, 0.0, None, ALU.is_equal, None, [self.ident_f.res], [self.ident_f.res])
        self.cp("dve", self.ident_b.ap, self.ident_f.ap, [self.ident_f.res], [self.ident_b.res])
        self.cact2 = A.alloc(F32, [8, 2])
        craw = A.alloc(F32, [2, 128])
        P.dma("sp", craw.ap[0:8, 0, :], self.c.rearrange("(k p) -> k p", p=128), [], [craw.res])
        P.dma("sp", craw.ap[0:8, 1, :], self.c_ctx.rearrange("(k p) -> k p", p=128), [], [craw.res])
        for v in range(2):
            pv = Tile(self.bank[7].ap[:, 8 * v:8 * v + 8], self.bank[7].res)
            self.tr(pv.ap, craw.ap[0:8, v, :], self.ident_f.ap[0:8, 0:8], [craw.res, self.ident_f.res], [pv.res])
            self.act(self.cact2.ap[:, :, v], pv.ap, AF.Silu, [pv.res], [self.cact2.res])
        self.modpp = A.alloc(F32, [48, 2])
        self.gpp = A.alloc(F32, [4, 8])
        self.A1 = A.alloc(F32, [8, 2])
        self.A2 = A.alloc(F32, [8, 2])
        self.G = [[A.alloc(F32, [1024]) for v in range(2)] for i in range(2)]
        self.sctx = A.alloc(F32, [2, 1024])
        self.sctx_res = [Res("sctx0"), Res("sctx1")]
        for i in range(2):
            P.dma("sp", self.sctx.ap[:, i, :], self.ctx[128 * i:128 * (i + 1), :], [], [self.sctx_res[i]])
        self.negpi = A.alloc(F32, [1])
        self.memset("pool", self.negpi.ap, -math.pi, [self.negpi.res])
        self.small = A.alloc(F32, [64])
        self.small_tiles = [Tile(self.small.ap[:, i:i + 1]) for i in range(64)]
        self.small_n = 0
        A.mark_persistent()

    def make_crep(self):
        A = self.A
        self.crep = [A.alloc(F32, [8, 128]) for _ in range(2)]
        ones = A.alloc(F32, [128])
        self.memset("pool", ones.ap, 1.0, [ones.res])
        for v in range(2):
            for k in range(8):
                self.ts("dve", self.crep[v].ap[:, k, :], ones.ap, self.cact2.ap[:, k, v:v + 1], None, ALU.mult, None,
                        [ones.res, self.cact2.res], [self.crep[v].res])

    def scalar_slot(self):
        i = self.small_n % 64
        self.small_n += 1
        return self.small_tiles[i]

    def bcast_tile(self, l, ntile, v, wm, dst_ap, dst_res, gain_idx, tmpb, tmpg, psb):
        P = self.P
        for k in range(8):
            self.mm(psb.ap, self.crep[v].ap[:, k, :], wm.ap[:, k, :], k == 0, k == 7, [self.crep[v].res, wm.res], [psb.res])
        if gain_idx is None:
            self.tt("dve", dst_ap, psb.ap, tmpb.ap, ALU.add, [psb.res, tmpb.res], [dst_res])
        else:
            self.tt("dve", dst_ap, psb.ap, tmpb.ap, ALU.add, [psb.res, tmpb.res], [dst_res])
            self.tt("dve", dst_ap, dst_ap, tmpg.ap, ALU.mult, [dst_res, tmpg.res], [dst_res])

    def load_bcast_row(self, eng, tile, src_row):
        self.P.dma(eng, tile.ap, src_row.partition_broadcast(128), [], [tile.res])

    def mod_phase(self, l):
        A = self.A
        P = self.P
        P.barrier()
        A.reset()
        self.make_crep()
        wms = [A.alloc(F32, [8, 512]) for _ in range(2)]
        bpp = A.alloc(F32, [48])
        tmpb = [A.alloc(F32, [512]) for _ in range(2)]
        tmpg = [A.alloc(F32, [512]) for _ in range(2)]
        braw = A.alloc(F32, [128])
        graw = A.alloc(F32, [128])
        P.dma("sp", braw.ap[0:48, :], self.b_mod.rearrange("(j p) -> j p", p=128), [], [braw.res])
        P.dma("sp", graw.ap[0:32, :], self.norm_g.rearrange("g (k p) -> (g k) p", p=128), [], [graw.res])
        pb_ = Tile(self.bank[7].ap[:, 64:112], self.bank[7].res)
        self.tr(pb_.ap, braw.ap[0:48, :], self.ident_f.ap[0:48, 0:48], [braw.res, self.ident_f.res], [pb_.res])
        self.cp("dve", bpp.ap, pb_.ap, [pb_.res], [bpp.res])
        pg_ = Tile(self.bank[7].ap[:, 128:160], self.bank[7].res)
        self.tr(pg_.ap, graw.ap[0:32, :], self.ident_f.ap[0:32, 0:32], [graw.res, self.ident_f.res], [pg_.res])
        self.cp("dve", self.gpp.ap, pg_.ap.rearrange("p (g k) -> p g k", g=4), [pg_.res], [self.gpp.res])
        psA = Tile(self.bank[7].ap[:, 0:8], self.bank[7].res)
        for nt in range(12):
            wm = wms[nt % 2]
            P.dma("sp", wm.ap, self.w_mod[:, nt * 512:(nt + 1) * 512].rearrange("(k p) n -> p k n", p=128), [], [wm.res])
            for jj in range(4):
                for k in range(8):
                    self.mm(psA.ap[:, 2 * jj:2 * jj + 2], wm.ap[:, k, jj * 128:(jj + 1) * 128], self.cact2.ap[:, k, :],
                            k == 0, k == 7, [wm.res, self.cact2.res], [psA.res])
            self.tt("dve", self.modpp.ap[:, nt * 4:(nt + 1) * 4, :], psA.ap.rearrange("p (a b) -> p a b", b=2),
                    bpp.ap[:, nt * 4:(nt + 1) * 4].unsqueeze(2).broadcast_to([128, 4, 2]), ALU.add,
                    [psA.res, bpp.res], [self.modpp.res])
            gi = {4: (0, 0), 5: (0, 1), 10: (1, 0), 11: (1, 1)}.get(nt)
            if gi is not None:
                i, half = gi
                tb = tmpb[half]
                tg = tmpg[half]
                self.load_bcast_row("sp", tb, self.b_mod[nt * 512:(nt + 1) * 512])
                self.load_bcast_row("sp", tg, self.norm_g[1 + 2 * i, half * 512:(half + 1) * 512])
                for v in range(2):
                    psb = self.bank[5 + v]
                    self.bcast_tile(l, nt, v, wm, self.G[i][v].ap[:, half * 512:(half + 1) * 512], self.G[i][v].res, 1, tb, tg, psb)
        for (Ax, sc_off, gidx) in ((self.A1, 8, 0), (self.A2, 32, 2)):
            self.stt("dve", Ax.ap, self.modpp.ap[:, sc_off:sc_off + 8, :], 1.0,
                     self.gpp.ap[:, gidx, :].unsqueeze(2).broadcast_to([128, 8, 2]), ALU.add, ALU.mult,
                     [self.modpp.res, self.gpp.res], [Ax.res])

    def rstd_of(self, src_ap, src_reads, junk, n=1024):
        ss = self.scalar_slot()
        self.memset("pool", ss.ap, 0.0, [ss.res])
        self.act(junk.ap, src_ap, AF.Square, src_reads + [ss.res], [junk.res, ss.res], accum_out=ss.ap)
        self.act(ss.ap, ss.ap, AF.Sqrt, [ss.res], [ss.res], bias=EPS, scale=1.0 / n)
        self.P.op("dve", lambda e: e.reciprocal(out=ss.ap, in_=ss.ap), [ss.res], [ss.res])
        return ss

    def prenorm_a(self, h_ap, h_res, hs):
        rstd = self.rstd_of(h_ap, [h_res], hs)
        self.act(hs.ap, h_ap, AF.Copy, [h_res, rstd.res], [hs.res], scale=rstd.ap)

    def prenorm_b(self, Ax, Bx_ap, Bx_res, v, hs, dstT_ap, dstT_res, psT):
        pst = psT.ap.bitcast(BF16)
        for k in range(8):
            self.tr(pst[:, k * 128:(k + 1) * 128], hs.ap[:, k * 128:(k + 1) * 128], self.ident_b.ap,
                    [hs.res, self.ident_b.res], [psT.res])
        p3 = pst.rearrange("p (a b) -> p a b", a=8)
        self.tt("dve", dstT_ap, p3, Ax.ap[:, :, v].unsqueeze(2).broadcast_to([128, 8, 128]), ALU.mult,
                [psT.res, Ax.res], [dstT_res])
        self.tt("pool", dstT_ap, dstT_ap, Bx_ap[:, :, v].unsqueeze(2).broadcast_to([128, 8, 128]), ALU.add,
                [dstT_res, Bx_res], [dstT_res])

    def prenorm_T(self, h_ap, h_res, Ax, Bx_ap, Bx_res, v, hs, dstT_ap, dstT_res, psT):
        self.prenorm_a(h_ap, h_res, hs)
        self.prenorm_b(Ax, Bx_ap, Bx_res, v, hs, dstT_ap, dstT_res, psT)

    def postnorm_residual(self, po_ap, po_res, Gt, h_ap, h_res, junk, tmp):
        rstd = self.rstd_of(po_ap, [po_res], junk)
        self.stt("dve", tmp.ap, po_ap, rstd.ap, Gt.ap, ALU.mult, ALU.mult, [po_res, rstd.res, Gt.res], [tmp.res])
        self.tt("pool", h_ap, tmp.ap, h_ap, ALU.add, [tmp.res, h_res], [h_res])

    def mlp_phase(self, l):
        A = self.A
        P = self.P
        P.barrier()
        A.reset()
        w1 = A.alloc(BF16, [8, 4096])
        w2 = A.alloc(BF16, [32, 1024])
        w1r = [Res() for _ in range(8)]
        w2r = [Res() for _ in range(8)]
        for k in range(8):
            P.dma("pool", w1.ap[:, k, :], self.w_ff1[128 * k:128 * (k + 1), :], [], [w1r[k]])
        for q in range(8):
            P.dma("pool", w2.ap[:, 4 * q:4 * q + 4, :],
                  self.w_ff2[512 * q:512 * (q + 1), :].rearrange("(j p) n -> p j n", p=128), [], [w2r[q]])
        hid = A.alloc(BF16, [32, 256])
        hidr = [Res() for _ in range(32)]
        hnT = A.alloc(BF16, [8, 256])
        ht = [A.alloc(F32, [1024]) for _ in range(2)]
        hs = A.alloc(BF16, [1024])
        tmp = A.alloc(F32, [1024])
        rl = [A.alloc(F32, [256]) for _ in range(2)]
        B2_ap = self.modpp.ap[:, 24:32, :]
        upd_ctx = l < 2
        tiles = [("lat", i) for i in range(16)] + ([("ctx", 0)] if upd_ctx else [])
        if self.dbg.startswith("mlp1"):
            tiles = tiles[:1]
        ht4 = ht + [A.alloc(F32, [1024]) for _ in range(2)]
        hs2 = [hs, A.alloc(BF16, [1024])]
        junk2 = hs2[0]

        def pre(idx):
            kind, ti = tiles[idx]
            hview = []
            for sub in range(2):
                if kind == "lat":
                    t128 = ti * 2 + sub
                    h = ht4[2 * (idx % 2) + sub]
                    P.dma("sp", h.ap, self.x[128 * t128:128 * (t128 + 1), :], [], [h.res])
                    hview.append((h.ap, h.res))
                else:
                    hview.append((self.sctx.ap[:, sub, :], self.sctx_res[sub]))
                self.prenorm_a(hview[sub][0], hview[sub][1], hs2[sub])
            return hview

        def pre_b(idx):
            kind, ti = tiles[idx]
            v = 0 if kind == "lat" else 1
            for sub in range(2):
                self.prenorm_b(self.A2, B2_ap, self.modpp.res, v, hs2[sub], hnT.ap[:, :, sub * 128:(sub + 1) * 128], hnT.res, self.bank[0])

        hv_next = pre(0)
        pre_b(0)
        for idx, (kind, ti) in enumerate(tiles):
            v = 0 if kind == "lat" else 1
            hview = hv_next
            for j in range(32):
                pb = self.bank[1 + (j % 2)]
                pj = Tile(pb.ap[:, 0:256], pb.res)
                for k in range(8):
                    self.mm(pj.ap, w1.ap[:, k, 128 * j:128 * (j + 1)], hnT.ap[:, k, :], k == 0, k == 7,
                            [w1r[k], hnT.res], [pj.res])
                r = rl[j % 2]
                self.act(r.ap, pj.ap, AF.Relu, [pj.res], [r.res])
                self.tt("pool" if j % 2 else "dve", hid.ap[:, j, :], r.ap, r.ap, ALU.mult, [r.res], [hidr[j]])
            if idx + 1 < len(tiles):
                hv_next = pre(idx + 1)
            for sub in range(2):
                for half in range(2):
                    pb = self.bank[3 + 2 * sub + half]
                    for j in range(32):
                        self.mm(pb.ap, hid.ap[:, j, sub * 128:(sub + 1) * 128], w2.ap[:, j, half * 512:(half + 1) * 512],
                                j == 0, j == 31, [hidr[j], w2r[j // 4]], [pb.res])
            if idx + 1 < len(tiles):
                pre_b(idx + 1)
            for sub in range(2):
                po_ap = self.psall[:, (3 + 2 * sub) * 512:(5 + 2 * sub) * 512]
                pres = [self.bank[3 + 2 * sub].res, self.bank[4 + 2 * sub].res]
                rstd = self.rstd_of(po_ap, pres, junk2)
                self.stt("dve", tmp.ap, po_ap, rstd.ap, self.G[1][v].ap, ALU.mult, ALU.mult,
                         pres + [rstd.res, self.G[1][v].res], [tmp.res])
                h_ap, h_res = hview[sub]
                self.tt("pool", h_ap, tmp.ap, h_ap, ALU.add, [tmp.res, h_res], [h_res])
                if kind == "lat":
                    t128 = ti * 2 + sub
                    P.dma("sp", self.out[128 * t128:128 * (t128 + 1), :], h_ap, [h_res], [self.out_res[t128]])

    def cmul(self, eng, out_re, out_im, a_re, a_im, b_re, b_im, t1, t2, reads, wres, neg_im=False):
        rs_ = reads
        self.tt(eng, t1.ap, a_re, b_re, ALU.mult, rs_, [t1.res])
        self.tt(eng, t2.ap, a_im, b_im, ALU.mult, rs_, [t2.res])
        self.tt(eng, out_re, t1.ap, t2.ap, ALU.subtract, [t1.res, t2.res], wres)
        self.tt(eng, t1.ap, a_re, b_im, ALU.mult, rs_, [t1.res])
        self.tt(eng, t2.ap, a_im, b_re, ALU.mult, rs_, [t2.res])
        if neg_im:
            self.stt(eng, out_im, t1.ap, -1.0, t2.ap, ALU.mult, ALU.subtract, [t1.res, t2.res], wres)
        else:
            self.tt(eng, out_im, t1.ap, t2.ap, ALU.add, [t1.res, t2.res], wres)

    def even_a(self, l):
        A = self.A
        P = self.P
        nc = self.nc
        AX = mybir.AxisListType.X
        P.barrier()
        A.reset()
        NB = 544
        RTm = [A.alloc(BF16, [32, 2, 128]) for _ in range(2)]
        for r_ in range(2):
            self.memset("pool", RTm[r_].ap, 0.0, [RTm[r_].res])
        Ob = [[A.alloc(BF16, [16, 128]) for _ in range(2)] for _ in range(2)]
        Tm = A.alloc(BF16, [32, 128])
        ASr = A.alloc(F32, [10, 32])
        ASi = A.alloc(F32, [10, 32])
        ASn = A.alloc(F32, [10, 32])
        keep_top = A.top
        T32 = A.alloc(F32, [32, 128])
        nat = A.alloc(F32, [4, 128])
        lre = A.alloc(F32, [32]); lim = A.alloc(F32, [32]); ldt = A.alloc(F32, [32])
        for (src, dst, bnk) in ((self.s5_lam_re, lre, 0), (self.s5_lam_im, lim, 1)):
            P.dma("sp", nat.ap[0:32, bnk, :], src.rearrange("d (q r) p -> (d q) (r p)", r=2), [], [nat.res])
            pt = Tile(self.bank[7].ap[:, 32 * bnk:32 * bnk + 32], self.bank[7].res)
            self.tr(pt.ap, nat.ap[0:32, bnk, :], self.ident_f.ap[0:32, 0:32], [nat.res, self.ident_f.res], [pt.res])
            self.cp("dve", dst.ap, pt.ap, [pt.res], [dst.res])
        P.dma("sp", nat.ap[0:32, 2, 0:2], self.s5_log_dt.rearrange("d (q r) -> (d q) r", r=2), [], [nat.res])
        dtT = A.alloc(F32, [32])
        pt = Tile(self.bank[7].ap[0:2, 64:96], self.bank[7].res)
        self.tr(pt.ap, nat.ap[0:32, 2, 0:2], self.ident_f.ap[0:32, 0:32], [nat.res, self.ident_f.res], [pt.res])
        self.cp("dve", dtT.ap[0:2, :], pt.ap, [pt.res], [dtT.res])
        sel_i = A.alloc(I32, [128]); sel = A.alloc(F32, [128]); sel2 = A.alloc(F32, [128])
        P.op("pool", lambda e: e.iota(sel_i.ap[0:2, :], [[1, 128]], base=0, channel_multiplier=-64), [], [sel_i.res])
        self.cp("dve", sel.ap[0:2, :], sel_i.ap[0:2, :], [sel_i.res], [sel.res])
        self.ts("dve", sel2.ap[0:2, :], sel.ap[0:2, :], 0.0, None, ALU.is_ge, None, [sel.res], [sel2.res])
        self.ts("dve", sel.ap[0:2, :], sel.ap[0:2, :], 64.0, None, ALU.is_lt, None, [sel.res], [sel.res])
        self.tt("dve", sel.ap[0:2, :], sel.ap[0:2, :], sel2.ap[0:2, :], ALU.mult, [sel.res, sel2.res], [sel.res])
        pt = Tile(self.bank[7].ap[:, 96:128], self.bank[7].res)
        self.mm(pt.ap, sel.ap[0:2, :], dtT.ap[0:2, :], True, True, [sel.res, dtT.res], [pt.res])
        self.cp("dve", ldt.ap, pt.ap, [pt.res], [ldt.res])
        def v32():
            return A.alloc(F32, [32])
        dt = v32(); xm = v32(); mag = v32(); imag = v32(); th = v32(); fr = v32(); frc = v32(); w1 = v32(); w2 = v32()
        sn = v32(); cs = v32(); are = v32(); aim = v32(); ire = v32(); iim = v32(); den = v32(); fre = v32(); fim = v32(); nre = v32()
        self.act(dt.ap, ldt.ap, AF.Exp, [ldt.res], [dt.res])
        self.ts("dve", lre.ap, lre.ap, -1e-4, None, ALU.min, None, [lre.res], [lre.res])
        self.tt("dve", xm.ap, lre.ap, dt.ap, ALU.mult, [lre.res, dt.res], [xm.res])
        self.act(mag.ap, xm.ap, AF.Exp, [xm.res], [mag.res])
        self.act(imag.ap, xm.ap, AF.Exp, [xm.res], [imag.res], scale=-1.0)
        self.tt("dve", th.ap, lim.ap, dt.ap, ALU.mult, [lim.res, dt.res], [th.res])
        ki = A.alloc(I32, [32]); kf = v32()
        self.ts("dve", fr.ap, th.ap, 1.0 / (2.0 * math.pi), None, ALU.mult, None, [th.res], [fr.res])
        self.cp("dve", ki.ap, fr.ap, [fr.res], [ki.res])
        self.cp("dve", kf.ap, ki.ap, [ki.res], [kf.res])
        self.tt("dve", fr.ap, fr.ap, kf.ap, ALU.subtract, [fr.res, kf.res], [fr.res])

        def wrap(x):
            self.ts("dve", w1.ap, x.ap, 0.5, None, ALU.is_gt, None, [x.res], [w1.res])
            self.ts("dve", w2.ap, x.ap, -0.5, None, ALU.is_lt, None, [x.res], [w2.res])
            self.tt("dve", x.ap, x.ap, w1.ap, ALU.subtract, [x.res, w1.res], [x.res])
            self.tt("dve", x.ap, x.ap, w2.ap, ALU.add, [x.res, w2.res], [x.res])
        wrap(fr)
        self.ts("dve", frc.ap, fr.ap, 0.25, None, ALU.add, None, [fr.res], [frc.res])
        wrap(frc)
        self.act(sn.ap, fr.ap, AF.Sin, [fr.res], [sn.res], scale=2.0 * math.pi)
        self.act(cs.ap, frc.ap, AF.Sin, [frc.res], [cs.res], scale=2.0 * math.pi)
        self.tt("dve", are.ap, mag.ap, cs.ap, ALU.mult, [mag.res, cs.res], [are.res])
        self.tt("dve", aim.ap, mag.ap, sn.ap, ALU.mult, [mag.res, sn.res], [aim.res])
        self.tt("dve", ire.ap, imag.ap, cs.ap, ALU.mult, [imag.res, cs.res], [ire.res])
        self.stt("dve", iim.ap, imag.ap, -1.0, sn.ap, ALU.mult, ALU.mult, [imag.res, sn.res], [iim.res])
        self.tt("dve", den.ap, lre.ap, lre.ap, ALU.mult, [lre.res], [den.res])
        self.tt("dve", w1.ap, lim.ap, lim.ap, ALU.mult, [lim.res], [w1.res])
        self.tt("dve", den.ap, den.ap, w1.ap, ALU.add, [den.res, w1.res], [den.res])
        self.P.op("dve", lambda e: e.reciprocal(out=den.ap, in_=den.ap), [den.res], [den.res])
        self.ts("dve", nre.ap, are.ap, -1.0, None, ALU.add, None, [are.res], [nre.res])
        self.tt("dve", w1.ap, nre.ap, lre.ap, ALU.mult, [nre.res, lre.res], [w1.res])
        self.tt("dve", w2.ap, aim.ap, lim.ap, ALU.mult, [aim.res, lim.res], [w2.res])
        self.tt("dve", fre.ap, w1.ap, w2.ap, ALU.add, [w1.res, w2.res], [fre.res])
        self.tt("dve", fre.ap, fre.ap, den.ap, ALU.mult, [fre.res, den.res], [fre.res])
        self.tt("dve", w1.ap, aim.ap, lre.ap, ALU.mult, [aim.res, lre.res], [w1.res])
        self.tt("dve", w2.ap, nre.ap, lim.ap, ALU.mult, [nre.res, lim.res], [w2.res])
        self.tt("dve", fim.ap, w1.ap, w2.ap, ALU.subtract, [w1.res, w2.res], [fim.res])
        self.tt("dve", fim.ap, fim.ap, den.ap, ALU.mult, [fim.res, den.res], [fim.res])
        Epr = A.alloc(F32, [9, 32]); Epi = A.alloc(F32, [9, 32]); Enr = A.alloc(F32, [8, 32]); Eni = A.alloc(F32, [8, 32])
        s1 = v32(); s2 = v32()
        self.memset("pool", Epr.ap[:, 0, :], 1.0, [Epr.res]); self.memset("pool", Epi.ap[:, 0, :], 0.0, [Epi.res])
        self.memset("pool", Enr.ap[:, 0, :], 1.0, [Enr.res]); self.memset("pool", Eni.ap[:, 0, :], 0.0, [Eni.res])
        for j in range(1, 9):
            self.cmul("dve", Epr.ap[:, j, :], Epi.ap[:, j, :], Epr.ap[:, j - 1, :], Epi.ap[:, j - 1, :], are.ap, aim.ap, s1, s2,
                      [Epr.res, Epi.res, are.res, aim.res], [Epr.res, Epi.res])
        for j in range(1, 8):
            self.cmul("dve", Enr.ap[:, j, :], Eni.ap[:, j, :], Enr.ap[:, j - 1, :], Eni.ap[:, j - 1, :], ire.ap, iim.ap, s1, s2,
                      [Enr.res, Eni.res, ire.res, iim.res], [Enr.res, Eni.res])
        self.cp("dve", ASr.ap[:, 0, :], Epr.ap[:, 8, :], [Epr.res], [ASr.res])
        self.cp("dve", ASi.ap[:, 0, :], Epi.ap[:, 8, :], [Epi.res], [ASi.res])
        for k in range(1, 10):
            self.cmul("dve", ASr.ap[:, k, :], ASi.ap[:, k, :], ASr.ap[:, k - 1, :], ASi.ap[:, k - 1, :],
                      ASr.ap[:, k - 1, :], ASi.ap[:, k - 1, :], s1, s2, [ASr.res, ASi.res], [ASr.res, ASi.res])
        self.ts("dve", ASn.ap, ASi.ap, -1.0, None, ALU.mult, None, [ASi.res], [ASn.res])
        if self.dbg.startswith("s5preA"):
            return
        Br = A.alloc(F32, [2, 16, 16]); Bi = A.alloc(F32, [2, 16, 16]); Bbr = A.alloc(F32, [2, 16, 16]); Bbi = A.alloc(F32, [2, 16, 16])
        P.dma("sp", Br.ap, self.s5_b_re.rearrange("d (q r) p h -> (r p) d q h", r=2), [], [Br.res])
        P.dma("sp", Bi.ap, self.s5_b_im.rearrange("d (q r) p h -> (r p) d q h", r=2), [], [Bi.res])
        b1 = A.alloc(F32, [2, 16, 16]); b2 = A.alloc(F32, [2, 16, 16])
        fre3 = fre.ap.rearrange("p (d q) -> p d q", d=2).unsqueeze(3).broadcast_to([128, 2, 16, 16])
        fim3 = fim.ap.rearrange("p (d q) -> p d q", d=2).unsqueeze(3).broadcast_to([128, 2, 16, 16])
        self.cmul("dve", Bbr.ap, Bbi.ap, fre3, fim3, Br.ap, Bi.ap, b1, b2, [fre.res, fim.res, Br.res, Bi.res], [Bbr.res, Bbi.res])
        Cr = A.alloc(F32, [2, 16, 16]); Ci = A.alloc(F32, [2, 16, 16])
        cnat = [A.alloc(F32, [128]) for _ in range(2)]
        ci_ = 0
        for (src, dstC) in ((self.s5_c_re, Cr), (self.s5_c_im, Ci)):
            for d in range(2):
                for ch in range(2):
                    cn = cnat[ci_ % 2]
                    for ql in range(8):
                        q = ch * 8 + ql
                        P.dma("sp", cn.ap[16 * ql:16 * ql + 16, :].rearrange("h (r p) -> h r p", r=2),
                              src[d, 2 * q:2 * q + 2].rearrange("r h p -> h r p"), [], [cn.res])
                    pt = Tile(self.bank[6].ap[:, 128 * (ci_ % 4):128 * (ci_ % 4) + 128], self.bank[6].res)
                    self.tr(pt.ap, cn.ap, self.ident_f.ap, [cn.res, self.ident_f.res], [pt.res])
                    self.cp("dve", dstC.ap[:, d, ch * 8:ch * 8 + 8, :], pt.ap.rearrange("p (q h) -> p q h", q=8), [pt.res], [dstC.res])
                    ci_ += 1
        dnat = A.alloc(F32, [16]); dT = A.alloc(F32, [32]); rep_i = A.alloc(I32, [128]); rep = A.alloc(F32, [128]); dcol = A.alloc(F32, [32])
        P.dma("sp", dnat.ap[0:32, :], self.s5_d.rearrange("(g h) -> g h", h=16), [], [dnat.res])
        pt = Tile(self.bank[7].ap[0:16, 128:160], self.bank[7].res)
        self.tr(pt.ap, dnat.ap[0:32, :], self.ident_f.ap[0:32, 0:32], [dnat.res, self.ident_f.res], [pt.res])
        self.cp("dve", dT.ap[0:16, :], pt.ap, [pt.res], [dT.res])
        P.op("pool", lambda e: e.iota(rep_i.ap[0:16, :], [[1, 128]], base=16, channel_multiplier=-1), [], [rep_i.res])
        self.ts("dve", rep_i.ap[0:16, :], rep_i.ap[0:16, :], 15, None, ALU.bitwise_and, None, [rep_i.res], [rep_i.res])
        self.cp("dve", rep.ap[0:16, :], rep_i.ap[0:16, :], [rep_i.res], [rep.res])
        self.ts("dve", rep.ap[0:16, :], rep.ap[0:16, :], 0.0, None, ALU.is_equal, None, [rep.res], [rep.res])
        pt = Tile(self.bank[7].ap[:, 160:192], self.bank[7].res)
        self.mm(pt.ap, rep.ap[0:16, :], dT.ap[0:16, :], True, True, [rep.res, dT.res], [pt.res])
        self.cp("dve", dcol.ap, pt.ap, [pt.res], [dcol.res])
        cbi = A.alloc(I32, [8, 16]); cbf = A.alloc(F32, [8, 16]); rbi = A.alloc(I32, [1]); rbf = A.alloc(F32, [1])
        mkf = A.alloc(F32, [128]); mkb = A.alloc(F32, [128])
        P.op("pool", lambda e: e.iota(cbi.ap, [[1, 8], [0, 16]], base=0, channel_multiplier=0), [], [cbi.res])
        self.cp("dve", cbf.ap, cbi.ap, [cbi.res], [cbf.res])
        P.op("pool", lambda e: e.iota(rbi.ap, [[1, 1]], base=0, channel_multiplier=1), [], [rbi.res])
        self.ts("dve", rbi.ap, rbi.ap, 4, None, ALU.arith_shift_right, None, [rbi.res], [rbi.res])
        self.cp("dve", rbf.ap, rbi.ap, [rbi.res], [rbf.res])
        cbf2 = cbf.ap.rearrange("p a b -> p (a b)")
        self.ts("dve", mkf.ap, cbf2, rbf.ap, None, ALU.is_ge, None, [cbf.res, rbf.res], [mkf.res])
        self.ts("dve", mkb.ap, cbf2, rbf.ap, None, ALU.is_le, None, [cbf.res, rbf.res], [mkb.res])
        if self.dbg.startswith("s5preB"):
            return
        hmi = A.alloc(I32, [2]); hm = A.alloc(F32, [2]); tmpT = A.alloc(F32, [128])
        P.op("pool", lambda e: e.iota(hmi.ap, [[0, 2]], base=0, channel_multiplier=1), [], [hmi.res])
        self.ts("dve", hmi.ap, hmi.ap, 6, None, ALU.arith_shift_right, None, [hmi.res], [hmi.res])
        self.cp("dve", hm.ap, hmi.ap, [hmi.res], [hm.res])
        self.ts("dve", hm.ap[:, 0:1], hm.ap[:, 0:1], -1.0, -1.0, ALU.add, ALU.mult, [hm.res], [hm.res])
        big = [A.alloc(F32, [16, 8, 16]) for _ in range(6)]
        Pr, Pi, Qr, Qi, g1, g2 = big

        def esel(E, d, j0=0, n=8):
            return E.ap[:, j0:j0 + n, 16 * d:16 * d + 16].rearrange("p j q -> p q j").unsqueeze(3).broadcast_to([128, 16, n, 16])

        def ebc(E, d, j):
            return E.ap[:, j, 16 * d:16 * d + 16].unsqueeze(2).unsqueeze(3).broadcast_to([128, 16, 8, 16])

        def bcj(X, d):
            return X.ap[:, d, :, :].unsqueeze(2).broadcast_to([128, 16, 8, 16])
        for d in range(2):
            EP_r, EP_i = (Enr, Eni) if d == 0 else (Epr, Epi)
            EQ_r, EQ_i = (Epr, Epi) if d == 0 else (Enr, Eni)
            rr = [Epr.res, Epi.res, Enr.res, Eni.res, Bbr.res, Bbi.res, Cr.res, Ci.res]
            if self.dbg.startswith("s5preE"):
                continue
            self.cmul("dve", Pr.ap, Pi.ap, esel(EP_r, d), esel(EP_i, d), bcj(Bbr, d), bcj(Bbi, d), g1, g2, rr, [Pr.res, Pi.res])
            self.cmul("dve", Qr.ap, Qi.ap, esel(EQ_r, d), esel(EQ_i, d), bcj(Cr, d), bcj(Ci, d), g1, g2, rr, [Qr.res, Qi.res], neg_im=True)
            for r in range(2):
                self.ts("dve", g1.ap, Pr.ap, hm.ap[:, r:r + 1], None, ALU.mult, None, [Pr.res, hm.res], [g1.res])
                self.ts("dve", g2.ap, Pi.ap, hm.ap[:, r:r + 1], None, ALU.mult, None, [Pi.res, hm.res], [g2.res])
                for q in range(16):
                    g = 2 * q + r
                    pt = Tile(self.bank[1 + (q % 2)].ap[:, 0:128], self.bank[1 + (q % 2)].res)
                    self.mm(pt.ap, g1.ap[:, q].rearrange("p a b -> p (a b)"), Qr.ap[:, q].rearrange("p a b -> p (a b)"),
                            True, False, [g1.res, Qr.res], [pt.res])
                    self.mm(pt.ap, g2.ap[:, q].rearrange("p a b -> p (a b)"), Qi.ap[:, q].rearrange("p a b -> p (a b)"),
                            False, True, [g2.res, Qi.res], [pt.res])
                    if d == 0:
                        self.tt("dve", T32.ap[:, g, :], pt.ap, mkf.ap, ALU.mult, [pt.res, mkf.res], [T32.res])
                    else:
                        self.tt("dve", tmpT.ap, pt.ap, mkb.ap, ALU.mult, [pt.res, mkb.res], [tmpT.res])
                        self.tt("dve", T32.ap[:, g, :], T32.ap[:, g, :], tmpT.ap, ALU.add, [T32.res, tmpT.res], [T32.res])
            jO = 1 if d == 0 else 8
            self.tt("dve", g1.ap, ebc(Epr, d, jO), Qr.ap, ALU.mult, rr + [Qr.res], [g1.res])
            self.tt("dve", g2.ap, ebc(Epi, d, jO), Qi.ap, ALU.mult, rr + [Qi.res], [g2.res])
            self.tt("dve", Ob[d][0].ap.rearrange("p q (a b) -> p q a b", a=8), g1.ap, g2.ap, ALU.add, [g1.res, g2.res], [Ob[d][0].res])
            self.tt("dve", g1.ap, ebc(Epr, d, jO), Qi.ap, ALU.mult, rr + [Qi.res], [g1.res])
            self.tt("dve", g2.ap, ebc(Epi, d, jO), Qr.ap, ALU.mult, rr + [Qr.res], [g2.res])
            self.tt("dve", Ob[d][1].ap.rearrange("p q (a b) -> p q a b", a=8), g1.ap, g2.ap, ALU.subtract, [g1.res, g2.res], [Ob[d][1].res])
            if d == 0:
                self.cmul("dve", Qr.ap, Qi.ap, ebc(Epr, 0, 7), ebc(Epi, 0, 7), Pr.ap, Pi.ap, g1, g2, rr + [Pr.res, Pi.res], [Qr.res, Qi.res])
                Rr_, Ri_ = Qr, Qi
            else:
                Rr_, Ri_ = Pr, Pi
            for q in range(16 if not self.dbg.startswith("s5preD") else 0):
                for (ri, Rx) in ((0, Rr_), (1, Ri_)):
                    pt = Tile(self.bank[3 + (q % 2)].ap[:, 128 * ri:128 * ri + 128], self.bank[3 + (q % 2)].res)
                    self.tr(pt.ap, Rx.ap[:, q].rearrange("p a b -> p (a b)"), self.ident_f.ap, [Rx.res, self.ident_f.res], [pt.res])
                pb2 = self.bank[3 + (q % 2)]
                for r_ in range(2 if not self.dbg.startswith("s5preCF") else 0):
                    self.cp("dve", RTm[r_].ap[:, 2 * q + d, :, 64 * r_:64 * r_ + 64],
                            pb2.ap[:, 0:256].rearrange("p (a b) -> p a b", a=2)[:, :, 64 * r_:64 * r_ + 64], [pb2.res], [RTm[r_].res])
        for g in range(32):
            self.stt("dve", Tm.ap[:, g, :], self.ident_f.ap, dcol.ap[:, g:g + 1], T32.ap[:, g, :], ALU.mult, ALU.add,
                     [self.ident_f.res, dcol.res, T32.res], [Tm.res])
        self.Ob = Ob
        if self.dbg.startswith("s5pre"):
            return
        P.barrier()
        A.top = keep_top
        U = A.alloc(BF16, [32, NB])
        Ur = [[Res() for _ in range(5)] for _ in range(32)]
        u_top = A.top
        wu = A.alloc(BF16, [8, 512])
        P.dma("pool", wu.ap, self.w_in[:, 0:512].rearrange("(k p) n -> p k n", p=128), [], [wu.res])
        hTg = A.alloc(BF16, [8, 1024])
        ub2 = A.alloc(BF16, [32, 8, 16])
        hTs = A.alloc(BF16, [8, 8, 128])
        self.memset("pool", ub2.ap, 0.0, [ub2.res])
        hs = A.alloc(BF16, [1024])
        ht = [A.alloc(F32, [1024]) for _ in range(2)]
        B1_ap = self.modpp.ap[:, 0:8, :]
        groups = [(0, 32)] + [(32 + 128 * i, 128) for i in range(4)]
        for gi, (n0, nb) in enumerate(groups):
            ntile = nb // 16
            for i in range(ntile):
                if gi == 0:
                    h_ap, h_res, v = self.sctx.ap[:, i, :], self.sctx_res[i], 1
                else:
                    h = ht[i % 2]
                    t128 = (gi - 1) * 8 + i
                    P.dma("sp", h.ap, self.x[128 * t128:128 * (t128 + 1), :], [], [h.res])
                    h_ap, h_res, v = h.ap, h.res, 0
                self.prenorm_T(h_ap, h_res, self.A1, B1_ap, self.modpp.res, v, hs, hTg.ap[:, :, 128 * i:128 * (i + 1)], hTg.res, self.bank[0])
            for k in range(8):
                self.cp("dve" if k % 2 else "pool", hTs.ap[:, k, :, 0:nb], hTg.ap[:, k, 0:8 * nb].rearrange("p (b t) -> p t b", t=8),
                        [hTg.res], [hTs.res])
            for tau in range(8):
                pb = self.bank[1 + (tau % 2)]
                for k in range(8):
                    self.mm(pb.ap[0:nb, :], hTs.ap[:, k, tau, 0:nb], wu.ap[:, k, :], k == 0, k == 7, [hTs.res, wu.res], [pb.res])
                self.cp("dve", ub2.ap[0:nb, :, tau, :], pb.ap[0:nb, :].rearrange("p (g h) -> p g h", h=16),
                        [pb.res], [ub2.res])
            for g8 in range(4 if not self.dbg.startswith("s5a1x") else 0):
                pb = self.bank[3 + (g8 % 2)]
                pbt = pb.ap.bitcast(BF16)
                for gl in range(8):
                    g = g8 * 8 + gl
                    self.tr(pbt[:, 128 * gl:128 * gl + 128], ub2.ap[:, g].rearrange("p a b -> p (a b)"), self.ident_b.ap,
                            [ub2.res, self.ident_b.res], [pb.res])
                for gl in range(8 if not self.dbg.startswith("s5a1y") else 0):
                    g = g8 * 8 + gl
                    self.cp("dve", U.ap[:, g, n0:n0 + nb], pbt[:, 128 * gl:128 * gl + nb], [pb.res], [Ur[g][gi]])
        if self.dbg.startswith("s5a1"):
            return
        P.barrier()
        A.top = u_top
        gbm = A.alloc(BF16, [5, 8, 512])
        gbr = [Res() for _ in range(5)]
        gbm_end = A.top
        Sx2 = [[[A.alloc(F32, [NB]) for _ in range(2)] for _ in range(2)] for _ in range(2)]
        SE = [[[A.alloc(BF16, [NB + 1]) for _ in range(2)] for _ in range(2)] for _ in range(2)]
        for par in range(2):
            for d in range(2):
                for ri in range(2):
                    self.memset("pool", SE[par][d][ri].ap, 0.0, [SE[par][d][ri].res])
        gtmp = [A.alloc(BF16, [128]) for _ in range(2)]
        nq = 16 if not self.dbg.startswith("s5q1") else 1
        for q in range(nq):
            par = q % 2
            for d in range(2):
                qd = 2 * q + d
                psV = [Tile(self.psall[:, 512:1056], Res()), Tile(self.psall[:, 1536:2080], Res())]
                vres = [[self.bank[1].res, self.bank[2].res], [self.bank[3].res, self.bank[4].res]]
                if d == 0:
                    splits = [(0, 0, 512), (512, 512, 32)]
                else:
                    splits = [(0, 32, 512), (512, 0, 32)]
                for ri in range(2):
                    for (oc, uc, n) in splits:
                        for r in range(2):
                            g = 2 * q + r
                            ur = Ur[g]
                            self.mm(psV[ri].ap[:, oc:oc + n], RTm[r].ap[:, qd, ri, :], U.ap[:, g, uc:uc + n],
                                    r == 0, r == 1, [RTm[r].res] + ur, vres[ri])
                self.cp("dve", Sx2[d][0][0].ap, psV[0].ap, vres[0], [Sx2[d][0][0].res])
                self.cp("dve", Sx2[d][0][1].ap, psV[1].ap, vres[1], [Sx2[d][0][1].res])
            cur = 0
            for k in range(10):
                dl = 1 << k
                for d in range(2):
                    col = 16 * d + q
                    Sx = Sx2[d]
                    a_r = ASr.ap[:, k, col:col + 1]
                    a_i = ASi.ap[:, k, col:col + 1]
                    a_n = ASn.ap[:, k, col:col + 1]
                    o_re, o_im = Sx[cur][0], Sx[cur][1]
                    n_re, n_im = Sx[1 - cur][0], Sx[1 - cur][1]
                    if d == 0:
                        dst_s, src_s, same_s = slice(dl, NB), slice(0, NB - dl), slice(0, dl)
                    else:
                        dst_s, src_s, same_s = slice(0, NB - dl), slice(dl, NB), slice(NB - dl, NB)
                    rd = [o_re.res, o_im.res, ASr.res, ASi.res, ASn.res]
                    self.cp("act", n_re.ap[:, same_s], o_re.ap[:, same_s], [o_re.res], [n_re.res])
                    self.cp("act", n_im.ap[:, same_s], o_im.ap[:, same_s], [o_im.res], [n_im.res])
                    self.stt("dve", n_re.ap[:, dst_s], o_re.ap[:, src_s], a_r, o_re.ap[:, dst_s], ALU.mult, ALU.add, rd, [n_re.res])
                    self.stt("dve", n_im.ap[:, dst_s], o_im.ap[:, src_s], a_r, o_im.ap[:, dst_s], ALU.mult, ALU.add, rd, [n_im.res])
                    self.stt("dve", n_re.ap[:, dst_s], o_im.ap[:, src_s], a_n, n_re.ap[:, dst_s], ALU.mult, ALU.add, rd + [n_re.res], [n_re.res])
                    self.stt("dve", n_im.ap[:, dst_s], o_re.ap[:, src_s], a_i, n_im.ap[:, dst_s], ALU.mult, ALU.add, rd + [n_im.res], [n_im.res])
                cur = 1 - cur
            for d in range(2):
                off = 1 if d == 0 else 0
                for ri in range(2):
                    self.cp("act", SE[par][d][ri].ap[:, off:off + NB], Sx2[d][cur][ri].ap, [Sx2[d][cur][ri].res], [SE[par][d][ri].res])
            for r in range(2):
                g = 2 * q + r
                for ci, (n0, nb) in enumerate(groups):
                    pb = self.bank[5 + ((2 * q + r + ci) % 2)]
                    py = Tile(pb.ap[0:nb, 0:128], pb.res)
                    self.mm(py.ap, U.ap[:, g, n0:n0 + nb], Tm.ap[:, g, :], True, False, Ur[g] + [Tm.res], [py.res])
                    bidx = (n0 - 32) if n0 >= 32 else 512 + n0
                    for d, c0_ in ((0, n0), (1, bidx + 1)):
                        for ri in range(2):
                            last = (d == 1 and ri == 1)
                            self.mm(py.ap, SE[par][d][ri].ap[64 * r:64 * r + 64, c0_:c0_ + nb], self.Ob[d][ri].ap[64 * r:64 * r + 64, q, :],
                                    False, last, [SE[par][d][ri].res, self.Ob[d][ri].res], [py.res])
                    gt_ = gtmp[(2 * q + r + ci) % 2]
                    self.act(gt_.ap[0:nb, :], py.ap, AF.Gelu, [py.res], [gt_.res])
                    self.cp("dve", gbm.ap[0:nb, ci, :, 16 * g:16 * g + 16], gt_.ap[0:nb, :].rearrange("p (t h) -> p t h", t=8), [gt_.res], [gbr[ci]])
        if self.dbg.startswith("s5a3"):
            return
        P.barrier()
        A.top = keep_top
        gT = A.alloc(BF16, [4, L + LC])
        A.top = gbm_end
        wg = A.alloc(BF16, [4, 512])
        P.dma("pool", wg.ap, self.s5_w_glu.rearrange("(k p) n -> p k n", p=128), [], [wg.res])
        for ci, (n0, nb) in enumerate(groups):
            for tp in range(8):
                pb = self.bank[1 + (tp % 2)]
                pbt = pb.ap.bitcast(BF16)
                for kk in range(4):
                    self.tr(pbt[:, 128 * kk:128 * kk + nb], gbm.ap[0:nb, ci, tp, 128 * kk:128 * kk + 128], self.ident_b.ap[0:nb, 0:nb],
                            [gbr[ci], self.ident_b.res], [pb.res])
                dst = gT.ap[:, :, 8 * n0:8 * (n0 + nb)].rearrange("p k (b t) -> p k b t", t=8)[:, :, :, tp]
                src = pbt[:, 0:512].rearrange("p (k b) -> p k b", k=4)[:, :, 0:nb]
                self.cp("dve", dst, src, [pb.res], [gT.res])
        sg = [A.alloc(F32, [512]) for _ in range(2)]
        so = [A.alloc(BF16, [4, 512]) for _ in range(2)]
        NTOK = L + LC
        ti = 0
        for t0 in range(0, NTOK, 512):
            n = min(512, NTOK - t0)
            sot = so[ti % 2]
            for mcol in range(4):
                pb = self.bank[3 + (mcol % 2)]
                for kk in range(4):
                    self.mm(pb.ap[:, 0:n], wg.ap[:, kk, 128 * mcol:128 * (mcol + 1)], gT.ap[:, kk, t0:t0 + n], kk == 0, kk == 3,
                            [wg.res, gT.res], [pb.res])
                sgt = sg[mcol % 2]
                self.act(sgt.ap[:, 0:n], pb.ap[:, 0:n], AF.Sigmoid, [pb.res], [sgt.res])
                self.tt("dve", sot.ap[:, mcol, 0:n], sgt.ap[:, 0:n], gT.ap[:, mcol, t0:t0 + n], ALU.mult, [sgt.res, gT.res], [sot.res])
            P.dma("sp", self.s5T_out[:, t0:t0 + n].rearrange("(k p) t -> p k t", p=128), sot.ap[:, :, 0:n], [sot.res], [Res()])
            ti += 1

    def even_b(self, l):
        A = self.A
        P = self.P
        upd_ctx = l < 2
        P.barrier()
        A.reset()
        NT = L + LC
        NTILE = NT // 128
        AX = mybir.AxisListType.X
        win = A.alloc(BF16, [8, 1536])
        winr = [Res() for _ in range(8)]
        for k in range(8):
            P.dma("pool", win.ap[:, k, :], self.w_in[128 * k:128 * (k + 1), 512:2048], [], [winr[k]])
        wout = A.alloc(BF16, [8, 1024])
        P.dma("pool", wout.ap, self.w_out_even.rearrange("(k p) n -> p k n", p=128), [], [wout.res])
        kT = A.alloc(BF16, [4, NT])
        kTr = [Res() for _ in range(NTILE)]
        vt = A.alloc(BF16, [NTILE, 512])
        vtr = [Res() for _ in range(NTILE)]
        hT = A.alloc(BF16, [8, 128])
        hs = A.alloc(BF16, [1024])
        ht = [A.alloc(F32, [1024]) for _ in range(2)]
        B1_ap = self.modpp.ap[:, 0:8, :]

        def tile_src(t, i):
            if t < 2:
                return self.sctx.ap[:, t, :], self.sctx_res[t], 1
            h = ht[i % 2]
            P.dma("sp", h.ap, self.x[128 * (t - 2):128 * (t - 1), :], [], [h.res])
            return h.ap, h.res, 0

        def pre_a(t, i):
            h_ap, h_res, v = tile_src(t, i)
            self.prenorm_a(h_ap, h_res, hs)
            return h_ap, h_res, v

        def pre_b(v):
            self.prenorm_b(self.A1, B1_ap, self.modpp.res, v, hs, hT.ap, hT.res, self.bank[0])

        nxt = pre_a(0, 0)
        pre_b(nxt[2])
        for t in range(NTILE):
            if t + 1 < NTILE:
                nxt = pre_a(t + 1, t + 1)
            pk = self.bank[1]
            for mc in range(4):
                for k in range(8):
                    self.mm(pk.ap[:, 128 * mc:128 * (mc + 1)], win.ap[:, k, 512 + 128 * mc:512 + 128 * (mc + 1)], hT.ap[:, k, :],
                            k == 0, k == 7, [winr[k], hT.res], [pk.res])
            self.cp("act", kT.ap[:, :, 128 * t:128 * (t + 1)], pk.ap.rearrange("p (a b) -> p a b", a=4), [pk.res], [kTr[t]])
            pv = self.bank[7]
            for k in range(8):
                self.mm(pv.ap, hT.ap[:, k, :], win.ap[:, k, 1024:1536], k == 0, k == 7, [winr[k], hT.res], [pv.res])
            self.cp("dve", vt.ap[:, t, :], pv.ap, [pv.res], [vtr[t]])
            if t + 1 < NTILE:
                pre_b(nxt[2])
        Bd = self.nc.dram_tensor("Bd%d" % l, [8, 15, 64, 94], F32).ap()
        Bd_res = Res()
        Bc = A.alloc(F32, [8, 15, 64])
        top_b2 = A.top
        fill = A.alloc(F32, [15 * 94])
        self.memset("pool", fill.ap, -30000.0, [fill.res])
        for hd in range(8):
            P.dma("sp", Bd[hd].rearrange("a q k -> q a k"), fill.ap[0:64, :].rearrange("p (a k) -> p a k", a=15), [fill.res], [Bd_res])
        bt = Bd.tensor
        dst = bass.AP(bt, 0, [[15 * 64 * 94, 8], [64 * 94, 15], [95, 64], [1, 31]])
        rt = self.na_rpb.tensor
        srcp = bass.AP(rt, self.na_rpb.offset, [[465, 8], [31, 15], [0, 64], [1, 31]])
        P.dma("sp", dst, srcp, [Bd_res], [Bd_res])
        for half in range(2):
            P.dma("sp", Bc.ap[64 * half:64 * half + 64], Bd[:, :, :, 15:79].rearrange("h a q k -> q h a k"), [Bd_res], [Bc.res])
        ii = A.alloc(I32, [64])
        kcf = A.alloc(F32, [64])
        qi = A.alloc(I32, [1])
        qf = A.alloc(F32, [1])
        c0 = A.alloc(F32, [1])
        m1 = A.alloc(F32, [64])
        m2 = A.alloc(F32, [64])
        P.op("pool", lambda e: e.iota(ii.ap, [[1, 64]], base=0, channel_multiplier=0), [], [ii.res])
        self.cp("dve", kcf.ap, ii.ap, [ii.res], [kcf.res])
        P.op("pool", lambda e: e.iota(qi.ap, [[1, 1]], base=0, channel_multiplier=1), [], [qi.res])
        self.ts("dve", qi.ap, qi.ap, 63, None, ALU.bitwise_and, None, [qi.res], [qi.res])
        self.cp("dve", qf.ap, qi.ap, [qi.res], [qf.res])
        self.ts("dve", c0.ap, qf.ap, -8.0, 0.0, ALU.add, ALU.max, [qf.res], [c0.res])
        self.ts("dve", c0.ap, c0.ap, 48.0, None, ALU.min, None, [c0.res], [c0.res])
        self.ts("dve", m1.ap, kcf.ap, c0.ap, None, ALU.is_ge, None, [kcf.res, c0.res], [m1.res])
        self.ts("dve", m2.ap, kcf.ap, -16.0, c0.ap, ALU.add, ALU.is_lt, [kcf.res, c0.res], [m2.res])
        self.tt("dve", m1.ap, m1.ap, m2.ap, ALU.mult, [m1.res, m2.res], [m1.res])
        self.ts("dve", m1.ap, m1.ap, -1.0, 30000.0, ALU.add, ALU.mult, [m1.res], [m1.res])
        Bc2 = Bc.ap.rearrange("p h a k -> p (h a) k")
        self.tt("dve", Bc2, Bc2, m1.ap.unsqueeze(1).broadcast_to([128, 120, 64]), ALU.add, [Bc.res, m1.res], [Bc.res])
        P.barrier()
        A.top = top_b2
        qT = A.alloc(BF16, [4, 128])
        tSs = [A.alloc(F32, [768]) for _ in range(2)]
        Pts = [A.alloc(BF16, [896]) for _ in range(2)]
        PtTs = [A.alloc(BF16, [896])] * 2
        natok = A.alloc(BF16, [512])
        naT = A.alloc(BF16, [4, 128])
        s5t = [A.alloc(BF16, [4, 128]) for _ in range(2)]
        sm = A.alloc(F32, [8, 2])
        rinv = A.alloc(F32, [8])
        mx = A.alloc(F32, [8])
        nmx = A.alloc(F32, [8])
        tmp = A.alloc(F32, [1024])
        junk = hs
        mx_r = [Res() for _ in range(8)]
        nmx_r = [Res() for _ in range(8)]
        sm_r3 = [[Res() for _ in range(3)] for _ in range(8)]
        sm_r = [r_ for l3 in sm_r3 for r_ in l3]
        tS_r = [[Res() for _ in range(3)] for _ in range(2)]
        Pt_r = [[Res() for _ in range(3)] for _ in range(2)]
        psS = Tile(self.psall[:, 1024:2048], Res())
        psS_res = [self.bank[2].res, self.bank[3].res]
        psT = self.bank[6]
        psO = self.bank[7]
        units = [("lat", m) for m in range(32)] + ([("ctx", i) for i in range(2)] if upd_ctx else [])
        if self.dbg.startswith("na1"):
            units = units[:1] + units[5:6] + units[31:32] + units[32:]
        def unit_tile(u):
            kind_, m_ = units[u]
            return 2 + m_ if kind_ == "lat" else m_

        def qproj():
            pq = self.bank[1]
            for mc in range(4):
                for k in range(8):
                    self.mm(pq.ap[:, 128 * mc:128 * (mc + 1)], win.ap[:, k, 128 * mc:128 * (mc + 1)], hT.ap[:, k, :],
                            k == 0, k == 7, [winr[k], hT.res], [pq.res])
            self.cp("act", qT.ap, pq.ap.rearrange("p (a b) -> p a b", a=4), [pq.res], [qT.res])

        cur_src = pre_a(unit_tile(0), 0)
        pre_b(cur_src[2])
        qproj()
        for ui, (kind, m) in enumerate(units):
            t = 2 + m if kind == "lat" else m
            h_ap, h_res, _v = cur_src
            nxt_src = None
            if ui + 1 < len(units):
                nxt_src = pre_a(unit_tile(ui + 1), ui + 1)
            if kind == "lat":
                rs = min(max(2 * m - 4, 0), 54)
                wt0 = 2 + rs // 2
                nwin = 640
            else:
                rs = 0
                wt0 = 0
                nwin = 0
            ncol = nwin + 256
            nchunk = ncol // 128
            wins_of = {}
            def stage_a1(hd):
                tS, Pt, PtT = tSs[hd % 2], Pts[hd % 2], PtTs[hd % 2]
                tSr, Ptr = tS_r[hd % 2], Pt_r[hd % 2]
                mxr, nmxr, smr = mx_r[hd], nmx_r[hd], sm_r3[hd]
                mc, po = hd // 2, 64 * (hd % 2)
                q_l = qT.ap[po:po + 64, mc, :]
                if kind == "lat":
                    kr = [kTr[wt0 + i] for i in range(5)]
                    self.mm(psS.ap[:, 0:512], q_l, kT.ap[po:po + 64, mc, 128 * wt0:128 * wt0 + 512], True, True,
                            [qT.res] + kr, psS_res)
                    self.mm(psS.ap[:, 512:640], q_l, kT.ap[po:po + 64, mc, 128 * wt0 + 512:128 * wt0 + 640], True, True,
                            [qT.res] + kr, psS_res)
                self.mm(psS.ap[:, nwin:nwin + 256], q_l, kT.ap[po:po + 64, mc, 0:256], True, True,
                        [qT.res, kTr[0], kTr[1]], psS_res)
                wins = []
                if kind == "lat":
                    for e in range(2):
                        qr = 2 * m + e
                        r0 = min(max(qr - 4, 0), 56)
                        j0 = r0 - rs
                        a0 = r0 - qr + 7
                        wins.append((e, j0))
                        self.stt("dve", tS.ap[64 * e:64 * e + 64, 0:512].rearrange("p (a k) -> p a k", a=8),
                                 psS.ap[64 * e:64 * e + 64, 64 * j0:64 * j0 + 512].rearrange("p (a k) -> p a k", a=8), 0.125,
                                 Bc.ap[64 * e:64 * e + 64, hd, a0:a0 + 8, :], ALU.mult, ALU.add,
                                 psS_res + [Bc.res], [tSr[e]])
                o0 = 512 if kind == "lat" else 0
                self.ts("dve", tS.ap[:, o0:o0 + 256], psS.ap[:, nwin:nwin + 256], 0.125, None, ALU.mult, None, psS_res, [tSr[2]])
                self.P.op("dve", lambda e, o_=mx.ap[:, hd:hd + 1], i_=tS.ap[:, 0:o0 + 256]: e.reduce_max(out=o_, in_=i_, axis=AX),
                          tSr, [mxr])
                self.ts("dve", nmx.ap[:, hd:hd + 1], mx.ap[:, hd:hd + 1], -1.0, None, ALU.mult, None, [mxr], [nmxr])
                self.memset("pool", sm.ap[:, hd, :], 0.0, smr)
                wins_of[hd] = (wins, o0)
            def stage_a2(hd):
                tS, Pt = tSs[hd % 2], Pts[hd % 2]
                tSr, Ptr = tS_r[hd % 2], Pt_r[hd % 2]
                nmxr, smr = nmx_r[hd], sm_r3[hd]
                wins, o0 = wins_of[hd]
                if kind == "lat":
                    self.memset("pool", Pt.ap[:, 0:640], 0.0, Ptr)
                for (e, j0) in wins:
                    self.act(Pt.ap[64 * e:64 * e + 64, 64 * j0:64 * j0 + 512], tS.ap[64 * e:64 * e + 64, 0:512], AF.Exp,
                             [tSr[e], nmxr, smr[e]], [Ptr[e], smr[e]], bias=nmx.ap[64 * e:64 * e + 64, hd:hd + 1], scale=1.0,
                             accum_out=sm.ap[64 * e:64 * e + 64, hd, 0:1])
                self.act(Pt.ap[:, nwin:nwin + 256], tS.ap[:, o0:o0 + 256], AF.Exp, [tSr[2], nmxr, smr[2]], [Ptr[2], smr[2]],
                         bias=nmx.ap[:, hd:hd + 1], scale=1.0, accum_out=sm.ap[:, hd, 1:2])
            def stage_b(hd):
                Pt, PtT = Pts[hd % 2], PtTs[hd % 2]
                Ptr = Pt_r[hd % 2]
                pst = psT.ap.bitcast(BF16)
                for c in range(nchunk):
                    self.tr(pst[:, 128 * c:128 * (c + 1)], Pt.ap[:, 128 * c:128 * (c + 1)], self.ident_b.ap,
                            Ptr + [self.ident_b.res], [psT.res])
                self.cp("act" if hd % 2 else "dve", PtT.ap[:, 0:ncol], pst[:, 0:ncol], [psT.res], [PtT.res])
                for c in range(nchunk):
                    if kind == "lat" and c < 5:
                        vtile = wt0 + c
                    else:
                        vtile = c - (5 if kind == "lat" else 0)
                    self.mm(psO.ap[:, 64 * hd:64 * hd + 64], PtT.ap[:, 128 * c:128 * (c + 1)], vt.ap[:, vtile, 64 * hd:64 * hd + 64],
                            c == 0, c == nchunk - 1, [PtT.res, vtr[vtile]], [psO.res])
            stage_a1(0)
            stage_a1(1)
            stage_a2(0)
            for hd in range(8):
                if hd + 2 < 8:
                    stage_a1(hd + 2)
                if hd + 1 < 8:
                    stage_a2(hd + 1)
                stage_b(hd)
            if nxt_src is not None:
                pre_b(nxt_src[2])
                qproj()
                cur_src = nxt_src
            self.tt("dve", rinv.ap, sm.ap[:, :, 0], sm.ap[:, :, 1], ALU.add, sm_r, [rinv.res])
            self.P.op("dve", lambda e: e.reciprocal(out=rinv.ap, in_=rinv.ap), [rinv.res], [rinv.res])
            self.tt("dve", natok.ap.rearrange("p (h d) -> p h d", h=8), psO.ap.rearrange("p (h d) -> p h d", h=8),
                    rinv.ap.unsqueeze(2).broadcast_to([128, 8, 64]), ALU.mult, [psO.res, rinv.res], [natok.res])
            pst = psT.ap.bitcast(BF16)
            for c in range(4):
                self.tr(pst[:, 128 * c:128 * (c + 1)], natok.ap[:, 128 * c:128 * (c + 1)], self.ident_b.ap,
                        [natok.res, self.ident_b.res], [psT.res])
            self.cp("act", naT.ap, pst[:, 0:512].rearrange("p (a b) -> p a b", a=4), [psT.res], [naT.res])
            s5 = s5t[ui % 2]
            P.dma("sp", s5.ap, self.s5T_in[:, 128 * t:128 * (t + 1)].rearrange("(k p) t -> p k t", p=128), [], [s5.res])
            for half in range(2):
                pb = self.bank[4 + half]
                for k in range(8):
                    lhsT = s5.ap[:, k, :] if k < 4 else naT.ap[:, k - 4, :]
                    self.mm(pb.ap, lhsT, wout.ap[:, k, 512 * half:512 * (half + 1)], k == 0, k == 7,
                            [s5.res, naT.res, wout.res], [pb.res])
            po_ap = self.psall[:, 2048:3072]
            pres = [self.bank[4].res, self.bank[5].res]
            v = 0 if kind == "lat" else 1
            rstd = self.rstd_of(po_ap, pres, junk)
            self.stt("dve", tmp.ap, po_ap, rstd.ap, self.G[0][v].ap, ALU.mult, ALU.mult, pres + [rstd.res, self.G[0][v].res], [tmp.res])
            self.tt("pool", h_ap, tmp.ap, h_ap, ALU.add, [tmp.res, h_res], [h_res])
            if kind == "lat":
                P.dma("sp", self.out[128 * m:128 * (m + 1), :], h_ap, [h_res], [self.out_res[m]])


    def gen_tables(self):
        A = self.A
        P = self.P
        P.barrier()
        A.reset()
        W = 1024
        kf = A.alloc(F32, [4096])
        ki = A.alloc(I32, [4096])
        tcol = A.alloc(F32, [32])
        ti = A.alloc(I32, [32])
        P.op("pool", lambda e: e.iota(ki.ap, [[1, 4096]], base=0, channel_multiplier=0), [], [ki.res])
        self.cp("dve", kf.ap, ki.ap, [ki.res], [kf.res])
        P.op("pool", lambda e: e.iota(ti.ap, [[128, 32]], base=0, channel_multiplier=1), [], [ti.res])
        self.cp("dve", tcol.ap, ti.ap, [ti.res], [tcol.res])
        negpi = self.negpi
        t1 = [A.alloc(I32, [W]) for _ in range(2)]
        t2 = [A.alloc(I32, [W]) for _ in range(2)]
        t3 = [A.alloc(F32, [W]) for _ in range(2)]
        t4 = [A.alloc(F32, [W]) for _ in range(2)]
        ob = [A.alloc(BF16, [W]) for _ in range(4)]
        it = 0
        nchunks = 32 if not self.dbg.startswith("tab1") else 1
        sc = 2.0 * math.pi / 4096.0
        for c in range(nchunks):
            for kt in range(4096 // W if not (self.dbg.startswith("odd1") or self.dbg.startswith("tab1")) else 1):
                ai, ci, bf, cf = t1[it % 2], t2[it % 2], t3[it % 2], t4[it % 2]
                os_, oc = ob[(2 * it) % 4], ob[(2 * it + 1) % 4]
                self.ts("dve", ai.ap, kf.ap[:, kt * W:(kt + 1) * W], tcol.ap[:, c:c + 1], 2048.0, ALU.mult, ALU.add,
                        [kf.res, tcol.res], [ai.res])
                self.ts("dve", ci.ap, kf.ap[:, kt * W:(kt + 1) * W], tcol.ap[:, c:c + 1], 3072.0, ALU.mult, ALU.add,
                        [kf.res, tcol.res], [ci.res])
                self.ts("dve", ai.ap, ai.ap, 4095, None, ALU.bitwise_and, None, [ai.res], [ai.res])
                self.ts("dve", ci.ap, ci.ap, 4095, None, ALU.bitwise_and, None, [ci.res], [ci.res])
                self.cp("pool", bf.ap, ai.ap, [ai.res], [bf.res])
                self.cp("pool", cf.ap, ci.ap, [ci.res], [cf.res])
                self.act(os_.ap, bf.ap, AF.Sin, [bf.res, negpi.res], [os_.res], bias=negpi.ap, scale=sc)
                self.act(oc.ap, cf.ap, AF.Sin, [cf.res, negpi.res], [oc.res], bias=negpi.ap, scale=sc)
                P.dma("sp", self.Stab[128 * c:128 * (c + 1), kt * W:(kt + 1) * W], os_.ap, [os_.res], [self.tab_res])
                P.dma("sp", self.Ctab[128 * c:128 * (c + 1), kt * W:(kt + 1) * W], oc.ap, [oc.res], [self.tab_res])
                it += 1

    def gen_channel_tables(self, CD, SDn):
        A = self.A
        P = self.P
        kf = A.alloc(F32, [1024])
        ki = A.alloc(I32, [1024])
        tcol = A.alloc(F32, [8])
        ti = A.alloc(I32, [8])
        P.op("pool", lambda e: e.iota(ki.ap, [[1, 1024]], base=0, channel_multiplier=0), [], [ki.res])
        self.cp("dve", kf.ap, ki.ap, [ki.res], [kf.res])
        P.op("pool", lambda e: e.iota(ti.ap, [[128, 8]], base=0, channel_multiplier=1), [], [ti.res])
        self.cp("dve", tcol.ap, ti.ap, [ti.res], [tcol.res])
        ai = A.alloc(I32, [1024])
        ci = A.alloc(I32, [1024])
        b_ = A.alloc(F32, [1024])
        c3 = A.alloc(F32, [1024])
        negpi = self.negpi
        sc = 2.0 * math.pi / 1024.0
        for k in range(8):
            self.ts("dve", ai.ap, kf.ap, tcol.ap[:, k:k + 1], 512.0, ALU.mult, ALU.add, [kf.res, tcol.res], [ai.res])
            self.ts("dve", ci.ap, kf.ap, tcol.ap[:, k:k + 1], 768.0, ALU.mult, ALU.add, [kf.res, tcol.res], [ci.res])
            self.ts("dve", ai.ap, ai.ap, 1023, None, ALU.bitwise_and, None, [ai.res], [ai.res])
            self.ts("dve", ci.ap, ci.ap, 1023, None, ALU.bitwise_and, None, [ci.res], [ci.res])
            self.cp("pool", b_.ap, ai.ap, [ai.res], [b_.res])
            self.cp("pool", c3.ap, ci.ap, [ci.res], [c3.res])
            self.act(b_.ap, b_.ap, AF.Sin, [b_.res, negpi.res], [b_.res], bias=negpi.ap, scale=sc)
            self.act(c3.ap, c3.ap, AF.Sin, [c3.res, negpi.res], [c3.res], bias=negpi.ap, scale=sc)
            self.ts("dve", SDn.ap[:, k, :], b_.ap, -1.0 / 2048.0, None, ALU.mult, None, [b_.res], [SDn.res])
            self.ts("pool", CD.ap[:, k, :], c3.ap, 1.0 / 2048.0, None, ALU.mult, None, [c3.res], [CD.res])

    def odd_mixer(self, l):
        A = self.A
        P = self.P
        o = l // 2
        upd_ctx = l < 2
        P.barrier()
        A.reset()
        ycc = A.alloc(BF16, [8, 256])
        ysc = A.alloc(BF16, [8, 256])
        base2 = A.top
        hl = A.alloc(BF16, [32, 1024])
        hlr = [Res() for _ in range(32)]
        hc = A.alloc(BF16, [2, 1024])
        hcr = [Res() for _ in range(2)]
        nv = 2 if upd_ctx else 1
        with_tmp = A.top
        A1b = [A.alloc(F32, [1024]) for _ in range(nv)]
        B1b = [A.alloc(F32, [1024]) for _ in range(nv)]
        self.make_crep()
        wms = [A.alloc(F32, [8, 512]) for _ in range(2)]
        tb = A.alloc(F32, [512])
        tg = A.alloc(F32, [512])
        ones = A.alloc(F32, [512])
        self.memset("pool", ones.ap, 1.0, [ones.res])
        for nt in range(4):
            wm = wms[nt % 2]
            half = nt % 2
            P.dma("sp", wm.ap, self.w_mod[:, nt * 512:(nt + 1) * 512].rearrange("(k p) n -> p k n", p=128), [], [wm.res])
            self.load_bcast_row("sp", tb, self.b_mod[nt * 512:(nt + 1) * 512])
            if nt >= 2:
                self.load_bcast_row("sp", tg, self.norm_g[0, half * 512:(half + 1) * 512])
            for v in range(nv):
                psb = self.bank[5 + v]
                dst = (B1b if nt < 2 else A1b)[v]
                d_ap = dst.ap[:, half * 512:(half + 1) * 512]
                for k in range(8):
                    self.mm(psb.ap, self.crep[v].ap[:, k, :], wm.ap[:, k, :], k == 0, k == 7, [self.crep[v].res, wm.res], [psb.res])
                if nt < 2:
                    self.tt("dve", d_ap, psb.ap, tb.ap, ALU.add, [psb.res, tb.res], [dst.res])
                else:
                    self.stt("dve", d_ap, psb.ap, 1.0, tb.ap, ALU.add, ALU.add, [psb.res, tb.res], [dst.res])
                    self.tt("dve", d_ap, d_ap, tg.ap, ALU.mult, [dst.res, tg.res], [dst.res])
        ht = [A.alloc(F32, [1024]) for _ in range(2)]
        junk = A.alloc(BF16, [1024])
        tmp = A.alloc(F32, [1024])
        src = self.x
        for t in range(32 + (2 if upd_ctx else 0)):
            if t < 32:
                h = ht[t % 2]
                P.dma("sp", h.ap, src[128 * t:128 * (t + 1), :], [], [h.res])
                h_ap, h_res, v, d_ap, d_res = h.ap, h.res, 0, hl.ap[:, t, :], hlr[t]
            else:
                i = t - 32
                h_ap, h_res, v, d_ap, d_res = self.sctx.ap[:, i, :], self.sctx_res[i], 1, hc.ap[:, i, :], hcr[i]
            rstd = self.rstd_of(h_ap, [h_res], junk)
            self.stt("dve", tmp.ap, h_ap, rstd.ap, A1b[v].ap, ALU.mult, ALU.mult, [h_res, rstd.res, A1b[v].res], [tmp.res])
            self.tt("pool", d_ap, tmp.ap, B1b[v].ap, ALU.add, [tmp.res, B1b[v].res], [d_res])
        P.barrier()
        A.top = with_tmp
        NT = 256
        ctile = [A.alloc(BF16, [32, NT]) for _ in range(2)]
        stile = [A.alloc(BF16, [32, NT]) for _ in range(2)]
        yev = [A.alloc(BF16, [8, NT]) for _ in range(4)]
        nkt = 4096 // NT
        if self.dbg.startswith("odd1"):
            nkt = 1
        for kt in range(nkt):
            ct, stl = ctile[kt % 2], stile[kt % 2]
            P.dma("sp", ct.ap, self.Ctab[:, kt * NT:(kt + 1) * NT].rearrange("(c p) k -> p c k", p=128), [self.tab_res], [ct.res])
            P.dma("sp", stl.ap, self.Stab[:, kt * NT:(kt + 1) * NT].rearrange("(c p) k -> p c k", p=128), [self.tab_res], [stl.res])
            yc, ys = yev[(2 * kt) % 4], yev[(2 * kt + 1) % 4]
            for dc in range(8):
                for (tab, yo, bi) in ((ct, yc, 1), (stl, ys, 2)):
                    pb = self.bank[bi + 2 * (dc % 2)]
                    pj = Tile(pb.ap[:, 0:NT], pb.res)
                    for c in range(32):
                        self.mm(pj.ap, hl.ap[:, c, 128 * dc:128 * (dc + 1)], tab.ap[:, c, :], c == 0, c == 31,
                                [hlr[c], tab.res], [pj.res])
                    if bi == 1:
                        self.cp("act", yo.ap[:, dc, :], pj.ap, [pj.res], [yo.res])
                    else:
                        self.cp("dve", yo.ap[:, dc, :], pj.ap, [pj.res], [yo.res])
            P.dma("sp", self.Yc[:, kt * NT:(kt + 1) * NT].rearrange("(c p) k -> p c k", p=128), yc.ap, [yc.res], [self.Y_res[kt]])
            P.dma("sp", self.Ys[:, kt * NT:(kt + 1) * NT].rearrange("(c p) k -> p c k", p=128), ys.ap, [ys.res], [self.Y_res[kt]])
        if upd_ctx:
            ct, stl = ctile[nkt % 2], stile[nkt % 2]
            csrc = self.Ctab.rearrange("(t s) k -> t s k", s=16)[:, 0, 0:256].rearrange("(c p) k -> p c k", p=128)
            ssrc = self.Stab.rearrange("(t s) k -> t s k", s=16)[:, 0, 0:256].rearrange("(c p) k -> p c k", p=128)
            P.dma("sp", ct.ap[:, 0:2, :], csrc, [self.tab_res], [ct.res])
            P.dma("sp", stl.ap[:, 0:2, :], ssrc, [self.tab_res], [stl.res])
            for dc in range(8):
                for (tab, yo, bi) in ((ct, ycc, 1), (stl, ysc, 2)):
                    pb = self.bank[bi + 2 * (dc % 2)]
                    pj = Tile(pb.ap[:, 0:256], pb.res)
                    for c in range(2):
                        self.mm(pj.ap, hc.ap[:, c, 128 * dc:128 * (dc + 1)], tab.ap[:, c, :], c == 0, c == 1,
                                [hcr[c], tab.res], [pj.res])
                    self.ts("dve", yo.ap[:, dc, :], pj.ap, 4.0, None, ALU.mult, None, [pj.res], [yo.res])
        P.barrier()
        A.top = base2
        CD = A.alloc(BF16, [8, 1024])
        SDn = A.alloc(BF16, [8, 1024])
        wf = A.alloc(BF16, [8, 1024])
        P.dma("pool", wf.ap, self.w_fourier.rearrange("(k p) n -> p k n", p=128), [], [wf.res])
        yct = [A.alloc(BF16, [8, 256]) for _ in range(2)]
        yst = [A.alloc(BF16, [8, 256]) for _ in range(2)]
        hfT = [A.alloc(BF16, [8, 256]) for _ in range(2)]
        ht = [A.alloc(F32, [1024]) for _ in range(2)]
        junk = A.alloc(BF16, [1024])
        tmp = A.alloc(F32, [1024])
        ycc2, ysc2 = ycc, ysc
        mark = A.top
        self.gen_channel_tables(CD, SDn)
        A.top = mark
        units = [("lat", kt) for kt in range(nkt)] + ([("ctx", 0)] if upd_ctx else [])
        for ui, (kind, kt) in enumerate(units):
            v = 0 if kind == "lat" else 1
            if kind == "lat":
                yc, ys = yct[ui % 2], yst[ui % 2]
                P.dma("sp", yc.ap, self.Yc[:, kt * 256:(kt + 1) * 256].rearrange("(c p) k -> p c k", p=128), [self.Y_res[kt]], [yc.res])
                P.dma("sp", ys.ap, self.Ys[:, kt * 256:(kt + 1) * 256].rearrange("(c p) k -> p c k", p=128), [self.Y_res[kt]], [ys.res])
            else:
                yc, ys = ycc2, ysc2
            hf = hfT[ui % 2]
            for ec in range(8):
                pb = self.bank[1 + (ec % 2)]
                pj = Tile(pb.ap[:, 0:256], pb.res)
                for dc in range(8):
                    self.mm(pj.ap, CD.ap[:, dc, 128 * ec:128 * (ec + 1)], yc.ap[:, dc, :], dc == 0, False, [CD.res, yc.res], [pj.res])
                for dc in range(8):
                    self.mm(pj.ap, SDn.ap[:, dc, 128 * ec:128 * (ec + 1)], ys.ap[:, dc, :], False, dc == 7, [SDn.res, ys.res], [pj.res])
                self.cp("act" if ec % 2 else "dve", hf.ap[:, ec, :], pj.ap, [pj.res], [hf.res])
            for sub in range(2):
                for half in range(2):
                    pb = self.bank[3 + 2 * sub + half]
                    for ec in range(8):
                        self.mm(pb.ap, hf.ap[:, ec, sub * 128:(sub + 1) * 128], wf.ap[:, ec, half * 512:(half + 1) * 512],
                                ec == 0, ec == 7, [hf.res, wf.res], [pb.res])
                po_ap = self.psall[:, (3 + 2 * sub) * 512:(5 + 2 * sub) * 512]
                pres = [self.bank[3 + 2 * sub].res, self.bank[4 + 2 * sub].res]
                if kind == "lat":
                    t128 = kt * 2 + sub
                    h = ht[sub]
                    P.dma("sp", h.ap, self.x[128 * t128:128 * (t128 + 1), :], [], [h.res])
                    h_ap, h_res = h.ap, h.res
                else:
                    h_ap, h_res = self.sctx.ap[:, sub, :], self.sctx_res[sub]
                rstd = self.rstd_of(po_ap, pres, junk)
                self.stt("dve", tmp.ap, po_ap, rstd.ap, self.G[0][v].ap, ALU.mult, ALU.mult,
                         pres + [rstd.res, self.G[0][v].res], [tmp.res])
                self.tt("pool", h_ap, tmp.ap, h_ap, ALU.add, [tmp.res, h_res], [h_res])
                if kind == "lat":
                    P.dma("sp", self.out[128 * t128:128 * (t128 + 1), :], h_ap, [h_res], [self.out_res[t128]])


_CACHE = {}


def _get_nc(step, dbg=""):
    key = (step, dbg)
    if key not in _CACHE:
        _CACHE[key] = Builder(step, dbg).build()
    return _CACHE[key]


def _launch(step, per_core, shared, dbg=""):
    nc = _get_nc(step, dbg)
    in_maps = []
    for b in range(len(per_core)):
        m = dict(shared)
        m.update(per_core[b])
        in_maps.append(m)
    res = run_bass_kernel_spmd(nc, in_maps, core_ids=list(range(len(per_core))))
    return res.results


def _f32(a):
    return np.ascontiguousarray(a, dtype=np.float32)


def run_steps(inputs, steps, ncores=8, dbg=""):
    h = [_f32(inputs["x"][b]) for b in range(ncores)]
    s = [_f32(inputs["ctx"][b]) for b in range(ncores)]
    cs = [_f32(inputs["c"][b]) for b in range(ncores)]
    s5T = None
    for step in steps:
        kind, l = step
        e = l // 2
        shared = {"c_ctx": _f32(inputs["c_ctx"]), "w_mod": _f32(inputs["w_mod"][l]), "b_mod": _f32(inputs["b_mod"][l]),
                  "norm_g": _f32(inputs["norm_g"][l])}
        if kind == "mlp":
            shared["w_ff1"] = _f32(inputs["w_ff1"][l])
            shared["w_ff2"] = _f32(inputs["w_ff2"][l])
        elif kind == "odd":
            shared["w_fourier"] = _f32(inputs["w_fourier"][e])
        elif kind == "evenA":
            shared["w_in"] = _f32(inputs["w_in"][e])
            for n in ["s5_lam_re", "s5_lam_im", "s5_log_dt", "s5_b_re", "s5_b_im", "s5_c_re", "s5_c_im", "s5_d", "s5_w_glu"]:
                shared[n] = _f32(inputs[n][e])
        elif kind == "evenB":
            shared["w_in"] = _f32(inputs["w_in"][e])
            shared["w_out_even"] = _f32(inputs["w_out_even"][e])
            shared["na_rpb"] = _f32(inputs["na_rpb"][e])
        per_core = []
        for b in range(ncores):
            m = {"x": h[b], "c": cs[b], "ctx": s[b]}
            if kind == "evenB":
                m["s5T"] = s5T[b]
            per_core.append(m)
        res = _launch(step, per_core, shared, dbg)
        if kind == "evenA":
            s5T = [np.ascontiguousarray(r["s5T"]) for r in res]
        else:
            h = [_f32(r["out"]) for r in res]
            s = [_f32(r["sout"]) for r in res]
    return h, s, s5T


def kernel_multi(**inputs):
    steps = []
    for l in range(DEPTH):
        if l % 2 == 0:
            steps += [("evenA", l), ("evenB", l)]
        else:
            steps += [("odd", l)]
        steps += [("mlp", l)]
    h, s, _ = run_steps(inputs, steps, 8)
    return np.stack(h, 0)


def kernel(**inputs):
    nc = _get_nc(("fused", None))
    names = ["c_ctx", "w_mod", "b_mod", "norm_g", "w_in", "w_out_even", "s5_lam_re", "s5_lam_im", "s5_log_dt", "s5_b_re", "s5_b_im",
             "s5_c_re", "s5_c_im", "s5_d", "s5_w_glu", "na_rpb", "w_fourier", "w_ff1", "w_ff2"]
    shared = {n: _f32(inputs[n]) for n in names}
    ncores = 8
    in_maps = []
    for b in range(ncores):
        m = dict(shared)
        m["x"] = _f32(inputs["x"][b])
        m["c"] = _f32(inputs["c"][b])
        m["ctx"] = _f32(inputs["ctx"][b])
        in_maps.append(m)
    res = run_bass_kernel_spmd(nc, in_maps, core_ids=list(range(ncores)))
    return np.stack([_f32(r["out"]) for r in res.results], 0)
```

```python
import contextlib
import math
import os

import numpy as np
import concourse.bass as bass
import concourse.mybir as mybir
from concourse.bass_utils import run_bass_kernel_spmd

F32 = mybir.dt.float32
BF16 = mybir.dt.bfloat16
I32 = mybir.dt.int32
AF = mybir.ActivationFunctionType
ALU = mybir.AluOpType

D = 1024
L = 4096
LC = 256
DFF = 4096
DEPTH = 4
EPS = 1e-6
ENGS = ["pe", "act", "dve", "pool", "sp"]


class Res:
    __slots__ = ("name", "lastw", "readers")

    def __init__(self, name="r"):
        self.name = name
        self.lastw = None
        self.readers = []


class Op:
    __slots__ = ("eng", "fn", "deps", "is_dma", "signal", "sigval", "dslot", "dval")


class Prog:
    NDSLOT = 16

    def __init__(self, nc):
        self.nc = nc
        self.ops = []
        self.per = {e: [] for e in ENGS}
        self.ndma = {e: 0 for e in ENGS}
        self.pending_barrier = {e: None for e in ENGS}
        self.dmas_since_barrier = []

    def barrier(self):
        deps = []
        for e in ENGS:
            for op in reversed(self.per[e]):
                if not op.is_dma:
                    deps.append(op)
                    break
        deps.extend(self.dmas_since_barrier)
        self.dmas_since_barrier = []
        for e in ENGS:
            old = self.pending_barrier[e]
            self.pending_barrier[e] = (old or []) + deps

    def _add(self, eng, fn, reads, writes, is_dma):
        op = Op()
        op.eng = eng
        op.fn = fn
        op.is_dma = is_dma
        op.signal = False
        deps = set()
        for r in reads:
            if r.lastw is not None:
                deps.add(r.lastw)
        for w in writes:
            if w.lastw is not None:
                deps.add(w.lastw)
            for rd in w.readers:
                deps.add(rd)
        if self.pending_barrier[eng] is not None:
            deps.update(self.pending_barrier[eng])
            self.pending_barrier[eng] = None
        deps.discard(op)
        op.deps = [d for d in deps if not (eng == "pe" and d.eng == "pe" and not d.is_dma and not is_dma)]
        for r in reads:
            r.readers.append(op)
        for w in writes:
            w.lastw = op
            w.readers = []
        self.per[eng].append(op)
        self.ops.append(op)
        if is_dma:
            k = self.ndma[eng]
            self.ndma[eng] += 1
            op.dslot = k % self.NDSLOT
            op.dval = 16 * (k // self.NDSLOT + 1)
            self.dmas_since_barrier.append(op)
        return op

    def op(self, eng, fn, reads=(), writes=()):
        return self._add(eng, fn, list(reads), list(writes), False)

    def dma(self, eng, out, in_, reads=(), writes=()):
        return self._add(eng, lambda e: e.dma_start(out=out, in_=in_), list(reads), list(writes), True)

    def emit(self):
        nc = self.nc
        for op in self.ops:
            for d in op.deps:
                if not d.is_dma:
                    d.signal = True
        for e in ENGS:
            cnt = 0
            for op in self.per[e]:
                if not op.is_dma and op.signal:
                    cnt += 1
                    op.sigval = cnt
        with contextlib.ExitStack() as st:
            sems = {e: st.enter_context(nc.semaphore("s_" + e)) for e in ENGS}
            dsems = {e: [st.enter_context(nc.semaphore("d_%s%d" % (e, i))) for i in range(self.NDSLOT)]
                     for e in ENGS if self.ndma[e] > 0}
            block = st.enter_context(nc.Block())

            def run_engine(ename, eng):
                seen = {}
                dseen = {}
                for op in self.per[ename]:
                    for d in op.deps:
                        if d.is_dma:
                            key = (d.eng, d.dslot)
                            if dseen.get(key, 0) < d.dval:
                                eng.wait_ge(dsems[d.eng][d.dslot], d.dval)
                                dseen[key] = d.dval
                        else:
                            if seen.get(d.eng, 0) < d.sigval:
                                eng.wait_ge(sems[d.eng], d.sigval)
                                seen[d.eng] = d.sigval
                    if op.is_dma:
                        key = (ename, op.dslot)
                        if op.dval > 16 and dseen.get(key, 0) < op.dval - 16:
                            eng.wait_ge(dsems[ename][op.dslot], op.dval - 16)
                            dseen[key] = op.dval - 16
                        ins = op.fn(eng)
                        ins.then_inc(dsems[ename][op.dslot], 16)
                    else:
                        ins = op.fn(eng)
                        if op.signal:
                            ins.then_inc(sems[ename], 1)
                if ename in dsems:
                    k = self.ndma[ename]
                    for s in range(self.NDSLOT):
                        n = (k - s + self.NDSLOT - 1) // self.NDSLOT
                        if n > 0 and dseen.get((ename, s), 0) < 16 * n:
                            eng.wait_ge(dsems[ename][s], 16 * n)

            block.tensor(lambda e: run_engine("pe", e))
            block.scalar(lambda e: run_engine("act", e))
            block.vector(lambda e: run_engine("dve", e))
            block.gpsimd(lambda e: run_engine("pool", e))
            block.sync(lambda e: run_engine("sp", e))


class Tile:
    __slots__ = ("ap", "res")

    def __init__(self, ap, res=None):
        self.ap = ap
        self.res = res or Res()

    def __getitem__(self, k):
        return self.ap[k]


class Arena:
    def __init__(self, tensor, ncols):
        self.t = tensor
        self.ncols = ncols
        self.top = 0
        self.base = 0

    def alloc(self, dtype, shape):
        n = 1
        for s in shape:
            n *= s
        units = n * (2 if dtype in (F32, I32) else 1)
        units = (units + 31) // 32 * 32
        assert self.top + units <= self.ncols, ("arena overflow", self.top, units, self.ncols)
        ap = self.t[:, self.top:self.top + n * (2 if dtype in (F32, I32) else 1)]
        self.top += units
        self.maxtop = max(getattr(self, "maxtop", 0), self.top)
        if dtype != BF16:
            ap = ap.bitcast(dtype)
        if len(shape) == 2:
            ap = ap.rearrange("p (a b) -> p a b", a=shape[0])
        elif len(shape) == 3:
            ap = ap.rearrange("p (a b c) -> p a b c", a=shape[0], b=shape[1])
        return Tile(ap)

    def mark_persistent(self):
        self.base = self.top

    def reset(self):
        self.top = self.base


class Builder:
    def __init__(self, step, dbg=""):
        self.dbg = dbg
        self.step = step
        kind, l = step
        nc = bass.Bass("TRN2", target_bir_lowering=False)
        self.nc = nc
        self.P = Prog(nc)

        def din(name, shape, dt=F32):
            return nc.dram_tensor(name, list(shape), dt, kind="ExternalInput").ap()

        if kind == "fused":
            self.init_fused(din)
            return
        self.x = din("x", [L, D])
        self.c = din("c", [D])
        self.ctx = din("ctx", [LC, D])
        self.c_ctx = din("c_ctx", [D])
        self.w_mod = din("w_mod", [D, 6 * D])
        self.b_mod = din("b_mod", [6 * D])
        self.norm_g = din("norm_g", [4, D])
        if kind == "mlp":
            self.w_ff1 = din("w_ff1", [D, DFF])
            self.w_ff2 = din("w_ff2", [DFF, D])
        if kind == "odd":
            self.w_fourier = din("w_fourier", [D, D])
        if kind in ("evenA", "evenB"):
            self.w_in = din("w_in", [D, 2048])
        if kind == "evenA":
            self.s5_lam_re = din("s5_lam_re", [2, 32, 64])
            self.s5_lam_im = din("s5_lam_im", [2, 32, 64])
            self.s5_log_dt = din("s5_log_dt", [2, 32])
            self.s5_b_re = din("s5_b_re", [2, 32, 64, 16])
            self.s5_b_im = din("s5_b_im", [2, 32, 64, 16])
            self.s5_c_re = din("s5_c_re", [2, 32, 16, 64])
            self.s5_c_im = din("s5_c_im", [2, 32, 16, 64])
            self.s5_d = din("s5_d", [512])
            self.s5_w_glu = din("s5_w_glu", [512, 512])
            self.s5T_out = nc.dram_tensor("s5T", [512, L + LC], BF16, kind="ExternalOutput").ap()
        if kind == "evenB":
            self.w_out_even = din("w_out_even", [D, D])
            self.na_rpb = din("na_rpb", [8, 15, 31])
            self.s5T_in = din("s5T", [512, L + LC], BF16)
        if kind != "evenA":
            self.out = nc.dram_tensor("out", [L, D], F32, kind="ExternalOutput").ap()
            self.sout = nc.dram_tensor("sout", [LC, D], F32, kind="ExternalOutput").ap()
        self.out_res = [Res("out%d" % i) for i in range(L // 128)]
        if kind == "odd":
            self.Ctab = nc.dram_tensor("Ctab", [L, L], BF16).ap()
            self.Stab = nc.dram_tensor("Stab", [L, L], BF16).ap()
            self.tab_res = Res("tab")
            self.Yc = nc.dram_tensor("Yc", [D, L], BF16).ap()
            self.Ys = nc.dram_tensor("Ys", [D, L], BF16).ap()
            self.Y_res = [Res("Y%d" % i) for i in range(16)]

    def init_fused(self, din):
        nc = self.nc
        self.x_in = din("x", [L, D])
        self.c = din("c", [D])
        self.ctx = din("ctx", [LC, D])
        self.c_ctx = din("c_ctx", [D])
        W = {}
        W["w_mod"] = din("w_mod", [DEPTH, D, 6 * D])
        W["b_mod"] = din("b_mod", [DEPTH, 6 * D])
        W["norm_g"] = din("norm_g", [DEPTH, 4, D])
        W["w_in"] = din("w_in", [2, D, 2048])
        W["w_out_even"] = din("w_out_even", [2, D, D])
        W["s5_lam_re"] = din("s5_lam_re", [2, 2, 32, 64])
        W["s5_lam_im"] = din("s5_lam_im", [2, 2, 32, 64])
        W["s5_log_dt"] = din("s5_log_dt", [2, 2, 32])
        W["s5_b_re"] = din("s5_b_re", [2, 2, 32, 64, 16])
        W["s5_b_im"] = din("s5_b_im", [2, 2, 32, 64, 16])
        W["s5_c_re"] = din("s5_c_re", [2, 2, 32, 16, 64])
        W["s5_c_im"] = din("s5_c_im", [2, 2, 32, 16, 64])
        W["s5_d"] = din("s5_d", [2, 512])
        W["s5_w_glu"] = din("s5_w_glu", [2, 512, 512])
        W["na_rpb"] = din("na_rpb", [2, 8, 15, 31])
        W["w_fourier"] = din("w_fourier", [2, D, D])
        W["w_ff1"] = din("w_ff1", [DEPTH, D, DFF])
        W["w_ff2"] = din("w_ff2", [DEPTH, DFF, D])
        self.W = W
        self.out_final = nc.dram_tensor("out", [L, D], F32, kind="ExternalOutput").ap()
        self.hA = nc.dram_tensor("hA", [L, D], F32).ap()
        s5 = nc.dram_tensor("s5T", [512, L + LC], BF16).ap()
        self.s5T_out = s5
        self.s5T_in = s5
        self.out_res = [Res("out%d" % i) for i in range(L // 128)]
        self.Ctab = nc.dram_tensor("Ctab", [L, L], BF16).ap()
        self.Stab = nc.dram_tensor("Stab", [L, L], BF16).ap()
        self.tab_res = Res("tab")
        self.Yc = nc.dram_tensor("Yc", [D, L], BF16).ap()
        self.Ys = nc.dram_tensor("Ys", [D, L], BF16).ap()
        self.Y_res = [Res("Y%d" % i) for i in range(16)]

    def set_layer(self, l):
        W = self.W
        e = l // 2
        self.w_mod = W["w_mod"][l]
        self.b_mod = W["b_mod"][l]
        self.norm_g = W["norm_g"][l]
        self.w_ff1 = W["w_ff1"][l]
        self.w_ff2 = W["w_ff2"][l]
        if l % 2 == 0:
            self.w_in = W["w_in"][e]
            self.w_out_even = W["w_out_even"][e]
            for n in ["s5_lam_re", "s5_lam_im", "s5_log_dt", "s5_b_re", "s5_b_im", "s5_c_re", "s5_c_im", "s5_d", "s5_w_glu", "na_rpb"]:
                setattr(self, n, W[n][e])
        else:
            self.w_fourier = W["w_fourier"][e]

    def build_fused(self):
        P = self.P
        self.setup_consts()
        bufs = [self.x_in, self.hA, self.out_final]
        step = 0
        for l in range(DEPTH):
            self.set_layer(l)
            self.mod_phase(l)
            self.x = bufs[0] if step == 0 else (self.out_final if step % 2 == 0 else self.hA)
            self.out = self.hA if step % 2 == 0 else self.out_final
            if l % 2 == 0:
                self.even_a(l)
                self.even_b(l)
            else:
                if l == 1:
                    self.gen_tables()
                self.odd_mixer(l)
            step += 1
            self.x = self.out_final if step % 2 == 0 else self.hA
            self.out = self.hA if step % 2 == 0 else self.out_final
            self.mlp_phase(l)
            step += 1

    def mm(self, out, lhsT, rhs, start, stop, reads, writes):
        self.P.op("pe", lambda e: e.matmul(out, lhsT=lhsT, rhs=rhs, start=start, stop=stop), reads, writes)

    def tr(self, out, in_, ident, reads, writes):
        self.P.op("pe", lambda e: e.transpose(out, in_, ident), reads, writes)

    def act(self, out, in_, func, reads, writes, bias=None, scale=None, accum_out=None, eng="act"):
        kw = {}
        if bias is not None:
            kw["bias"] = bias
        if scale is not None:
            kw["scale"] = scale
        if accum_out is not None:
            kw["accum_out"] = accum_out
        self.P.op(eng, lambda e: e.activation(out=out, in_=in_, func=func, **kw), reads, writes)

    def tt(self, eng, out, in0, in1, op, reads, writes):
        self.P.op(eng, lambda e: e.tensor_tensor(out=out, in0=in0, in1=in1, op=op), reads, writes)

    def ts(self, eng, out, in0, s1, s2, op0, op1, reads, writes):
        if op1 is None:
            self.P.op(eng, lambda e: e.tensor_single_scalar(out=out, in_=in0, scalar=s1, op=op0), reads, writes)
        else:
            self.P.op(eng, lambda e: e.tensor_scalar(out=out, in0=in0, scalar1=s1, scalar2=s2, op0=op0, op1=op1), reads, writes)

    def stt(self, eng, out, in0, scalar, in1, op0, op1, reads, writes):
        self.P.op(eng, lambda e: e.scalar_tensor_tensor(out=out, in0=in0, scalar=scalar, in1=in1, op0=op0, op1=op1), reads, writes)

    def cp(self, eng, out, in_, reads, writes):
        if eng == "act":
            self.P.op(eng, lambda e: e.copy(out=out, in_=in_), reads, writes)
        else:
            self.P.op(eng, lambda e: e.tensor_copy(out=out, in_=in_), reads, writes)

    def memset(self, eng, ap, val, writes):
        self.P.op(eng, lambda e: e.memset(ap, val), [], writes)

    def build(self):
        nc = self.nc
        P = self.P
        with contextlib.ExitStack() as st:
            NCOLS = 106400
            arena_t = st.enter_context(nc.sbuf_tensor("arena", [128, NCOLS], BF16))
            self.A = Arena(arena_t, NCOLS)
            psall = st.enter_context(nc.psum_tensor("psall", [128, 4096], F32))
            self.psall = psall
            self.bank = [Tile(psall[:, 512 * i:512 * (i + 1)], Res("bank%d" % i)) for i in range(8)]
            kind, l = self.step
            if kind == "fused":
                self.build_fused()
                P.emit()
                return nc
            self.setup_consts()
            self.mod_phase(l)
            if kind == "mlp":
                self.mlp_phase(l)
            elif kind == "odd":
                self.gen_tables()
                self.odd_mixer(l)
            elif kind == "evenA":
                self.even_a(l)
            elif kind == "evenB":
                self.even_b(l)
            if kind != "evenA":
                P.barrier()
                for i in range(2):
                    P.dma("sp", self.sout[128 * i:128 * (i + 1), :], self.sctx.ap[:, i, :], [self.sctx_res[i]], [Res()])
            P.emit()
        return nc

    def setup_consts(self):
        A = self.A
        P = self.P
        it = A.alloc(I32, [128])
        self.ident_f = A.alloc(F32, [128])
        self.ident_b = A.alloc(BF16, [128])
        P.op("pool", lambda e: e.iota(it.ap, [[1, 128]], base=0, channel_multiplier=-1), [], [it.res])
        self.cp("dve", self.ident_f.ap, it.ap, [it.res], [self.ident_f.res])
        self.ts("dve", self.ident_f.ap, self.ident_f.ap, 0.0, None, ALU.is_equal, None, [self.ident_f.res], [self.ident_f.res])
        self.cp("dve", self.ident_b.ap, self.ident_f.ap, [self.ident_f.res], [self.ident_b.res])
        self.cact2 = A.alloc(F32, [8, 2])
        craw = A.alloc(F32, [2, 128])
        P.dma("sp", craw.ap[0:8, 0, :], self.c.rearrange("(k p) -> k p", p=128), [], [craw.res])
        P.dma("sp", craw.ap[0:8, 1, :], self.c_ctx.rearrange("(k p) -> k p", p=128), [], [craw.res])
        for v in range(2):
            pv = Tile(self.bank[7].ap[:, 8 * v:8 * v + 8], self.bank[7].res)
            self.tr(pv.ap, craw.ap[0:8, v, :], self.ident_f.ap[0:8, 0:8], [craw.res, self.ident_f.res], [pv.res])
            self.act(self.cact2.ap[:, :, v], pv.ap, AF.Silu, [pv.res], [self.cact2.res])
        self.modpp = A.alloc(F32, [48, 2])
        self.gpp = A.alloc(F32, [4, 8])
        self.A1 = A.alloc(F32, [8, 2])
        self.A2 = A.alloc(F32, [8, 2])
        self.G = [[A.alloc(F32, [1024]) for v in range(2)] for i in range(2)]
        self.sctx = A.alloc(F32, [2, 1024])
        self.sctx_res = [Res("sctx0"), Res("sctx1")]
        for i in range(2):
            P.dma("sp", self.sctx.ap[:, i, :], self.ctx[128 * i:128 * (i + 1), :], [], [self.sctx_res[i]])
        self.negpi = A.alloc(F32, [1])
        self.memset("pool", self.negpi.ap, -math.pi, [self.negpi.res])
        self.small = A.alloc(F32, [64])
        self.small_tiles = [Tile(self.small.ap[:, i:i + 1]) for i in range(64)]
        self.small_n = 0
        A.mark_persistent()

    def make_crep(self):
        A = self.A
        self.crep = [A.alloc(F32, [8, 128]) for _ in range(2)]
        ones = A.alloc(F32, [128])
        self.memset("pool", ones.ap, 1.0, [ones.res])
        for v in range(2):
            for k in range(8):
                self.ts("dve", self.crep[v].ap[:, k, :], ones.ap, self.cact2.ap[:, k, v:v + 1], None, ALU.mult, None,
                        [ones.res, self.cact2.res], [self.crep[v].res])

    def scalar_slot(self):
        i = self.small_n % 64
        self.small_n += 1
        return self.small_tiles[i]

    def bcast_tile(self, l, ntile, v, wm, dst_ap, dst_res, gain_idx, tmpb, tmpg, psb):
        P = self.P
        for k in range(8):
            self.mm(psb.ap, self.crep[v].ap[:, k, :], wm.ap[:, k, :], k == 0, k == 7, [self.crep[v].res, wm.res], [psb.res])
        if gain_idx is None:
            self.tt("dve", dst_ap, psb.ap, tmpb.ap, ALU.add, [psb.res, tmpb.res], [dst_res])
        else:
            self.tt("dve", dst_ap, psb.ap, tmpb.ap, ALU.add, [psb.res, tmpb.res], [dst_res])
            self.tt("dve", dst_ap, dst_ap, tmpg.ap, ALU.mult, [dst_res, tmpg.res], [dst_res])

    def load_bcast_row(self, eng, tile, src_row):
        self.P.dma(eng, tile.ap, src_row.partition_broadcast(128), [], [tile.res])

    def mod_phase(self, l):
        A = self.A
        P = self.P
        P.barrier()
        A.reset()
        self.make_crep()
        wms = [A.alloc(F32, [8, 512]) for _ in range(2)]
        bpp = A.alloc(F32, [48])
        tmpb = [A.alloc(F32, [512]) for _ in range(2)]
        tmpg = [A.alloc(F32, [512]) for _ in range(2)]
        braw = A.alloc(F32, [128])
        graw = A.alloc(F32, [128])
        P.dma("sp", braw.ap[0:48, :], self.b_mod.rearrange("(j p) -> j p", p=128), [], [braw.res])
        P.dma("sp", graw.ap[0:32, :], self.norm_g.rearrange("g (k p) -> (g k) p", p=128), [], [graw.res])
        pb_ = Tile(self.bank[7].ap[:, 64:112], self.bank[7].res)
        self.tr(pb_.ap, braw.ap[0:48, :], self.ident_f.ap[0:48, 0:48], [braw.res, self.ident_f.res], [pb_.res])
        self.cp("dve", bpp.ap, pb_.ap, [pb_.res], [bpp.res])
        pg_ = Tile(self.bank[7].ap[:, 128:160], self.bank[7].res)
        self.tr(pg_.ap, graw.ap[0:32, :], self.ident_f.ap[0:32, 0:32], [graw.res, self.ident_f.res], [pg_.res])
        self.cp("dve", self.gpp.ap, pg_.ap.rearrange("p (g k) -> p g k", g=4), [pg_.res], [self.gpp.res])
        psA = Tile(self.bank[7].ap[:, 0:8], self.bank[7].res)
        for nt in range(12):
            wm = wms[nt % 2]
            P.dma("sp", wm.ap, self.w_mod[:, nt * 512:(nt + 1) * 512].rearrange("(k p) n -> p k n", p=128), [], [wm.res])
            for jj in range(4):
                for k in range(8):
                    self.mm(psA.ap[:, 2 * jj:2 * jj + 2], wm.ap[:, k, jj * 128:(jj + 1) * 128], self.cact2.ap[:, k, :],
                            k == 0, k == 7, [wm.res, self.cact2.res], [psA.res])
            self.tt("dve", self.modpp.ap[:, nt * 4:(nt + 1) * 4, :], psA.ap.rearrange("p (a b) -> p a b", b=2),
                    bpp.ap[:, nt * 4:(nt + 1) * 4].unsqueeze(2).broadcast_to([128, 4, 2]), ALU.add,
                    [psA.res, bpp.res], [self.modpp.res])
            gi = {4: (0, 0), 5: (0, 1), 10: (1, 0), 11: (1, 1)}.get(nt)
            if gi is not None:
                i, half = gi
                tb = tmpb[half]
                tg = tmpg[half]
                self.load_bcast_row("sp", tb, self.b_mod[nt * 512:(nt + 1) * 512])
                self.load_bcast_row("sp", tg, self.norm_g[1 + 2 * i, half * 512:(half + 1) * 512])
                for v in range(2):
                    psb = self.bank[5 + v]
                    self.bcast_tile(l, nt, v, wm, self.G[i][v].ap[:, half * 512:(half + 1) * 512], self.G[i][v].res, 1, tb, tg, psb)
        for (Ax, sc_off, gidx) in ((self.A1, 8, 0), (self.A2, 32, 2)):
            self.stt("dve", Ax.ap, self.modpp.ap[:, sc_off:sc_off + 8, :], 1.0,
                     self.gpp.ap[:, gidx, :].unsqueeze(2).broadcast_to([128, 8, 2]), ALU.add, ALU.mult,
                     [self.modpp.res, self.gpp.res], [Ax.res])

    def rstd_of(self, src_ap, src_reads, junk, n=1024):
        ss = self.scalar_slot()
        self.memset("pool", ss.ap, 0.0, [ss.res])
        self.act(junk.ap, src_ap, AF.Square, src_reads + [ss.res], [junk.res, ss.res], accum_out=ss.ap)
        self.act(ss.ap, ss.ap, AF.Sqrt, [ss.res], [ss.res], bias=EPS, scale=1.0 / n)
        self.P.op("dve", lambda e: e.reciprocal(out=ss.ap, in_=ss.ap), [ss.res], [ss.res])
        return ss

    def prenorm_a(self, h_ap, h_res, hs):
        rstd = self.rstd_of(h_ap, [h_res], hs)
        self.act(hs.ap, h_ap, AF.Copy, [h_res, rstd.res], [hs.res], scale=rstd.ap)

    def prenorm_b(self, Ax, Bx_ap, Bx_res, v, hs, dstT_ap, dstT_res, psT):
        pst = psT.ap.bitcast(BF16)
        for k in range(8):
            self.tr(pst[:, k * 128:(k + 1) * 128], hs.ap[:, k * 128:(k + 1) * 128], self.ident_b.ap,
                    [hs.res, self.ident_b.res], [psT.res])
        p3 = pst.rearrange("p (a b) -> p a b", a=8)
        self.tt("dve", dstT_ap, p3, Ax.ap[:, :, v].unsqueeze(2).broadcast_to([128, 8, 128]), ALU.mult,
                [psT.res, Ax.res], [dstT_res])
        self.tt("pool", dstT_ap, dstT_ap, Bx_ap[:, :, v].unsqueeze(2).broadcast_to([128, 8, 128]), ALU.add,
                [dstT_res, Bx_res], [dstT_res])

    def prenorm_T(self, h_ap, h_res, Ax, Bx_ap, Bx_res, v, hs, dstT_ap, dstT_res, psT):
        self.prenorm_a(h_ap, h_res, hs)
        self.prenorm_b(Ax, Bx_ap, Bx_res, v, hs, dstT_ap, dstT_res, psT)

    def postnorm_residual(self, po_ap, po_res, Gt, h_ap, h_res, junk, tmp):
        rstd = self.rstd_of(po_ap, [po_res], junk)
        self.stt("dve", tmp.ap, po_ap, rstd.ap, Gt.ap, ALU.mult, ALU.mult, [po_res, rstd.res, Gt.res], [tmp.res])
        self.tt("pool", h_ap, tmp.ap, h_ap, ALU.add, [tmp.res, h_res], [h_res])

    def mlp_phase(self, l):
        A = self.A
        P = self.P
        P.barrier()
        A.reset()
        w1 = A.alloc(BF16, [8, 4096])
        w2 = A.alloc(BF16, [32, 1024])
        w1r = [Res() for _ in range(8)]
        w2r = [Res() for _ in range(8)]
        for k in range(8):
            P.dma("pool", w1.ap[:, k, :], self.w_ff1[128 * k:128 * (k + 1), :], [], [w1r[k]])
        for q in range(8):
            P.dma("pool", w2.ap[:, 4 * q:4 * q + 4, :],
                  self.w_ff2[512 * q:512 * (q + 1), :].rearrange("(j p) n -> p j n", p=128), [], [w2r[q]])
        hid = A.alloc(BF16, [32, 256])
        hidr = [Res() for _ in range(32)]
        hnT = A.alloc(BF16, [8, 256])
        ht = [A.alloc(F32, [1024]) for _ in range(2)]
        hs = A.alloc(BF16, [1024])
        tmp = A.alloc(F32, [1024])
        rl = [A.alloc(F32, [256]) for _ in range(2)]
        B2_ap = self.modpp.ap[:, 24:32, :]
        upd_ctx = l < 2
        tiles = [("lat", i) for i in range(16)] + ([("ctx", 0)] if upd_ctx else [])
        if self.dbg.startswith("mlp1"):
            tiles = tiles[:1]
        ht4 = ht + [A.alloc(F32, [1024]) for _ in range(2)]
        hs2 = [hs, A.alloc(BF16, [1024])]
        junk2 = hs2[0]

        def pre(idx):
            kind, ti = tiles[idx]
            hview = []
            for sub in range(2):
                if kind == "lat":
                    t128 = ti * 2 + sub
                    h = ht4[2 * (idx % 2) + sub]
                    P.dma("sp", h.ap, self.x[128 * t128:128 * (t128 + 1), :], [], [h.res])
                    hview.append((h.ap, h.res))
                else:
                    hview.append((self.sctx.ap[:, sub, :], self.sctx_res[sub]))
                self.prenorm_a(hview[sub][0], hview[sub][1], hs2[sub])
            return hview

        def pre_b(idx):
            kind, ti = tiles[idx]
            v = 0 if kind == "lat" else 1
            for sub in range(2):
                self.prenorm_b(self.A2, B2_ap, self.modpp.res, v, hs2[sub], hnT.ap[:, :, sub * 128:(sub + 1) * 128], hnT.res, self.bank[0])

        hv_next = pre(0)
        pre_b(0)
        for idx, (kind, ti) in enumerate(tiles):
            v = 0 if kind == "lat" else 1
            hview = hv_next
            for j in range(32):
                pb = self.bank[1 + (j % 2)]
                pj = Tile(pb.ap[:, 0:256], pb.res)
                for k in range(8):
                    self.mm(pj.ap, w1.ap[:, k, 128 * j:128 * (j + 1)], hnT.ap[:, k, :], k == 0, k == 7,
                            [w1r[k], hnT.res], [pj.res])
                r = rl[j % 2]
                self.act(r.ap, pj.ap, AF.Relu, [pj.res], [r.res])
                self.tt("pool" if j % 2 else "dve", hid.ap[:, j, :], r.ap, r.ap, ALU.mult, [r.res], [hidr[j]])
            if idx + 1 < len(tiles):
                hv_next = pre(idx + 1)
            for sub in range(2):
                for half in range(2):
                    pb = self.bank[3 + 2 * sub + half]
                    for j in range(32):
                        self.mm(pb.ap, hid.ap[:, j, sub * 128:(sub + 1) * 128], w2.ap[:, j, half * 512:(half + 1) * 512],
                                j == 0, j == 31, [hidr[j], w2r[j // 4]], [pb.res])
            if idx + 1 < len(tiles):
                pre_b(idx + 1)
            for sub in range(2):
                po_ap = self.psall[:, (3 + 2 * sub) * 512:(5 + 2 * sub) * 512]
                pres = [self.bank[3 + 2 * sub].res, self.bank[4 + 2 * sub].res]
                rstd = self.rstd_of(po_ap, pres, junk2)
                self.stt("dve", tmp.ap, po_ap, rstd.ap, self.G[1][v].ap, ALU.mult, ALU.mult,
                         pres + [rstd.res, self.G[1][v].res], [tmp.res])
                h_ap, h_res = hview[sub]
                self.tt("pool", h_ap, tmp.ap, h_ap, ALU.add, [tmp.res, h_res], [h_res])
                if kind == "lat":
                    t128 = ti * 2 + sub
                    P.dma("sp", self.out[128 * t128:128 * (t128 + 1), :], h_ap, [h_res], [self.out_res[t128]])

    def cmul(self, eng, out_re, out_im, a_re, a_im, b_re, b_im, t1, t2, reads, wres, neg_im=False):
        rs_ = reads
        self.tt(eng, t1.ap, a_re, b_re, ALU.mult, rs_, [t1.res])
        self.tt(eng, t2.ap, a_im, b_im, ALU.mult, rs_, [t2.res])
        self.tt(eng, out_re, t1.ap, t2.ap, ALU.subtract, [t1.res, t2.res], wres)
        self.tt(eng, t1.ap, a_re, b_im, ALU.mult, rs_, [t1.res])
        self.tt(eng, t2.ap, a_im, b_re, ALU.mult, rs_, [t2.res])
        if neg_im:
            self.stt(eng, out_im, t1.ap, -1.0, t2.ap, ALU.mult, ALU.subtract, [t1.res, t2.res], wres)
        else:
            self.tt(eng, out_im, t1.ap, t2.ap, ALU.add, [t1.res, t2.res], wres)

    def even_a(self, l):
        A = self.A
        P = self.P
        nc = self.nc
        AX = mybir.AxisListType.X
        P.barrier()
        A.reset()
        NB = 544
        RTm = [A.alloc(BF16, [32, 2, 128]) for _ in range(2)]
        for r_ in range(2):
            self.memset("pool", RTm[r_].ap, 0.0, [RTm[r_].res])
        Ob = [[A.alloc(BF16, [16, 128]) for _ in range(2)] for _ in range(2)]
        Tm = A.alloc(BF16, [32, 128])
        ASr = A.alloc(F32, [10, 32])
        ASi = A.alloc(F32, [10, 32])
        ASn = A.alloc(F32, [10, 32])
        keep_top = A.top
        T32 = A.alloc(F32, [32, 128])
        nat = A.alloc(F32, [4, 128])
        lre = A.alloc(F32, [32]); lim = A.alloc(F32, [32]); ldt = A.alloc(F32, [32])
        for (src, dst, bnk) in ((self.s5_lam_re, lre, 0), (self.s5_lam_im, lim, 1)):
            P.dma("sp", nat.ap[0:32, bnk, :], src.rearrange("d (q r) p -> (d q) (r p)", r=2), [], [nat.res])
            pt = Tile(self.bank[7].ap[:, 32 * bnk:32 * bnk + 32], self.bank[7].res)
            self.tr(pt.ap, nat.ap[0:32, bnk, :], self.ident_f.ap[0:32, 0:32], [nat.res, self.ident_f.res], [pt.res])
            self.cp("dve", dst.ap, pt.ap, [pt.res], [dst.res])
        P.dma("sp", nat.ap[0:32, 2, 0:2], self.s5_log_dt.rearrange("d (q r) -> (d q) r", r=2), [], [nat.res])
        dtT = A.alloc(F32, [32])
        pt = Tile(self.bank[7].ap[0:2, 64:96], self.bank[7].res)
        self.tr(pt.ap, nat.ap[0:32, 2, 0:2], self.ident_f.ap[0:32, 0:32], [nat.res, self.ident_f.res], [pt.res])
        self.cp("dve", dtT.ap[0:2, :], pt.ap, [pt.res], [dtT.res])
        sel_i = A.alloc(I32, [128]); sel = A.alloc(F32, [128]); sel2 = A.alloc(F32, [128])
        P.op("pool", lambda e: e.iota(sel_i.ap[0:2, :], [[1, 128]], base=0, channel_multiplier=-64), [], [sel_i.res])
        self.cp("dve", sel.ap[0:2, :], sel_i.ap[0:2, :], [sel_i.res], [sel.res])
        self.ts("dve", sel2.ap[0:2, :], sel.ap[0:2, :], 0.0, None, ALU.is_ge, None, [sel.res], [sel2.res])
        self.ts("dve", sel.ap[0:2, :], sel.ap[0:2, :], 64.0, None, ALU.is_lt, None, [sel.res], [sel.res])
        self.tt("dve", sel.ap[0:2, :], sel.ap[0:2, :], sel2.ap[0:2, :], ALU.mult, [sel.res, sel2.res], [sel.res])
        pt = Tile(self.bank[7].ap[:, 96:128], self.bank[7].res)
        self.mm(pt.ap, sel.ap[0:2, :], dtT.ap[0:2, :], True, True, [sel.res, dtT.res], [pt.res])
        self.cp("dve", ldt.ap, pt.ap, [pt.res], [ldt.res])
        def v32():
            return A.alloc(F32, [32])
        dt = v32(); xm = v32(); mag = v32(); imag = v32(); th = v32(); fr = v32(); frc = v32(); w1 = v32(); w2 = v32()
        sn = v32(); cs = v32(); are = v32(); aim = v32(); ire = v32(); iim = v32(); den = v32(); fre = v32(); fim = v32(); nre = v32()
        self.act(dt.ap, ldt.ap, AF.Exp, [ldt.res], [dt.res])
        self.ts("dve", lre.ap, lre.ap, -1e-4, None, ALU.min, None, [lre.res], [lre.res])
        self.tt("dve", xm.ap, lre.ap, dt.ap, ALU.mult, [lre.res, dt.res], [xm.res])
        self.act(mag.ap, xm.ap, AF.Exp, [xm.res], [mag.res])
        self.act(imag.ap, xm.ap, AF.Exp, [xm.res], [imag.res], scale=-1.0)
        self.tt("dve", th.ap, lim.ap, dt.ap, ALU.mult, [lim.res, dt.res], [th.res])
        ki = A.alloc(I32, [32]); kf = v32()
        self.ts("dve", fr.ap, th.ap, 1.0 / (2.0 * math.pi), None, ALU.mult, None, [th.res], [fr.res])
        self.cp("dve", ki.ap, fr.ap, [fr.res], [ki.res])
        self.cp("dve", kf.ap, ki.ap, [ki.res], [kf.res])
        self.tt("dve", fr.ap, fr.ap, kf.ap, ALU.subtract, [fr.res, kf.res], [fr.res])

        def wrap(x):
            self.ts("dve", w1.ap, x.ap, 0.5, None, ALU.is_gt, None, [x.res], [w1.res])
            self.ts("dve", w2.ap, x.ap, -0.5, None, ALU.is_lt, None, [x.res], [w2.res])
            self.tt("dve", x.ap, x.ap, w1.ap, ALU.subtract, [x.res, w1.res], [x.res])
            self.tt("dve", x.ap, x.ap, w2.ap, ALU.add, [x.res, w2.res], [x.res])
        wrap(fr)
        self.ts("dve", frc.ap, fr.ap, 0.25, None, ALU.add, None, [fr.res], [frc.res])
        wrap(frc)
        self.act(sn.ap, fr.ap, AF.Sin, [fr.res], [sn.res], scale=2.0 * math.pi)
        self.act(cs.ap, frc.ap, AF.Sin, [frc.res], [cs.res], scale=2.0 * math.pi)
        self.tt("dve", are.ap, mag.ap, cs.ap, ALU.mult, [mag.res, cs.res], [are.res])
        self.tt("dve", aim.ap, mag.ap, sn.ap, ALU.mult, [mag.res, sn.res], [aim.res])
        self.tt("dve", ire.ap, imag.ap, cs.ap, ALU.mult, [imag.res, cs.res], [ire.res])
        self.stt("dve", iim.ap, imag.ap, -1.0, sn.ap, ALU.mult, ALU.mult, [imag.res, sn.res], [iim.res])
        self.tt("dve", den.ap, lre.ap, lre.ap, ALU.mult, [lre.res], [den.res])
        self.tt("dve", w1.ap, lim.ap, lim.ap, ALU.mult, [lim.res], [w1.res])
        self.tt("dve", den.ap, den.ap, w1.ap, ALU.add, [den.res, w1.res], [den.res])
        self.P.op("dve", lambda e: e.reciprocal(out=den.ap, in_=den.ap), [den.res], [den.res])
        self.ts("dve", nre.ap, are.ap, -1.0, None, ALU.add, None, [are.res], [nre.res])
        self.tt("dve", w1.ap, nre.ap, lre.ap, ALU.mult, [nre.res, lre.res], [w1.res])
        self.tt("dve", w2.ap, aim.ap, lim.ap, ALU.mult, [aim.res, lim.res], [w2.res])
        self.tt("dve", fre.ap, w1.ap, w2.ap, ALU.add, [w1.res, w2.res], [fre.res])
        self.tt("dve", fre.ap, fre.ap, den.ap, ALU.mult, [fre.res, den.res], [fre.res])
        self.tt("dve", w1.ap, aim.ap, lre.ap, ALU.mult, [aim.res, lre.res], [w1.res])
        self.tt("dve", w2.ap, nre.ap, lim.ap, ALU.mult, [nre.res, lim.res], [w2.res])
        self.tt("dve", fim.ap, w1.ap, w2.ap, ALU.subtract, [w1.res, w2.res], [fim.res])
        self.tt("dve", fim.ap, fim.ap, den.ap, ALU.mult, [fim.res, den.res], [fim.res])
        Epr = A.alloc(F32, [9, 32]); Epi = A.alloc(F32, [9, 32]); Enr = A.alloc(F32, [8, 32]); Eni = A.alloc(F32, [8, 32])
        s1 = v32(); s2 = v32()
        self.memset("pool", Epr.ap[:, 0, :], 1.0, [Epr.res]); self.memset("pool", Epi.ap[:, 0, :], 0.0, [Epi.res])
        self.memset("pool", Enr.ap[:, 0, :], 1.0, [Enr.res]); self.memset("pool", Eni.ap[:, 0, :], 0.0, [Eni.res])
        for j in range(1, 9):
            self.cmul("dve", Epr.ap[:, j, :], Epi.ap[:, j, :], Epr.ap[:, j - 1, :], Epi.ap[:, j - 1, :], are.ap, aim.ap, s1, s2,
                      [Epr.res, Epi.res, are.res, aim.res], [Epr.res, Epi.res])
        for j in range(1, 8):
            self.cmul("dve", Enr.ap[:, j, :], Eni.ap[:, j, :], Enr.ap[:, j - 1, :], Eni.ap[:, j - 1, :], ire.ap, iim.ap, s1, s2,
                      [Enr.res, Eni.res, ire.res, iim.res], [Enr.res, Eni.res])
        self.cp("dve", ASr.ap[:, 0, :], Epr.ap[:, 8, :], [Epr.res], [ASr.res])
        self.cp("dve", ASi.ap[:, 0, :], Epi.ap[:, 8, :], [Epi.res], [ASi.res])
        for k in range(1, 10):
            self.cmul("dve", ASr.ap[:, k, :], ASi.ap[:, k, :], ASr.ap[:, k - 1, :], ASi.ap[:, k - 1, :],
                      ASr.ap[:, k - 1, :], ASi.ap[:, k - 1, :], s1, s2, [ASr.res, ASi.res], [ASr.res, ASi.res])
        self.ts("dve", ASn.ap, ASi.ap, -1.0, None, ALU.mult, None, [ASi.res], [ASn.res])
        if self.dbg.startswith("s5preA"):
            return
        Br = A.alloc(F32, [2, 16, 16]); Bi = A.alloc(F32, [2, 16, 16]); Bbr = A.alloc(F32, [2, 16, 16]); Bbi = A.alloc(F32, [2, 16, 16])
        P.dma("sp", Br.ap, self.s5_b_re.rearrange("d (q r) p h -> (r p) d q h", r=2), [], [Br.res])
        P.dma("sp", Bi.ap, self.s5_b_im.rearrange("d (q r) p h -> (r p) d q h", r=2), [], [Bi.res])
        b1 = A.alloc(F32, [2, 16, 16]); b2 = A.alloc(F32, [2, 16, 16])
        fre3 = fre.ap.rearrange("p (d q) -> p d q", d=2).unsqueeze(3).broadcast_to([128, 2, 16, 16])
        fim3 = fim.ap.rearrange("p (d q) -> p d q", d=2).unsqueeze(3).broadcast_to([128, 2, 16, 16])
        self.cmul("dve", Bbr.ap, Bbi.ap, fre3, fim3, Br.ap, Bi.ap, b1, b2, [fre.res, fim.res, Br.res, Bi.res], [Bbr.res, Bbi.res])
        Cr = A.alloc(F32, [2, 16, 16]); Ci = A.alloc(F32, [2, 16, 16])
        cnat = [A.alloc(F32, [128]) for _ in range(2)]
        ci_ = 0
        for (src, dstC) in ((self.s5_c_re, Cr), (self.s5_c_im, Ci)):
            for d in range(2):
                for ch in range(2):
                    cn = cnat[ci_ % 2]
                    for ql in range(8):
                        q = ch * 8 + ql
                        P.dma("sp", cn.ap[16 * ql:16 * ql + 16, :].rearrange("h (r p) -> h r p", r=2),
                              src[d, 2 * q:2 * q + 2].rearrange("r h p -> h r p"), [], [cn.res])
                    pt = Tile(self.bank[6].ap[:, 128 * (ci_ % 4):128 * (ci_ % 4) + 128], self.bank[6].res)
                    self.tr(pt.ap, cn.ap, self.ident_f.ap, [cn.res, self.ident_f.res], [pt.res])
                    self.cp("dve", dstC.ap[:, d, ch * 8:ch * 8 + 8, :], pt.ap.rearrange("p (q h) -> p q h", q=8), [pt.res], [dstC.res])
                    ci_ += 1
        dnat = A.alloc(F32, [16]); dT = A.alloc(F32, [32]); rep_i = A.alloc(I32, [128]); rep = A.alloc(F32, [128]); dcol = A.alloc(F32, [32])
        P.dma("sp", dnat.ap[0:32, :], self.s5_d.rearrange("(g h) -> g h", h=16), [], [dnat.res])
        pt = Tile(self.bank[7].ap[0:16, 128:160], self.bank[7].res)
        self.tr(pt.ap, dnat.ap[0:32, :], self.ident_f.ap[0:32, 0:32], [dnat.res, self.ident_f.res], [pt.res])
        self.cp("dve", dT.ap[0:16, :], pt.ap, [pt.res], [dT.res])
        P.op("pool", lambda e: e.iota(rep_i.ap[0:16, :], [[1, 128]], base=16, channel_multiplier=-1), [], [rep_i.res])
        self.ts("dve", rep_i.ap[0:16, :], rep_i.ap[0:16, :], 15, None, ALU.bitwise_and, None, [rep_i.res], [rep_i.res])
        self.cp("dve", rep.ap[0:16, :], rep_i.ap[0:16, :], [rep_i.res], [rep.res])
        self.ts("dve", rep.ap[0:16, :], rep.ap[0:16, :], 0.0, None, ALU.is_equal, None, [rep.res], [rep.res])
        pt = Tile(self.bank[7].ap[:, 160:192], self.bank[7].res)
        self.mm(pt.ap, rep.ap[0:16, :], dT.ap[0:16, :], True, True, [rep.res, dT.res], [pt.res])
        self.cp("dve", dcol.ap, pt.ap, [pt.res], [dcol.res])
        cbi = A.alloc(I32, [8, 16]); cbf = A.alloc(F32, [8, 16]); rbi = A.alloc(I32, [1]); rbf = A.alloc(F32, [1])
        mkf = A.alloc(F32, [128]); mkb = A.alloc(F32, [128])
        P.op("pool", lambda e: e.iota(cbi.ap, [[1, 8], [0, 16]], base=0, channel_multiplier=0), [], [cbi.res])
        self.cp("dve", cbf.ap, cbi.ap, [cbi.res], [cbf.res])
        P.op("pool", lambda e: e.iota(rbi.ap, [[1, 1]], base=0, channel_multiplier=1), [], [rbi.res])
        self.ts("dve", rbi.ap, rbi.ap, 4, None, ALU.arith_shift_right, None, [rbi.res], [rbi.res])
        self.cp("dve", rbf.ap, rbi.ap, [rbi.res], [rbf.res])
        cbf2 = cbf.ap.rearrange("p a b -> p (a b)")
        self.ts("dve", mkf.ap, cbf2, rbf.ap, None, ALU.is_ge, None, [cbf.res, rbf.res], [mkf.res])
        self.ts("dve", mkb.ap, cbf2, rbf.ap, None, ALU.is_le, None, [cbf.res, rbf.res], [mkb.res])
        if self.dbg.startswith("s5preB"):
            return
        hmi = A.alloc(I32, [2]); hm = A.alloc(F32, [2]); tmpT = A.alloc(F32, [128])
        P.op("pool", lambda e: e.iota(hmi.ap, [[0, 2]], base=0, channel_multiplier=1), [], [hmi.res])
        self.ts("dve", hmi.ap, hmi.ap, 6, None, ALU.arith_shift_right, None, [hmi.res], [hmi.res])
        self.cp("dve", hm.ap, hmi.ap, [hmi.res], [hm.res])
        self.ts("dve", hm.ap[:, 0:1], hm.ap[:, 0:1], -1.0, -1.0, ALU.add, ALU.mult, [hm.res], [hm.res])
        big = [A.alloc(F32, [16, 8, 16]) for _ in range(6)]
        Pr, Pi, Qr, Qi, g1, g2 = big

        def esel(E, d, j0=0, n=8):
            return E.ap[:, j0:j0 + n, 16 * d:16 * d + 16].rearrange("p j q -> p q j").unsqueeze(3).broadcast_to([128, 16, n, 16])

        def ebc(E, d, j):
            return E.ap[:, j, 16 * d:16 * d + 16].unsqueeze(2).unsqueeze(3).broadcast_to([128, 16, 8, 16])

        def bcj(X, d):
            return X.ap[:, d, :, :].unsqueeze(2).broadcast_to([128, 16, 8, 16])
        for d in range(2):
            EP_r, EP_i = (Enr, Eni) if d == 0 else (Epr, Epi)
            EQ_r, EQ_i = (Epr, Epi) if d == 0 else (Enr, Eni)
            rr = [Epr.res, Epi.res, Enr.res, Eni.res, Bbr.res, Bbi.res, Cr.res, Ci.res]
            if self.dbg.startswith("s5preE"):
                continue
            self.cmul("dve", Pr.ap, Pi.ap, esel(EP_r, d), esel(EP_i, d), bcj(Bbr, d), bcj(Bbi, d), g1, g2, rr, [Pr.res, Pi.res])
            self.cmul("dve", Qr.ap, Qi.ap, esel(EQ_r, d), esel(EQ_i, d), bcj(Cr, d), bcj(Ci, d), g1, g2, rr, [Qr.res, Qi.res], neg_im=True)
            for r in range(2):
                self.ts("dve", g1.ap, Pr.ap, hm.ap[:, r:r + 1], None, ALU.mult, None, [Pr.res, hm.res], [g1.res])
                self.ts("dve", g2.ap, Pi.ap, hm.ap[:, r:r + 1], None, ALU.mult, None, [Pi.res, hm.res], [g2.res])
                for q in range(16):
                    g = 2 * q + r
                    pt = Tile(self.bank[1 + (q % 2)].ap[:, 0:128], self.bank[1 + (q % 2)].res)
                    self.mm(pt.ap, g1.ap[:, q].rearrange("p a b -> p (a b)"), Qr.ap[:, q].rearrange("p a b -> p (a b)"),
                            True, False, [g1.res, Qr.res], [pt.res])
                    self.mm(pt.ap, g2.ap[:, q].rearrange("p a b -> p (a b)"), Qi.ap[:, q].rearrange("p a b -> p (a b)"),
                            False, True, [g2.res, Qi.res], [pt.res])
                    if d == 0:
                        self.tt("dve", T32.ap[:, g, :], pt.ap, mkf.ap, ALU.mult, [pt.res, mkf.res], [T32.res])
                    else:
                        self.tt("dve", tmpT.ap, pt.ap, mkb.ap, ALU.mult, [pt.res, mkb.res], [tmpT.res])
                        self.tt("dve", T32.ap[:, g, :], T32.ap[:, g, :], tmpT.ap, ALU.add, [T32.res, tmpT.res], [T32.res])
            jO = 1 if d == 0 else 8
            self.tt("dve", g1.ap, ebc(Epr, d, jO), Qr.ap, ALU.mult, rr + [Qr.res], [g1.res])
            self.tt("dve", g2.ap, ebc(Epi, d, jO), Qi.ap, ALU.mult, rr + [Qi.res], [g2.res])
            self.tt("dve", Ob[d][0].ap.rearrange("p q (a b) -> p q a b", a=8), g1.ap, g2.ap, ALU.add, [g1.res, g2.res], [Ob[d][0].res])
            self.tt("dve", g1.ap, ebc(Epr, d, jO), Qi.ap, ALU.mult, rr + [Qi.res], [g1.res])
            self.tt("dve", g2.ap, ebc(Epi, d, jO), Qr.ap, ALU.mult, rr + [Qr.res], [g2.res])
            self.tt("dve", Ob[d][1].ap.rearrange("p q (a b) -> p q a b", a=8), g1.ap, g2.ap, ALU.subtract, [g1.res, g2.res], [Ob[d][1].res])
            if d == 0:
                self.cmul("dve", Qr.ap, Qi.ap, ebc(Epr, 0, 7), ebc(Epi, 0, 7), Pr.ap, Pi.ap, g1, g2, rr + [Pr.res, Pi.res], [Qr.res, Qi.res])
                Rr_, Ri_ = Qr, Qi
            else:
                Rr_, Ri_ = Pr, Pi
            for q in range(16 if not self.dbg.startswith("s5preD") else 0):
                for (ri, Rx) in ((0, Rr_), (1, Ri_)):
                    pt = Tile(self.bank[3 + (q % 2)].ap[:, 128 * ri:128 * ri + 128], self.bank[3 + (q % 2)].res)
                    self.tr(pt.ap, Rx.ap[:, q].rearrange("p a b -> p (a b)"), self.ident_f.ap, [Rx.res, self.ident_f.res], [pt.res])
                pb2 = self.bank[3 + (q % 2)]
                for r_ in range(2 if not self.dbg.startswith("s5preCF") else 0):
                    self.cp("dve", RTm[r_].ap[:, 2 * q + d, :, 64 * r_:64 * r_ + 64],
                            pb2.ap[:, 0:256].rearrange("p (a b) -> p a b", a=2)[:, :, 64 * r_:64 * r_ + 64], [pb2.res], [RTm[r_].res])
        for g in range(32):
            self.stt("dve", Tm.ap[:, g, :], self.ident_f.ap, dcol.ap[:, g:g + 1], T32.ap[:, g, :], ALU.mult, ALU.add,
                     [self.ident_f.res, dcol.res, T32.res], [Tm.res])
        self.Ob = Ob
        if self.dbg.startswith("s5pre"):
            return
        P.barrier()
        A.top = keep_top
        U = A.alloc(BF16, [32, NB])
        Ur = [[Res() for _ in range(5)] for _ in range(32)]
        u_top = A.top
        wu = A.alloc(BF16, [8, 512])
        P.dma("pool", wu.ap, self.w_in[:, 0:512].rearrange("(k p) n -> p k n", p=128), [], [wu.res])
        hTg = A.alloc(BF16, [8, 1024])
        ub2 = A.alloc(BF16, [32, 8, 16])
        hTs = A.alloc(BF16, [8, 8, 128])
        self.memset("pool", ub2.ap, 0.0, [ub2.res])
        hs = A.alloc(BF16, [1024])
        ht = [A.alloc(F32, [1024]) for _ in range(2)]
        B1_ap = self.modpp.ap[:, 0:8, :]
        groups = [(0, 32)] + [(32 + 128 * i, 128) for i in range(4)]
        for gi, (n0, nb) in enumerate(groups):
            ntile = nb // 16
            for i in range(ntile):
                if gi == 0:
                    h_ap, h_res, v = self.sctx.ap[:, i, :], self.sctx_res[i], 1
                else:
                    h = ht[i % 2]
                    t128 = (gi - 1) * 8 + i
                    P.dma("sp", h.ap, self.x[128 * t128:128 * (t128 + 1), :], [], [h.res])
                    h_ap, h_res, v = h.ap, h.res, 0
                self.prenorm_T(h_ap, h_res, self.A1, B1_ap, self.modpp.res, v, hs, hTg.ap[:, :, 128 * i:128 * (i + 1)], hTg.res, self.bank[0])
            for k in range(8):
                self.cp("dve" if k % 2 else "pool", hTs.ap[:, k, :, 0:nb], hTg.ap[:, k, 0:8 * nb].rearrange("p (b t) -> p t b", t=8),
                        [hTg.res], [hTs.res])
            for tau in range(8):
                pb = self.bank[1 + (tau % 2)]
                for k in range(8):
                    self.mm(pb.ap[0:nb, :], hTs.ap[:, k, tau, 0:nb], wu.ap[:, k, :], k == 0, k == 7, [hTs.res, wu.res], [pb.res])
                self.cp("dve", ub2.ap[0:nb, :, tau, :], pb.ap[0:nb, :].rearrange("p (g h) -> p g h", h=16),
                        [pb.res], [ub2.res])
            for g8 in range(4 if not self.dbg.startswith("s5a1x") else 0):
                pb = self.bank[3 + (g8 % 2)]
                pbt = pb.ap.bitcast(BF16)
                for gl in range(8):
                    g = g8 * 8 + gl
                    self.tr(pbt[:, 128 * gl:128 * gl + 128], ub2.ap[:, g].rearrange("p a b -> p (a b)"), self.ident_b.ap,
                            [ub2.res, self.ident_b.res], [pb.res])
                for gl in range(8 if not self.dbg.startswith("s5a1y") else 0):
                    g = g8 * 8 + gl
                    self.cp("dve", U.ap[:, g, n0:n0 + nb], pbt[:, 128 * gl:128 * gl + nb], [pb.res], [Ur[g][gi]])
        if self.dbg.startswith("s5a1"):
            return
        P.barrier()
        A.top = u_top
        gbm = A.alloc(BF16, [5, 8, 512])
        gbr = [Res() for _ in range(5)]
        gbm_end = A.top
        Sx2 = [[[A.alloc(F32, [NB]) for _ in range(2)] for _ in range(2)] for _ in range(2)]
        SE = [[[A.alloc(BF16, [NB + 1]) for _ in range(2)] for _ in range(2)] for _ in range(2)]
        for par in range(2):
            for d in range(2):
                for ri in range(2):
                    self.memset("pool", SE[par][d][ri].ap, 0.0, [SE[par][d][ri].res])
        gtmp = [A.alloc(BF16, [128]) for _ in range(2)]
        nq = 16 if not self.dbg.startswith("s5q1") else 1
        for q in range(nq):
            par = q % 2
            for d in range(2):
                qd = 2 * q + d
                psV = [Tile(self.psall[:, 512:1056], Res()), Tile(self.psall[:, 1536:2080], Res())]
                vres = [[self.bank[1].res, self.bank[2].res], [self.bank[3].res, self.bank[4].res]]
                if d == 0:
                    splits = [(0, 0, 512), (512, 512, 32)]
                else:
                    splits = [(0, 32, 512), (512, 0, 32)]
                for ri in range(2):
                    for (oc, uc, n) in splits:
                        for r in range(2):
                            g = 2 * q + r
                            ur = Ur[g]
                            self.mm(psV[ri].ap[:, oc:oc + n], RTm[r].ap[:, qd, ri, :], U.ap[:, g, uc:uc + n],
                                    r == 0, r == 1, [RTm[r].res] + ur, vres[ri])
                self.cp("dve", Sx2[d][0][0].ap, psV[0].ap, vres[0], [Sx2[d][0][0].res])
                self.cp("dve", Sx2[d][0][1].ap, psV[1].ap, vres[1], [Sx2[d][0][1].res])
            cur = 0
            for k in range(10):
                dl = 1 << k
                for d in range(2):
                    col = 16 * d + q
                    Sx = Sx2[d]
                    a_r = ASr.ap[:, k, col:col + 1]
                    a_i = ASi.ap[:, k, col:col + 1]
                    a_n = ASn.ap[:, k, col:col + 1]
                    o_re, o_im = Sx[cur][0], Sx[cur][1]
                    n_re, n_im = Sx[1 - cur][0], Sx[1 - cur][1]
                    if d == 0:
                        dst_s, src_s, same_s = slice(dl, NB), slice(0, NB - dl), slice(0, dl)
                    else:
                        dst_s, src_s, same_s = slice(0, NB - dl), slice(dl, NB), slice(NB - dl, NB)
                    rd = [o_re.res, o_im.res, ASr.res, ASi.res, ASn.res]
                    self.cp("act", n_re.ap[:, same_s], o_re.ap[:, same_s], [o_re.res], [n_re.res])
                    self.cp("act", n_im.ap[:, same_s], o_im.ap[:, same_s], [o_im.res], [n_im.res])
                    self.stt("dve", n_re.ap[:, dst_s], o_re.ap[:, src_s], a_r, o_re.ap[:, dst_s], ALU.mult, ALU.add, rd, [n_re.res])
                    self.stt("dve", n_im.ap[:, dst_s], o_im.ap[:, src_s], a_r, o_im.ap[:, dst_s], ALU.mult, ALU.add, rd, [n_im.res])
                    self.stt("dve", n_re.ap[:, dst_s], o_im.ap[:, src_s], a_n, n_re.ap[:, dst_s], ALU.mult, ALU.add, rd + [n_re.res], [n_re.res])
                    self.stt("dve", n_im.ap[:, dst_s], o_re.ap[:, src_s], a_i, n_im.ap[:, dst_s], ALU.mult, ALU.add, rd + [n_im.res], [n_im.res])
                cur = 1 - cur
            for d in range(2):
                off = 1 if d == 0 else 0
                for ri in range(2):
                    self.cp("act", SE[par][d][ri].ap[:, off:off + NB], Sx2[d][cur][ri].ap, [Sx2[d][cur][ri].res], [SE[par][d][ri].res])
            for r in range(2):
                g = 2 * q + r
                for ci, (n0, nb) in enumerate(groups):
                    pb = self.bank[5 + ((2 * q + r + ci) % 2)]
                    py = Tile(pb.ap[0:nb, 0:128], pb.res)
                    self.mm(py.ap, U.ap[:, g, n0:n0 + nb], Tm.ap[:, g, :], True, False, Ur[g] + [Tm.res], [py.res])
                    bidx = (n0 - 32) if n0 >= 32 else 512 + n0
                    for d, c0_ in ((0, n0), (1, bidx + 1)):
                        for ri in range(2):
                            last = (d == 1 and ri == 1)
                            self.mm(py.ap, SE[par][d][ri].ap[64 * r:64 * r + 64, c0_:c0_ + nb], self.Ob[d][ri].ap[64 * r:64 * r + 64, q, :],
                                    False, last, [SE[par][d][ri].res, self.Ob[d][ri].res], [py.res])
                    gt_ = gtmp[(2 * q + r + ci) % 2]
                    self.act(gt_.ap[0:nb, :], py.ap, AF.Gelu, [py.res], [gt_.res])
                    self.cp("dve", gbm.ap[0:nb, ci, :, 16 * g:16 * g + 16], gt_.ap[0:nb, :].rearrange("p (t h) -> p t h", t=8), [gt_.res], [gbr[ci]])
        if self.dbg.startswith("s5a3"):
            return
        P.barrier()
        A.top = keep_top
        gT = A.alloc(BF16, [4, L + LC])
        A.top = gbm_end
        wg = A.alloc(BF16, [4, 512])
        P.dma("pool", wg.ap, self.s5_w_glu.rearrange("(k p) n -> p k n", p=128), [], [wg.res])
        for ci, (n0, nb) in enumerate(groups):
            for tp in range(8):
                pb = self.bank[1 + (tp % 2)]
                pbt = pb.ap.bitcast(BF16)
                for kk in range(4):
                    self.tr(pbt[:, 128 * kk:128 * kk + nb], gbm.ap[0:nb, ci, tp, 128 * kk:128 * kk + 128], self.ident_b.ap[0:nb, 0:nb],
                            [gbr[ci], self.ident_b.res], [pb.res])
                dst = gT.ap[:, :, 8 * n0:8 * (n0 + nb)].rearrange("p k (b t) -> p k b t", t=8)[:, :, :, tp]
                src = pbt[:, 0:512].rearrange("p (k b) -> p k b", k=4)[:, :, 0:nb]
                self.cp("dve", dst, src, [pb.res], [gT.res])
        sg = [A.alloc(F32, [512]) for _ in range(2)]
        so = [A.alloc(BF16, [4, 512]) for _ in range(2)]
        NTOK = L + LC
        ti = 0
        for t0 in range(0, NTOK, 512):
            n = min(512, NTOK - t0)
            sot = so[ti % 2]
            for mcol in range(4):
                pb = self.bank[3 + (mcol % 2)]
                for kk in range(4):
                    self.mm(pb.ap[:, 0:n], wg.ap[:, kk, 128 * mcol:128 * (mcol + 1)], gT.ap[:, kk, t0:t0 + n], kk == 0, kk == 3,
                            [wg.res, gT.res], [pb.res])
                sgt = sg[mcol % 2]
                self.act(sgt.ap[:, 0:n], pb.ap[:, 0:n], AF.Sigmoid, [pb.res], [sgt.res])
                self.tt("dve", sot.ap[:, mcol, 0:n], sgt.ap[:, 0:n], gT.ap[:, mcol, t0:t0 + n], ALU.mult, [sgt.res, gT.res], [sot.res])
            P.dma("sp", self.s5T_out[:, t0:t0 + n].rearrange("(k p) t -> p k t", p=128), sot.ap[:, :, 0:n], [sot.res], [Res()])
            ti += 1

    def even_b(self, l):
        A = self.A
        P = self.P
        upd_ctx = l < 2
        P.barrier()
        A.reset()
        NT = L + LC
        NTILE = NT // 128
        AX = mybir.AxisListType.X
        win = A.alloc(BF16, [8, 1536])
        winr = [Res() for _ in range(8)]
        for k in range(8):
            P.dma("pool", win.ap[:, k, :], self.w_in[128 * k:128 * (k + 1), 512:2048], [], [winr[k]])
        wout = A.alloc(BF16, [8, 1024])
        P.dma("pool", wout.ap, self.w_out_even.rearrange("(k p) n -> p k n", p=128), [], [wout.res])
        kT = A.alloc(BF16, [4, NT])
        kTr = [Res() for _ in range(NTILE)]
        vt = A.alloc(BF16, [NTILE, 512])
        vtr = [Res() for _ in range(NTILE)]
        hT = A.alloc(BF16, [8, 128])
        hs = A.alloc(BF16, [1024])
        ht = [A.alloc(F32, [1024]) for _ in range(2)]
        B1_ap = self.modpp.ap[:, 0:8, :]

        def tile_src(t, i):
            if t < 2:
                return self.sctx.ap[:, t, :], self.sctx_res[t], 1
            h = ht[i % 2]
            P.dma("sp", h.ap, self.x[128 * (t - 2):128 * (t - 1), :], [], [h.res])
            return h.ap, h.res, 0

        def pre_a(t, i):
            h_ap, h_res, v = tile_src(t, i)
            self.prenorm_a(h_ap, h_res, hs)
            return h_ap, h_res, v

        def pre_b(v):
            self.prenorm_b(self.A1, B1_ap, self.modpp.res, v, hs, hT.ap, hT.res, self.bank[0])

        nxt = pre_a(0, 0)
        pre_b(nxt[2])
        for t in range(NTILE):
            if t + 1 < NTILE:
                nxt = pre_a(t + 1, t + 1)
            pk = self.bank[1]
            for mc in range(4):
                for k in range(8):
                    self.mm(pk.ap[:, 128 * mc:128 * (mc + 1)], win.ap[:, k, 512 + 128 * mc:512 + 128 * (mc + 1)], hT.ap[:, k, :],
                            k == 0, k == 7, [winr[k], hT.res], [pk.res])
            self.cp("act", kT.ap[:, :, 128 * t:128 * (t + 1)], pk.ap.rearrange("p (a b) -> p a b", a=4), [pk.res], [kTr[t]])
            pv = self.bank[7]
            for k in range(8):
                self.mm(pv.ap, hT.ap[:, k, :], win.ap[:, k, 1024:1536], k == 0, k == 7, [winr[k], hT.res], [pv.res])
            self.cp("dve", vt.ap[:, t, :], pv.ap, [pv.res], [vtr[t]])
            if t + 1 < NTILE:
                pre_b(nxt[2])
        Bd = self.nc.dram_tensor("Bd%d" % l, [8, 15, 64, 94], F32).ap()
        Bd_res = Res()
        Bc = A.alloc(F32, [8, 15, 64])
        top_b2 = A.top
        fill = A.alloc(F32, [15 * 94])
        self.memset("pool", fill.ap, -30000.0, [fill.res])
        for hd in range(8):
            P.dma("sp", Bd[hd].rearrange("a q k -> q a k"), fill.ap[0:64, :].rearrange("p (a k) -> p a k", a=15), [fill.res], [Bd_res])
        bt = Bd.tensor
        dst = bass.AP(bt, 0, [[15 * 64 * 94, 8], [64 * 94, 15], [95, 64], [1, 31]])
        rt = self.na_rpb.tensor
        srcp = bass.AP(rt, self.na_rpb.offset, [[465, 8], [31, 15], [0, 64], [1, 31]])
        P.dma("sp", dst, srcp, [Bd_res], [Bd_res])
        for half in range(2):
            P.dma("sp", Bc.ap[64 * half:64 * half + 64], Bd[:, :, :, 15:79].rearrange("h a q k -> q h a k"), [Bd_res], [Bc.res])
        ii = A.alloc(I32, [64])
        kcf = A.alloc(F32, [64])
        qi = A.alloc(I32, [1])
        qf = A.alloc(F32, [1])
        c0 = A.alloc(F32, [1])
        m1 = A.alloc(F32, [64])
        m2 = A.alloc(F32, [64])
        P.op("pool", lambda e: e.iota(ii.ap, [[1, 64]], base=0, channel_multiplier=0), [], [ii.res])
        self.cp("dve", kcf.ap, ii.ap, [ii.res], [kcf.res])
        P.op("pool", lambda e: e.iota(qi.ap, [[1, 1]], base=0, channel_multiplier=1), [], [qi.res])
        self.ts("dve", qi.ap, qi.ap, 63, None, ALU.bitwise_and, None, [qi.res], [qi.res])
        self.cp("dve", qf.ap, qi.ap, [qi.res], [qf.res])
        self.ts("dve", c0.ap, qf.ap, -8.0, 0.0, ALU.add, ALU.max, [qf.res], [c0.res])
        self.ts("dve", c0.ap, c0.ap, 48.0, None, ALU.min, None, [c0.res], [c0.res])
        self.ts("dve", m1.ap, kcf.ap, c0.ap, None, ALU.is_ge, None, [kcf.res, c0.res], [m1.res])
        self.ts("dve", m2.ap, kcf.ap, -16.0, c0.ap, ALU.add, ALU.is_lt, [kcf.res, c0.res], [m2.res])
        self.tt("dve", m1.ap, m1.ap, m2.ap, ALU.mult, [m1.res, m2.res], [m1.res])
        self.ts("dve", m1.ap, m1.ap, -1.0, 30000.0, ALU.add, ALU.mult, [m1.res], [m1.res])
        Bc2 = Bc.ap.rearrange("p h a k -> p (h a) k")
        self.tt("dve", Bc2, Bc2, m1.ap.unsqueeze(1).broadcast_to([128, 120, 64]), ALU.add, [Bc.res, m1.res], [Bc.res])
        P.barrier()
        A.top = top_b2
        qT = A.alloc(BF16, [4, 128])
        tSs = [A.alloc(F32, [768]) for _ in range(2)]
        Pts = [A.alloc(BF16, [896]) for _ in range(2)]
        PtTs = [A.alloc(BF16, [896])] * 2
        natok = A.alloc(BF16, [512])
        naT = A.alloc(BF16, [4, 128])
        s5t = [A.alloc(BF16, [4, 128]) for _ in range(2)]
        sm = A.alloc(F32, [8, 2])
        rinv = A.alloc(F32, [8])
        mx = A.alloc(F32, [8])
        nmx = A.alloc(F32, [8])
        tmp = A.alloc(F32, [1024])
        junk = hs
        mx_r = [Res() for _ in range(8)]
        nmx_r = [Res() for _ in range(8)]
        sm_r3 = [[Res() for _ in range(3)] for _ in range(8)]
        sm_r = [r_ for l3 in sm_r3 for r_ in l3]
        tS_r = [[Res() for _ in range(3)] for _ in range(2)]
        Pt_r = [[Res() for _ in range(3)] for _ in range(2)]
        psS = Tile(self.psall[:, 1024:2048], Res())
        psS_res = [self.bank[2].res, self.bank[3].res]
        psT = self.bank[6]
        psO = self.bank[7]
        units = [("lat", m) for m in range(32)] + ([("ctx", i) for i in range(2)] if upd_ctx else [])
        if self.dbg.startswith("na1"):
            units = units[:1] + units[5:6] + units[31:32] + units[32:]
        def unit_tile(u):
            kind_, m_ = units[u]
            return 2 + m_ if kind_ == "lat" else m_

        def qproj():
            pq = self.bank[1]
            for mc in range(4):
                for k in range(8):
                    self.mm(pq.ap[:, 128 * mc:128 * (mc + 1)], win.ap[:, k, 128 * mc:128 * (mc + 1)], hT.ap[:, k, :],
                            k == 0, k == 7, [winr[k], hT.res], [pq.res])
            self.cp("act", qT.ap, pq.ap.rearrange("p (a b) -> p a b", a=4), [pq.res], [qT.res])

        cur_src = pre_a(unit_tile(0), 0)
        pre_b(cur_src[2])
        qproj()
        for ui, (kind, m) in enumerate(units):
            t = 2 + m if kind == "lat" else m
            h_ap, h_res, _v = cur_src
            nxt_src = None
            if ui + 1 < len(units):
                nxt_src = pre_a(unit_tile(ui + 1), ui + 1)
            if kind == "lat":
                rs = min(max(2 * m - 4, 0), 54)
                wt0 = 2 + rs // 2
                nwin = 640
            else:
                rs = 0
                wt0 = 0
                nwin = 0
            ncol = nwin + 256
            nchunk = ncol // 128
            wins_of = {}
            def stage_a1(hd):
                tS, Pt, PtT = tSs[hd % 2], Pts[hd % 2], PtTs[hd % 2]
                tSr, Ptr = tS_r[hd % 2], Pt_r[hd % 2]
                mxr, nmxr, smr = mx_r[hd], nmx_r[hd], sm_r3[hd]
                mc, po = hd // 2, 64 * (hd % 2)
                q_l = qT.ap[po:po + 64, mc, :]
                if kind == "lat":
                    kr = [kTr[wt0 + i] for i in range(5)]
                    self.mm(psS.ap[:, 0:512], q_l, kT.ap[po:po + 64, mc, 128 * wt0:128 * wt0 + 512], True, True,
                            [qT.res] + kr, psS_res)
                    self.mm(psS.ap[:, 512:640], q_l, kT.ap[po:po + 64, mc, 128 * wt0 + 512:128 * wt0 + 640], True, True,
                            [qT.res] + kr, psS_res)
                self.mm(psS.ap[:, nwin:nwin + 256], q_l, kT.ap[po:po + 64, mc, 0:256], True, True,
                        [qT.res, kTr[0], kTr[1]], psS_res)
                wins = []
                if kind == "lat":
                    for e in range(2):
                        qr = 2 * m + e
                        r0 = min(max(qr - 4, 0), 56)
                        j0 = r0 - rs
                        a0 = r0 - qr + 7
                        wins.append((e, j0))
                        self.stt("dve", tS.ap[64 * e:64 * e + 64, 0:512].rearrange("p (a k) -> p a k", a=8),
                                 psS.ap[64 * e:64 * e + 64, 64 * j0:64 * j0 + 512].rearrange("p (a k) -> p a k", a=8), 0.125,
                                 Bc.ap[64 * e:64 * e + 64, hd, a0:a0 + 8, :], ALU.mult, ALU.add,
                                 psS_res + [Bc.res], [tSr[e]])
                o0 = 512 if kind == "lat" else 0
                self.ts("dve", tS.ap[:, o0:o0 + 256], psS.ap[:, nwin:nwin + 256], 0.125, None, ALU.mult, None, psS_res, [tSr[2]])
                self.P.op("dve", lambda e, o_=mx.ap[:, hd:hd + 1], i_=tS.ap[:, 0:o0 + 256]: e.reduce_max(out=o_, in_=i_, axis=AX),
                          tSr, [mxr])
                self.ts("dve", nmx.ap[:, hd:hd + 1], mx.ap[:, hd:hd + 1], -1.0, None, ALU.mult, None, [mxr], [nmxr])
                self.memset("pool", sm.ap[:, hd, :], 0.0, smr)
                wins_of[hd] = (wins, o0)
            def stage_a2(hd):
                tS, Pt = tSs[hd % 2], Pts[hd % 2]
                tSr, Ptr = tS_r[hd % 2], Pt_r[hd % 2]
                nmxr, smr = nmx_r[hd], sm_r3[hd]
                wins, o0 = wins_of[hd]
                if kind == "lat":
                    self.memset("pool", Pt.ap[:, 0:640], 0.0, Ptr)
                for (e, j0) in wins:
                    self.act(Pt.ap[64 * e:64 * e + 64, 64 * j0:64 * j0 + 512], tS.ap[64 * e:64 * e + 64, 0:512], AF.Exp,
                             [tSr[e], nmxr, smr[e]], [Ptr[e], smr[e]], bias=nmx.ap[64 * e:64 * e + 64, hd:hd + 1], scale=1.0,
                             accum_out=sm.ap[64 * e:64 * e + 64, hd, 0:1])
                self.act(Pt.ap[:, nwin:nwin + 256], tS.ap[:, o0:o0 + 256], AF.Exp, [tSr[2], nmxr, smr[2]], [Ptr[2], smr[2]],
                         bias=nmx.ap[:, hd:hd + 1], scale=1.0, accum_out=sm.ap[:, hd, 1:2])
            def stage_b(hd):
                Pt, PtT = Pts[hd % 2], PtTs[hd % 2]
                Ptr = Pt_r[hd % 2]
                pst = psT.ap.bitcast(BF16)
                for c in range(nchunk):
                    self.tr(pst[:, 128 * c:128 * (c + 1)], Pt.ap[:, 128 * c:128 * (c + 1)], self.ident_b.ap,
                            Ptr + [self.ident_b.res], [psT.res])
                self.cp("act" if hd % 2 else "dve", PtT.ap[:, 0:ncol], pst[:, 0:ncol], [psT.res], [PtT.res])
                for c in range(nchunk):
                    if kind == "lat" and c < 5:
                        vtile = wt0 + c
                    else:
                        vtile = c - (5 if kind == "lat" else 0)
                    self.mm(psO.ap[:, 64 * hd:64 * hd + 64], PtT.ap[:, 128 * c:128 * (c + 1)], vt.ap[:, vtile, 64 * hd:64 * hd + 64],
                            c == 0, c == nchunk - 1, [PtT.res, vtr[vtile]], [psO.res])
            stage_a1(0)
            stage_a1(1)
            stage_a2(0)
            for hd in range(8):
                if hd + 2 < 8:
                    stage_a1(hd + 2)
                if hd + 1 < 8:
                    stage_a2(hd + 1)
                stage_b(hd)
            if nxt_src is not None:
                pre_b(nxt_src[2])
                qproj()
                cur_src = nxt_src
            self.tt("dve", rinv.ap, sm.ap[:, :, 0], sm.ap[:, :, 1], ALU.add, sm_r, [rinv.res])
            self.P.op("dve", lambda e: e.reciprocal(out=rinv.ap, in_=rinv.ap), [rinv.res], [rinv.res])
            self.tt("dve", natok.ap.rearrange("p (h d) -> p h d", h=8), psO.ap.rearrange("p (h d) -> p h d", h=8),
                    rinv.ap.unsqueeze(2).broadcast_to([128, 8, 64]), ALU.mult, [psO.res, rinv.res], [natok.res])
            pst = psT.ap.bitcast(BF16)
            for c in range(4):
                self.tr(pst[:, 128 * c:128 * (c + 1)], natok.ap[:, 128 * c:128 * (c + 1)], self.ident_b.ap,
                        [natok.res, self.ident_b.res], [psT.res])
            self.cp("act", naT.ap, pst[:, 0:512].rearrange("p (a b) -> p a b", a=4), [psT.res], [naT.res])
            s5 = s5t[ui % 2]
            P.dma("sp", s5.ap, self.s5T_in[:, 128 * t:128 * (t + 1)].rearrange("(k p) t -> p k t", p=128), [], [s5.res])
            for half in range(2):
                pb = self.bank[4 + half]
                for k in range(8):
                    lhsT = s5.ap[:, k, :] if k < 4 else naT.ap[:, k - 4, :]
                    self.mm(pb.ap, lhsT, wout.ap[:, k, 512 * half:512 * (half + 1)], k == 0, k == 7,
                            [s5.res, naT.res, wout.res], [pb.res])
            po_ap = self.psall[:, 2048:3072]
            pres = [self.bank[4].res, self.bank[5].res]
            v = 0 if kind == "lat" else 1
            rstd = self.rstd_of(po_ap, pres, junk)
            self.stt("dve", tmp.ap, po_ap, rstd.ap, self.G[0][v].ap, ALU.mult, ALU.mult, pres + [rstd.res, self.G[0][v].res], [tmp.res])
            self.tt("pool", h_ap, tmp.ap, h_ap, ALU.add, [tmp.res, h_res], [h_res])
            if kind == "lat":
                P.dma("sp", self.out[128 * m:128 * (m + 1), :], h_ap, [h_res], [self.out_res[m]])


    def gen_tables(self):
        A = self.A
        P = self.P
        P.barrier()
        A.reset()
        W = 1024
        kf = A.alloc(F32, [4096])
        ki = A.alloc(I32, [4096])
        tcol = A.alloc(F32, [32])
        ti = A.alloc(I32, [32])
        P.op("pool", lambda e: e.iota(ki.ap, [[1, 4096]], base=0, channel_multiplier=0), [], [ki.res])
        self.cp("dve", kf.ap, ki.ap, [ki.res], [kf.res])
        P.op("pool", lambda e: e.iota(ti.ap, [[128, 32]], base=0, channel_multiplier=1), [], [ti.res])
        self.cp("dve", tcol.ap, ti.ap, [ti.res], [tcol.res])
        negpi = self.negpi
        t1 = [A.alloc(I32, [W]) for _ in range(2)]
        t2 = [A.alloc(I32, [W]) for _ in range(2)]
        t3 = [A.alloc(F32, [W]) for _ in range(2)]
        t4 = [A.alloc(F32, [W]) for _ in range(2)]
        ob = [A.alloc(BF16, [W]) for _ in range(4)]
        it = 0
        nchunks = 32 if not self.dbg.startswith("tab1") else 1
        sc = 2.0 * math.pi / 4096.0
        for c in range(nchunks):
            for kt in range(4096 // W if not (self.dbg.startswith("odd1") or self.dbg.startswith("tab1")) else 1):
                ai, ci, bf, cf = t1[it % 2], t2[it % 2], t3[it % 2], t4[it % 2]
                os_, oc = ob[(2 * it) % 4], ob[(2 * it + 1) % 4]
                self.ts("dve", ai.ap, kf.ap[:, kt * W:(kt + 1) * W], tcol.ap[:, c:c + 1], 2048.0, ALU.mult, ALU.add,
                        [kf.res, tcol.res], [ai.res])
                self.ts("dve", ci.ap, kf.ap[:, kt * W:(kt + 1) * W], tcol.ap[:, c:c + 1], 3072.0, ALU.mult, ALU.add,
                        [kf.res, tcol.res], [ci.res])
                self.ts("dve", ai.ap, ai.ap, 4095, None, ALU.bitwise_and, None, [ai.res], [ai.res])
                self.ts("dve", ci.ap, ci.ap, 4095, None, ALU.bitwise_and, None, [ci.res], [ci.res])
                self.cp("pool", bf.ap, ai.ap, [ai.res], [bf.res])
                self.cp("pool", cf.ap, ci.ap, [ci.res], [cf.res])
                self.act(os_.ap, bf.ap, AF.Sin, [bf.res, negpi.res], [os_.res], bias=negpi.ap, scale=sc)
                self.act(oc.ap, cf.ap, AF.Sin, [cf.res, negpi.res], [oc.res], bias=negpi.ap, scale=sc)
                P.dma("sp", self.Stab[128 * c:128 * (c + 1), kt * W:(kt + 1) * W], os_.ap, [os_.res], [self.tab_res])
                P.dma("sp", self.Ctab[128 * c:128 * (c + 1), kt * W:(kt + 1) * W], oc.ap, [oc.res], [self.tab_res])
                it += 1

    def gen_channel_tables(self, CD, SDn):
        A = self.A
        P = self.P
        kf = A.alloc(F32, [1024])
        ki = A.alloc(I32, [1024])
        tcol = A.alloc(F32, [8])
        ti = A.alloc(I32, [8])
        P.op("pool", lambda e: e.iota(ki.ap, [[1, 1024]], base=0, channel_multiplier=0), [], [ki.res])
        self.cp("dve", kf.ap, ki.ap, [ki.res], [kf.res])
        P.op("pool", lambda e: e.iota(ti.ap, [[128, 8]], base=0, channel_multiplier=1), [], [ti.res])
        self.cp("dve", tcol.ap, ti.ap, [ti.res], [tcol.res])
        ai = A.alloc(I32, [1024])
        ci = A.alloc(I32, [1024])
        b_ = A.alloc(F32, [1024])
        c3 = A.alloc(F32, [1024])
        negpi = self.negpi
        sc = 2.0 * math.pi / 1024.0
        for k in range(8):
            self.ts("dve", ai.ap, kf.ap, tcol.ap[:, k:k + 1], 512.0, ALU.mult, ALU.add, [kf.res, tcol.res], [ai.res])
            self.ts("dve", ci.ap, kf.ap, tcol.ap[:, k:k + 1], 768.0, ALU.mult, ALU.add, [kf.res, tcol.res], [ci.res])
            self.ts("dve", ai.ap, ai.ap, 1023, None, ALU.bitwise_and, None, [ai.res], [ai.res])
            self.ts("dve", ci.ap, ci.ap, 1023, None, ALU.bitwise_and, None, [ci.res], [ci.res])
            self.cp("pool", b_.ap, ai.ap, [ai.res], [b_.res])
            self.cp("pool", c3.ap, ci.ap, [ci.res], [c3.res])
            self.act(b_.ap, b_.ap, AF.Sin, [b_.res, negpi.res], [b_.res], bias=negpi.ap, scale=sc)
            self.act(c3.ap, c3.ap, AF.Sin, [c3.res, negpi.res], [c3.res], bias=negpi.ap, scale=sc)
            self.ts("dve", SDn.ap[:, k, :], b_.ap, -1.0 / 2048.0, None, ALU.mult, None, [b_.res], [SDn.res])
            self.ts("pool", CD.ap[:, k, :], c3.ap, 1.0 / 2048.0, None, ALU.mult, None, [c3.res], [CD.res])

    def odd_mixer(self, l):
        A = self.A
        P = self.P
        o = l // 2
        upd_ctx = l < 2
        P.barrier()
        A.reset()
        ycc = A.alloc(BF16, [8, 256])
        ysc = A.alloc(BF16, [8, 256])
        base2 = A.top
        hl = A.alloc(BF16, [32, 1024])
        hlr = [Res() for _ in range(32)]
        hc = A.alloc(BF16, [2, 1024])
        hcr = [Res() for _ in range(2)]
        nv = 2 if upd_ctx else 1
        with_tmp = A.top
        A1b = [A.alloc(F32, [1024]) for _ in range(nv)]
        B1b = [A.alloc(F32, [1024]) for _ in range(nv)]
        self.make_crep()
        wms = [A.alloc(F32, [8, 512]) for _ in range(2)]
        tb = A.alloc(F32, [512])
        tg = A.alloc(F32, [512])
        ones = A.alloc(F32, [512])
        self.memset("pool", ones.ap, 1.0, [ones.res])
        for nt in range(4):
            wm = wms[nt % 2]
            half = nt % 2
            P.dma("sp", wm.ap, self.w_mod[:, nt * 512:(nt + 1) * 512].rearrange("(k p) n -> p k n", p=128), [], [wm.res])
            self.load_bcast_row("sp", tb, self.b_mod[nt * 512:(nt + 1) * 512])
            if nt >= 2:
                self.load_bcast_row("sp", tg, self.norm_g[0, half * 512:(half + 1) * 512])
            for v in range(nv):
                psb = self.bank[5 + v]
                dst = (B1b if nt < 2 else A1b)[v]
                d_ap = dst.ap[:, half * 512:(half + 1) * 512]
                for k in range(8):
                    self.mm(psb.ap, self.crep[v].ap[:, k, :], wm.ap[:, k, :], k == 0, k == 7, [self.crep[v].res, wm.res], [psb.res])
                if nt < 2:
                    self.tt("dve", d_ap, psb.ap, tb.ap, ALU.add, [psb.res, tb.res], [dst.res])
                else:
                    self.stt("dve", d_ap, psb.ap, 1.0, tb.ap, ALU.add, ALU.add, [psb.res, tb.res], [dst.res])
                    self.tt("dve", d_ap, d_ap, tg.ap, ALU.mult, [dst.res, tg.res], [dst.res])
        ht = [A.alloc(F32, [1024]) for _ in range(2)]
        junk = A.alloc(BF16, [1024])
        tmp = A.alloc(F32, [1024])
        src = self.x
        for t in range(32 + (2 if upd_ctx else 0)):
            if t < 32:
                h = ht[t % 2]
                P.dma("sp", h.ap, src[128 * t:128 * (t + 1), :], [], [h.res])
                h_ap, h_res, v, d_ap, d_res = h.ap, h.res, 0, hl.ap[:, t, :], hlr[t]
            else:
                i = t - 32
                h_ap, h_res, v, d_ap, d_res = self.sctx.ap[:, i, :], self.sctx_res[i], 1, hc.ap[:, i, :], hcr[i]
            rstd = self.rstd_of(h_ap, [h_res], junk)
            self.stt("dve", tmp.ap, h_ap, rstd.ap, A1b[v].ap, ALU.mult, ALU.mult, [h_res, rstd.res, A1b[v].res], [tmp.res])
            self.tt("pool", d_ap, tmp.ap, B1b[v].ap, ALU.add, [tmp.res, B1b[v].res], [d_res])
        P.barrier()
        A.top = with_tmp
        NT = 256
        ctile = [A.alloc(BF16, [32, NT]) for _ in range(2)]
        stile = [A.alloc(BF16, [32, NT]) for _ in range(2)]
        yev = [A.alloc(BF16, [8, NT]) for _ in range(4)]
        nkt = 4096 // NT
        if self.dbg.startswith("odd1"):
            nkt = 1
        for kt in range(nkt):
            ct, stl = ctile[kt % 2], stile[kt % 2]
            P.dma("sp", ct.ap, self.Ctab[:, kt * NT:(kt + 1) * NT].rearrange("(c p) k -> p c k", p=128), [self.tab_res], [ct.res])
            P.dma("sp", stl.ap, self.Stab[:, kt * NT:(kt + 1) * NT].rearrange("(c p) k -> p c k", p=128), [self.tab_res], [stl.res])
            yc, ys = yev[(2 * kt) % 4], yev[(2 * kt + 1) % 4]
            for dc in range(8):
                for (tab, yo, bi) in ((ct, yc, 1), (stl, ys, 2)):
                    pb = self.bank[bi + 2 * (dc % 2)]
                    pj = Tile(pb.ap[:, 0:NT], pb.res)
                    for c in range(32):
                        self.mm(pj.ap, hl.ap[:, c, 128 * dc:128 * (dc + 1)], tab.ap[:, c, :], c == 0, c == 31,
                                [hlr[c], tab.res], [pj.res])
                    if bi == 1:
                        self.cp("act", yo.ap[:, dc, :], pj.ap, [pj.res], [yo.res])
                    else:
                        self.cp("dve", yo.ap[:, dc, :], pj.ap, [pj.res], [yo.res])
            P.dma("sp", self.Yc[:, kt * NT:(kt + 1) * NT].rearrange("(c p) k -> p c k", p=128), yc.ap, [yc.res], [self.Y_res[kt]])
            P.dma("sp", self.Ys[:, kt * NT:(kt + 1) * NT].rearrange("(c p) k -> p c k", p=128), ys.ap, [ys.res], [self.Y_res[kt]])
        if upd_ctx:
            ct, stl = ctile[nkt % 2], stile[nkt % 2]
            csrc = self.Ctab.rearrange("(t s) k -> t s k", s=16)[:, 0, 0:256].rearrange("(c p) k -> p c k", p=128)
            ssrc = self.Stab.rearrange("(t s) k -> t s k", s=16)[:, 0, 0:256].rearrange("(c p) k -> p c k", p=128)
            P.dma("sp", ct.ap[:, 0:2, :], csrc, [self.tab_res], [ct.res])
            P.dma("sp", stl.ap[:, 0:2, :], ssrc, [self.tab_res], [stl.res])
            for dc in range(8):
                for (tab, yo, bi) in ((ct, ycc, 1), (stl, ysc, 2)):
                    pb = self.bank[bi + 2 * (dc % 2)]
                    pj = Tile(pb.ap[:, 0:256], pb.res)
                    for c in range(2):
                        self.mm(pj.ap, hc.ap[:, c, 128 * dc:128 * (dc + 1)], tab.ap[:, c, :], c == 0, c == 1,
                                [hcr[c], tab.res], [pj.res])
                    self.ts("dve", yo.ap[:, dc, :], pj.ap, 4.0, None, ALU.mult, None, [pj.res], [yo.res])
        P.barrier()
        A.top = base2
        CD = A.alloc(BF16, [8, 1024])
        SDn = A.alloc(BF16, [8, 1024])
        wf = A.alloc(BF16, [8, 1024])
        P.dma("pool", wf.ap, self.w_fourier.rearrange("(k p) n -> p k n", p=128), [], [wf.res])
        yct = [A.alloc(BF16, [8, 256]) for _ in range(2)]
        yst = [A.alloc(BF16, [8, 256]) for _ in range(2)]
        hfT = [A.alloc(BF16, [8, 256]) for _ in range(2)]
        ht = [A.alloc(F32, [1024]) for _ in range(2)]
        junk = A.alloc(BF16, [1024])
        tmp = A.alloc(F32, [1024])
        ycc2, ysc2 = ycc, ysc
        mark = A.top
        self.gen_channel_tables(CD, SDn)
        A.top = mark
        units = [("lat", kt) for kt in range(nkt)] + ([("ctx", 0)] if upd_ctx else [])
        for ui, (kind, kt) in enumerate(units):
            v = 0 if kind == "lat" else 1
            if kind == "lat":
                yc, ys = yct[ui % 2], yst[ui % 2]
                P.dma("sp", yc.ap, self.Yc[:, kt * 256:(kt + 1) * 256].rearrange("(c p) k -> p c k", p=128), [self.Y_res[kt]], [yc.res])
                P.dma("sp", ys.ap, self.Ys[:, kt * 256:(kt + 1) * 256].rearrange("(c p) k -> p c k", p=128), [self.Y_res[kt]], [ys.res])
            else:
                yc, ys = ycc2, ysc2
            hf = hfT[ui % 2]
            for ec in range(8):
                pb = self.bank[1 + (ec % 2)]
                pj = Tile(pb.ap[:, 0:256], pb.res)
                for dc in range(8):
                    self.mm(pj.ap, CD.ap[:, dc, 128 * ec:128 * (ec + 1)], yc.ap[:, dc, :], dc == 0, False, [CD.res, yc.res], [pj.res])
                for dc in range(8):
                    self.mm(pj.ap, SDn.ap[:, dc, 128 * ec:128 * (ec + 1)], ys.ap[:, dc, :], False, dc == 7, [SDn.res, ys.res], [pj.res])
                self.cp("act" if ec % 2 else "dve", hf.ap[:, ec, :], pj.ap, [pj.res], [hf.res])
            for sub in range(2):
                for half in range(2):
                    pb = self.bank[3 + 2 * sub + half]
                    for ec in range(8):
                        self.mm(pb.ap, hf.ap[:, ec, sub * 128:(sub + 1) * 128], wf.ap[:, ec, half * 512:(half + 1) * 512],
                                ec == 0, ec == 7, [hf.res, wf.res], [pb.res])
                po_ap = self.psall[:, (3 + 2 * sub) * 512:(5 + 2 * sub) * 512]
                pres = [self.bank[3 + 2 * sub].res, self.bank[4 + 2 * sub].res]
                if kind == "lat":
                    t128 = kt * 2 + sub
                    h = ht[sub]
                    P.dma("sp", h.ap, self.x[128 * t128:128 * (t128 + 1), :], [], [h.res])
                    h_ap, h_res = h.ap, h.res
                else:
                    h_ap, h_res = self.sctx.ap[:, sub, :], self.sctx_res[sub]
                rstd = self.rstd_of(po_ap, pres, junk)
                self.stt("dve", tmp.ap, po_ap, rstd.ap, self.G[0][v].ap, ALU.mult, ALU.mult,
                         pres + [rstd.res, self.G[0][v].res], [tmp.res])
                self.tt("pool", h_ap, tmp.ap, h_ap, ALU.add, [tmp.res, h_res], [h_res])
                if kind == "lat":
                    P.dma("sp", self.out[128 * t128:128 * (t128 + 1), :], h_ap, [h_res], [self.out_res[t128]])


_CACHE = {}


def _get_nc(step, dbg=""):
    key = (step, dbg)
    if key not in _CACHE:
        _CACHE[key] = Builder(step, dbg).build()
    return _CACHE[key]


def _launch(step, per_core, shared, dbg=""):
    nc = _get_nc(step, dbg)
    in_maps = []
    for b in range(len(per_core)):
        m = dict(shared)
        m.update(per_core[b])
        in_maps.append(m)
    res = run_bass_kernel_spmd(nc, in_maps, core_ids=list(range(len(per_core))))
    return res.results


def _f32(a):
    return np.ascontiguousarray(a, dtype=np.float32)


def run_steps(inputs, steps, ncores=8, dbg=""):
    h = [_f32(inputs["x"][b]) for b in range(ncores)]
    s = [_f32(inputs["ctx"][b]) for b in range(ncores)]
    cs = [_f32(inputs["c"][b]) for b in range(ncores)]
    s5T = None
    for step in steps:
        kind, l = step
        e = l // 2
        shared = {"c_ctx": _f32(inputs["c_ctx"]), "w_mod": _f32(inputs["w_mod"][l]), "b_mod": _f32(inputs["b_mod"][l]),
                  "norm_g": _f32(inputs["norm_g"][l])}
        if kind == "mlp":
            shared["w_ff1"] = _f32(inputs["w_ff1"][l])
            shared["w_ff2"] = _f32(inputs["w_ff2"][l])
        elif kind == "odd":
            shared["w_fourier"] = _f32(inputs["w_fourier"][e])
        elif kind == "evenA":
            shared["w_in"] = _f32(inputs["w_in"][e])
            for n in ["s5_lam_re", "s5_lam_im", "s5_log_dt", "s5_b_re", "s5_b_im", "s5_c_re", "s5_c_im", "s5_d", "s5_w_glu"]:
                shared[n] = _f32(inputs[n][e])
        elif kind == "evenB":
            shared["w_in"] = _f32(inputs["w_in"][e])
            shared["w_out_even"] = _f32(inputs["w_out_even"][e])
            shared["na_rpb"] = _f32(inputs["na_rpb"][e])
        per_core = []
        for b in range(ncores):
            m = {"x": h[b], "c": cs[b], "ctx": s[b]}
            if kind == "evenB":
                m["s5T"] = s5T[b]
            per_core.append(m)
        res = _launch(step, per_core, shared, dbg)
        if kind == "evenA":
            s5T = [np.ascontiguousarray(r["s5T"]) for r in res]
        else:
            h = [_f32(r["out"]) for r in res]
            s = [_f32(r["sout"]) for r in res]
    return h, s, s5T


def kernel_multi(**inputs):
    steps = []
    for l in range(DEPTH):
        if l % 2 == 0:
            steps += [("evenA", l), ("evenB", l)]
        else:
            steps += [("odd", l)]
        steps += [("mlp", l)]
    h, s, _ = run_steps(inputs, steps, 8)
    return np.stack(h, 0)


def kernel(**inputs):
    nc = _get_nc(("fused", None))
    names = ["c_ctx", "w_mod", "b_mod", "norm_g", "w_in", "w_out_even", "s5_lam_re", "s5_lam_im", "s5_log_dt", "s5_b_re", "s5_b_im",
             "s5_c_re", "s5_c_im", "s5_d", "s5_w_glu", "na_rpb", "w_fourier", "w_ff1", "w_ff2"]
    shared = {n: _f32(inputs[n]) for n in names}
    ncores = 8
    in_maps = []
    for b in range(ncores):
        m = dict(shared)
        m["x"] = _f32(inputs["x"][b])
        m["c"] = _f32(inputs["c"][b])
        m["ctx"] = _f32(inputs["ctx"][b])
        in_maps.append(m)
    res = run_bass_kernel_spmd(nc, in_maps, core_ids=list(range(ncores)))
    return np.stack([_f32(r["out"]) for r in res.results], 0)
```
